# Optimizing a Trainium2 kernel written in Bass

```python
import math
import jax, jax.numpy as jnp
from jax import lax
import numpy as np

D_MODEL = 1024
BATCH = 16
SEQ = 256
DEPTH = 2
DEC_BATCH = 2
DEC_SEQ = 2048
PAST_LEN = 512

GRID_W = 64
N_MIXERS = 2
N_DIFF_LAYERS = (DEPTH + 1) // 2
N_SWA_LAYERS = DEPTH // 2
DIFF_HEADS = 8
DIFF_HD = 64
SWA_HEADS = 16
SWA_KV_HEADS = 4
SWA_GROUP = SWA_HEADS // SWA_KV_HEADS
SWA_HD = 64
ROT_DIM = 64
WINDOW = 128
BLOCK = 128
D_FF = -(-8 * D_MODEL // (3 * 256)) * 256
D_QKV_DIFF = 3 * DIFF_HEADS * 2 * DIFF_HD
D_QKV_SWA = (SWA_HEADS + 2 * SWA_KV_HEADS) * SWA_HD
ROPE_BASE = 10000.0
EPS = 1e-6
NEG_INF = -1e30

kernel_name = "hybrid_diff_swa_dit_step"


def rmsnorm(x, g):
    xf = x.astype(jnp.float32)
    y = xf * lax.rsqrt(jnp.mean(xf * xf, axis=-1, keepdims=True) + EPS)
    return (y * g.astype(jnp.float32)).astype(x.dtype)


def modulation(cond, w_mod, b_mod):
    m = jax.nn.silu(cond) @ w_mod + b_mod
    return jnp.split(m[:, None, :], 6, axis=-1)


def pre_norm_modulate(x, g, shift, scale):
    return rmsnorm(x, g) * (1 + scale) + shift


def post_norm_residual(x, y, g, gate):
    return x + gate * rmsnorm(y, g)


def swiglu(h, w_gate, w_up, w_down):
    return (jax.nn.silu(h @ w_gate) * (h @ w_up)) @ w_down


def axial_rope_tables(n_lat, rot_dim):
    rows = n_lat // GRID_W
    t = jnp.arange(rows * GRID_W)
    row = (t // GRID_W).astype(jnp.float32)
    col = (t % GRID_W).astype(jnp.float32)
    nf = rot_dim // 4
    inv = ROPE_BASE ** (-jnp.arange(nf, dtype=jnp.float32) / nf)
    ar = row[:, None] * inv[None, :]
    ac = col[:, None] * inv[None, :]
    ang = jnp.concatenate([ar, ar, ac, ac], axis=-1)
    return jnp.cos(ang), jnp.sin(ang)


def apply_axial_rope(x, cos, sin):
    x1, x2, x3, x4 = jnp.split(x, 4, axis=-1)
    rot = jnp.concatenate([-x2, x1, -x4, x3], axis=-1)
    shape = (1, x.shape[1]) + (1,) * (x.ndim - 3) + (x.shape[-1],)
    out = x.astype(jnp.float32) * cos.reshape(shape) + rot.astype(jnp.float32) * sin.reshape(shape)
    return out.astype(x.dtype)


def diff_lambda_value(lam_params, lam_init):
    lp = lam_params.astype(jnp.float32)
    return jnp.exp(jnp.sum(lp[0] * lp[1])) - jnp.exp(jnp.sum(lp[2] * lp[3])) + lam_init


def diff_project(h, w_qkv):
    B, L, _ = h.shape
    q, k, v = jnp.split(h @ w_qkv, 3, axis=-1)
    q = q.reshape(B, L, DIFF_HEADS, 2, DIFF_HD)
    k = k.reshape(B, L, DIFF_HEADS, 2, DIFF_HD)
    v = v.reshape(B, L, DIFF_HEADS, 2 * DIFF_HD)
    return q, k, v


def diff_block_attention(q, k, v, lam):
    B, Lq = q.shape[:2]
    nb = Lq // BLOCK
    scale = DIFF_HD ** -0.5
    qb = q.reshape(B, nb, BLOCK, DIFF_HEADS, 2, DIFF_HD).swapaxes(0, 1)

    def one_block(qi):
        s = jnp.einsum('bqhmd,bkhmd->bhmqk', qi, k).astype(jnp.float32) * scale
        p = jax.nn.softmax(s, axis=-1)
        a = p[:, :, 0] - lam * p[:, :, 1]
        return jnp.einsum('bhqk,bkhe->bqhe', a.astype(v.dtype), v)

    out = lax.map(one_block, qb)
    return out.swapaxes(0, 1).reshape(B, Lq, DIFF_HEADS, 2 * DIFF_HD)


def diff_output(o, subln_g, lam_init, w_o):
    B, L = o.shape[:2]
    o = rmsnorm(o, subln_g) * (1.0 - lam_init)
    return o.reshape(B, L, DIFF_HEADS * 2 * DIFF_HD) @ w_o


def swa_project(h, w_qkv):
    B, L, _ = h.shape
    nq = SWA_HEADS * SWA_HD
    nkv = SWA_KV_HEADS * SWA_HD
    qkv = h @ w_qkv
    q = qkv[..., :nq].reshape(B, L, SWA_KV_HEADS, SWA_GROUP, SWA_HD)
    k = qkv[..., nq:nq + nkv].reshape(B, L, SWA_KV_HEADS, SWA_HD)
    v = qkv[..., nq + nkv:].reshape(B, L, SWA_KV_HEADS, SWA_HD)
    return q, k, v


def sink_softmax(s, sink):
    sk = sink.astype(jnp.float32).reshape(SWA_KV_HEADS, SWA_GROUP)[None, :, :, None, None]
    m = jnp.maximum(jnp.max(s, axis=-1, keepdims=True), sk)
    e = jnp.exp(s - m)
    return e / (jnp.sum(e, axis=-1, keepdims=True) + jnp.exp(sk - m))


def swa_context_attention(q, k, v, sink):
    B, Lq = q.shape[:2]
    nb = Lq // BLOCK
    scale = SWA_HD ** -0.5
    qb = q.reshape(B, nb, BLOCK, SWA_KV_HEADS, SWA_GROUP, SWA_HD).swapaxes(0, 1)

    def one_block(qi):
        s = jnp.einsum('bqkgd,bjkd->bkgqj', qi, k).astype(jnp.float32) * scale
        p = sink_softmax(s, sink)
        return jnp.einsum('bkgqj,bjkd->bqkgd', p.astype(v.dtype), v)

    out = lax.map(one_block, qb)
    return out.swapaxes(0, 1).reshape(B, Lq, SWA_HEADS * SWA_HD)


def swa_latent_attention(q, k, v, k_ctx, v_ctx, sink):
    B, L = q.shape[:2]
    nb = L // BLOCK
    Lc = k_ctx.shape[1]
    span = BLOCK + 2 * WINDOW
    scale = SWA_HD ** -0.5
    pad = ((0, 0), (WINDOW, WINDOW), (0, 0), (0, 0))
    kp = jnp.pad(k, pad)
    vp = jnp.pad(v, pad)
    qb = q.reshape(B, nb, BLOCK, SWA_KV_HEADS, SWA_GROUP, SWA_HD).swapaxes(0, 1)

    def one_block(args):
        qi, i = args
        start = i * BLOCK
        kn = lax.dynamic_slice_in_dim(kp, start, span, axis=1)
        vn = lax.dynamic_slice_in_dim(vp, start, span, axis=1)
        q_pos = start + jnp.arange(BLOCK)
        k_pos = start - WINDOW + jnp.arange(span)
        valid = ((k_pos[None, :] >= 0) & (k_pos[None, :] < L)
                 & (jnp.abs(q_pos[:, None] - k_pos[None, :]) <= WINDOW))
        s_lat = jnp.einsum('bqkgd,bjkd->bkgqj', qi, kn).astype(jnp.float32) * scale
        s_lat = jnp.where(valid, s_lat, NEG_INF)
        s_ctx = jnp.einsum('bqkgd,bjkd->bkgqj', qi, k_ctx).astype(jnp.float32) * scale
        p = sink_softmax(jnp.concatenate([s_ctx, s_lat], axis=-1), sink).astype(v.dtype)
        return (jnp.einsum('bkgqj,bjkd->bqkgd', p[..., :Lc], v_ctx)
                + jnp.einsum('bkgqj,bjkd->bqkgd', p[..., Lc:], vn))

    out = lax.map(one_block, (qb, jnp.arange(nb)))
    return out.swapaxes(0, 1).reshape(B, L, SWA_HEADS * SWA_HD)


def setup_inputs(seed: int = 0) -> dict:
    key = jax.random.key(seed)
    ks = jax.random.split(key, 24)
    f32 = jnp.float32
    n = lambda k, s: jax.random.normal(k, s, f32)
    D = D_MODEL
    return {
        "x_prompt": n(ks[0], (BATCH, SEQ, D)),
        "x_sample": n(ks[1], (DEC_BATCH, DEC_SEQ, D)),
        "cache_diff_k": n(ks[2], (DEC_BATCH, N_DIFF_LAYERS, PAST_LEN, DIFF_HEADS, 2 * DIFF_HD)),
        "cache_diff_v": n(ks[3], (DEC_BATCH, N_DIFF_LAYERS, PAST_LEN, DIFF_HEADS, 2 * DIFF_HD)),
        "cache_swa_k": n(ks[4], (DEC_BATCH, N_SWA_LAYERS, PAST_LEN, SWA_KV_HEADS, SWA_HD)),
        "cache_swa_v": n(ks[5], (DEC_BATCH, N_SWA_LAYERS, PAST_LEN, SWA_KV_HEADS, SWA_HD)),
        "c": n(ks[6], (DEC_BATCH, D)),
        "c_ctx": n(ks[7], (D,)),
        "w_mod": n(ks[8], (DEPTH, D, 6 * D)) * (0.5 * D ** -0.5),
        "b_mod": n(ks[9], (DEPTH, 6 * D)) * 0.01,
        "norm_g": 1.0 + 0.05 * n(ks[10], (DEPTH, 4, D)),
        "w_qkv_diff": n(ks[11], (N_DIFF_LAYERS, D, D_QKV_DIFF)) * D ** -0.5,
        "diff_lambda": n(ks[12], (N_DIFF_LAYERS, 4, DIFF_HD)) * 0.1,
        "diff_subln_g": 1.0 + 0.05 * n(ks[13], (N_DIFF_LAYERS, 2 * DIFF_HD)),
        "w_o_diff": n(ks[14], (N_DIFF_LAYERS, DIFF_HEADS * 2 * DIFF_HD, D)) * (DIFF_HEADS * 2 * DIFF_HD) ** -0.5,
        "w_qkv_swa": n(ks[15], (N_SWA_LAYERS, D, D_QKV_SWA)) * D ** -0.5,
        "swa_sink": n(ks[16], (N_SWA_LAYERS, SWA_HEADS)) * 0.5,
        "w_o_swa": n(ks[17], (N_SWA_LAYERS, SWA_HEADS * SWA_HD, D)) * (SWA_HEADS * SWA_HD) ** -0.5,
        "w_gate": n(ks[18], (DEPTH, D, D_FF)) * D ** -0.5,
        "w_up": n(ks[19], (DEPTH, D, D_FF)) * D ** -0.5,
        "w_down": n(ks[20], (DEPTH, D_FF, D)) * D_FF ** -0.5,
    }


def reference(x_prompt, x_sample, cache_diff_k, cache_diff_v, cache_swa_k, cache_swa_v, c, c_ctx,
              w_mod, b_mod, norm_g, w_qkv_diff, diff_lambda, diff_subln_g, w_o_diff,
              w_qkv_swa, swa_sink, w_o_swa, w_gate, w_up, w_down):
    Bp, Lp = x_prompt.shape[:2]
    Bs, Ls = x_sample.shape[:2]
    Lc = cache_diff_k.shape[2]
    cos, sin = axial_rope_tables(Ls, ROT_DIM)
    xp, xs = x_prompt, x_sample
    diff_k_out, diff_v_out, swa_k_out, swa_v_out = [], [], [], []

    for i in range(DEPTH):
        mp = modulation(c_ctx[None, :], w_mod[i], b_mod[i])
        ms = modulation(c, w_mod[i], b_mod[i])
        hp = pre_norm_modulate(xp, norm_g[i, 0], mp[0], mp[1])
        hs = pre_norm_modulate(xs, norm_g[i, 0], ms[0], ms[1])
        j = i // N_MIXERS
        if i % N_MIXERS == 0:
            lam_init = 0.8 - 0.6 * math.exp(-0.3 * i)
            lam = diff_lambda_value(diff_lambda[j], lam_init)
            qp, kp, vp = diff_project(hp, w_qkv_diff[j])
            yp = diff_output(diff_block_attention(qp, kp, vp, lam), diff_subln_g[j], lam_init, w_o_diff[j])
            diff_k_out.append(kp.reshape(Bp, Lp, DIFF_HEADS, 2 * DIFF_HD))
            diff_v_out.append(vp)
            qs, ks_, vs = diff_project(hs, w_qkv_diff[j])
            qs = apply_axial_rope(qs, cos, sin)
            ks_ = apply_axial_rope(ks_, cos, sin)
            kc = cache_diff_k[:, j].reshape(Bs, Lc, DIFF_HEADS, 2, DIFF_HD)
            k_all = jnp.concatenate([kc, ks_], axis=1)
            v_all = jnp.concatenate([cache_diff_v[:, j], vs], axis=1)
            ys = diff_output(diff_block_attention(qs, k_all, v_all, lam), diff_subln_g[j], lam_init, w_o_diff[j])
        else:
            qp, kp, vp = swa_project(hp, w_qkv_swa[j])
            yp = swa_context_attention(qp, kp, vp, swa_sink[j]) @ w_o_swa[j]
            swa_k_out.append(kp)
            swa_v_out.append(vp)
            qs, ks_, vs = swa_project(hs, w_qkv_swa[j])
            qs = apply_axial_rope(qs, cos, sin)
            ks_ = apply_axial_rope(ks_, cos, sin)
            ys = swa_latent_attention(qs, ks_, vs, cache_swa_k[:, j], cache_swa_v[:, j], swa_sink[j]) @ w_o_swa[j]
        xp = post_norm_residual(xp, yp, norm_g[i, 1], mp[2])
        xs = post_norm_residual(xs, ys, norm_g[i, 1], ms[2])
        hp = pre_norm_modulate(xp, norm_g[i, 2], mp[3], mp[4])
        hs = pre_norm_modulate(xs, norm_g[i, 2], ms[3], ms[4])
        xp = post_norm_residual(xp, swiglu(hp, w_gate[i], w_up[i], w_down[i]), norm_g[i, 3], mp[5])
        xs = post_norm_residual(xs, swiglu(hs, w_gate[i], w_up[i], w_down[i]), norm_g[i, 3], ms[5])

    new_diff_k = jnp.stack(diff_k_out, axis=1)
    new_diff_v = jnp.stack(diff_v_out, axis=1)
    new_swa_k = jnp.stack(swa_k_out, axis=1)
    new_swa_v = jnp.stack(swa_v_out, axis=1)
    return (xp, xs, new_diff_k, new_diff_v, new_swa_k, new_swa_v)
```

```python
import os
import numpy as np
import concourse.bass as bass
import concourse.mybir as mybir
from concourse.bass_utils import run_bass_kernel_spmd
from contextlib import ExitStack

F32 = mybir.dt.float32
BF16 = mybir.dt.bfloat16
AF = mybir.ActivationFunctionType
ALU = mybir.AluOpType

D = 1024
KC = 8
DFF = 2816
NF = 22
TP = 512
TW = 768
TT = 1280
TFULL = 2048
LC = 512
EPS = 1e-6
NCORES = 8
STAGE = int(os.environ.get("KSTAGE", "99"))
KHEADS = int(os.environ.get("KHEADS", "8"))
KSKIP = os.environ.get("KSKIP", "")


class Buf:
    __slots__ = ("name", "writers", "readers", "dsem", "ndma", "war", "excl")

    def __init__(self, name, excl=False):
        self.name = name
        self.excl = excl
        self.writers = []
        self.readers = []
        self.war = []
        self.dsem = None
        self.ndma = 0


class Op:
    __slots__ = ("eng", "idx", "fn", "deps", "dma", "buf", "ordinal", "flag", "count", "waits", "ring")

    def __init__(self, eng, idx, fn):
        self.eng = eng
        self.idx = idx
        self.fn = fn
        self.deps = []
        self.dma = False
        self.buf = None
        self.ordinal = 0
        self.flag = False
        self.count = 0
        self.waits = []
        self.ring = False


ENGS = ["pe", "act", "dve", "pool", "sp"]


class Prog:
    def __init__(self, nc, es):
        self.nc = nc
        self.es = es
        self.ops = {e: [] for e in ENGS}
        self.dma_bufs = []
        self.out_dmas = []

    def op(self, eng, fn, reads=(), writes=(), dma_buf=None, is_out=False, ring=False):
        o = Op(eng, len(self.ops[eng]), fn)
        o.ring = ring
        deps = o.deps
        for b in reads:
            for w in b.writers:
                deps.append((w, True))
            if b.excl:
                for r in b.readers:
                    if r.eng != eng:
                        deps.append((r, False))
            b.readers.append(o)
        for b in writes:
            if b.readers:
                b.war = [r for r in b.readers if r is not o]
                b.writers = [o]
                b.readers = []
            else:
                b.writers.append(o)
            for r in b.war:
                deps.append((r, False))
        if dma_buf is not None:
            o.dma = True
            o.buf = dma_buf
            if dma_buf.dsem is None:
                self.dma_bufs.append(dma_buf)
                dma_buf.dsem = True
            dma_buf.ndma += 1
            o.ordinal = dma_buf.ndma
            if is_out:
                self.out_dmas.append(o)
        self.ops[eng].append(o)
        return o

    def alias(self, new_bufs, old_bufs):
        pend = []
        for b in old_bufs:
            pend.extend(b.readers)
            pend.extend(b.writers)
        for nb in new_bufs:
            nb.readers.extend(pend)
            nb.war = []

    def finish(self):
        o = Op("sp", len(self.ops["sp"]), None)
        for d in self.out_dmas:
            o.deps.append((d, True))
        self.ops["sp"].append(o)

    def emit(self):
        nc = self.nc
        es = self.es
        esem = {e: es.enter_context(nc.semaphore("s_" + e)) for e in ["pe", "act", "dve", "pool"]}
        for i, b in enumerate(self.dma_bufs):
            b.dsem = es.enter_context(nc.semaphore("d%d" % i))
        for e in ENGS:
            waited = {}
            for o in self.ops[e]:
                need = {}
                for (d, raw) in o.deps:
                    if d.dma:
                        key = ("d", id(d.buf))
                        val = d.ordinal
                        if waited.get(key, 0) >= val:
                            continue
                        if need.get(key, (0, None))[0] < val:
                            need[key] = (val, d)
                    else:
                        if d.eng == e and e == "pe":
                            continue
                        key = ("e", d.eng)
                        val = d.idx + 1
                        if waited.get(key, 0) >= val:
                            continue
                        if need.get(key, (0, None))[0] < val:
                            need[key] = (val, d)
                for key, (val, d) in need.items():
                    waited[key] = val
                    if not d.dma:
                        d.flag = True
                    o.waits.append(d)
        for e in ["pe", "act", "dve", "pool"]:
            c = 0
            for o in self.ops[e]:
                if o.flag and not o.dma:
                    c += 1
                    o.count = c
        handles = {"pe": "tensor", "act": "scalar", "dve": "vector", "pool": "gpsimd", "sp": "sync"}
        stats = {}
        with nc.Block() as block:
            for e in ENGS:
                ops = self.ops[e]
                stats[e] = len(ops)

                def body(eng, ops=ops, e=e):
                    for o in ops:
                        for d in o.waits:
                            if d.dma:
                                eng.wait_ge(d.buf.dsem, 16 * d.ordinal)
                            else:
                                eng.wait_ge(esem[d.eng], d.count)
                        if o.fn is None:
                            continue
                        inst = o.fn(eng)
                        if o.ring:
                            assert not o.flag
                            inst.then_inc(self.ring_sem, 16)
                        elif o.dma:
                            inst.then_inc(o.buf.dsem, 16)
                        elif o.flag:
                            inst.then_inc(esem[e], 1)

                getattr(block, handles[e])(body)
        return stats


def build_program():
    nc = bass.Bass("TRN2", target_bir_lowering=False, monotonic_sem_count=0)
    es = ExitStack()
    P = Prog(nc, es)

    def din(name, shape):
        return nc.dram_tensor(name, list(shape), F32, kind="ExternalInput").ap()

    def dout(name, shape):
        return nc.dram_tensor(name, list(shape), F32, kind="ExternalOutput").ap()

    xp_d = din("xp", [TP, D])
    xw_d = din("xw", [TW, D])
    xf_d = din("xf", [TFULL, D])
    ckd_d = din("ckd", [8, 128, 4, 128])
    cvd_d = din("cvd", [8, 128, 4, 128])
    cks_d = din("cks", [128, 4, 256])
    cvs_d = din("cvs", [128, 4, 256])
    NSM = 16 + 64 + 96 + 1 + 256 + 16
    sm_d = din("sm", [128, NSM])
    ident_d = din("ident", [128, 128])
    ropew_d = din("ropew", [128, 2, TW])
    ropef_d = din("ropef", [128, 2, TFULL])
    maskb_d = din("maskb", [128, 8, 128])
    wmod_d = din("wmod", [2, 8, 128, 8, 768])
    wd0_d = din("wd0", [8, 128, 8, 640])
    wo0_d = din("wo0", [2, 128, 8, 512])
    ws1q_d = din("ws1q", [4, 128, 8, 512])
    ws1k_d = din("ws1k", [128, 8, 768])
    wo1_d = din("wo1", [4, 64, 16, 256])
    wgu_d = din("wgu", [2, 8, 128, 3, 8, 256])
    wdn_d = din("wdn", [2, 4, 128, 2, 22, 128])

    yp_d = dout("yp", [TP, D])
    ys_d = dout("ys", [512, D])
    ndk_d = dout("ndk", [TP, 1024])
    ndv_d = dout("ndv", [TP, 1024])
    nsk_d = dout("nsk", [TP, 256])
    nsv_d = dout("nsv", [TP, 256])

    def sb(name, shape, dt):
        return es.enter_context(nc.sbuf_tensor(name, list(shape), dt))

    xT = sb("xT", [128, KC, TT], F32)
    hT = sb("hT", [128, KC, TT], BF16)
    ytmp = hT.bitcast(F32)
    BIG = sb("BIG", [128, 28160], BF16)
    slots = [sb("wslot%d" % i, [128, 6144], BF16) for i in range(2)]
    kTh = sb("kTh", [128, 3072], BF16)
    Vh = sb("Vh", [128, 3584], BF16)
    qTh = sb("qTh", [128, TT], BF16)
    Es = [sb("E%d" % i, [128, 512], BF16) for i in range(4)]
    xin = [sb("xin%d" % i, [128, 1024], F32) for i in range(2)]
    cst = sb("cst", [128, 1024], F32)
    ropeW = sb("ropeW", [128, 2, TW], F32)
    ropeF = sb("ropeF", [128, 2, TFULL], BF16)
    maskb = sb("maskbs", [128, 8, 128], BF16)
    ident = sb("idents", [128, 128], F32)
    identb = sb("identb", [128, 128], BF16)
    ones = sb("ones", [128, 128], BF16)
    sm = sb("sms", [128, NSM], F32)
    modT = sb("modT", [128, 2, 48, 2], F32)
    der = sb("der", [128, 2, 4, 8, 2], F32)
    scb = sb("scb", [128, 8, 2], BF16)
    lamt = sb("lamt", [128, 8], F32)
    sinkE = sb("sinkE", [128, 16], F32)
    sqs = [sb("sq%d" % i, [128, 512], BF16) for i in range(2)]
    tmps = [sb("tmp%d" % i, [128, 512], F32) for i in range(4)]
    T1 = sb("T1", [128, 512], F32)
    ostage = [sb("ost%d" % i, [128, 512], F32) for i in range(2)]

    banks = [es.enter_context(nc.psum_tensor("bank%d" % i, [128, 512], F32)) for i in range(8)]
    Bbank = [Buf("bank%d" % i, excl=True) for i in range(8)]

    B_xT = [Buf("xT_p"), Buf("xT_w0"), Buf("xT_w1")]
    B_hT = [Buf("hT_p"), Buf("hT_w0"), Buf("hT_w1")]
    B_slot = [Buf("slot0"), Buf("slot1")]
    B_kTh = Buf("kTh_p"), Buf("kTh_f"), Buf("kTh_c")
    B_Vh = Buf("Vh_p"), Buf("Vh_f"), Buf("Vh_c")
    B_qTh = [Buf("qTh_p"), Buf("qTh_w")]
    B_E = [Buf("E%d" % i) for i in range(4)]
    B_xin = [Buf("xin0"), Buf("xin1")]
    B_cst = Buf("cst")
    B_cstv = Buf("cstv")
    B_const = Buf("consts")
    B_sm = Buf("sm")
    B_mod = [Buf("mod0"), Buf("mod1")]
    B_der = [Buf("der0"), Buf("der1")]
    B_scb = Buf("scb")
    B_lam = Buf("lam")
    B_sq = [Buf("sq0"), Buf("sq1")]
    B_tmp = [Buf("tmp%d" % i) for i in range(4)]
    B_T1 = Buf("T1")
    B_ost = [Buf("ost0"), Buf("ost1")]
    B_hfT = [Buf("hfT%d" % i) for i in range(4)]
    B_oT = [Buf("oT_p"), Buf("oT_w0"), Buf("oT_w1")]
    B_aT = [Buf("aT0"), Buf("aT1"), Buf("aT2")]
    B_qTs = [Buf("qTs_p"), Buf("qTs_s")]
    B_kTs = Buf("kTs")
    B_oS = [Buf("oS_p"), Buf("oS_s")]

    hfT = BIG[:, 0:16384].rearrange("p (k t) -> p k t", k=8)
    oT = BIG[:, 16384:16384 + 10240].rearrange("p (k t) -> p k t", k=8)
    aT = BIG[:, 0:28160].rearrange("p (f t) -> p f t", f=22)
    qTs = BIG[:, 0:8192].rearrange("p (k t) -> p k t", k=8)
    kTs = BIG[:, 8192:8192 + 3584].rearrange("p (k t) -> p k t", k=2)
    oS = BIG[:, 11776:11776 + 16384].rearrange("p (h t) -> p h t", h=16)

    TCH = [(0, 512), (512, 512), (1024, 256)]

    rr = {"bank": 0, "tmp": 0, "sq": 0, "slot": 0, "xin": 0, "ost": 0, "E": 0}

    def nxt(kind, n):
        v = rr[kind]
        rr[kind] = (v + 1) % n
        return v

    reserved = set()

    def nbank():
        while True:
            b = nxt("bank", 8)
            if b not in reserved:
                return b

    open_grp = {}

    def mm(bi, out_ap, lhsT, rhs, start, stop, reads):
        if start and open_grp.get(bi):
            import traceback
            traceback.print_stack(limit=6)
            print("OPEN GROUP on bank", bi, "opened at:", open_grp[bi])
        if start:
            import traceback
            open_grp[bi] = "".join(traceback.format_stack(limit=5)[:-1])
        if stop:
            open_grp[bi] = None
        P.op("pe", lambda e: e.matmul(out_ap, lhsT, rhs, start=start, stop=stop),
             reads=reads, writes=[Bbank[bi]])

    def act(out_ap, in_ap, func, reads, writes, bias=None, scale=None):
        kw = {}
        if bias is not None:
            kw["bias"] = bias
        if scale is not None:
            kw["scale"] = scale
        P.op("act", lambda e: e.activation(out_ap, in_ap, func, **kw), reads=reads, writes=writes)

    def dve_tt(out_ap, a, b, op, reads, writes, eng="dve"):
        P.op(eng, lambda e: e.tensor_tensor(out_ap, a, b, op), reads=reads, writes=writes)

    def dve_stt(out_ap, a, scalar, b, op0, op1, reads, writes):
        P.op("dve", lambda e: e.scalar_tensor_tensor(out_ap, a, scalar, b, op0, op1), reads=reads, writes=writes)

    def dve_ts(out_ap, a, s1, s2, op0, op1, reads, writes, eng="dve"):
        if op1 is None:
            P.op(eng, lambda e: e.tensor_scalar(out_ap, a, s1, None, op0), reads=reads, writes=writes)
        else:
            P.op(eng, lambda e: e.tensor_scalar(out_ap, a, s1, s2, op0, op1), reads=reads, writes=writes)

    def dve_copy(out_ap, in_ap, reads, writes, eng="dve"):
        P.op(eng, lambda e: e.tensor_copy(out_ap, in_ap), reads=reads, writes=writes)

    def dve_recip(out_ap, in_ap, reads, writes):
        P.op("dve", lambda e: e.reciprocal(out_ap, in_ap), reads=reads, writes=writes)

    def dma_in(eng, out_ap, in_ap, buf, extra_writes=(), cast=False):
        if eng == "pool":
            P.op("pool", lambda e: e.dma_start(out=out_ap, in_=in_ap, max_dma_last_dim=4096),
                 reads=[], writes=[buf] + list(extra_writes), dma_buf=Buf("sw"))
        elif cast:
            P.op(eng, lambda e: e.dma_start(out=out_ap, in_=in_ap, max_dma_last_dim=4096),
                 reads=[], writes=[buf] + list(extra_writes), dma_buf=buf)
        else:
            P.op(eng, lambda e: e.dma_start(out=out_ap, in_=in_ap),
                 reads=[], writes=[buf] + list(extra_writes), dma_buf=buf)

    def dma_out(out_ap, in_ap, buf):
        P.op("sp", lambda e: e.dma_start(out=out_ap, in_=in_ap), reads=[buf], writes=[], dma_buf=buf, is_out=True)

    def load_w(srcs, nelem, view=None, parts=128):
        si = nxt("slot", 2)
        if not isinstance(srcs, (list, tuple)):
            srcs = [srcs]
        for i, src_ap in enumerate(srcs):
            dst = slots[si][0:parts, i * nelem:(i + 1) * nelem]
            dma_in("pool", dst, src_ap, B_slot[si], cast=True)
        return si

    dma_in("sp", sm[:, :], sm_d, B_sm)
    dma_in("sp", ident[:, :], ident_d, B_const)
    dma_in("sp", ropeW[:, :, :], ropew_d, B_const)
    dma_in("pool", ropeF[:, :, :].rearrange("p a t -> p (a t)"), ropef_d.rearrange("p a t -> p (a t)"), B_const, cast=True)
    dma_in("pool", maskb[:, :, :].rearrange("p a t -> p (a t)"), maskb_d.rearrange("p a t -> p (a t)"), B_const, cast=True)
    P.op("dve", lambda e: e.memset(ones[:, :], 1.0), reads=[], writes=[B_const])
    dve_copy(identb[:, :], ident[:, :], [B_const], [B_const])

    O_COND, O_NG, O_BM, O_SUBG, O_LAM, O_SINK = 0, 16, 80, 176, 177, 433
    cond_v = sm[:, O_COND:O_COND + 16].rearrange("p (k c) -> p k c", c=2)
    act(scb[:, :, :], cond_v, AF.Silu, [B_sm], [B_scb])
    LAM_INIT = 0.8 - 0.6 * float(np.exp(-0.3 * 0))
    lam_v = sm[:, O_LAM:O_LAM + 256].rearrange("p (a d) -> p a d", a=4)
    P.op("dve", lambda e: e.tensor_tensor(tmps[0][:, 0:64], lam_v[:, 0, :], lam_v[:, 1, :], ALU.mult),
         reads=[B_sm], writes=[B_tmp[0]])
    P.op("dve", lambda e: e.tensor_tensor(tmps[0][:, 64:128], lam_v[:, 2, :], lam_v[:, 3, :], ALU.mult),
         reads=[B_sm], writes=[B_tmp[0]])
    P.op("dve", lambda e: e.reduce_sum(lamt[:, 0:2], tmps[0][:, 0:128].rearrange("p (a d) -> p a d", a=2),
                                       mybir.AxisListType.X), reads=[B_tmp[0]], writes=[B_lam])
    act(lamt[:, 2:4], lamt[:, 0:2], AF.Exp, [B_lam], [B_lam])
    dve_tt(lamt[:, 4:5], lamt[:, 3:4], lamt[:, 2:3], ALU.subtract, [B_lam], [B_lam])
    dve_ts(lamt[:, 4:5], lamt[:, 4:5], -LAM_INIT, None, ALU.add, None, [B_lam], [B_lam])
    dve_ts(lamt[:, 5:6], sm[:, O_SUBG:O_SUBG + 1], 1.0 - LAM_INIT, None, ALU.mult, None, [B_sm, B_lam], [B_lam])
    act(sinkE[:, :], sm[:, O_SINK:O_SINK + 16], AF.Exp, [B_sm], [B_lam])

    def modulation(l):
        bi = nbank()
        for piece in range(8):
            si = load_w(wmod_d[l, piece].rearrange("p k c -> p (k c)"), 6144)
            wv = slots[si][:, 0:6144].rearrange("p (k c) -> p k c", k=8)
            for oc in range(6):
                o = piece * 6 + oc
                for k in range(KC):
                    mm(bi, banks[bi][:, o * 2:o * 2 + 2], wv[:, k, oc * 128:(oc + 1) * 128], scb[:, k, :],
                       k == 0, k == KC - 1, [B_slot[si], B_scb])
        bv = banks[bi][:, 0:96].rearrange("p (o c) -> p o c", c=2)
        for c in range(2):
            dve_tt(modT[:, l, :, c], bv[:, :, c], sm[:, O_BM + l * 48:O_BM + (l + 1) * 48], ALU.add,
                   [Bbank[bi], B_sm], [B_mod[l]])
        for c in range(2):
            def g(n):
                return sm[:, O_NG + (l * 4 + n) * 8: O_NG + (l * 4 + n) * 8 + 8]
            dve_stt(der[:, l, 0, :, c], modT[:, l, 8:16, c], 1.0, g(0), ALU.add, ALU.mult, [B_mod[l], B_sm], [B_der[l]])
            dve_tt(der[:, l, 1, :, c], modT[:, l, 16:24, c], g(1), ALU.mult, [B_mod[l], B_sm], [B_der[l]])
            dve_stt(der[:, l, 2, :, c], modT[:, l, 32:40, c], 1.0, g(2), ALU.add, ALU.mult, [B_mod[l], B_sm], [B_der[l]])
            dve_tt(der[:, l, 3, :, c], modT[:, l, 40:48, c], g(3), ALU.mult, [B_mod[l], B_sm], [B_der[l]])

    def mod_scalars(l, which, c):
        def gs(k):
            return der[:, l, 2 * which, k, c:c + 1]

        def sh(k):
            return modT[:, l, 24 * which + k, c:c + 1]

        def gg(k):
            return der[:, l, 2 * which + 1, k, c:c + 1]
        return gs, sh, gg

    def load_T(src_d, row0, dst, dcol0, dbufs):
        xi = nxt("xin", 2)
        dma_in("sp", xin[xi][:, :], src_d[row0:row0 + 128, :], B_xin[xi])
        for half in range(2):
            bi = nbank()
            for j in range(4):
                cidx = half * 4 + j
                P.op("pe", lambda e, bi=bi, j=j, cidx=cidx, xi=xi: e.transpose(
                    banks[bi][:, j * 128:(j + 1) * 128], xin[xi][:, cidx * 128:(cidx + 1) * 128], ident[:, :]),
                    reads=[B_xin[xi], B_const], writes=[Bbank[bi]])
            src = banks[bi][:, :].rearrange("p (j t) -> p j t", j=4)
            dsta = dst[:, half * 4:half * 4 + 4, dcol0:dcol0 + 128]
            if half == 0:
                act(dsta, src, AF.Copy, [Bbank[bi]], dbufs)
            else:
                dve_copy(dsta, src, [Bbank[bi]], dbufs)

    def rstd_from_bank(bs, n, nfeat, br):
        t = nxt("tmp", 4)
        act(tmps[t][:, 0:n], banks[bs][:, 0:n], AF.Ln, [Bbank[bs]], [B_tmp[t]], bias=EPS, scale=1.0 / nfeat)
        act(banks[br][:, 0:n], tmps[t][:, 0:n], AF.Exp, [B_tmp[t]], [Bbank[br]], scale=-0.5)

    def prenorm(src, sbufs, c0, n, dst, dbufs, dc0, l, which, c):
        gs, sh, _ = mod_scalars(l, which, c)
        bs = nbank()
        for k in range(KC):
            q = nxt("sq", 2)
            act(sqs[q][:, 0:n], src[:, k, c0:c0 + n], AF.Square, sbufs, [B_sq[q]])
            mm(bs, banks[bs][:, 0:n], ones[:, :], sqs[q][:, 0:n], k == 0, k == KC - 1, [B_sq[q], B_const])
        br = nbank()
        rstd_from_bank(bs, n, D, br)
        for k in range(KC):
            t = nxt("tmp", 4)
            dve_tt(tmps[t][:, 0:n], src[:, k, c0:c0 + n], banks[br][:, 0:n], ALU.mult, sbufs + [Bbank[br]], [B_tmp[t]])
            act(dst[:, k, dc0:dc0 + n], tmps[t][:, 0:n], AF.Identity, [B_tmp[t], B_mod[l], B_der[l]], dbufs,
                bias=sh(k), scale=gs(k))

    class PostNorm:
        def __init__(self, c0, n, xbufs, l, which, c):
            self.c0, self.n, self.xbufs, self.l, self.which, self.c = c0, n, xbufs, l, which, c
            self.bs = nbank()
            reserved.add(self.bs)
            self.cnt = 0

        def add(self, bi, dch):
            n = self.n
            yv = ytmp[:, dch, 0:n]
            act(yv, banks[bi][:, 0:n], AF.Copy, [Bbank[bi]], B_hT)
            q = nxt("sq", 2)
            act(sqs[q][:, 0:n], banks[bi][:, 0:n], AF.Square, [Bbank[bi]], [B_sq[q]])
            mm(self.bs, banks[self.bs][:, 0:n], ones[:, :], sqs[q][:, 0:n], self.cnt == 0, self.cnt == KC - 1,
               [B_sq[q], B_const])
            self.cnt += 1

        def finish(self):
            n, c0 = self.n, self.c0
            _, _, gg = mod_scalars(self.l, self.which, self.c)
            br = nbank()
            reserved.discard(self.bs)
            rstd_from_bank(self.bs, n, D, br)
            for dch in range(KC):
                t = nxt("tmp", 4)
                dve_tt(tmps[t][:, 0:n], ytmp[:, dch, 0:n], banks[br][:, 0:n], ALU.mult, B_hT + [Bbank[br]], [B_tmp[t]])
                xa = xT[:, dch, c0:c0 + n]
                dve_stt(xa, tmps[t][:, 0:n], gg(dch), xa, ALU.mult, ALU.add,
                        [B_tmp[t], B_der[self.l]] + self.xbufs, self.xbufs)

    def linear_fm(bi, wfun, rhsfun, n, reads, nk=KC):
        for k in range(nk):
            mm(bi, banks[bi][:, 0:n], wfun(k), rhsfun(k), k == 0, k == nk - 1, reads)

    def rope_epilogue(ba, bb, n, cos_ap, sin_ap, out_ap, obufs):
        t1 = nxt("tmp", 4)
        dve_tt(tmps[t1][:, 0:n], banks[ba][:, 0:n], cos_ap, ALU.mult, [Bbank[ba], B_const], [B_tmp[t1]])
        t2 = nxt("tmp", 4)
        dve_tt(tmps[t2][:, 0:n], banks[bb][:, 0:n], sin_ap, ALU.mult, [Bbank[bb], B_const], [B_tmp[t2]])
        dve_tt(out_ap, tmps[t1][:, 0:n], tmps[t2][:, 0:n], ALU.add, [B_tmp[t1], B_tmp[t2]], obufs)

    SCALE = 0.125

    def attn_job(grp, q_ap, n, blocks, e, qreads):
        sbk = [grp * 4, grp * 4 + 1]
        bo, bl = grp * 4 + 2, grp * 4 + 3
        nb = len(blocks)

        def qk(j):
            k_ap, kreads, _, _, mi = blocks[j]
            s = sbk[j % 2]
            mm(s, banks[s][:, 0:n], k_ap, q_ap, True, mi is None, qreads + kreads)
            if mi is not None:
                for g in range(n // 128):
                    mm(s, banks[s][:, g * 128:(g + 1) * 128], identb[:, :], maskb[:, mi, :], False,
                       g == n // 128 - 1, [B_const])

        qk(0)
        for j in range(nb):
            if j + 1 < nb:
                qk(j + 1)
            s = sbk[j % 2]
            ei = nxt("E", 4)
            act(Es[ei][:, 0:n], banks[s][:, 0:n], AF.Exp, [Bbank[s]], [B_E[ei]], scale=SCALE)
            _, _, v_ap, vreads, _ = blocks[j]
            mm(bo, banks[bo][0:e, 0:n], v_ap, Es[ei][:, 0:n], j == 0, j == nb - 1, [B_E[ei]] + vreads)
            mm(bl, banks[bl][0:e, 0:n], ones[:, 0:e], Es[ei][:, 0:n], j == 0, j == nb - 1, [B_E[ei], B_const])
        return bo, bl

    if STAGE >= -4:
        modulation(0)

    for t in range(4):
        load_T(xp_d, t * 128, xT, t * 128, [B_xT[0]])
    for t in range(6):
        load_T(xw_d, t * 128, xT, 512 + t * 128, [B_xT[1] if t < 4 else B_xT[2]])

    xfT = ytmp
    for fc in range(4 if STAGE >= -3 else 0):
        for t in range(4):
            load_T(xf_d, fc * 512 + t * 128, xfT, t * 128, B_hT)
        prenorm(xfT, B_hT, 0, 512, hfT, [B_hfT[fc]], fc * 512, 0, 0, 1)
    for ci, (c0, n) in enumerate(TCH if STAGE >= -3 else []):
        prenorm(xT, [B_xT[ci]], c0, n, hT, [B_hT[ci]], c0, 0, 0, 0 if ci == 0 else 1)

    for hd in range(KHEADS if STAGE >= -2 else 0):
        si = load_w(wd0_d[hd].rearrange("p k c -> p (k c)"), 5120)
        wv = slots[si][:, 0:5120].rearrange("p (k c) -> p k c", k=8)
        WQ, WQR, WKR, WK, WV = 0, 128, 256, 384, 512
        rs = [B_slot[si]]
        if "c" not in KSKIP:
            dma_in("sp", cst[:, 0:512].rearrange("p (j c) -> p j c", j=4), ckd_d[hd], B_cst)
            bi = nbank()
            for j in range(4):
                P.op("pe", lambda e, bi=bi, j=j: e.transpose(banks[bi][:, j * 128:(j + 1) * 128],
                                                            cst[:, j * 128:(j + 1) * 128], ident[:, :]),
                     reads=[B_cst, B_const], writes=[Bbank[bi]])
            act(kTh[:, 2560:3072], banks[bi][:, :], AF.Copy, [Bbank[bi]], [B_kTh[2]])
            dma_in("sp", cst[:, 512:1024], cvd_d[hd].rearrange("p j c -> p (j c)"), B_cstv)
            dve_copy(Vh[:, 2560:3072], cst[:, 512:1024], [B_cstv], [B_Vh[2]])
        bi = nbank()
        linear_fm(bi, lambda k: wv[:, k, WQ:WQ + 128], lambda k: hT[:, k, 0:512], 512, rs + [B_hT[0]])
        act(qTh[:, 0:512], banks[bi][:, :], AF.Copy, [Bbank[bi]], [B_qTh[0]])
        bi = nbank()
        linear_fm(bi, lambda k: wv[:, k, WK:WK + 128], lambda k: hT[:, k, 0:512], 512, rs + [B_hT[0]])
        act(kTh[:, 0:512], banks[bi][:, :], AF.Copy, [Bbank[bi]], [B_kTh[0]])
        for t in range(0 if "t" in KSKIP else 4):
            bi = nbank()
            linear_fm(bi, lambda k, t=t: hT[:, k, t * 128:(t + 1) * 128], lambda k: wv[:, k, WK:WK + 256], 256,
                      rs + [B_hT[0]])
            oi = nxt("ost", 2)
            act(ostage[oi][:, 0:256], banks[bi][:, 0:256], AF.Copy, [Bbank[bi]], [B_ost[oi]])
            dve_copy(Vh[:, t * 128:(t + 1) * 128], banks[bi][:, 128:256], [Bbank[bi]], [B_Vh[0]])
            if "o" not in KSKIP:
                dma_out(ndk_d[t * 128:(t + 1) * 128, hd * 128:(hd + 1) * 128], ostage[oi][:, 0:128], B_ost[oi])
                dma_out(ndv_d[t * 128:(t + 1) * 128, hd * 128:(hd + 1) * 128], ostage[oi][:, 128:256], B_ost[oi])
        for ci in (() if "r" in KSKIP else (1, 2)):
            c0, n = TCH[ci]
            ba = nbank()
            linear_fm(ba, lambda k: wv[:, k, WQ:WQ + 128], lambda k: hT[:, k, c0:c0 + n], n, rs + [B_hT[ci]])
            bb = nbank()
            linear_fm(bb, lambda k: wv[:, k, WQR:WQR + 128], lambda k: hT[:, k, c0:c0 + n], n, rs + [B_hT[ci]])
            rope_epilogue(ba, bb, n, ropeW[:, 0, c0 - 512:c0 - 512 + n], ropeW[:, 1, c0 - 512:c0 - 512 + n],
                          qTh[:, c0:c0 + n], [B_qTh[1]])
        for fc in range(0 if "f" in KSKIP else 4):
            f0 = fc * 512
            ba = nbank()
            linear_fm(ba, lambda k: wv[:, k, WK:WK + 128], lambda k: hfT[:, k, f0:f0 + 512], 512, rs + [B_hfT[fc]])
            bb = nbank()
            linear_fm(bb, lambda k: wv[:, k, WKR:WKR + 128], lambda k: hfT[:, k, f0:f0 + 512], 512, rs + [B_hfT[fc]])
            rope_epilogue(ba, bb, 512, ropeF[:, 0, f0:f0 + 512], ropeF[:, 1, f0:f0 + 512],
                          kTh[:, 512 + f0:512 + f0 + 512], [B_kTh[1]])
            bi = nbank()
            for t in range(4):
                for k in range(KC):
                    mm(bi, banks[bi][:, t * 128:(t + 1) * 128], hfT[:, k, f0 + t * 128:f0 + (t + 1) * 128],
                       wv[:, k, WV:WV + 128], k == 0, k == KC - 1, rs + [B_hfT[fc]])
            dve_copy(Vh[:, 512 + f0:512 + f0 + 512], banks[bi][:, :], [Bbank[bi]], [B_Vh[1]])

        def diff_finalize(grp, bo, bl, n, mapi, out_ap, obufs):
            t = nxt("tmp", 4)
            dve_recip(tmps[t][:, 0:n], banks[bl][:, 0:n], [Bbank[bl]], [B_tmp[t]])
            if mapi == 0:
                dve_tt(T1[:, 0:n], banks[bo][:, 0:n], tmps[t][:, 0:n], ALU.mult, [Bbank[bo], B_tmp[t]], [B_T1])
                return
            t2 = nxt("tmp", 4)
            dve_tt(tmps[t2][:, 0:n], banks[bo][:, 0:n], tmps[t][:, 0:n], ALU.mult, [Bbank[bo], B_tmp[t]], [B_tmp[t2]])
            t3 = nxt("tmp", 4)
            dve_stt(tmps[t3][:, 0:n], tmps[t2][:, 0:n], lamt[:, 4:5], T1[:, 0:n], ALU.mult, ALU.add,
                    [B_tmp[t2], B_lam, B_T1], [B_tmp[t3]])
            q = nxt("sq", 2)
            act(sqs[q][:, 0:n], tmps[t3][:, 0:n], AF.Square, [B_tmp[t3]], [B_sq[q]])
            bs = grp * 4
            mm(bs, banks[bs][:, 0:n], ones[:, :], sqs[q][:, 0:n], True, True, [B_sq[q], B_const])
            br = grp * 4 + 1
            rstd_from_bank(bs, n, 128, br)
            dve_stt(out_ap, tmps[t3][:, 0:n], lamt[:, 5:6], banks[br][:, 0:n], ALU.mult, ALU.mult,
                    [B_tmp[t3], B_lam, Bbank[br]], obufs)

        for pb in range(0 if "p" in KSKIP else 2):
            for mapi in range(2):
                r0 = mapi * 64
                blocks = []
                for j in range(2):
                    kc = pb * 256 + j * 128
                    blocks.append((kTh[r0:r0 + 64, kc:kc + 128], [B_kTh[0]],
                                   Vh[:, kc:kc + 128], [B_Vh[0]], None))
                bo, bl = attn_job(mapi, qTh[r0:r0 + 64, pb * 256:(pb + 1) * 256], 256, blocks, 128, [B_qTh[0]])
                diff_finalize(mapi, bo, bl, 256, mapi, oT[:, hd, pb * 256:(pb + 1) * 256], [B_oT[0]])
        for ci in (() if "w" in KSKIP else (1, 2)):
            c0, n = TCH[ci]
            for mapi in range(2):
                r0 = mapi * 64
                blocks = []
                for j in range(16):
                    kc = 512 + j * 128
                    blocks.append((kTh[r0:r0 + 64, kc:kc + 128], [B_kTh[1]], Vh[:, kc:kc + 128], [B_Vh[1]], None))
                for j in range(4):
                    kc = 2560 + j * 128
                    blocks.append((kTh[r0:r0 + 64, kc:kc + 128], [B_kTh[2]], Vh[:, kc:kc + 128], [B_Vh[2]], None))
                bo, bl = attn_job(mapi, qTh[r0:r0 + 64, c0:c0 + n], n, blocks, 128, [B_qTh[1]])
                diff_finalize(mapi, bo, bl, n, mapi, oT[:, hd, c0:c0 + n], [B_oT[ci]])

    def out_proj_l0(l):
        for ci, (c0, n) in enumerate(TCH):
            pn = PostNorm(c0, n, [B_xT[ci]], l, 0, 0 if ci == 0 else 1)
            for half in range(2):
                si = load_w(wo0_d[half].rearrange("p k c -> p (k c)"), 4096)
                wv = slots[si][:, 0:4096].rearrange("p (k c) -> p k c", k=8)
                for dl in range(4):
                    bi = nbank()
                    linear_fm(bi, lambda k: wv[:, k, dl * 128:(dl + 1) * 128], lambda k: oT[:, k, c0:c0 + n], n,
                              [B_slot[si], B_oT[ci]])
                    pn.add(bi, half * 4 + dl)
            pn.finish()

    def ffn(l, chunks):
        for (c0, n, xb, hb, ab, c) in chunks:
            prenorm(xT, xb, c0, n, hT, [hb], c0, l, 1, c)
        for piece in range(8):
            nf = min(3, NF - 3 * piece)
            si = load_w(wgu_d[l, piece, :, 0:nf].rearrange("p f k c -> p (f k c)"), nf * 2048)
            wv = slots[si][:, 0:nf * 2048].rearrange("p (f k c) -> p f k c", f=nf, k=8)
            for fl in range(nf):
                f = 3 * piece + fl
                for (c0, n, xb, hb, ab, c) in chunks:
                    bg = nbank()
                    linear_fm(bg, lambda k: wv[:, fl, k, 0:128], lambda k: hT[:, k, c0:c0 + n], n, [B_slot[si], hb])
                    bu = nbank()
                    linear_fm(bu, lambda k: wv[:, fl, k, 128:256], lambda k: hT[:, k, c0:c0 + n], n, [B_slot[si], hb])
                    t = nxt("tmp", 4)
                    act(tmps[t][:, 0:n], banks[bg][:, 0:n], AF.Silu, [Bbank[bg]], [B_tmp[t]])
                    dve_tt(aT[:, f, c0:c0 + n], tmps[t][:, 0:n], banks[bu][:, 0:n], ALU.mult,
                           [B_tmp[t], Bbank[bu]], [ab])
        for (c0, n, xb, hb, ab, c) in chunks:
            pn = PostNorm(c0, n, xb, l, 1, c)
            for j in range(4):
                si = load_w(wdn_d[l, j].rearrange("p d f c -> p (d f c)"), 5632)
                wv = slots[si][:, 0:5632].rearrange("p (d f c) -> p d f c", d=2, f=22)
                for dl in range(2):
                    bi = nbank()
                    for f in range(NF):
                        mm(bi, banks[bi][:, 0:n], wv[:, dl, f, :], aT[:, f, c0:c0 + n], f == 0, f == NF - 1,
                           [B_slot[si], ab])
                    pn.add(bi, 2 * j + dl)
            pn.finish()

    if STAGE >= -1:
        out_proj_l0(0)
    if STAGE >= 1:
        modulation(1)
    P.alias(B_aT, B_hfT + B_oT)
    if STAGE >= 0:
        ffn(0, [(TCH[i][0], TCH[i][1], [B_xT[i]], B_hT[i], B_aT[i], 0 if i == 0 else 1) for i in range(3)])

    if STAGE >= 2:
        P.alias(B_qTs + [B_kTs] + B_oS, B_aT)
        for ci, (c0, n) in enumerate(TCH):
            prenorm(xT, [B_xT[ci]], c0, n, hT, [B_hT[ci]], c0, 1, 0, 0 if ci == 0 else 1)
        OWN0 = 640
        B_hown = [B_hT[1], B_hT[2]]
        VS = Vh[:, 0:3584].rearrange("p (t c) -> p t c", t=14)
        B_VS = Buf("VS")
        P.alias([B_VS], list(B_Vh))
        for half in range(2):
            sa = load_w(ws1q_d[half].rearrange("p k c -> p (k c)"), 4096)
            sr = load_w(ws1q_d[2 + half].rearrange("p k c -> p (k c)"), 4096)
            wa = slots[sa][:, 0:4096].rearrange("p (k c) -> p k c", k=8)
            wr = slots[sr][:, 0:4096].rearrange("p (k c) -> p k c", k=8)
            for cl in range(4):
                cq = half * 4 + cl
                bi = nbank()
                linear_fm(bi, lambda k: wa[:, k, cl * 128:(cl + 1) * 128], lambda k: hT[:, k, 0:512], 512,
                          [B_slot[sa], B_hT[0]])
                act(qTs[:, cq, 0:512], banks[bi][:, :], AF.Copy, [Bbank[bi]], [B_qTs[0]])
                ba = nbank()
                linear_fm(ba, lambda k: wa[:, k, cl * 128:(cl + 1) * 128], lambda k: hT[:, k, OWN0:OWN0 + 512], 512,
                          [B_slot[sa]] + B_hown)
                bb = nbank()
                linear_fm(bb, lambda k: wr[:, k, cl * 128:(cl + 1) * 128], lambda k: hT[:, k, OWN0:OWN0 + 512], 512,
                          [B_slot[sr]] + B_hown)
                rope_epilogue(ba, bb, 512, ropeW[:, 0, 128:640], ropeW[:, 1, 128:640], qTs[:, cq, 512:1024],
                              [B_qTs[1]])
        sk = load_w(ws1k_d.rearrange("p k c -> p (k c)"), 6144)
        wk = slots[sk][:, 0:6144].rearrange("p (k c) -> p k c", k=8)
        rsk = [B_slot[sk]]
        for p in range(2):
            bi = nbank()
            linear_fm(bi, lambda k: wk[:, k, p * 128:(p + 1) * 128], lambda k: hT[:, k, 0:512], 512, rsk + [B_hT[0]])
            act(kTs[:, p, 0:512], banks[bi][:, :], AF.Copy, [Bbank[bi]], [B_kTs])
            for ci in (1, 2):
                c0, n = TCH[ci]
                ba = nbank()
                linear_fm(ba, lambda k: wk[:, k, p * 128:(p + 1) * 128], lambda k: hT[:, k, c0:c0 + n], n,
                          rsk + [B_hT[ci]])
                bb = nbank()
                linear_fm(bb, lambda k: wk[:, k, 256 + p * 128:256 + (p + 1) * 128], lambda k: hT[:, k, c0:c0 + n], n,
                          rsk + [B_hT[ci]])
                rope_epilogue(ba, bb, n, ropeW[:, 0, c0 - 512:c0 - 512 + n], ropeW[:, 1, c0 - 512:c0 - 512 + n],
                              kTs[:, p, c0:c0 + n], [B_kTs])
        for t in range(4):
            bi = nbank()
            for k in range(KC):
                mm(bi, banks[bi][:, 0:256], hT[:, k, t * 128:(t + 1) * 128], wk[:, k, 0:256], k == 0, k == KC - 1,
                   rsk + [B_hT[0]])
            for k in range(KC):
                mm(bi, banks[bi][:, 256:512], hT[:, k, t * 128:(t + 1) * 128], wk[:, k, 512:768], k == 0, k == KC - 1,
                   rsk + [B_hT[0]])
            oi = nxt("ost", 2)
            act(ostage[oi][:, :], banks[bi][:, :], AF.Copy, [Bbank[bi]], [B_ost[oi]])
            dve_copy(VS[:, t, :], banks[bi][:, 256:512], [Bbank[bi]], [B_VS])
            dma_out(nsk_d[t * 128:(t + 1) * 128, :], ostage[oi][:, 0:256], B_ost[oi])
            dma_out(nsv_d[t * 128:(t + 1) * 128, :], ostage[oi][:, 256:512], B_ost[oi])
        for t in range(6):
            bi = nbank()
            c0 = 512 + t * 128
            for k in range(KC):
                mm(bi, banks[bi][:, 0:256], hT[:, k, c0:c0 + 128], wk[:, k, 512:768], k == 0, k == KC - 1,
                   rsk + [B_hT[1] if t < 4 else B_hT[2]])
            dve_copy(VS[:, 4 + t, :], banks[bi][:, 0:256], [Bbank[bi]], [B_VS])
        dma_in("sp", cst[:, :].rearrange("p (j c) -> p j c", j=4), cks_d, B_cst)
        for p in range(2):
            bi = nbank()
            for j in range(4):
                P.op("pe", lambda e, bi=bi, j=j, p=p: e.transpose(
                    banks[bi][:, j * 128:(j + 1) * 128], cst[:, j * 256 + p * 128:j * 256 + (p + 1) * 128], ident[:, :]),
                    reads=[B_cst, B_const], writes=[Bbank[bi]])
            act(kTs[:, p, 1280:1792], banks[bi][:, :], AF.Copy, [Bbank[bi]], [B_kTs])
        xi = nxt("xin", 2)
        dma_in("sp", xin[xi][:, :], cvs_d.rearrange("p j c -> p (j c)"), B_xin[xi])
        dve_copy(Vh[:, 2560:3584], xin[xi][:, :], [B_xin[xi]], [B_VS])

        def swa_finalize(bo, bl, ng, qn, h0, oS_col0, obuf):
            t = nxt("tmp", 4)
            for g in range(ng):
                dve_ts(tmps[t][0:64, g * qn:(g + 1) * qn], banks[bl][0:64, g * qn:(g + 1) * qn],
                       sinkE[0:64, h0 + g:h0 + g + 1], None, ALU.add, None,
                       [Bbank[bl], B_lam], [B_tmp[t]])
            n = ng * qn
            t2 = nxt("tmp", 4)
            dve_recip(tmps[t2][0:64, 0:n], tmps[t][0:64, 0:n], [B_tmp[t]], [B_tmp[t2]])
            for g in range(ng):
                dve_tt(oS[0:64, h0 + g, oS_col0:oS_col0 + qn], banks[bo][0:64, g * qn:(g + 1) * qn],
                       tmps[t2][0:64, g * qn:(g + 1) * qn], ALU.mult, [Bbank[bo], B_tmp[t2]], [obuf])

        grp = 0
        for pb in range(2):
            for kv in range(4):
                p, r0 = kv // 2, (kv % 2) * 64
                for gh in range(2):
                    q_ap = qTs[r0:r0 + 64, p * 4 + gh * 2:p * 4 + gh * 2 + 2, pb * 256:(pb + 1) * 256]
                    blocks = []
                    for j in range(2):
                        kc = pb * 256 + j * 128
                        blocks.append((kTs[r0:r0 + 64, p, kc:kc + 128], [B_kTs],
                                       VS[:, pb * 2 + j, kv * 64:(kv + 1) * 64], [B_VS], None))
                    bo, bl = attn_job(grp, q_ap, 512, blocks, 64, [B_qTs[0]])
                    swa_finalize(bo, bl, 2, 256, kv * 4 + gh * 2, pb * 256, B_oS[0])
                    grp ^= 1
        for qb in range(4):
            for kv in range(4):
                p, r0 = kv // 2, (kv % 2) * 64
                q_ap = qTs[r0:r0 + 64, p * 4:p * 4 + 4, 512 + qb * 128:512 + (qb + 1) * 128]
                blocks = []
                for j in range(4):
                    kc = 1280 + j * 128
                    blocks.append((kTs[r0:r0 + 64, p, kc:kc + 128], [B_kTs],
                                   VS[:, 10 + j, kv * 64:(kv + 1) * 64], [B_VS], None))
                for dj, mi in ((0, 2 * qb), (1, None), (2, 2 * qb + 1)):
                    w = qb + dj
                    kc = 512 + w * 128
                    blocks.append((kTs[r0:r0 + 64, p, kc:kc + 128], [B_kTs],
                                   VS[:, 4 + w, kv * 64:(kv + 1) * 64], [B_VS], mi))
                bo, bl = attn_job(grp, q_ap, 512, blocks, 64, [B_qTs[1]])
                swa_finalize(bo, bl, 4, 128, kv * 4, 512 + qb * 128, B_oS[1])
                grp ^= 1
        L1CH = [(0, 512, [B_xT[0]], 0, 0, B_oS[0]), (OWN0, 512, [B_xT[1], B_xT[2]], 1, 512, B_oS[1])]
        for (c0, n, xb, c, oc0, ob) in L1CH:
            pn = PostNorm(c0, n, xb, 1, 0, c)
            for dq in range(4):
                si = load_w(wo1_d[dq].rearrange("p h c -> p (h c)"), 4096, parts=64)
                wv = slots[si][0:64, 0:4096].rearrange("p (h c) -> p h c", h=16)
                for dl in range(2):
                    bi = nbank()
                    for hh in range(16):
                        mm(bi, banks[bi][:, 0:n], wv[:, hh, dl * 128:(dl + 1) * 128], oS[0:64, hh, oc0:oc0 + n],
                           hh == 0, hh == 15, [B_slot[si], ob])
                    pn.add(bi, dq * 2 + dl)
            pn.finish()
        P.alias(B_aT, B_qTs + [B_kTs] + B_oS)
        if STAGE >= 3:
            ffn(1, [(0, 512, [B_xT[0]], B_hT[0], B_aT[0], 0),
                    (OWN0, 512, [B_xT[1], B_xT[2]], B_hT[1], B_aT[1], 1)])

    OWN0 = 640
    for t in range(8):
        c0 = t * 128 if t < 4 else OWN0 + (t - 4) * 128
        xb = [B_xT[0]] if t < 4 else [B_xT[1], B_xT[2]]
        xi = nxt("xin", 2)
        for half in range(2):
            bi = nbank()
            for j in range(4):
                cidx = half * 4 + j
                P.op("pe", lambda e, bi=bi, j=j, cidx=cidx, c0=c0: e.transpose(
                    banks[bi][:, j * 128:(j + 1) * 128], xT[:, cidx, c0:c0 + 128], ident[:, :]),
                    reads=xb + [B_const], writes=[Bbank[bi]])
            if half == 0:
                act(xin[xi][:, 0:512], banks[bi][:, :], AF.Copy, [Bbank[bi]], [B_xin[xi]])
            else:
                dve_copy(xin[xi][:, 512:1024], banks[bi][:, :], [Bbank[bi]], [B_xin[xi]])
        dst = yp_d[t * 128:(t + 1) * 128, :] if t < 4 else ys_d[(t - 4) * 128:(t - 3) * 128, :]
        dma_out(dst, xin[xi][:, :], B_xin[xi])

    P.finish()
    stats = P.emit()
    return nc, es, stats


def _rope_tables(pos):
    pos = np.asarray(pos)
    row = (pos // 64).astype(np.float32)
    col = (pos % 64).astype(np.float32)
    nf = 16
    inv = (np.float32(10000.0) ** (-np.arange(nf, dtype=np.float32) / np.float32(nf))).astype(np.float32)
    ar = row[:, None] * inv[None, :]
    ac = col[:, None] * inv[None, :]
    ang = np.concatenate([ar, ar, ac, ac], axis=-1).astype(np.float32)
    cos = np.cos(ang).astype(np.float32)
    sin = np.sin(ang).astype(np.float32)
    sign = np.concatenate([-np.ones(16), np.ones(16), -np.ones(16), np.ones(16)]).astype(np.float32)
    sin = sin * sign[None, :]
    cos2 = np.concatenate([cos, cos], axis=1).T
    sin2 = np.concatenate([sin, sin], axis=1).T
    return np.ascontiguousarray(cos2), np.ascontiguousarray(sin2)


_ROTSRC = np.concatenate([np.arange(16, 32), np.arange(0, 16), np.arange(48, 64), np.arange(32, 48)])


def _rot_cols(w):
    n = w.shape[1] // 64
    idx = (np.arange(n)[:, None] * 64 + _ROTSRC[None, :]).reshape(-1)
    return w[:, idx]


def _kmaj(w):
    return np.ascontiguousarray(w.reshape(8, 128, -1).transpose(1, 0, 2))


_PROG_CACHE = {}


def prep_inputs(x_prompt, x_sample, cache_diff_k, cache_diff_v, cache_swa_k, cache_swa_v, c, c_ctx,
                w_mod, b_mod, norm_g, w_qkv_diff, diff_lambda, diff_subln_g, w_o_diff,
                w_qkv_swa, swa_sink, w_o_swa, w_gate, w_up, w_down):
    f = np.float32
    A = lambda a: np.ascontiguousarray(np.asarray(a, dtype=f))
    x_prompt, x_sample = A(x_prompt), A(x_sample)
    cache_diff_k, cache_diff_v = A(cache_diff_k), A(cache_diff_v)
    cache_swa_k, cache_swa_v = A(cache_swa_k), A(cache_swa_v)
    c, c_ctx = A(c), A(c_ctx)
    w_mod, b_mod, norm_g = A(w_mod), A(b_mod), A(norm_g)
    w_qkv_diff, diff_lambda, diff_subln_g, w_o_diff = A(w_qkv_diff), A(diff_lambda), A(diff_subln_g), A(w_o_diff)
    w_qkv_swa, swa_sink, w_o_swa = A(w_qkv_swa), A(swa_sink), A(w_o_swa)
    w_gate, w_up, w_down = A(w_gate), A(w_up), A(w_down)

    wmod = np.ascontiguousarray(
        w_mod.reshape(2, 8, 128, 8, 768).transpose(0, 3, 2, 1, 4))
    wq, wk, wv = w_qkv_diff[0][:, 0:1024], w_qkv_diff[0][:, 1024:2048], w_qkv_diff[0][:, 2048:3072]
    wqr, wkr = _rot_cols(wq), _rot_cols(wk)
    wd0 = np.empty((8, 128, 8, 640), f)
    for h in range(8):
        s = slice(h * 128, (h + 1) * 128)
        wd0[h] = _kmaj(np.concatenate([wq[:, s], wqr[:, s], wkr[:, s], wk[:, s], wv[:, s]], axis=1))
    wo0 = np.stack([_kmaj(w_o_diff[0][:, 0:512]), _kmaj(w_o_diff[0][:, 512:1024])])
    ws = w_qkv_swa[0]
    sq_cols = []
    for p in range(2):
        for g in range(4):
            for kvl in range(2):
                hh = (2 * p + kvl) * 4 + g
                sq_cols.append(np.arange(hh * 64, (hh + 1) * 64))
    sq_cols = np.concatenate(sq_cols)
    wsq = ws[:, 0:1024]
    wsq_p = wsq[:, sq_cols]
    wsqr_p = _rot_cols(wsq)[:, sq_cols]
    ws1q = np.stack([_kmaj(wsq_p[:, 0:512]), _kmaj(wsq_p[:, 512:1024]),
                     _kmaj(wsqr_p[:, 0:512]), _kmaj(wsqr_p[:, 512:1024])])
    wsk, wsv = ws[:, 1024:1280], ws[:, 1280:1536]
    ws1k = _kmaj(np.concatenate([wsk, _rot_cols(wsk), wsv], axis=1))
    wo1 = np.ascontiguousarray(w_o_swa[0].reshape(16, 64, 4, 256).transpose(2, 1, 0, 3))
    wgu_f = np.zeros((2, 24, 128, 8, 256), f)
    for l in range(2):
        g_ = w_gate[l].reshape(8, 128, 22, 128).transpose(2, 1, 0, 3)
        u_ = w_up[l].reshape(8, 128, 22, 128).transpose(2, 1, 0, 3)
        wgu_f[l, 0:22, :, :, 0:128] = g_
        wgu_f[l, 0:22, :, :, 128:256] = u_
    wgu = np.ascontiguousarray(wgu_f.reshape(2, 8, 3, 128, 8, 256).transpose(0, 1, 3, 2, 4, 5))
    wdn = np.ascontiguousarray(w_down.reshape(2, 22, 128, 4, 2, 128).transpose(0, 3, 2, 4, 1, 5))
    ident = np.eye(128, dtype=f)
    cosf, sinf = _rope_tables(np.arange(TFULL))
    ropef = np.ascontiguousarray(np.stack([cosf, sinf], axis=1))

    normg_l = norm_g.reshape(8, 8, 128).transpose(2, 0, 1).reshape(128, 64)
    bmod_l = b_mod.reshape(2, 48, 128).transpose(2, 0, 1).reshape(128, 96)
    subg_l = diff_subln_g.reshape(128, 1)
    lam_l = np.broadcast_to(diff_lambda.reshape(1, 256), (128, 256))
    sink_l = np.broadcast_to(swa_sink.reshape(1, 16), (128, 16))

    in_maps = []
    for core in range(NCORES):
        b, ch = core // 4, core % 4
        xp = x_prompt[2 * core:2 * core + 2].reshape(512, D)
        pos = np.arange(ch * 512 - 128, ch * 512 + 640)
        valid = (pos >= 0) & (pos < 2048)
        xw = np.zeros((TW, D), f)
        xw[valid] = x_sample[b, pos[valid]]
        cosw, sinw = _rope_tables(np.clip(pos, 0, 2047))
        ropew = np.ascontiguousarray(np.stack([cosw, sinw], axis=1))
        maskb = np.zeros((128, 8, 128), f)
        for qb in range(4):
            for side in range(2):
                w = qb + (0 if side == 0 else 2)
                kpos = pos[w * 128:(w + 1) * 128]
                qpos = pos[(qb + 1) * 128:(qb + 2) * 128]
                ok = ((kpos[:, None] >= 0) & (kpos[:, None] < 2048)
                      & (np.abs(qpos[None, :] - kpos[:, None]) <= 128))
                maskb[:, 2 * qb + side, :] = np.where(ok, 0.0, -30000.0)
        cond_l = np.stack([c_ctx, c[b]], axis=-1).reshape(8, 128, 2).transpose(1, 0, 2).reshape(128, 16)
        sm = np.ascontiguousarray(np.concatenate([cond_l, normg_l, bmod_l, subg_l, lam_l, sink_l], axis=1), dtype=f)
        ckd = np.ascontiguousarray(cache_diff_k[b, 0].reshape(4, 128, 8, 128).transpose(2, 1, 0, 3))
        cvd = np.ascontiguousarray(cache_diff_v[b, 0].reshape(4, 128, 8, 128).transpose(2, 1, 0, 3))
        cks = np.ascontiguousarray(cache_swa_k[b, 0].reshape(4, 128, 256).transpose(1, 0, 2))
        cvs = np.ascontiguousarray(cache_swa_v[b, 0].reshape(4, 128, 256).transpose(1, 0, 2))
        in_maps.append(dict(xp=np.ascontiguousarray(xp), xw=xw, xf=x_sample[b], ckd=ckd, cvd=cvd, cks=cks, cvs=cvs,
                            sm=sm, ident=ident, ropew=ropew, ropef=ropef, maskb=maskb, wmod=wmod, wd0=wd0, wo0=wo0,
                            ws1q=ws1q, ws1k=ws1k, wo1=wo1, wgu=wgu, wdn=wdn))
    return in_maps


def kernel(**inputs):
    f = np.float32
    in_maps = prep_inputs(**inputs)
    if "nc" not in _PROG_CACHE:
        nc, es, stats = build_program()
        _PROG_CACHE["nc"] = (nc, es)
        if os.environ.get("KVERBOSE"):
            print("ops per engine:", stats)
    nc, _ = _PROG_CACHE["nc"]
    res = run_bass_kernel_spmd(nc, in_maps, core_ids=list(range(NCORES)))
    R = res.results
    y_prompt = np.concatenate([R[i]["yp"].reshape(2, 256, D) for i in range(NCORES)], axis=0)
    y_sample = np.stack([np.concatenate([R[b * 4 + ch]["ys"] for ch in range(4)], axis=0) for b in range(2)], axis=0)
    ndk = np.concatenate([R[i]["ndk"].reshape(2, 1, 256, 8, 128) for i in range(NCORES)], axis=0)
    ndv = np.concatenate([R[i]["ndv"].reshape(2, 1, 256, 8, 128) for i in range(NCORES)], axis=0)
    nsk = np.concatenate([R[i]["nsk"].reshape(2, 1, 256, 4, 64) for i in range(NCORES)], axis=0)
    nsv = np.concatenate([R[i]["nsv"].reshape(2, 1, 256, 4, 64) for i in range(NCORES)], axis=0)
    return (y_prompt.astype(f), y_sample.astype(f), ndk.astype(f), ndv.astype(f), nsk.astype(f), nsv.astype(f))
```

```python
import os
import numpy as np
import concourse.bass as bass
import concourse.mybir as mybir
from concourse.bass_utils import run_bass_kernel_spmd
from contextlib import ExitStack

F32 = mybir.dt.float32
BF16 = mybir.dt.bfloat16
AF = mybir.ActivationFunctionType
ALU = mybir.AluOpType

D = 1024
KC = 8
DFF = 2816
NF = 22
TP = 512
TW = 768
TT = 1280
TFULL = 2048
LC = 512
EPS = 1e-6
NCORES = 8
STAGE = int(os.environ.get("KSTAGE", "99"))
KHEADS = int(os.environ.get("KHEADS", "8"))
KSKIP = os.environ.get("KSKIP", "")


class Buf:
    __slots__ = ("name", "writers", "readers", "dsem", "ndma", "war", "excl")

    def __init__(self, name, excl=False):
        self.name = name
        self.excl = excl
        self.writers = []
        self.readers = []
        self.war = []
        self.dsem = None
        self.ndma = 0


class Op:
    __slots__ = ("eng", "idx", "fn", "deps", "dma", "buf", "ordinal", "flag", "count", "waits", "ring")

    def __init__(self, eng, idx, fn):
        self.eng = eng
        self.idx = idx
        self.fn = fn
        self.deps = []
        self.dma = False
        self.buf = None
        self.ordinal = 0
        self.flag = False
        self.count = 0
        self.waits = []
        self.ring = False


ENGS = ["pe", "act", "dve", "pool", "sp"]


class Prog:
    def __init__(self, nc, es):
        self.nc = nc
        self.es = es
        self.ops = {e: [] for e in ENGS}
        self.dma_bufs = []
        self.out_dmas = []

    def op(self, eng, fn, reads=(), writes=(), dma_buf=None, is_out=False, ring=False):
        o = Op(eng, len(self.ops[eng]), fn)
        o.ring = ring
        deps = o.deps
        for b in reads:
            for w in b.writers:
                deps.append((w, True))
            if b.excl:
                for r in b.readers:
                    if r.eng != eng:
                        deps.append((r, False))
            b.readers.append(o)
        for b in writes:
            if b.readers:
                b.war = [r for r in b.readers if r is not o]
                b.writers = [o]
                b.readers = []
            else:
                b.writers.append(o)
            for r in b.war:
                deps.append((r, False))
        if dma_buf is not None:
            o.dma = True
            o.buf = dma_buf
            if dma_buf.dsem is None:
                self.dma_bufs.append(dma_buf)
                dma_buf.dsem = True
            dma_buf.ndma += 1
            o.ordinal = dma_buf.ndma
            if is_out:
                self.out_dmas.append(o)
        self.ops[eng].append(o)
        return o

    def alias(self, new_bufs, old_bufs):
        pend = []
        for b in old_bufs:
            pend.extend(b.readers)
            pend.extend(b.writers)
        for nb in new_bufs:
            nb.readers.extend(pend)
            nb.war = []

    def finish(self):
        o = Op("sp", len(self.ops["sp"]), None)
        for d in self.out_dmas:
            o.deps.append((d, True))
        self.ops["sp"].append(o)

    def emit(self):
        nc = self.nc
        es = self.es
        esem = {e: es.enter_context(nc.semaphore("s_" + e)) for e in ["pe", "act", "dve", "pool"]}
        for i, b in enumerate(self.dma_bufs):
            b.dsem = es.enter_context(nc.semaphore("d%d" % i))
        for e in ENGS:
            waited = {}
            for o in self.ops[e]:
                need = {}
                for (d, raw) in o.deps:
                    if d.dma:
                        key = ("d", id(d.buf))
                        val = d.ordinal
                        if waited.get(key, 0) >= val:
                            continue
                        if need.get(key, (0, None))[0] < val:
                            need[key] = (val, d)
                    else:
                        if d.eng == e and e == "pe":
                            continue
                        key = ("e", d.eng)
                        val = d.idx + 1
                        if waited.get(key, 0) >= val:
                            continue
                        if need.get(key, (0, None))[0] < val:
                            need[key] = (val, d)
                for key, (val, d) in need.items():
                    waited[key] = val
                    if not d.dma:
                        d.flag = True
                    o.waits.append(d)
        for e in ["pe", "act", "dve", "pool"]:
            c = 0
            for o in self.ops[e]:
                if o.flag and not o.dma:
                    c += 1
                    o.count = c
        handles = {"pe": "tensor", "act": "scalar", "dve": "vector", "pool": "gpsimd", "sp": "sync"}
        stats = {}
        with nc.Block() as block:
            for e in ENGS:
                ops = self.ops[e]
                stats[e] = len(ops)

                def body(eng, ops=ops, e=e):
                    for o in ops:
                        for d in o.waits:
                            if d.dma:
                                eng.wait_ge(d.buf.dsem, 16 * d.ordinal)
                            else:
                                eng.wait_ge(esem[d.eng], d.count)
                        if o.fn is None:
                            continue
                        inst = o.fn(eng)
                        if o.ring:
                            assert not o.flag
                            inst.then_inc(self.ring_sem, 16)
                        elif o.dma:
                            inst.then_inc(o.buf.dsem, 16)
                        elif o.flag:
                            inst.then_inc(esem[e], 1)

                getattr(block, handles[e])(body)
        return stats


def build_program():
    nc = bass.Bass("TRN2", target_bir_lowering=False, monotonic_sem_count=0)
    es = ExitStack()
    P = Prog(nc, es)

    def din(name, shape):
        return nc.dram_tensor(name, list(shape), F32, kind="ExternalInput").ap()

    def dout(name, shape):
        return nc.dram_tensor(name, list(shape), F32, kind="ExternalOutput").ap()

    xp_d = din("xp", [TP, D])
    xw_d = din("xw", [TW, D])
    xf_d = din("xf", [TFULL, D])
    ckd_d = din("ckd", [8, 128, 4, 128])
    cvd_d = din("cvd", [8, 128, 4, 128])
    cks_d = din("cks", [128, 4, 256])
    cvs_d = din("cvs", [128, 4, 256])
    NSM = 16 + 64 + 96 + 1 + 256 + 16
    sm_d = din("sm", [128, NSM])
    ident_d = din("ident", [128, 128])
    ropew_d = din("ropew", [128, 2, TW])
    ropef_d = din("ropef", [128, 2, TFULL])
    maskb_d = din("maskb", [128, 8, 128])
    wmod_d = din("wmod", [2, 8, 128, 8, 768])
    wd0_d = din("wd0", [8, 128, 8, 640])
    wo0_d = din("wo0", [2, 128, 8, 512])
    ws1q_d = din("ws1q", [4, 128, 8, 512])
    ws1k_d = din("ws1k", [128, 8, 768])
    wo1_d = din("wo1", [2, 128, 8, 512])
    wgu_d = din("wgu", [2, 8, 128, 3, 8, 256])
    wdn_d = din("wdn", [2, 4, 128, 2, 22, 128])

    yp_d = dout("yp", [TP, D])
    ys_d = dout("ys", [512, D])
    ndk_d = dout("ndk", [TP, 1024])
    ndv_d = dout("ndv", [TP, 1024])
    nsk_d = dout("nsk", [TP, 256])
    nsv_d = dout("nsv", [TP, 256])

    def sb(name, shape, dt):
        return es.enter_context(nc.sbuf_tensor(name, list(shape), dt))

    xT = sb("xT", [128, KC, TT], F32)
    hT = sb("hT", [128, KC, TT], BF16)
    ytmp = hT.bitcast(F32)
    BIG = sb("BIG", [128, 28160], BF16)
    slots = [sb("wslot%d" % i, [128, 6144], BF16) for i in range(2)]
    kTh = sb("kTh", [128, 3072], BF16)
    Vh = sb("Vh", [128, 3584], BF16)
    qTh = sb("qTh", [128, 2, TT], BF16)
    Es = [sb("E%d" % i, [128, 512], BF16) for i in range(4)]
    xin = [sb("xin%d" % i, [128, 1024], F32) for i in range(2)]
    cst = sb("cst", [128, 1024], F32)
    ropeW = sb("ropeW", [128, 2, TW], F32)
    ropeF = sb("ropeF", [128, 2, TFULL], BF16)
    maskb = sb("maskbs", [128, 8, 128], BF16)
    ident = sb("idents", [128, 128], F32)
    identb = sb("identb", [128, 128], BF16)
    ones = sb("ones", [128, 128], BF16)
    sm = sb("sms", [128, NSM], F32)
    modT = sb("modT", [128, 2, 48, 2], F32)
    der = sb("der", [128, 2, 4, 8, 2], F32)
    scb = sb("scb", [128, 8, 2], BF16)
    lamt = sb("lamt", [128, 8], F32)
    sinkE = sb("sinkE", [128, 16], F32)
    sqs = [sb("sq%d" % i, [128, 512], BF16) for i in range(2)]
    tmps = [sb("tmp%d" % i, [128, 512], F32) for i in range(4)]
    T1 = sb("T1", [128, 512], F32)
    ostage = [xin[i] for i in range(2)]

    banks = [es.enter_context(nc.psum_tensor("bank%d" % i, [128, 512], F32)) for i in range(8)]
    Bbank = [Buf("bank%d" % i, excl=True) for i in range(8)]

    B_xT = [Buf("xT_p"), Buf("xT_w0"), Buf("xT_w1")]
    B_hT = [Buf("hT_p"), Buf("hT_w0"), Buf("hT_w1")]
    B_slot = [Buf("slot0"), Buf("slot1")]
    B_kTh = Buf("kTh_p"), Buf("kTh_f"), Buf("kTh_c")
    B_Vh = Buf("Vh_p"), Buf("Vh_f"), Buf("Vh_c")
    B_qTh = [Buf("qTh_p"), Buf("qTh_w")]
    B_E = [Buf("E%d" % i) for i in range(4)]
    B_xin = [Buf("xin0"), Buf("xin1")]
    B_cst = Buf("cst")
    B_cstv = Buf("cstv")
    B_const = Buf("consts")
    B_sm = Buf("sm")
    B_mod = [Buf("mod0"), Buf("mod1")]
    B_der = [Buf("der0"), Buf("der1")]
    B_scb = Buf("scb")
    B_lam = Buf("lam")
    B_sq = [Buf("sq0"), Buf("sq1")]
    B_tmp = [Buf("tmp%d" % i) for i in range(4)]
    B_T1 = Buf("T1")
    B_ost = B_xin
    B_hfT = [Buf("hfT%d" % i) for i in range(4)]
    B_oT = [Buf("oT_p"), Buf("oT_w0"), Buf("oT_w1")]
    B_aT = [Buf("aT0"), Buf("aT1"), Buf("aT2")]
    B_qTs = [Buf("qTs_p"), Buf("qTs_s")]
    B_kTs = Buf("kTs")
    B_oS = [Buf("oS_p"), Buf("oS_s")]

    hfT = BIG[:, 0:16384].rearrange("p (k t) -> p k t", k=8)
    oT = BIG[:, 16384:16384 + 10240].rearrange("p (k t) -> p k t", k=8)
    aT = BIG[:, 0:28160].rearrange("p (f t) -> p f t", f=22)
    qTs = BIG[:, 0:8192].rearrange("p (k t) -> p k t", k=8)
    kTs = BIG[:, 8192:8192 + 7168].rearrange("p (k t) -> p k t", k=4)
    oS = BIG[:, 15360:15360 + 8192].rearrange("p (h t) -> p h t", h=8)

    TCH = [(0, 512), (512, 512), (1024, 256)]

    rr = {"bank": 0, "tmp": 0, "sq": 0, "slot": 0, "xin": 0, "ost": 0, "E": 0}

    def nxt(kind, n):
        v = rr[kind]
        rr[kind] = (v + 1) % n
        return v

    reserved = set()

    def nbank():
        while True:
            b = nxt("bank", 8)
            if b not in reserved:
                return b

    open_grp = {}

    def mm(bi, out_ap, lhsT, rhs, start, stop, reads):
        if start and open_grp.get(bi):
            import traceback
            traceback.print_stack(limit=6)
            print("OPEN GROUP on bank", bi, "opened at:", open_grp[bi])
        if start:
            import traceback
            open_grp[bi] = "".join(traceback.format_stack(limit=5)[:-1])
        if stop:
            open_grp[bi] = None
        P.op("pe", lambda e: e.matmul(out_ap, lhsT, rhs, start=start, stop=stop),
             reads=reads, writes=[Bbank[bi]])

    def act(out_ap, in_ap, func, reads, writes, bias=None, scale=None):
        kw = {}
        if bias is not None:
            kw["bias"] = bias
        if scale is not None:
            kw["scale"] = scale
        P.op("act", lambda e: e.activation(out_ap, in_ap, func, **kw), reads=reads, writes=writes)

    def dve_tt(out_ap, a, b, op, reads, writes, eng="dve"):
        P.op(eng, lambda e: e.tensor_tensor(out_ap, a, b, op), reads=reads, writes=writes)

    def dve_stt(out_ap, a, scalar, b, op0, op1, reads, writes):
        P.op("dve", lambda e: e.scalar_tensor_tensor(out_ap, a, scalar, b, op0, op1), reads=reads, writes=writes)

    def dve_ts(out_ap, a, s1, s2, op0, op1, reads, writes, eng="dve"):
        if op1 is None:
            P.op(eng, lambda e: e.tensor_scalar(out_ap, a, s1, None, op0), reads=reads, writes=writes)
        else:
            P.op(eng, lambda e: e.tensor_scalar(out_ap, a, s1, s2, op0, op1), reads=reads, writes=writes)

    def dve_copy(out_ap, in_ap, reads, writes, eng="dve"):
        P.op(eng, lambda e: e.tensor_copy(out_ap, in_ap), reads=reads, writes=writes)

    def dve_recip(out_ap, in_ap, reads, writes):
        P.op("dve", lambda e: e.reciprocal(out_ap, in_ap), reads=reads, writes=writes)

    def dma_in(eng, out_ap, in_ap, buf, extra_writes=(), cast=False):
        if eng == "pool":
            P.op("pool", lambda e: e.dma_start(out=out_ap, in_=in_ap, max_dma_last_dim=4096),
                 reads=[], writes=[buf] + list(extra_writes), dma_buf=Buf("sw"))
        elif cast:
            P.op(eng, lambda e: e.dma_start(out=out_ap, in_=in_ap, max_dma_last_dim=4096),
                 reads=[], writes=[buf] + list(extra_writes), dma_buf=buf)
        else:
            P.op(eng, lambda e: e.dma_start(out=out_ap, in_=in_ap),
                 reads=[], writes=[buf] + list(extra_writes), dma_buf=buf)

    def dma_out(out_ap, in_ap, buf):
        P.op("sp", lambda e: e.dma_start(out=out_ap, in_=in_ap), reads=[buf], writes=[], dma_buf=buf, is_out=True)

    def load_w(srcs, nelem, view=None, parts=128):
        si = nxt("slot", 2)
        if not isinstance(srcs, (list, tuple)):
            srcs = [srcs]
        for i, src_ap in enumerate(srcs):
            dst = slots[si][0:parts, i * nelem:(i + 1) * nelem]
            dma_in("pool", dst, src_ap, B_slot[si], cast=True)
        return si

    dma_in("sp", sm[:, :], sm_d, B_sm)
    dma_in("sp", ident[:, :], ident_d, B_const)
    dma_in("sp", ropeW[:, :, :], ropew_d, B_const)
    dma_in("pool", ropeF[:, :, :].rearrange("p a t -> p (a t)"), ropef_d.rearrange("p a t -> p (a t)"), B_const, cast=True)
    dma_in("pool", maskb[:, :, :].rearrange("p a t -> p (a t)"), maskb_d.rearrange("p a t -> p (a t)"), B_const, cast=True)
    P.op("dve", lambda e: e.memset(ones[:, :], 1.0), reads=[], writes=[B_const])
    dve_copy(identb[:, :], ident[:, :], [B_const], [B_const])
    P.op("dve", lambda e: e.memset(qTh[:, :, :], 0.0), reads=[], writes=[B_qTh[0], B_qTh[1]])

    O_COND, O_NG, O_BM, O_SUBG, O_LAM, O_SINK = 0, 16, 80, 176, 177, 433
    cond_v = sm[:, O_COND:O_COND + 16].rearrange("p (k c) -> p k c", c=2)
    act(scb[:, :, :], cond_v, AF.Silu, [B_sm], [B_scb])
    LAM_INIT = 0.8 - 0.6 * float(np.exp(-0.3 * 0))
    lam_v = sm[:, O_LAM:O_LAM + 256].rearrange("p (a d) -> p a d", a=4)
    P.op("dve", lambda e: e.tensor_tensor(tmps[0][:, 0:64], lam_v[:, 0, :], lam_v[:, 1, :], ALU.mult),
         reads=[B_sm], writes=[B_tmp[0]])
    P.op("dve", lambda e: e.tensor_tensor(tmps[0][:, 64:128], lam_v[:, 2, :], lam_v[:, 3, :], ALU.mult),
         reads=[B_sm], writes=[B_tmp[0]])
    P.op("dve", lambda e: e.reduce_sum(lamt[:, 0:2], tmps[0][:, 0:128].rearrange("p (a d) -> p a d", a=2),
                                       mybir.AxisListType.X), reads=[B_tmp[0]], writes=[B_lam])
    act(lamt[:, 2:4], lamt[:, 0:2], AF.Exp, [B_lam], [B_lam])
    dve_tt(lamt[:, 4:5], lamt[:, 3:4], lamt[:, 2:3], ALU.subtract, [B_lam], [B_lam])
    dve_ts(lamt[:, 4:5], lamt[:, 4:5], -LAM_INIT, None, ALU.add, None, [B_lam], [B_lam])
    dve_ts(lamt[:, 5:6], sm[:, O_SUBG:O_SUBG + 1], 1.0 - LAM_INIT, None, ALU.mult, None, [B_sm, B_lam], [B_lam])
    act(sinkE[:, :], sm[:, O_SINK:O_SINK + 16], AF.Exp, [B_sm], [B_lam])

    def modulation(l):
        bi = nbank()
        for piece in range(8):
            si = load_w(wmod_d[l, piece].rearrange("p k c -> p (k c)"), 6144)
            wv = slots[si][:, 0:6144].rearrange("p (k c) -> p k c", k=8)
            for oc in range(6):
                o = piece * 6 + oc
                for k in range(KC):
                    mm(bi, banks[bi][:, o * 2:o * 2 + 2], wv[:, k, oc * 128:(oc + 1) * 128], scb[:, k, :],
                       k == 0, k == KC - 1, [B_slot[si], B_scb])
        bv = banks[bi][:, 0:96].rearrange("p (o c) -> p o c", c=2)
        for c in range(2):
            dve_tt(modT[:, l, :, c], bv[:, :, c], sm[:, O_BM + l * 48:O_BM + (l + 1) * 48], ALU.add,
                   [Bbank[bi], B_sm], [B_mod[l]])
        for c in range(2):
            def g(n):
                return sm[:, O_NG + (l * 4 + n) * 8: O_NG + (l * 4 + n) * 8 + 8]
            dve_stt(der[:, l, 0, :, c], modT[:, l, 8:16, c], 1.0, g(0), ALU.add, ALU.mult, [B_mod[l], B_sm], [B_der[l]])
            dve_tt(der[:, l, 1, :, c], modT[:, l, 16:24, c], g(1), ALU.mult, [B_mod[l], B_sm], [B_der[l]])
            dve_stt(der[:, l, 2, :, c], modT[:, l, 32:40, c], 1.0, g(2), ALU.add, ALU.mult, [B_mod[l], B_sm], [B_der[l]])
            dve_tt(der[:, l, 3, :, c], modT[:, l, 40:48, c], g(3), ALU.mult, [B_mod[l], B_sm], [B_der[l]])

    def mod_scalars(l, which, c):
        def gs(k):
            return der[:, l, 2 * which, k, c:c + 1]

        def sh(k):
            return modT[:, l, 24 * which + k, c:c + 1]

        def gg(k):
            return der[:, l, 2 * which + 1, k, c:c + 1]
        return gs, sh, gg

    def load_T(src_d, row0, dst, dcol0, dbufs):
        xi = nxt("xin", 2)
        dma_in("sp", xin[xi][:, :], src_d[row0:row0 + 128, :], B_xin[xi])
        for half in range(2):
            bi = nbank()
            for j in range(4):
                cidx = half * 4 + j
                P.op("pe", lambda e, bi=bi, j=j, cidx=cidx, xi=xi: e.transpose(
                    banks[bi][:, j * 128:(j + 1) * 128], xin[xi][:, cidx * 128:(cidx + 1) * 128], ident[:, :]),
                    reads=[B_xin[xi], B_const], writes=[Bbank[bi]])
            src = banks[bi][:, :].rearrange("p (j t) -> p j t", j=4)
            dsta = dst[:, half * 4:half * 4 + 4, dcol0:dcol0 + 128]
            if half == 0:
                act(dsta, src, AF.Copy, [Bbank[bi]], dbufs)
            else:
                dve_copy(dsta, src, [Bbank[bi]], dbufs)

    def rstd_from_bank(bs, n, nfeat, br):
        t = nxt("tmp", 4)
        act(tmps[t][:, 0:n], banks[bs][:, 0:n], AF.Ln, [Bbank[bs]], [B_tmp[t]], bias=EPS, scale=1.0 / nfeat)
        act(banks[br][:, 0:n], tmps[t][:, 0:n], AF.Exp, [B_tmp[t]], [Bbank[br]], scale=-0.5)

    def prenorm(src, sbufs, c0, n, dst, dbufs, dc0, l, which, c):
        gs, sh, _ = mod_scalars(l, which, c)
        bs = nbank()
        for k in range(KC):
            q = nxt("sq", 2)
            act(sqs[q][:, 0:n], src[:, k, c0:c0 + n], AF.Square, sbufs, [B_sq[q]])
            mm(bs, banks[bs][:, 0:n], ones[:, :], sqs[q][:, 0:n], k == 0, k == KC - 1, [B_sq[q], B_const])
        br = nbank()
        rstd_from_bank(bs, n, D, br)
        for k in range(KC):
            t = nxt("tmp", 4)
            dve_tt(tmps[t][:, 0:n], src[:, k, c0:c0 + n], banks[br][:, 0:n], ALU.mult, sbufs + [Bbank[br]], [B_tmp[t]])
            act(dst[:, k, dc0:dc0 + n], tmps[t][:, 0:n], AF.Identity, [B_tmp[t], B_mod[l], B_der[l]], dbufs,
                bias=sh(k), scale=gs(k))

    class PostNorm:
        def __init__(self, c0, n, xbufs, l, which, c):
            self.c0, self.n, self.xbufs, self.l, self.which, self.c = c0, n, xbufs, l, which, c
            self.bs = nbank()
            reserved.add(self.bs)
            self.cnt = 0

        def add(self, bi, dch):
            n = self.n
            yv = ytmp[:, dch, 0:n]
            act(yv, banks[bi][:, 0:n], AF.Copy, [Bbank[bi]], B_hT)
            q = nxt("sq", 2)
            act(sqs[q][:, 0:n], banks[bi][:, 0:n], AF.Square, [Bbank[bi]], [B_sq[q]])
            mm(self.bs, banks[self.bs][:, 0:n], ones[:, :], sqs[q][:, 0:n], self.cnt == 0, self.cnt == KC - 1,
               [B_sq[q], B_const])
            self.cnt += 1

        def finish(self):
            n, c0 = self.n, self.c0
            _, _, gg = mod_scalars(self.l, self.which, self.c)
            br = nbank()
            reserved.discard(self.bs)
            rstd_from_bank(self.bs, n, D, br)
            for dch in range(KC):
                t = nxt("tmp", 4)
                dve_tt(tmps[t][:, 0:n], ytmp[:, dch, 0:n], banks[br][:, 0:n], ALU.mult, B_hT + [Bbank[br]], [B_tmp[t]])
                xa = xT[:, dch, c0:c0 + n]
                dve_stt(xa, tmps[t][:, 0:n], gg(dch), xa, ALU.mult, ALU.add,
                        [B_tmp[t], B_der[self.l]] + self.xbufs, self.xbufs)

    def linear_fm(bi, wfun, rhsfun, n, reads, nk=KC):
        for k in range(nk):
            mm(bi, banks[bi][:, 0:n], wfun(k), rhsfun(k), k == 0, k == nk - 1, reads)

    def rope_epilogue(ba, bb, n, cos_ap, sin_ap, out_ap, obufs, out_hi=None):
        t1 = nxt("tmp", 4)
        dve_tt(tmps[t1][:, 0:n], banks[ba][:, 0:n], cos_ap, ALU.mult, [Bbank[ba], B_const], [B_tmp[t1]])
        t2 = nxt("tmp", 4)
        dve_tt(tmps[t2][:, 0:n], banks[bb][:, 0:n], sin_ap, ALU.mult, [Bbank[bb], B_const], [B_tmp[t2]])
        if out_hi is None:
            dve_tt(out_ap, tmps[t1][:, 0:n], tmps[t2][:, 0:n], ALU.add, [B_tmp[t1], B_tmp[t2]], obufs)
        else:
            dve_tt(out_ap, tmps[t1][0:64, 0:n], tmps[t2][0:64, 0:n], ALU.add, [B_tmp[t1], B_tmp[t2]], obufs)
            dve_tt(out_hi, tmps[t1][64:128, 0:n], tmps[t2][64:128, 0:n], ALU.add, [B_tmp[t1], B_tmp[t2]], obufs)

    SCALE = 0.125

    def attn_job(grp, q_ap, n, blocks, e, qreads):
        sbk = [grp * 4, grp * 4 + 1]
        bo, bl = grp * 4 + 2, grp * 4 + 3
        nb = len(blocks)

        def qk(j):
            k_ap, kreads, _, _, mi = blocks[j]
            s = sbk[j % 2]
            mm(s, banks[s][:, 0:n], k_ap, q_ap, True, mi is None, qreads + kreads)
            if mi is not None:
                for g in range(n // 128):
                    mm(s, banks[s][:, g * 128:(g + 1) * 128], identb[:, :], maskb[:, mi, :], False,
                       g == n // 128 - 1, [B_const])

        qk(0)
        for j in range(nb):
            if j + 1 < nb:
                qk(j + 1)
            s = sbk[j % 2]
            ei = nxt("E", 4)
            act(Es[ei][:, 0:n], banks[s][:, 0:n], AF.Exp, [Bbank[s]], [B_E[ei]], scale=SCALE)
            _, _, v_ap, vreads, _ = blocks[j]
            mm(bo, banks[bo][0:e, 0:n], v_ap, Es[ei][:, 0:n], j == 0, j == nb - 1, [B_E[ei]] + vreads)
            mm(bl, banks[bl][0:e, 0:n], ones[:, 0:e], Es[ei][:, 0:n], j == 0, j == nb - 1, [B_E[ei], B_const])
        return bo, bl

    if STAGE >= -4:
        modulation(0)

    for t in range(4):
        load_T(xp_d, t * 128, xT, t * 128, [B_xT[0]])
    for t in range(6):
        load_T(xw_d, t * 128, xT, 512 + t * 128, [B_xT[1] if t < 4 else B_xT[2]])

    xfT = ytmp
    for fc in range(4 if STAGE >= -3 else 0):
        for t in range(4):
            load_T(xf_d, fc * 512 + t * 128, xfT, t * 128, B_hT)
        prenorm(xfT, B_hT, 0, 512, hfT, [B_hfT[fc]], fc * 512, 0, 0, 1)
    for ci, (c0, n) in enumerate(TCH if STAGE >= -3 else []):
        prenorm(xT, [B_xT[ci]], c0, n, hT, [B_hT[ci]], c0, 0, 0, 0 if ci == 0 else 1)

    for hd in range(KHEADS if STAGE >= -2 else 0):
        si = load_w(wd0_d[hd].rearrange("p k c -> p (k c)"), 5120)
        wv = slots[si][:, 0:5120].rearrange("p (k c) -> p k c", k=8)
        WQ, WQR, WKR, WK, WV = 0, 128, 256, 384, 512
        rs = [B_slot[si]]
        if "c" not in KSKIP:
            dma_in("sp", cst[:, 0:512].rearrange("p (j c) -> p j c", j=4), ckd_d[hd], B_cst)
            bi = nbank()
            for j in range(4):
                P.op("pe", lambda e, bi=bi, j=j: e.transpose(banks[bi][:, j * 128:(j + 1) * 128],
                                                            cst[:, j * 128:(j + 1) * 128], ident[:, :]),
                     reads=[B_cst, B_const], writes=[Bbank[bi]])
            act(kTh[:, 2560:3072], banks[bi][:, :], AF.Copy, [Bbank[bi]], [B_kTh[2]])
            dma_in("sp", cst[:, 512:1024], cvd_d[hd].rearrange("p j c -> p (j c)"), B_cstv)
            dve_copy(Vh[:, 2560:3072], cst[:, 512:1024], [B_cstv], [B_Vh[2]])
        bi = nbank()
        linear_fm(bi, lambda k: wv[:, k, WQ:WQ + 128], lambda k: hT[:, k, 0:512], 512, rs + [B_hT[0]])
        act(qTh[0:64, 0, 0:512], banks[bi][0:64, :], AF.Copy, [Bbank[bi]], [B_qTh[0]])
        act(qTh[64:128, 1, 0:512], banks[bi][64:128, :], AF.Copy, [Bbank[bi]], [B_qTh[0]])
        bi = nbank()
        linear_fm(bi, lambda k: wv[:, k, WK:WK + 128], lambda k: hT[:, k, 0:512], 512, rs + [B_hT[0]])
        act(kTh[:, 0:512], banks[bi][:, :], AF.Copy, [Bbank[bi]], [B_kTh[0]])
        for t in range(0 if "t" in KSKIP else 4):
            bi = nbank()
            linear_fm(bi, lambda k, t=t: hT[:, k, t * 128:(t + 1) * 128], lambda k: wv[:, k, WK:WK + 256], 256,
                      rs + [B_hT[0]])
            oi = nxt("ost", 2)
            act(ostage[oi][:, 0:256], banks[bi][:, 0:256], AF.Copy, [Bbank[bi]], [B_ost[oi]])
            dve_copy(Vh[:, t * 128:(t + 1) * 128], banks[bi][:, 128:256], [Bbank[bi]], [B_Vh[0]])
            if "o" not in KSKIP:
                dma_out(ndk_d[t * 128:(t + 1) * 128, hd * 128:(hd + 1) * 128], ostage[oi][:, 0:128], B_ost[oi])
                dma_out(ndv_d[t * 128:(t + 1) * 128, hd * 128:(hd + 1) * 128], ostage[oi][:, 128:256], B_ost[oi])
        for ci in (() if "r" in KSKIP else (1, 2)):
            c0, n = TCH[ci]
            ba = nbank()
            linear_fm(ba, lambda k: wv[:, k, WQ:WQ + 128], lambda k: hT[:, k, c0:c0 + n], n, rs + [B_hT[ci]])
            bb = nbank()
            linear_fm(bb, lambda k: wv[:, k, WQR:WQR + 128], lambda k: hT[:, k, c0:c0 + n], n, rs + [B_hT[ci]])
            rope_epilogue(ba, bb, n, ropeW[:, 0, c0 - 512:c0 - 512 + n], ropeW[:, 1, c0 - 512:c0 - 512 + n],
                          qTh[0:64, 0, c0:c0 + n], [B_qTh[1]], out_hi=qTh[64:128, 1, c0:c0 + n])
        for fc in range(0 if "f" in KSKIP else 4):
            f0 = fc * 512
            ba = nbank()
            linear_fm(ba, lambda k: wv[:, k, WK:WK + 128], lambda k: hfT[:, k, f0:f0 + 512], 512, rs + [B_hfT[fc]])
            bb = nbank()
            linear_fm(bb, lambda k: wv[:, k, WKR:WKR + 128], lambda k: hfT[:, k, f0:f0 + 512], 512, rs + [B_hfT[fc]])
            rope_epilogue(ba, bb, 512, ropeF[:, 0, f0:f0 + 512], ropeF[:, 1, f0:f0 + 512],
                          kTh[:, 512 + f0:512 + f0 + 512], [B_kTh[1]])
            bi = nbank()
            for t in range(4):
                for k in range(KC):
                    mm(bi, banks[bi][:, t * 128:(t + 1) * 128], hfT[:, k, f0 + t * 128:f0 + (t + 1) * 128],
                       wv[:, k, WV:WV + 128], k == 0, k == KC - 1, rs + [B_hfT[fc]])
            dve_copy(Vh[:, 512 + f0:512 + f0 + 512], banks[bi][:, :], [Bbank[bi]], [B_Vh[1]])

        def diff_finalize(grp, bo, bl, n, mapi, out_ap, obufs):
            t = nxt("tmp", 4)
            dve_recip(tmps[t][:, 0:n], banks[bl][:, 0:n], [Bbank[bl]], [B_tmp[t]])
            if mapi == 0:
                dve_tt(T1[:, 0:n], banks[bo][:, 0:n], tmps[t][:, 0:n], ALU.mult, [Bbank[bo], B_tmp[t]], [B_T1])
                return
            t2 = nxt("tmp", 4)
            dve_tt(tmps[t2][:, 0:n], banks[bo][:, 0:n], tmps[t][:, 0:n], ALU.mult, [Bbank[bo], B_tmp[t]], [B_tmp[t2]])
            t3 = nxt("tmp", 4)
            dve_stt(tmps[t3][:, 0:n], tmps[t2][:, 0:n], lamt[:, 4:5], T1[:, 0:n], ALU.mult, ALU.add,
                    [B_tmp[t2], B_lam, B_T1], [B_tmp[t3]])
            q = nxt("sq", 2)
            act(sqs[q][:, 0:n], tmps[t3][:, 0:n], AF.Square, [B_tmp[t3]], [B_sq[q]])
            bs = grp * 4
            mm(bs, banks[bs][:, 0:n], ones[:, :], sqs[q][:, 0:n], True, True, [B_sq[q], B_const])
            br = grp * 4 + 1
            rstd_from_bank(bs, n, 128, br)
            dve_stt(out_ap, tmps[t3][:, 0:n], lamt[:, 5:6], banks[br][:, 0:n], ALU.mult, ALU.mult,
                    [B_tmp[t3], B_lam, Bbank[br]], obufs)

        for pb in range(0 if "p" in KSKIP else 2):
            for mapi in range(2):
                r0 = mapi * 64
                blocks = []
                for j in range(2):
                    kc = pb * 256 + j * 128
                    blocks.append((kTh[:, kc:kc + 128], [B_kTh[0]],
                                   Vh[:, kc:kc + 128], [B_Vh[0]], None))
                bo, bl = attn_job(mapi, qTh[:, mapi, pb * 256:(pb + 1) * 256], 256, blocks, 128, [B_qTh[0]])
                diff_finalize(mapi, bo, bl, 256, mapi, oT[:, hd, pb * 256:(pb + 1) * 256], [B_oT[0]])
        for ci in (() if "w" in KSKIP else (1, 2)):
            c0, n = TCH[ci]
            for mapi in range(2):
                r0 = mapi * 64
                blocks = []
                for j in range(16):
                    kc = 512 + j * 128
                    blocks.append((kTh[:, kc:kc + 128], [B_kTh[1]], Vh[:, kc:kc + 128], [B_Vh[1]], None))
                for j in range(4):
                    kc = 2560 + j * 128
                    blocks.append((kTh[:, kc:kc + 128], [B_kTh[2]], Vh[:, kc:kc + 128], [B_Vh[2]], None))
                bo, bl = attn_job(mapi, qTh[:, mapi, c0:c0 + n], n, blocks, 128, [B_qTh[1]])
                diff_finalize(mapi, bo, bl, n, mapi, oT[:, hd, c0:c0 + n], [B_oT[ci]])

    def out_proj_l0(l):
        for ci, (c0, n) in enumerate(TCH):
            pn = PostNorm(c0, n, [B_xT[ci]], l, 0, 0 if ci == 0 else 1)
            for half in range(2):
                si = load_w(wo0_d[half].rearrange("p k c -> p (k c)"), 4096)
                wv = slots[si][:, 0:4096].rearrange("p (k c) -> p k c", k=8)
                for dl in range(4):
                    bi = nbank()
                    linear_fm(bi, lambda k: wv[:, k, dl * 128:(dl + 1) * 128], lambda k: oT[:, k, c0:c0 + n], n,
                              [B_slot[si], B_oT[ci]])
                    pn.add(bi, half * 4 + dl)
            pn.finish()

    def ffn(l, chunks):
        for (c0, n, xb, hb, ab, c) in chunks:
            prenorm(xT, xb, c0, n, hT, [hb], c0, l, 1, c)
        for piece in range(8):
            nf = min(3, NF - 3 * piece)
            si = load_w(wgu_d[l, piece, :, 0:nf].rearrange("p f k c -> p (f k c)"), nf * 2048)
            wv = slots[si][:, 0:nf * 2048].rearrange("p (f k c) -> p f k c", f=nf, k=8)
            for fl in range(nf):
                f = 3 * piece + fl
                for (c0, n, xb, hb, ab, c) in chunks:
                    bg = nbank()
                    linear_fm(bg, lambda k: wv[:, fl, k, 0:128], lambda k: hT[:, k, c0:c0 + n], n, [B_slot[si], hb])
                    bu = nbank()
                    linear_fm(bu, lambda k: wv[:, fl, k, 128:256], lambda k: hT[:, k, c0:c0 + n], n, [B_slot[si], hb])
                    t = nxt("tmp", 4)
                    act(tmps[t][:, 0:n], banks[bg][:, 0:n], AF.Silu, [Bbank[bg]], [B_tmp[t]])
                    dve_tt(aT[:, f, c0:c0 + n], tmps[t][:, 0:n], banks[bu][:, 0:n], ALU.mult,
                           [B_tmp[t], Bbank[bu]], [ab])
        for (c0, n, xb, hb, ab, c) in chunks:
            pn = PostNorm(c0, n, xb, l, 1, c)
            for j in range(4):
                si = load_w(wdn_d[l, j].rearrange("p d f c -> p (d f c)"), 5632)
                wv = slots[si][:, 0:5632].rearrange("p (d f c) -> p d f c", d=2, f=22)
                for dl in range(2):
                    bi = nbank()
                    for f in range(NF):
                        mm(bi, banks[bi][:, 0:n], wv[:, dl, f, :], aT[:, f, c0:c0 + n], f == 0, f == NF - 1,
                           [B_slot[si], ab])
                    pn.add(bi, 2 * j + dl)
            pn.finish()

    if STAGE >= -1:
        out_proj_l0(0)
    if STAGE >= 1:
        modulation(1)
    P.alias(B_aT, B_hfT + B_oT)
    if STAGE >= 0:
        ffn(0, [(TCH[i][0], TCH[i][1], [B_xT[i]], B_hT[i], B_aT[i], 0 if i == 0 else 1) for i in range(3)])

    if STAGE >= 2:
        P.alias(B_qTs + [B_kTs] + B_oS, B_aT)
        for ci, (c0, n) in enumerate(TCH):
            prenorm(xT, [B_xT[ci]], c0, n, hT, [B_hT[ci]], c0, 1, 0, 0 if ci == 0 else 1)
        OWN0 = 640
        B_hown = [B_hT[1], B_hT[2]]
        P.op("dve", lambda e: e.memset(kTs[:, :, :], 0.0), reads=[], writes=[B_kTs])
        VS = Vh[:, 0:3584].rearrange("p (t c) -> p t c", t=14)
        B_VS = Buf("VS")
        P.alias([B_VS], list(B_Vh))
        for half in range(2):
            sa = load_w(ws1q_d[half].rearrange("p k c -> p (k c)"), 4096)
            sr = load_w(ws1q_d[2 + half].rearrange("p k c -> p (k c)"), 4096)
            wa = slots[sa][:, 0:4096].rearrange("p (k c) -> p k c", k=8)
            wr = slots[sr][:, 0:4096].rearrange("p (k c) -> p k c", k=8)
            for cl in range(4):
                cq = half * 4 + cl
                bi = nbank()
                linear_fm(bi, lambda k: wa[:, k, cl * 128:(cl + 1) * 128], lambda k: hT[:, k, 0:512], 512,
                          [B_slot[sa], B_hT[0]])
                act(qTs[:, cq, 0:512], banks[bi][:, :], AF.Copy, [Bbank[bi]], [B_qTs[0]])
                ba = nbank()
                linear_fm(ba, lambda k: wa[:, k, cl * 128:(cl + 1) * 128], lambda k: hT[:, k, OWN0:OWN0 + 512], 512,
                          [B_slot[sa]] + B_hown)
                bb = nbank()
                linear_fm(bb, lambda k: wr[:, k, cl * 128:(cl + 1) * 128], lambda k: hT[:, k, OWN0:OWN0 + 512], 512,
                          [B_slot[sr]] + B_hown)
                rope_epilogue(ba, bb, 512, ropeW[:, 0, 128:640], ropeW[:, 1, 128:640], qTs[:, cq, 512:1024],
                              [B_qTs[1]])
        sk = load_w(ws1k_d.rearrange("p k c -> p (k c)"), 6144)
        wk = slots[sk][:, 0:6144].rearrange("p (k c) -> p k c", k=8)
        rsk = [B_slot[sk]]
        for p in range(2):
            bi = nbank()
            linear_fm(bi, lambda k: wk[:, k, p * 128:(p + 1) * 128], lambda k: hT[:, k, 0:512], 512, rsk + [B_hT[0]])
            act(kTs[0:64, 2 * p, 0:512], banks[bi][0:64, :], AF.Copy, [Bbank[bi]], [B_kTs])
            act(kTs[64:128, 2 * p + 1, 0:512], banks[bi][64:128, :], AF.Copy, [Bbank[bi]], [B_kTs])
            for ci in (1, 2):
                c0, n = TCH[ci]
                ba = nbank()
                linear_fm(ba, lambda k: wk[:, k, p * 128:(p + 1) * 128], lambda k: hT[:, k, c0:c0 + n], n,
                          rsk + [B_hT[ci]])
                bb = nbank()
                linear_fm(bb, lambda k: wk[:, k, 256 + p * 128:256 + (p + 1) * 128], lambda k: hT[:, k, c0:c0 + n], n,
                          rsk + [B_hT[ci]])
                rope_epilogue(ba, bb, n, ropeW[:, 0, c0 - 512:c0 - 512 + n], ropeW[:, 1, c0 - 512:c0 - 512 + n],
                              kTs[0:64, 2 * p, c0:c0 + n], [B_kTs], out_hi=kTs[64:128, 2 * p + 1, c0:c0 + n])
        for t in range(4):
            bi = nbank()
            for k in range(KC):
                mm(bi, banks[bi][:, 0:256], hT[:, k, t * 128:(t + 1) * 128], wk[:, k, 0:256], k == 0, k == KC - 1,
                   rsk + [B_hT[0]])
            for k in range(KC):
                mm(bi, banks[bi][:, 256:512], hT[:, k, t * 128:(t + 1) * 128], wk[:, k, 512:768], k == 0, k == KC - 1,
                   rsk + [B_hT[0]])
            oi = nxt("ost", 2)
            act(ostage[oi][:, 0:512], banks[bi][:, :], AF.Copy, [Bbank[bi]], [B_ost[oi]])
            dve_copy(VS[:, t, :], banks[bi][:, 256:512], [Bbank[bi]], [B_VS])
            dma_out(nsk_d[t * 128:(t + 1) * 128, :], ostage[oi][:, 0:256], B_ost[oi])
            dma_out(nsv_d[t * 128:(t + 1) * 128, :], ostage[oi][:, 256:512], B_ost[oi])
        for t in range(6):
            bi = nbank()
            c0 = 512 + t * 128
            for k in range(KC):
                mm(bi, banks[bi][:, 0:256], hT[:, k, c0:c0 + 128], wk[:, k, 512:768], k == 0, k == KC - 1,
                   rsk + [B_hT[1] if t < 4 else B_hT[2]])
            dve_copy(VS[:, 4 + t, :], banks[bi][:, 0:256], [Bbank[bi]], [B_VS])
        dma_in("sp", cst[:, :].rearrange("p (j c) -> p j c", j=4), cks_d, B_cst)
        for p in range(2):
            bi = nbank()
            for j in range(4):
                P.op("pe", lambda e, bi=bi, j=j, p=p: e.transpose(
                    banks[bi][:, j * 128:(j + 1) * 128], cst[:, j * 256 + p * 128:j * 256 + (p + 1) * 128], ident[:, :]),
                    reads=[B_cst, B_const], writes=[Bbank[bi]])
            act(kTs[0:64, 2 * p, 1280:1792], banks[bi][0:64, :], AF.Copy, [Bbank[bi]], [B_kTs])
            act(kTs[64:128, 2 * p + 1, 1280:1792], banks[bi][64:128, :], AF.Copy, [Bbank[bi]], [B_kTs])
        xi = nxt("xin", 2)
        dma_in("sp", xin[xi][:, :], cvs_d.rearrange("p j c -> p (j c)"), B_xin[xi])
        dve_copy(Vh[:, 2560:3584], xin[xi][:, :], [B_xin[xi]], [B_VS])

        def swa_finalize(bo, bl, ng, qn, kv, g0, oS_col0, obuf):
            r0 = (kv % 2) * 64
            pr = kv // 2
            t = nxt("tmp", 4)
            for g in range(ng):
                hh = kv * 4 + g0 + g
                dve_ts(tmps[t][r0:r0 + 64, g * qn:(g + 1) * qn], banks[bl][r0:r0 + 64, g * qn:(g + 1) * qn],
                       sinkE[r0:r0 + 64, hh:hh + 1], None, ALU.add, None,
                       [Bbank[bl], B_lam], [B_tmp[t]])
            n = ng * qn
            t2 = nxt("tmp", 4)
            dve_recip(tmps[t2][r0:r0 + 64, 0:n], tmps[t][r0:r0 + 64, 0:n], [B_tmp[t]], [B_tmp[t2]])
            for g in range(ng):
                dve_tt(oS[r0:r0 + 64, pr * 4 + g0 + g, oS_col0:oS_col0 + qn],
                       banks[bo][r0:r0 + 64, g * qn:(g + 1) * qn],
                       tmps[t2][r0:r0 + 64, g * qn:(g + 1) * qn], ALU.mult, [Bbank[bo], B_tmp[t2]], [obuf])

        grp = 0
        for pb in range(2):
            for kv in range(4):
                p, r0 = kv // 2, (kv % 2) * 64
                for gh in range(2):
                    q_ap = qTs[:, p * 4 + gh * 2:p * 4 + gh * 2 + 2, pb * 256:(pb + 1) * 256]
                    blocks = []
                    for j in range(2):
                        kc = pb * 256 + j * 128
                        blocks.append((kTs[:, kv, kc:kc + 128], [B_kTs],
                                       VS[:, pb * 2 + j, p * 128:(p + 1) * 128], [B_VS], None))
                    bo, bl = attn_job(grp, q_ap, 512, blocks, 128, [B_qTs[0]])
                    swa_finalize(bo, bl, 2, 256, kv, gh * 2, pb * 256, B_oS[0])
                    grp ^= 1
        for qb in range(4):
            for kv in range(4):
                p, r0 = kv // 2, (kv % 2) * 64
                q_ap = qTs[:, p * 4:p * 4 + 4, 512 + qb * 128:512 + (qb + 1) * 128]
                blocks = []
                for j in range(4):
                    kc = 1280 + j * 128
                    blocks.append((kTs[:, kv, kc:kc + 128], [B_kTs],
                                   VS[:, 10 + j, p * 128:(p + 1) * 128], [B_VS], None))
                for dj, mi in ((0, 2 * qb), (1, None), (2, 2 * qb + 1)):
                    w = qb + dj
                    kc = 512 + w * 128
                    blocks.append((kTs[:, kv, kc:kc + 128], [B_kTs],
                                   VS[:, 4 + w, p * 128:(p + 1) * 128], [B_VS], mi))
                bo, bl = attn_job(grp, q_ap, 512, blocks, 128, [B_qTs[1]])
                swa_finalize(bo, bl, 4, 128, kv, 0, 512 + qb * 128, B_oS[1])
                grp ^= 1
        L1CH = [(0, 512, [B_xT[0]], 0, 0, B_oS[0]), (OWN0, 512, [B_xT[1], B_xT[2]], 1, 512, B_oS[1])]
        for (c0, n, xb, c, oc0, ob) in L1CH:
            pn = PostNorm(c0, n, xb, 1, 0, c)
            for half in range(2):
                si = load_w(wo1_d[half].rearrange("p k c -> p (k c)"), 4096)
                wv = slots[si][:, 0:4096].rearrange("p (k c) -> p k c", k=8)
                for dl in range(4):
                    bi = nbank()
                    linear_fm(bi, lambda k: wv[:, k, dl * 128:(dl + 1) * 128], lambda k: oS[:, k, oc0:oc0 + n], n,
                              [B_slot[si], ob])
                    pn.add(bi, half * 4 + dl)
            pn.finish()
        P.alias(B_aT, B_qTs + [B_kTs] + B_oS)
        if STAGE >= 3:
            ffn(1, [(0, 512, [B_xT[0]], B_hT[0], B_aT[0], 0),
                    (OWN0, 512, [B_xT[1], B_xT[2]], B_hT[1], B_aT[1], 1)])

    OWN0 = 640
    for t in range(8):
        c0 = t * 128 if t < 4 else OWN0 + (t - 4) * 128
        xb = [B_xT[0]] if t < 4 else [B_xT[1], B_xT[2]]
        xi = nxt("xin", 2)
        for half in range(2):
            bi = nbank()
            for j in range(4):
                cidx = half * 4 + j
                P.op("pe", lambda e, bi=bi, j=j, cidx=cidx, c0=c0: e.transpose(
                    banks[bi][:, j * 128:(j + 1) * 128], xT[:, cidx, c0:c0 + 128], ident[:, :]),
                    reads=xb + [B_const], writes=[Bbank[bi]])
            if half == 0:
                act(xin[xi][:, 0:512], banks[bi][:, :], AF.Copy, [Bbank[bi]], [B_xin[xi]])
            else:
                dve_copy(xin[xi][:, 512:1024], banks[bi][:, :], [Bbank[bi]], [B_xin[xi]])
        dst = yp_d[t * 128:(t + 1) * 128, :] if t < 4 else ys_d[(t - 4) * 128:(t - 3) * 128, :]
        dma_out(dst, xin[xi][:, :], B_xin[xi])

    P.finish()
    stats = P.emit()
    return nc, es, stats


def _rope_tables(pos):
    pos = np.asarray(pos)
    row = (pos // 64).astype(np.float32)
    col = (pos % 64).astype(np.float32)
    nf = 16
    inv = (np.float32(10000.0) ** (-np.arange(nf, dtype=np.float32) / np.float32(nf))).astype(np.float32)
    ar = row[:, None] * inv[None, :]
    ac = col[:, None] * inv[None, :]
    ang = np.concatenate([ar, ar, ac, ac], axis=-1).astype(np.float32)
    cos = np.cos(ang).astype(np.float32)
    sin = np.sin(ang).astype(np.float32)
    sign = np.concatenate([-np.ones(16), np.ones(16), -np.ones(16), np.ones(16)]).astype(np.float32)
    sin = sin * sign[None, :]
    cos2 = np.concatenate([cos, cos], axis=1).T
    sin2 = np.concatenate([sin, sin], axis=1).T
    return np.ascontiguousarray(cos2), np.ascontiguousarray(sin2)


_ROTSRC = np.concatenate([np.arange(16, 32), np.arange(0, 16), np.arange(48, 64), np.arange(32, 48)])


def _rot_cols(w):
    n = w.shape[1] // 64
    idx = (np.arange(n)[:, None] * 64 + _ROTSRC[None, :]).reshape(-1)
    return w[:, idx]


def _kmaj(w):
    return np.ascontiguousarray(w.reshape(8, 128, -1).transpose(1, 0, 2))


_PROG_CACHE = {}


def prep_inputs(x_prompt, x_sample, cache_diff_k, cache_diff_v, cache_swa_k, cache_swa_v, c, c_ctx,
                w_mod, b_mod, norm_g, w_qkv_diff, diff_lambda, diff_subln_g, w_o_diff,
                w_qkv_swa, swa_sink, w_o_swa, w_gate, w_up, w_down):
    f = np.float32
    A = lambda a: np.ascontiguousarray(np.asarray(a, dtype=f))
    x_prompt, x_sample = A(x_prompt), A(x_sample)
    cache_diff_k, cache_diff_v = A(cache_diff_k), A(cache_diff_v)
    cache_swa_k, cache_swa_v = A(cache_swa_k), A(cache_swa_v)
    c, c_ctx = A(c), A(c_ctx)
    w_mod, b_mod, norm_g = A(w_mod), A(b_mod), A(norm_g)
    w_qkv_diff, diff_lambda, diff_subln_g, w_o_diff = A(w_qkv_diff), A(diff_lambda), A(diff_subln_g), A(w_o_diff)
    w_qkv_swa, swa_sink, w_o_swa = A(w_qkv_swa), A(swa_sink), A(w_o_swa)
    w_gate, w_up, w_down = A(w_gate), A(w_up), A(w_down)

    wmod = np.ascontiguousarray(
        w_mod.reshape(2, 8, 128, 8, 768).transpose(0, 3, 2, 1, 4))
    wq, wk, wv = w_qkv_diff[0][:, 0:1024], w_qkv_diff[0][:, 1024:2048], w_qkv_diff[0][:, 2048:3072]
    wqr, wkr = _rot_cols(wq), _rot_cols(wk)
    wd0 = np.empty((8, 128, 8, 640), f)
    for h in range(8):
        s = slice(h * 128, (h + 1) * 128)
        wd0[h] = _kmaj(np.concatenate([wq[:, s], wqr[:, s], wkr[:, s], wk[:, s], wv[:, s]], axis=1))
    wo0 = np.stack([_kmaj(w_o_diff[0][:, 0:512]), _kmaj(w_o_diff[0][:, 512:1024])])
    ws = w_qkv_swa[0]
    sq_cols = []
    for p in range(2):
        for g in range(4):
            for kvl in range(2):
                hh = (2 * p + kvl) * 4 + g
                sq_cols.append(np.arange(hh * 64, (hh + 1) * 64))
    sq_cols = np.concatenate(sq_cols)
    wsq = ws[:, 0:1024]
    wsq_p = wsq[:, sq_cols]
    wsqr_p = _rot_cols(wsq)[:, sq_cols]
    ws1q = np.stack([_kmaj(wsq_p[:, 0:512]), _kmaj(wsq_p[:, 512:1024]),
                     _kmaj(wsqr_p[:, 0:512]), _kmaj(wsqr_p[:, 512:1024])])
    wsk, wsv = ws[:, 1024:1280], ws[:, 1280:1536]
    ws1k = _kmaj(np.concatenate([wsk, _rot_cols(wsk), wsv], axis=1))
    wos_p = w_o_swa[0][sq_cols, :]
    wo1 = np.stack([_kmaj(wos_p[:, 0:512]), _kmaj(wos_p[:, 512:1024])])
    wgu_f = np.zeros((2, 24, 128, 8, 256), f)
    for l in range(2):
        g_ = w_gate[l].reshape(8, 128, 22, 128).transpose(2, 1, 0, 3)
        u_ = w_up[l].reshape(8, 128, 22, 128).transpose(2, 1, 0, 3)
        wgu_f[l, 0:22, :, :, 0:128] = g_
        wgu_f[l, 0:22, :, :, 128:256] = u_
    wgu = np.ascontiguousarray(wgu_f.reshape(2, 8, 3, 128, 8, 256).transpose(0, 1, 3, 2, 4, 5))
    wdn = np.ascontiguousarray(w_down.reshape(2, 22, 128, 4, 2, 128).transpose(0, 3, 2, 4, 1, 5))
    ident = np.eye(128, dtype=f)
    cosf, sinf = _rope_tables(np.arange(TFULL))
    ropef = np.ascontiguousarray(np.stack([cosf, sinf], axis=1))

    normg_l = norm_g.reshape(8, 8, 128).transpose(2, 0, 1).reshape(128, 64)
    bmod_l = b_mod.reshape(2, 48, 128).transpose(2, 0, 1).reshape(128, 96)
    subg_l = diff_subln_g.reshape(128, 1)
    lam_l = np.broadcast_to(diff_lambda.reshape(1, 256), (128, 256))
    sink_l = np.broadcast_to(swa_sink.reshape(1, 16), (128, 16))

    in_maps = []
    for core in range(NCORES):
        b, ch = core // 4, core % 4
        xp = x_prompt[2 * core:2 * core + 2].reshape(512, D)
        pos = np.arange(ch * 512 - 128, ch * 512 + 640)
        valid = (pos >= 0) & (pos < 2048)
        xw = np.zeros((TW, D), f)
        xw[valid] = x_sample[b, pos[valid]]
        cosw, sinw = _rope_tables(np.clip(pos, 0, 2047))
        ropew = np.ascontiguousarray(np.stack([cosw, sinw], axis=1))
        maskb = np.zeros((128, 8, 128), f)
        for qb in range(4):
            for side in range(2):
                w = qb + (0 if side == 0 else 2)
                kpos = pos[w * 128:(w + 1) * 128]
                qpos = pos[(qb + 1) * 128:(qb + 2) * 128]
                ok = ((kpos[:, None] >= 0) & (kpos[:, None] < 2048)
                      & (np.abs(qpos[None, :] - kpos[:, None]) <= 128))
                maskb[:, 2 * qb + side, :] = np.where(ok, 0.0, -30000.0)
        cond_l = np.stack([c_ctx, c[b]], axis=-1).reshape(8, 128, 2).transpose(1, 0, 2).reshape(128, 16)
        sm = np.ascontiguousarray(np.concatenate([cond_l, normg_l, bmod_l, subg_l, lam_l, sink_l], axis=1), dtype=f)
        ckd = np.ascontiguousarray(cache_diff_k[b, 0].reshape(4, 128, 8, 128).transpose(2, 1, 0, 3))
        cvd = np.ascontiguousarray(cache_diff_v[b, 0].reshape(4, 128, 8, 128).transpose(2, 1, 0, 3))
        cks = np.ascontiguousarray(cache_swa_k[b, 0].reshape(4, 128, 256).transpose(1, 0, 2))
        cvs = np.ascontiguousarray(cache_swa_v[b, 0].reshape(4, 128, 256).transpose(1, 0, 2))
        in_maps.append(dict(xp=np.ascontiguousarray(xp), xw=xw, xf=x_sample[b], ckd=ckd, cvd=cvd, cks=cks, cvs=cvs,
                            sm=sm, ident=ident, ropew=ropew, ropef=ropef, maskb=maskb, wmod=wmod, wd0=wd0, wo0=wo0,
                            ws1q=ws1q, ws1k=ws1k, wo1=wo1, wgu=wgu, wdn=wdn))
    return in_maps


def kernel(**inputs):
    f = np.float32
    in_maps = prep_inputs(**inputs)
    if "nc" not in _PROG_CACHE:
        nc, es, stats = build_program()
        _PROG_CACHE["nc"] = (nc, es)
        if os.environ.get("KVERBOSE"):
            print("ops per engine:", stats)
    nc, _ = _PROG_CACHE["nc"]
    res = run_bass_kernel_spmd(nc, in_maps, core_ids=list(range(NCORES)))
    R = res.results
    y_prompt = np.concatenate([R[i]["yp"].reshape(2, 256, D) for i in range(NCORES)], axis=0)
    y_sample = np.stack([np.concatenate([R[b * 4 + ch]["ys"] for ch in range(4)], axis=0) for b in range(2)], axis=0)
    ndk = np.concatenate([R[i]["ndk"].reshape(2, 1, 256, 8, 128) for i in range(NCORES)], axis=0)
    ndv = np.concatenate([R[i]["ndv"].reshape(2, 1, 256, 8, 128) for i in range(NCORES)], axis=0)
    nsk = np.concatenate([R[i]["nsk"].reshape(2, 1, 256, 4, 64) for i in range(NCORES)], axis=0)
    nsv = np.concatenate([R[i]["nsv"].reshape(2, 1, 256, 4, 64) for i in range(NCORES)], axis=0)
    return (y_prompt.astype(f), y_sample.astype(f), ndk.astype(f), ndv.astype(f), nsk.astype(f), nsv.astype(f))
```

```python
import os
import numpy as np
import concourse.bass as bass
import concourse.mybir as mybir
from concourse.bass_utils import run_bass_kernel_spmd
from contextlib import ExitStack

F32 = mybir.dt.float32
BF16 = mybir.dt.bfloat16
AF = mybir.ActivationFunctionType
ALU = mybir.AluOpType

D = 1024
KC = 8
DFF = 2816
NF = 22
TP = 512
TW = 768
TT = 1280
TFULL = 2048
LC = 512
EPS = 1e-6
NCORES = 8
STAGE = int(os.environ.get("KSTAGE", "99"))
KHEADS = int(os.environ.get("KHEADS", "8"))
KSKIP = os.environ.get("KSKIP", "")


class Buf:
    __slots__ = ("name", "writers", "readers", "dsem", "ndma", "war", "excl")

    def __init__(self, name, excl=False):
        self.name = name
        self.excl = excl
        self.writers = []
        self.readers = []
        self.war = []
        self.dsem = None
        self.ndma = 0


class Op:
    __slots__ = ("eng", "idx", "fn", "deps", "dma", "buf", "ordinal", "flag", "count", "waits", "ring")

    def __init__(self, eng, idx, fn):
        self.eng = eng
        self.idx = idx
        self.fn = fn
        self.deps = []
        self.dma = False
        self.buf = None
        self.ordinal = 0
        self.flag = False
        self.count = 0
        self.waits = []
        self.ring = False


ENGS = ["pe", "act", "dve", "pool", "sp"]


class Prog:
    def __init__(self, nc, es):
        self.nc = nc
        self.es = es
        self.ops = {e: [] for e in ENGS}
        self.dma_bufs = []
        self.out_dmas = []

    def op(self, eng, fn, reads=(), writes=(), dma_buf=None, is_out=False, ring=False):
        o = Op(eng, len(self.ops[eng]), fn)
        o.ring = ring
        deps = o.deps
        for b in reads:
            for w in b.writers:
                deps.append((w, True))
            if b.excl:
                for r in b.readers:
                    if r.eng != eng:
                        deps.append((r, False))
            b.readers.append(o)
        for b in writes:
            if b.readers:
                b.war = [r for r in b.readers if r is not o]
                b.writers = [o]
                b.readers = []
            else:
                b.writers.append(o)
            for r in b.war:
                deps.append((r, False))
        if dma_buf is not None:
            o.dma = True
            o.buf = dma_buf
            if dma_buf.dsem is None:
                self.dma_bufs.append(dma_buf)
                dma_buf.dsem = True
            dma_buf.ndma += 1
            o.ordinal = dma_buf.ndma
            if is_out:
                self.out_dmas.append(o)
        self.ops[eng].append(o)
        return o

    def alias(self, new_bufs, old_bufs):
        pend = []
        for b in old_bufs:
            pend.extend(b.readers)
            pend.extend(b.writers)
        for nb in new_bufs:
            nb.readers.extend(pend)
            nb.war = []

    def finish(self):
        o = Op("sp", len(self.ops["sp"]), None)
        for d in self.out_dmas:
            o.deps.append((d, True))
        self.ops["sp"].append(o)

    def emit(self):
        nc = self.nc
        es = self.es
        esem = {e: es.enter_context(nc.semaphore("s_" + e)) for e in ["pe", "act", "dve", "pool"]}
        for i, b in enumerate(self.dma_bufs):
            b.dsem = es.enter_context(nc.semaphore("d%d" % i))
        for e in ENGS:
            waited = {}
            for o in self.ops[e]:
                need = {}
                for (d, raw) in o.deps:
                    if d.dma:
                        key = ("d", id(d.buf))
                        val = d.ordinal
                        if waited.get(key, 0) >= val:
                            continue
                        if need.get(key, (0, None))[0] < val:
                            need[key] = (val, d)
                    else:
                        if d.eng == e and e == "pe":
                            continue
                        key = ("e", d.eng)
                        val = d.idx + 1
                        if waited.get(key, 0) >= val:
                            continue
                        if need.get(key, (0, None))[0] < val:
                            need[key] = (val, d)
                for key, (val, d) in need.items():
                    waited[key] = val
                    if not d.dma:
                        d.flag = True
                    o.waits.append(d)
        for e in ["pe", "act", "dve", "pool"]:
            c = 0
            for o in self.ops[e]:
                if o.flag and not o.dma:
                    c += 1
                    o.count = c
        handles = {"pe": "tensor", "act": "scalar", "dve": "vector", "pool": "gpsimd", "sp": "sync"}
        stats = {}
        with nc.Block() as block:
            for e in ENGS:
                ops = self.ops[e]
                stats[e] = len(ops)

                def body(eng, ops=ops, e=e):
                    for o in ops:
                        for d in o.waits:
                            if d.dma:
                                eng.wait_ge(d.buf.dsem, 16 * d.ordinal)
                            else:
                                eng.wait_ge(esem[d.eng], d.count)
                        if o.fn is None:
                            continue
                        inst = o.fn(eng)
                        if o.ring:
                            assert not o.flag
                            inst.then_inc(self.ring_sem, 16)
                        elif o.dma:
                            inst.then_inc(o.buf.dsem, 16)
                        elif o.flag:
                            inst.then_inc(esem[e], 1)

                getattr(block, handles[e])(body)
        return stats


def build_program():
    nc = bass.Bass("TRN2", target_bir_lowering=False, monotonic_sem_count=0)
    es = ExitStack()
    P = Prog(nc, es)

    def din(name, shape):
        return nc.dram_tensor(name, list(shape), F32, kind="ExternalInput").ap()

    def dout(name, shape):
        return nc.dram_tensor(name, list(shape), F32, kind="ExternalOutput").ap()

    xp_d = din("xp", [TP, D])
    xw_d = din("xw", [TW, D])
    xf_d = din("xf", [TFULL, D])
    ckd_d = din("ckd", [8, 128, 4, 128])
    cvd_d = din("cvd", [8, 128, 4, 128])
    cks_d = din("cks", [128, 4, 256])
    cvs_d = din("cvs", [128, 4, 256])
    NSM = 16 + 64 + 96 + 1 + 256 + 16
    sm_d = din("sm", [128, NSM])
    ident_d = din("ident", [128, 128])
    ropew_d = din("ropew", [128, 2, TW])
    ropef_d = din("ropef", [128, 2, TFULL])
    maskb_d = din("maskb", [128, 8, 128])
    wmod_d = din("wmod", [2, 8, 128, 8, 768])
    wd0_d = din("wd0", [8, 128, 8, 640])
    wo0_d = din("wo0", [2, 128, 8, 512])
    ws1q_d = din("ws1q", [4, 128, 8, 512])
    ws1k_d = din("ws1k", [128, 8, 768])
    wo1_d = din("wo1", [2, 128, 8, 512])
    wgu_d = din("wgu", [2, 8, 128, 3, 8, 256])
    wdn_d = din("wdn", [2, 4, 128, 2, 22, 128])

    yp_d = dout("yp", [TP, D])
    ys_d = dout("ys", [512, D])
    ndk_d = dout("ndk", [TP, 1024])
    ndv_d = dout("ndv", [TP, 1024])
    nsk_d = dout("nsk", [TP, 256])
    nsv_d = dout("nsv", [TP, 256])

    def sb(name, shape, dt):
        return es.enter_context(nc.sbuf_tensor(name, list(shape), dt))

    xT = sb("xT", [128, KC, TT], F32)
    hT = sb("hT", [128, KC, TT], BF16)
    ytmp = hT.bitcast(F32)
    BIG = sb("BIG", [128, 28160], BF16)
    slots = [sb("wslot%d" % i, [128, 6144], BF16) for i in range(2)]
    kTh = sb("kTh", [128, 3072], BF16)
    Vh = sb("Vh", [128, 3584], BF16)
    qTh = sb("qTh", [128, 2, TT], BF16)
    Es = [sb("E%d" % i, [128, 2, 512], BF16) for i in range(2)]
    xin = [sb("xin%d" % i, [128, 1024], F32) for i in range(2)]
    cst = sb("cst", [128, 1024], F32)
    ropeW = sb("ropeW", [128, 2, TW], F32)
    ropeF = sb("ropeF", [128, 2, TFULL], BF16)
    maskb = sb("maskbs", [128, 8, 128], BF16)
    ident = sb("idents", [128, 128], F32)
    identb = sb("identb", [128, 128], BF16)
    ones = sb("ones", [128, 128], BF16)
    sm = sb("sms", [128, NSM], F32)
    modT = sb("modT", [128, 2, 48, 2], F32)
    der = sb("der", [128, 2, 4, 8, 2], F32)
    scb = sb("scb", [128, 8, 2], BF16)
    lamt = sb("lamt", [128, 8], F32)
    sinkE = sb("sinkE", [128, 16], F32)
    sqs = [sb("sq%d" % i, [128, 512], BF16) for i in range(2)]
    tmps = [sb("tmp%d" % i, [128, 512], F32) for i in range(4)]
    T1 = sb("T1", [128, 512], F32)
    ostage = [xin[i] for i in range(2)]

    PS = es.enter_context(nc.psum_tensor("PS", [128, 8, 512], F32))

    class BankView:
        def __init__(self, i):
            self.i = i

        def __getitem__(self, idx):
            return PS[idx[0], self.i, idx[1]]

    banks = [BankView(i) for i in range(8)]
    Bbank = [Buf("bank%d" % i, excl=True) for i in range(8)]

    B_xT = [Buf("xT_p"), Buf("xT_w0"), Buf("xT_w1")]
    B_hT = [Buf("hT_p"), Buf("hT_w0"), Buf("hT_w1")]
    B_slot = [Buf("slot0"), Buf("slot1")]
    B_kTh = Buf("kTh_p"), Buf("kTh_f"), Buf("kTh_c")
    B_Vh = Buf("Vh_p"), Buf("Vh_f"), Buf("Vh_c")
    B_qTh = [Buf("qTh_p"), Buf("qTh_w")]
    B_E = [Buf("E%d" % i) for i in range(2)]
    B_xin = [Buf("xin0"), Buf("xin1")]
    B_cst = Buf("cst")
    B_cstv = Buf("cstv")
    B_const = Buf("consts")
    B_sm = Buf("sm")
    B_mod = [Buf("mod0"), Buf("mod1")]
    B_der = [Buf("der0"), Buf("der1")]
    B_scb = Buf("scb")
    B_lam = Buf("lam")
    B_sq = [Buf("sq0"), Buf("sq1")]
    B_tmp = [Buf("tmp%d" % i) for i in range(4)]
    B_T1 = Buf("T1")
    B_ost = B_xin
    B_hfT = [Buf("hfT%d" % i) for i in range(4)]
    B_oT = [Buf("oT_p"), Buf("oT_w0"), Buf("oT_w1")]
    B_aT = [Buf("aT0"), Buf("aT1"), Buf("aT2")]
    B_qTs = [Buf("qTs_p"), Buf("qTs_s")]
    B_kTs = Buf("kTs")
    B_oS = [Buf("oS_p"), Buf("oS_s")]

    hfT = BIG[:, 0:16384].rearrange("p (k t) -> p k t", k=8)
    oT = BIG[:, 16384:16384 + 10240].rearrange("p (k t) -> p k t", k=8)
    aT = BIG[:, 0:28160].rearrange("p (f t) -> p f t", f=22)
    qTs = BIG[:, 0:8192].rearrange("p (k t) -> p k t", k=8)
    kTs = BIG[:, 8192:8192 + 7168].rearrange("p (k t) -> p k t", k=4)
    oS = BIG[:, 15360:15360 + 8192].rearrange("p (h t) -> p h t", h=8)

    TCH = [(0, 512), (512, 512), (1024, 256)]

    rr = {"bank": 0, "tmp": 0, "sq": 0, "slot": 0, "xin": 0, "ost": 0, "E": 0}

    def nxt(kind, n):
        v = rr[kind]
        rr[kind] = (v + 1) % n
        return v

    reserved = set()

    def nbank():
        while True:
            b = nxt("bank", 8)
            if b not in reserved:
                return b

    open_grp = {}

    def mm(bi, out_ap, lhsT, rhs, start, stop, reads):
        if start and open_grp.get(bi):
            import traceback
            traceback.print_stack(limit=6)
            print("OPEN GROUP on bank", bi, "opened at:", open_grp[bi])
        if start:
            import traceback
            open_grp[bi] = "".join(traceback.format_stack(limit=5)[:-1])
        if stop:
            open_grp[bi] = None
        P.op("pe", lambda e: e.matmul(out_ap, lhsT, rhs, start=start, stop=stop),
             reads=reads, writes=[Bbank[bi]])

    def act(out_ap, in_ap, func, reads, writes, bias=None, scale=None):
        kw = {}
        if bias is not None:
            kw["bias"] = bias
        if scale is not None:
            kw["scale"] = scale
        P.op("act", lambda e: e.activation(out_ap, in_ap, func, **kw), reads=reads, writes=writes)

    def dve_tt(out_ap, a, b, op, reads, writes, eng="dve"):
        P.op(eng, lambda e: e.tensor_tensor(out_ap, a, b, op), reads=reads, writes=writes)

    def dve_stt(out_ap, a, scalar, b, op0, op1, reads, writes):
        P.op("dve", lambda e: e.scalar_tensor_tensor(out_ap, a, scalar, b, op0, op1), reads=reads, writes=writes)

    def dve_ts(out_ap, a, s1, s2, op0, op1, reads, writes, eng="dve"):
        if op1 is None:
            P.op(eng, lambda e: e.tensor_scalar(out_ap, a, s1, None, op0), reads=reads, writes=writes)
        else:
            P.op(eng, lambda e: e.tensor_scalar(out_ap, a, s1, s2, op0, op1), reads=reads, writes=writes)

    def dve_copy(out_ap, in_ap, reads, writes, eng="dve"):
        P.op(eng, lambda e: e.tensor_copy(out_ap, in_ap), reads=reads, writes=writes)

    def dve_recip(out_ap, in_ap, reads, writes):
        P.op("dve", lambda e: e.reciprocal(out_ap, in_ap), reads=reads, writes=writes)

    def dma_in(eng, out_ap, in_ap, buf, extra_writes=(), cast=False):
        if eng == "pool":
            P.op("pool", lambda e: e.dma_start(out=out_ap, in_=in_ap, max_dma_last_dim=4096),
                 reads=[], writes=[buf] + list(extra_writes), dma_buf=Buf("sw"))
        elif cast:
            P.op(eng, lambda e: e.dma_start(out=out_ap, in_=in_ap, max_dma_last_dim=4096),
                 reads=[], writes=[buf] + list(extra_writes), dma_buf=buf)
        else:
            P.op(eng, lambda e: e.dma_start(out=out_ap, in_=in_ap),
                 reads=[], writes=[buf] + list(extra_writes), dma_buf=buf)

    def dma_out(out_ap, in_ap, buf):
        P.op("sp", lambda e: e.dma_start(out=out_ap, in_=in_ap), reads=[buf], writes=[], dma_buf=buf, is_out=True)

    def load_w(srcs, nelem, view=None, parts=128):
        si = nxt("slot", 2)
        if not isinstance(srcs, (list, tuple)):
            srcs = [srcs]
        for i, src_ap in enumerate(srcs):
            dst = slots[si][0:parts, i * nelem:(i + 1) * nelem]
            dma_in("pool", dst, src_ap, B_slot[si], cast=True)
        return si

    dma_in("sp", sm[:, :], sm_d, B_sm)
    dma_in("sp", ident[:, :], ident_d, B_const)
    dma_in("sp", ropeW[:, :, :], ropew_d, B_const)
    dma_in("pool", ropeF[:, :, :].rearrange("p a t -> p (a t)"), ropef_d.rearrange("p a t -> p (a t)"), B_const, cast=True)
    dma_in("pool", maskb[:, :, :].rearrange("p a t -> p (a t)"), maskb_d.rearrange("p a t -> p (a t)"), B_const, cast=True)
    P.op("dve", lambda e: e.memset(ones[:, :], 1.0), reads=[], writes=[B_const])
    dve_copy(identb[:, :], ident[:, :], [B_const], [B_const])
    P.op("dve", lambda e: e.memset(qTh[:, :, :], 0.0), reads=[], writes=[B_qTh[0], B_qTh[1]])

    O_COND, O_NG, O_BM, O_SUBG, O_LAM, O_SINK = 0, 16, 80, 176, 177, 433
    cond_v = sm[:, O_COND:O_COND + 16].rearrange("p (k c) -> p k c", c=2)
    act(scb[:, :, :], cond_v, AF.Silu, [B_sm], [B_scb])
    LAM_INIT = 0.8 - 0.6 * float(np.exp(-0.3 * 0))
    lam_v = sm[:, O_LAM:O_LAM + 256].rearrange("p (a d) -> p a d", a=4)
    P.op("dve", lambda e: e.tensor_tensor(tmps[0][:, 0:64], lam_v[:, 0, :], lam_v[:, 1, :], ALU.mult),
         reads=[B_sm], writes=[B_tmp[0]])
    P.op("dve", lambda e: e.tensor_tensor(tmps[0][:, 64:128], lam_v[:, 2, :], lam_v[:, 3, :], ALU.mult),
         reads=[B_sm], writes=[B_tmp[0]])
    P.op("dve", lambda e: e.reduce_sum(lamt[:, 0:2], tmps[0][:, 0:128].rearrange("p (a d) -> p a d", a=2),
                                       mybir.AxisListType.X), reads=[B_tmp[0]], writes=[B_lam])
    act(lamt[:, 2:4], lamt[:, 0:2], AF.Exp, [B_lam], [B_lam])
    dve_tt(lamt[:, 4:5], lamt[:, 3:4], lamt[:, 2:3], ALU.subtract, [B_lam], [B_lam])
    dve_ts(lamt[:, 4:5], lamt[:, 4:5], -LAM_INIT, None, ALU.add, None, [B_lam], [B_lam])
    dve_ts(lamt[:, 5:6], sm[:, O_SUBG:O_SUBG + 1], 1.0 - LAM_INIT, None, ALU.mult, None, [B_sm, B_lam], [B_lam])
    act(sinkE[:, :], sm[:, O_SINK:O_SINK + 16], AF.Exp, [B_sm], [B_lam])

    def mod_piece(l, piece):
        bi = nbank()
        si = load_w(wmod_d[l, piece].rearrange("p k c -> p (k c)"), 6144)
        wv = slots[si][:, 0:6144].rearrange("p (k c) -> p k c", k=8)
        for oc in range(6):
            for k in range(KC):
                mm(bi, banks[bi][:, oc * 2:oc * 2 + 2], wv[:, k, oc * 128:(oc + 1) * 128], scb[:, k, :],
                   k == 0, k == KC - 1, [B_slot[si], B_scb])
        bv = banks[bi][:, 0:12].rearrange("p (o c) -> p o c", c=2)
        o0 = piece * 6
        for c in range(2):
            dve_tt(modT[:, l, o0:o0 + 6, c], bv[:, :, c], sm[:, O_BM + l * 48 + o0:O_BM + l * 48 + o0 + 6], ALU.add,
                   [Bbank[bi], B_sm], [B_mod[l]])

    def mod_finish(l):
        for c in range(2):
            def g(n):
                return sm[:, O_NG + (l * 4 + n) * 8: O_NG + (l * 4 + n) * 8 + 8]
            dve_stt(der[:, l, 0, :, c], modT[:, l, 8:16, c], 1.0, g(0), ALU.add, ALU.mult, [B_mod[l], B_sm], [B_der[l]])
            dve_tt(der[:, l, 1, :, c], modT[:, l, 16:24, c], g(1), ALU.mult, [B_mod[l], B_sm], [B_der[l]])
            dve_stt(der[:, l, 2, :, c], modT[:, l, 32:40, c], 1.0, g(2), ALU.add, ALU.mult, [B_mod[l], B_sm], [B_der[l]])
            dve_tt(der[:, l, 3, :, c], modT[:, l, 40:48, c], g(3), ALU.mult, [B_mod[l], B_sm], [B_der[l]])

    def modulation(l):
        for piece in range(8):
            mod_piece(l, piece)
        mod_finish(l)

    def mod_scalars(l, which, c):
        def gs(k):
            return der[:, l, 2 * which, k, c:c + 1]

        def sh(k):
            return modT[:, l, 24 * which + k, c:c + 1]

        def gg(k):
            return der[:, l, 2 * which + 1, k, c:c + 1]
        return gs, sh, gg

    def load_T(src_d, row0, dst, dcol0, dbufs):
        xi = nxt("xin", 2)
        dma_in("sp", xin[xi][:, :], src_d[row0:row0 + 128, :], B_xin[xi])
        for half in range(2):
            bi = nbank()
            for j in range(4):
                cidx = half * 4 + j
                P.op("pe", lambda e, bi=bi, j=j, cidx=cidx, xi=xi: e.transpose(
                    banks[bi][:, j * 128:(j + 1) * 128], xin[xi][:, cidx * 128:(cidx + 1) * 128], ident[:, :]),
                    reads=[B_xin[xi], B_const], writes=[Bbank[bi]])
            src = banks[bi][:, :].rearrange("p (j t) -> p j t", j=4)
            dsta = dst[:, half * 4:half * 4 + 4, dcol0:dcol0 + 128]
            if half == 0:
                act(dsta, src, AF.Copy, [Bbank[bi]], dbufs)
            else:
                dve_copy(dsta, src, [Bbank[bi]], dbufs)

    def rstd_from_bank(bs, n, nfeat, br):
        t = nxt("tmp", 4)
        act(tmps[t][:, 0:n], banks[bs][:, 0:n], AF.Ln, [Bbank[bs]], [B_tmp[t]], bias=EPS, scale=1.0 / nfeat)
        act(banks[br][:, 0:n], tmps[t][:, 0:n], AF.Exp, [B_tmp[t]], [Bbank[br]], scale=-0.5)

    def prenorm(src, sbufs, c0, n, dst, dbufs, dc0, l, which, c):
        gs, sh, _ = mod_scalars(l, which, c)
        bs = nbank()
        for k in range(KC):
            q = nxt("sq", 2)
            act(sqs[q][:, 0:n], src[:, k, c0:c0 + n], AF.Square, sbufs, [B_sq[q]])
            mm(bs, banks[bs][:, 0:n], ones[:, :], sqs[q][:, 0:n], k == 0, k == KC - 1, [B_sq[q], B_const])
        br = nbank()
        rstd_from_bank(bs, n, D, br)
        for k in range(KC):
            t = nxt("tmp", 4)
            dve_tt(tmps[t][:, 0:n], src[:, k, c0:c0 + n], banks[br][:, 0:n], ALU.mult, sbufs + [Bbank[br]], [B_tmp[t]])
            act(dst[:, k, dc0:dc0 + n], tmps[t][:, 0:n], AF.Identity, [B_tmp[t], B_mod[l], B_der[l]], dbufs,
                bias=sh(k), scale=gs(k))

    class PostNorm:
        def __init__(self, c0, n, xbufs, l, which, c):
            self.c0, self.n, self.xbufs, self.l, self.which, self.c = c0, n, xbufs, l, which, c
            self.bs = nbank()
            reserved.add(self.bs)
            self.cnt = 0

        def add(self, bi, dch):
            n = self.n
            yv = ytmp[:, dch, 0:n]
            act(yv, banks[bi][:, 0:n], AF.Copy, [Bbank[bi]], B_hT)
            q = nxt("sq", 2)
            act(sqs[q][:, 0:n], banks[bi][:, 0:n], AF.Square, [Bbank[bi]], [B_sq[q]])
            mm(self.bs, banks[self.bs][:, 0:n], ones[:, :], sqs[q][:, 0:n], self.cnt == 0, self.cnt == KC - 1,
               [B_sq[q], B_const])
            self.cnt += 1

        def finish(self):
            n, c0 = self.n, self.c0
            _, _, gg = mod_scalars(self.l, self.which, self.c)
            br = nbank()
            reserved.discard(self.bs)
            rstd_from_bank(self.bs, n, D, br)
            for dch in range(KC):
                t = nxt("tmp", 4)
                dve_tt(tmps[t][:, 0:n], ytmp[:, dch, 0:n], banks[br][:, 0:n], ALU.mult, B_hT + [Bbank[br]], [B_tmp[t]])
                xa = xT[:, dch, c0:c0 + n]
                dve_stt(xa, tmps[t][:, 0:n], gg(dch), xa, ALU.mult, ALU.add,
                        [B_tmp[t], B_der[self.l]] + self.xbufs, self.xbufs)

    def linear_fm(bi, wfun, rhsfun, n, reads, nk=KC):
        for k in range(nk):
            mm(bi, banks[bi][:, 0:n], wfun(k), rhsfun(k), k == 0, k == nk - 1, reads)

    def rope_epilogue(ba, bb, n, cos_ap, sin_ap, out_ap, obufs, out_hi=None):
        t1 = nxt("tmp", 4)
        dve_tt(tmps[t1][:, 0:n], banks[ba][:, 0:n], cos_ap, ALU.mult, [Bbank[ba], B_const], [B_tmp[t1]])
        t2 = nxt("tmp", 4)
        dve_tt(tmps[t2][:, 0:n], banks[bb][:, 0:n], sin_ap, ALU.mult, [Bbank[bb], B_const], [B_tmp[t2]])
        if out_hi is None:
            dve_tt(out_ap, tmps[t1][:, 0:n], tmps[t2][:, 0:n], ALU.add, [B_tmp[t1], B_tmp[t2]], obufs)
        else:
            dve_tt(out_ap, tmps[t1][0:64, 0:n], tmps[t2][0:64, 0:n], ALU.add, [B_tmp[t1], B_tmp[t2]], obufs)
            dve_tt(out_hi, tmps[t1][64:128, 0:n], tmps[t2][64:128, 0:n], ALU.add, [B_tmp[t1], B_tmp[t2]], obufs)

    SCALE = 0.125

    OBK = [4, 5]
    LBK = [6, 7]
    pstate = {"par": 0, "pending": None}

    def obank(slot, par):
        return OBK[slot]

    def run_pair(jobs, hook_after=2):
        hook = pstate["pending"]
        pstate["pending"] = None
        nj = len(jobs)
        n = jobs[0][1]
        nb = len(jobs[0][2])

        def qk(ji, j):
            q_ap, _, blocks, qreads = jobs[ji]
            k_ap, kreads, _, _, mi = blocks[j]
            s_ = (j % 2) * 2 + ji
            mm(s_, banks[s_][:, 0:n], k_ap, q_ap, True, mi is None, qreads + kreads)
            if mi is not None:
                for g in range(n // 128):
                    mm(s_, banks[s_][:, g * 128:(g + 1) * 128], identb[:, :], maskb[:, mi, :], False,
                       g == n // 128 - 1, [B_const])

        for ji in range(nj):
            qk(ji, 0)
        for j in range(nb):
            if j + 1 < nb:
                for ji in range(nj):
                    qk(ji, j + 1)
            sp = (j % 2) * 2
            ei = j % 2
            act(Es[ei][:, 0:nj, 0:n], PS[:, sp:sp + nj, 0:n], AF.Exp, [Bbank[sp + ji] for ji in range(nj)],
                [B_E[ei]], scale=SCALE)
            for ji in range(nj):
                _, _, v_ap, vreads, _ = jobs[ji][2][j]
                bo, bl = OBK[ji], LBK[ji]
                mm(bo, banks[bo][:, 0:n], v_ap, Es[ei][:, ji, 0:n], j == 0, j == nb - 1, [B_E[ei]] + vreads)
                mm(bl, banks[bl][:, 0:n], ones[:, :], Es[ei][:, ji, 0:n], j == 0, j == nb - 1, [B_E[ei], B_const])
            if hook is not None and j + 1 == hook_after:
                hook()
                hook = None
        if hook is not None:
            hook()
        return 0

    def flush_pending():
        if pstate["pending"] is not None:
            pstate["pending"]()
            pstate["pending"] = None

    for t in range(4):
        load_T(xp_d, t * 128, xT, t * 128, [B_xT[0]])
    for t in range(6):
        load_T(xw_d, t * 128, xT, 512 + t * 128, [B_xT[1] if t < 4 else B_xT[2]])

    if STAGE >= -4:
        modulation(0)

    xfT = ytmp
    for fc in range(4 if STAGE >= -3 else 0):
        for t in range(4):
            load_T(xf_d, fc * 512 + t * 128, xfT, t * 128, B_hT)
        prenorm(xfT, B_hT, 0, 512, hfT, [B_hfT[fc]], fc * 512, 0, 0, 1)
    for ci, (c0, n) in enumerate(TCH if STAGE >= -3 else []):
        prenorm(xT, [B_xT[ci]], c0, n, hT, [B_hT[ci]], c0, 0, 0, 0 if ci == 0 else 1)

    for hd in range(KHEADS if STAGE >= -2 else 0):
        si = load_w(wd0_d[hd].rearrange("p k c -> p (k c)"), 5120)
        wv = slots[si][:, 0:5120].rearrange("p (k c) -> p k c", k=8)
        WQ, WQR, WKR, WK, WV = 0, 128, 256, 384, 512
        rs = [B_slot[si]]
        if "c" not in KSKIP:
            dma_in("sp", cst[:, 0:512].rearrange("p (j c) -> p j c", j=4), ckd_d[hd], B_cst)
            bi = nbank()
            for j in range(4):
                P.op("pe", lambda e, bi=bi, j=j: e.transpose(banks[bi][:, j * 128:(j + 1) * 128],
                                                            cst[:, j * 128:(j + 1) * 128], ident[:, :]),
                     reads=[B_cst, B_const], writes=[Bbank[bi]])
            act(kTh[:, 2560:3072], banks[bi][:, :], AF.Copy, [Bbank[bi]], [B_kTh[2]])
            dma_in("sp", cst[:, 512:1024], cvd_d[hd].rearrange("p j c -> p (j c)"), B_cstv)
            dve_copy(Vh[:, 2560:3072], cst[:, 512:1024], [B_cstv], [B_Vh[2]])
        bi = nbank()
        linear_fm(bi, lambda k: wv[:, k, WQ:WQ + 128], lambda k: hT[:, k, 0:512], 512, rs + [B_hT[0]])
        act(qTh[0:64, 0, 0:512], banks[bi][0:64, :], AF.Copy, [Bbank[bi]], [B_qTh[0]])
        act(qTh[64:128, 1, 0:512], banks[bi][64:128, :], AF.Copy, [Bbank[bi]], [B_qTh[0]])
        bi = nbank()
        linear_fm(bi, lambda k: wv[:, k, WK:WK + 128], lambda k: hT[:, k, 0:512], 512, rs + [B_hT[0]])
        act(kTh[:, 0:512], banks[bi][:, :], AF.Copy, [Bbank[bi]], [B_kTh[0]])
        for t in range(0 if "t" in KSKIP else 4):
            bi = nbank()
            linear_fm(bi, lambda k, t=t: hT[:, k, t * 128:(t + 1) * 128], lambda k: wv[:, k, WK:WK + 256], 256,
                      rs + [B_hT[0]])
            oi = nxt("ost", 2)
            act(ostage[oi][:, 0:256], banks[bi][:, 0:256], AF.Copy, [Bbank[bi]], [B_ost[oi]])
            dve_copy(Vh[:, t * 128:(t + 1) * 128], banks[bi][:, 128:256], [Bbank[bi]], [B_Vh[0]])
            if "o" not in KSKIP:
                dma_out(ndk_d[t * 128:(t + 1) * 128, hd * 128:(hd + 1) * 128], ostage[oi][:, 0:128], B_ost[oi])
                dma_out(ndv_d[t * 128:(t + 1) * 128, hd * 128:(hd + 1) * 128], ostage[oi][:, 128:256], B_ost[oi])
        for ci in (() if "r" in KSKIP else (1, 2)):
            c0, n = TCH[ci]
            ba = nbank()
            linear_fm(ba, lambda k: wv[:, k, WQ:WQ + 128], lambda k: hT[:, k, c0:c0 + n], n, rs + [B_hT[ci]])
            bb = nbank()
            linear_fm(bb, lambda k: wv[:, k, WQR:WQR + 128], lambda k: hT[:, k, c0:c0 + n], n, rs + [B_hT[ci]])
            rope_epilogue(ba, bb, n, ropeW[:, 0, c0 - 512:c0 - 512 + n], ropeW[:, 1, c0 - 512:c0 - 512 + n],
                          qTh[0:64, 0, c0:c0 + n], [B_qTh[1]], out_hi=qTh[64:128, 1, c0:c0 + n])
        for fc in range(0 if "f" in KSKIP else 4):
            f0 = fc * 512
            ba = nbank()
            linear_fm(ba, lambda k: wv[:, k, WK:WK + 128], lambda k: hfT[:, k, f0:f0 + 512], 512, rs + [B_hfT[fc]])
            bb = nbank()
            linear_fm(bb, lambda k: wv[:, k, WKR:WKR + 128], lambda k: hfT[:, k, f0:f0 + 512], 512, rs + [B_hfT[fc]])
            rope_epilogue(ba, bb, 512, ropeF[:, 0, f0:f0 + 512], ropeF[:, 1, f0:f0 + 512],
                          kTh[:, 512 + f0:512 + f0 + 512], [B_kTh[1]])
            bi = nbank()
            for t in range(4):
                for k in range(KC):
                    mm(bi, banks[bi][:, t * 128:(t + 1) * 128], hfT[:, k, f0 + t * 128:f0 + (t + 1) * 128],
                       wv[:, k, WV:WV + 128], k == 0, k == KC - 1, rs + [B_hfT[fc]])
            dve_copy(Vh[:, 512 + f0:512 + f0 + 512], banks[bi][:, :], [Bbank[bi]], [B_Vh[1]])

        def diff_pair(q0_ap, q1_ap, n, blocks, qreads, out_ap, obufs):
            run_pair([(q0_ap, n, blocks, qreads), (q1_ap, n, blocks, qreads)])
            oA, oB = OBK
            ta = nxt("tmp", 4)
            act(tmps[ta][:, 0:n], banks[LBK[0]][:, 0:n], AF.Copy, [Bbank[LBK[0]]], [B_tmp[ta]])
            tb = nxt("tmp", 4)
            act(tmps[tb][:, 0:n], banks[LBK[1]][:, 0:n], AF.Copy, [Bbank[LBK[1]]], [B_tmp[tb]])
            tc = nxt("tmp", 4)
            dve_copy(tmps[tc][:, 0:n], banks[oA][:, 0:n], [Bbank[oA]], [B_tmp[tc]])
            td = nxt("tmp", 4)
            dve_copy(tmps[td][:, 0:n], banks[oB][:, 0:n], [Bbank[oB]], [B_tmp[td]])
            dve_recip(tmps[ta][:, 0:n], tmps[ta][:, 0:n], [B_tmp[ta]], [B_tmp[ta]])
            dve_tt(T1[:, 0:n], tmps[tc][:, 0:n], tmps[ta][:, 0:n], ALU.mult, [B_tmp[tc], B_tmp[ta]], [B_T1])
            dve_recip(tmps[tb][:, 0:n], tmps[tb][:, 0:n], [B_tmp[tb]], [B_tmp[tb]])
            dve_tt(tmps[td][:, 0:n], tmps[td][:, 0:n], tmps[tb][:, 0:n], ALU.mult, [B_tmp[td], B_tmp[tb]],
                   [B_tmp[td]])
            dve_stt(T1[:, 0:n], tmps[td][:, 0:n], lamt[:, 4:5], T1[:, 0:n], ALU.mult, ALU.add,
                    [B_tmp[td], B_lam, B_T1], [B_T1])

            def tail():
                q = nxt("sq", 2)
                act(sqs[q][:, 0:n], T1[:, 0:n], AF.Square, [B_T1], [B_sq[q]])
                bs = 2
                mm(bs, banks[bs][:, 0:n], ones[:, :], sqs[q][:, 0:n], True, True, [B_sq[q], B_const])
                br = 3
                rstd_from_bank(bs, n, 128, br)
                dve_stt(out_ap, T1[:, 0:n], lamt[:, 5:6], banks[br][:, 0:n], ALU.mult, ALU.mult,
                        [B_T1, B_lam, Bbank[br]], obufs)
            pstate["pending"] = tail

        for pb in range(0 if "p" in KSKIP else 2):
            blocks = []
            for j in range(2):
                kc = pb * 256 + j * 128
                blocks.append((kTh[:, kc:kc + 128], [B_kTh[0]], Vh[:, kc:kc + 128], [B_Vh[0]], None))
            diff_pair(qTh[:, 0, pb * 256:(pb + 1) * 256], qTh[:, 1, pb * 256:(pb + 1) * 256], 256, blocks,
                      [B_qTh[0]], oT[:, hd, pb * 256:(pb + 1) * 256], [B_oT[0]])
        for ci in (() if "w" in KSKIP else (1, 2)):
            c0, n = TCH[ci]
            blocks = []
            for j in range(16):
                kc = 512 + j * 128
                blocks.append((kTh[:, kc:kc + 128], [B_kTh[1]], Vh[:, kc:kc + 128], [B_Vh[1]], None))
            for j in range(4):
                kc = 2560 + j * 128
                blocks.append((kTh[:, kc:kc + 128], [B_kTh[2]], Vh[:, kc:kc + 128], [B_Vh[2]], None))
            diff_pair(qTh[:, 0, c0:c0 + n], qTh[:, 1, c0:c0 + n], n, blocks, [B_qTh[1]],
                      oT[:, hd, c0:c0 + n], [B_oT[ci]])
        if STAGE >= 1:
            mod_piece(1, hd)
    flush_pending()

    def out_proj_l0(l):
        for ci, (c0, n) in enumerate(TCH):
            pn = PostNorm(c0, n, [B_xT[ci]], l, 0, 0 if ci == 0 else 1)
            for half in range(2):
                si = load_w(wo0_d[half].rearrange("p k c -> p (k c)"), 4096)
                wv = slots[si][:, 0:4096].rearrange("p (k c) -> p k c", k=8)
                for dl in range(4):
                    bi = nbank()
                    linear_fm(bi, lambda k: wv[:, k, dl * 128:(dl + 1) * 128], lambda k: oT[:, k, c0:c0 + n], n,
                              [B_slot[si], B_oT[ci]])
                    pn.add(bi, half * 4 + dl)
            pn.finish()

    def ffn(l, chunks):
        for (c0, n, xb, hb, ab, c) in chunks:
            prenorm(xT, xb, c0, n, hT, [hb], c0, l, 1, c)
        for piece in range(8):
            nf = min(3, NF - 3 * piece)
            si = load_w(wgu_d[l, piece, :, 0:nf].rearrange("p f k c -> p (f k c)"), nf * 2048)
            wv = slots[si][:, 0:nf * 2048].rearrange("p (f k c) -> p f k c", f=nf, k=8)
            for fl in range(nf):
                f = 3 * piece + fl
                for (c0, n, xb, hb, ab, c) in chunks:
                    bg = nbank()
                    linear_fm(bg, lambda k: wv[:, fl, k, 0:128], lambda k: hT[:, k, c0:c0 + n], n, [B_slot[si], hb])
                    bu = nbank()
                    linear_fm(bu, lambda k: wv[:, fl, k, 128:256], lambda k: hT[:, k, c0:c0 + n], n, [B_slot[si], hb])
                    t = nxt("tmp", 4)
                    act(tmps[t][:, 0:n], banks[bg][:, 0:n], AF.Silu, [Bbank[bg]], [B_tmp[t]])
                    dve_tt(aT[:, f, c0:c0 + n], tmps[t][:, 0:n], banks[bu][:, 0:n], ALU.mult,
                           [B_tmp[t], Bbank[bu]], [ab])
        for (c0, n, xb, hb, ab, c) in chunks:
            pn = PostNorm(c0, n, xb, l, 1, c)
            for j in range(4):
                si = load_w(wdn_d[l, j].rearrange("p d f c -> p (d f c)"), 5632)
                wv = slots[si][:, 0:5632].rearrange("p (d f c) -> p d f c", d=2, f=22)
                for dl in range(2):
                    bi = nbank()
                    for f in range(NF):
                        mm(bi, banks[bi][:, 0:n], wv[:, dl, f, :], aT[:, f, c0:c0 + n], f == 0, f == NF - 1,
                           [B_slot[si], ab])
                    pn.add(bi, 2 * j + dl)
            pn.finish()

    if STAGE >= -1:
        out_proj_l0(0)
    if STAGE >= 1:
        mod_finish(1)
    P.alias(B_aT, B_hfT + B_oT)
    if STAGE >= 0:
        ffn(0, [(TCH[i][0], TCH[i][1], [B_xT[i]], B_hT[i], B_aT[i], 0 if i == 0 else 1) for i in range(3)])

    if STAGE >= 2:
        P.alias(B_qTs + [B_kTs] + B_oS, B_aT)
        for ci, (c0, n) in enumerate(TCH):
            prenorm(xT, [B_xT[ci]], c0, n, hT, [B_hT[ci]], c0, 1, 0, 0 if ci == 0 else 1)
        OWN0 = 640
        B_hown = [B_hT[1], B_hT[2]]
        P.op("dve", lambda e: e.memset(kTs[:, :, :], 0.0), reads=[], writes=[B_kTs])
        VS = Vh[:, 0:3584].rearrange("p (t c) -> p t c", t=14)
        B_VS = Buf("VS")
        P.alias([B_VS], list(B_Vh))
        for half in range(2):
            sa = load_w(ws1q_d[half].rearrange("p k c -> p (k c)"), 4096)
            sr = load_w(ws1q_d[2 + half].rearrange("p k c -> p (k c)"), 4096)
            wa = slots[sa][:, 0:4096].rearrange("p (k c) -> p k c", k=8)
            wr = slots[sr][:, 0:4096].rearrange("p (k c) -> p k c", k=8)
            for cl in range(4):
                cq = half * 4 + cl
                bi = nbank()
                linear_fm(bi, lambda k: wa[:, k, cl * 128:(cl + 1) * 128], lambda k: hT[:, k, 0:512], 512,
                          [B_slot[sa], B_hT[0]])
                act(qTs[:, cq, 0:512], banks[bi][:, :], AF.Copy, [Bbank[bi]], [B_qTs[0]])
                ba = nbank()
                linear_fm(ba, lambda k: wa[:, k, cl * 128:(cl + 1) * 128], lambda k: hT[:, k, OWN0:OWN0 + 512], 512,
                          [B_slot[sa]] + B_hown)
                bb = nbank()
                linear_fm(bb, lambda k: wr[:, k, cl * 128:(cl + 1) * 128], lambda k: hT[:, k, OWN0:OWN0 + 512], 512,
                          [B_slot[sr]] + B_hown)
                rope_epilogue(ba, bb, 512, ropeW[:, 0, 128:640], ropeW[:, 1, 128:640], qTs[:, cq, 512:1024],
                              [B_qTs[1]])
        sk = load_w(ws1k_d.rearrange("p k c -> p (k c)"), 6144)
        wk = slots[sk][:, 0:6144].rearrange("p (k c) -> p k c", k=8)
        rsk = [B_slot[sk]]
        for p in range(2):
            bi = nbank()
            linear_fm(bi, lambda k: wk[:, k, p * 128:(p + 1) * 128], lambda k: hT[:, k, 0:512], 512, rsk + [B_hT[0]])
            act(kTs[0:64, 2 * p, 0:512], banks[bi][0:64, :], AF.Copy, [Bbank[bi]], [B_kTs])
            act(kTs[64:128, 2 * p + 1, 0:512], banks[bi][64:128, :], AF.Copy, [Bbank[bi]], [B_kTs])
            for ci in (1, 2):
                c0, n = TCH[ci]
                ba = nbank()
                linear_fm(ba, lambda k: wk[:, k, p * 128:(p + 1) * 128], lambda k: hT[:, k, c0:c0 + n], n,
                          rsk + [B_hT[ci]])
                bb = nbank()
                linear_fm(bb, lambda k: wk[:, k, 256 + p * 128:256 + (p + 1) * 128], lambda k: hT[:, k, c0:c0 + n], n,
                          rsk + [B_hT[ci]])
                rope_epilogue(ba, bb, n, ropeW[:, 0, c0 - 512:c0 - 512 + n], ropeW[:, 1, c0 - 512:c0 - 512 + n],
                              kTs[0:64, 2 * p, c0:c0 + n], [B_kTs], out_hi=kTs[64:128, 2 * p + 1, c0:c0 + n])
        for t in range(4):
            bi = nbank()
            for k in range(KC):
                mm(bi, banks[bi][:, 0:256], hT[:, k, t * 128:(t + 1) * 128], wk[:, k, 0:256], k == 0, k == KC - 1,
                   rsk + [B_hT[0]])
            for k in range(KC):
                mm(bi, banks[bi][:, 256:512], hT[:, k, t * 128:(t + 1) * 128], wk[:, k, 512:768], k == 0, k == KC - 1,
                   rsk + [B_hT[0]])
            oi = nxt("ost", 2)
            act(ostage[oi][:, 0:512], banks[bi][:, :], AF.Copy, [Bbank[bi]], [B_ost[oi]])
            dve_copy(VS[:, t, :], banks[bi][:, 256:512], [Bbank[bi]], [B_VS])
            dma_out(nsk_d[t * 128:(t + 1) * 128, :], ostage[oi][:, 0:256], B_ost[oi])
            dma_out(nsv_d[t * 128:(t + 1) * 128, :], ostage[oi][:, 256:512], B_ost[oi])
        for t in range(6):
            bi = nbank()
            c0 = 512 + t * 128
            for k in range(KC):
                mm(bi, banks[bi][:, 0:256], hT[:, k, c0:c0 + 128], wk[:, k, 512:768], k == 0, k == KC - 1,
                   rsk + [B_hT[1] if t < 4 else B_hT[2]])
            dve_copy(VS[:, 4 + t, :], banks[bi][:, 0:256], [Bbank[bi]], [B_VS])
        dma_in("sp", cst[:, :].rearrange("p (j c) -> p j c", j=4), cks_d, B_cst)
        for p in range(2):
            bi = nbank()
            for j in range(4):
                P.op("pe", lambda e, bi=bi, j=j, p=p: e.transpose(
                    banks[bi][:, j * 128:(j + 1) * 128], cst[:, j * 256 + p * 128:j * 256 + (p + 1) * 128], ident[:, :]),
                    reads=[B_cst, B_const], writes=[Bbank[bi]])
            act(kTs[0:64, 2 * p, 1280:1792], banks[bi][0:64, :], AF.Copy, [Bbank[bi]], [B_kTs])
            act(kTs[64:128, 2 * p + 1, 1280:1792], banks[bi][64:128, :], AF.Copy, [Bbank[bi]], [B_kTs])
        xi = nxt("xin", 2)
        dma_in("sp", xin[xi][:, :], cvs_d.rearrange("p j c -> p (j c)"), B_xin[xi])
        dve_copy(Vh[:, 2560:3584], xin[xi][:, :], [B_xin[xi]], [B_VS])

        def swa_finalize(bo, bl, ng, qn, kv, g0, oS_col0, obuf):
            r0 = (kv % 2) * 64
            pr = kv // 2
            n = ng * qn
            t = nxt("tmp", 4)
            act(tmps[t][r0:r0 + 64, 0:n], banks[bl][r0:r0 + 64, 0:n], AF.Copy, [Bbank[bl]], [B_tmp[t]])
            to = nxt("tmp", 4)
            dve_copy(tmps[to][r0:r0 + 64, 0:n], banks[bo][r0:r0 + 64, 0:n], [Bbank[bo]], [B_tmp[to]])
            for g in range(ng):
                hh = kv * 4 + g0 + g
                dve_ts(tmps[t][r0:r0 + 64, g * qn:(g + 1) * qn], tmps[t][r0:r0 + 64, g * qn:(g + 1) * qn],
                       sinkE[r0:r0 + 64, hh:hh + 1], None, ALU.add, None,
                       [B_tmp[t], B_lam], [B_tmp[t]])
            dve_recip(tmps[t][r0:r0 + 64, 0:n], tmps[t][r0:r0 + 64, 0:n], [B_tmp[t]], [B_tmp[t]])
            for g in range(ng):
                dve_tt(oS[r0:r0 + 64, pr * 4 + g0 + g, oS_col0:oS_col0 + qn],
                       tmps[to][r0:r0 + 64, g * qn:(g + 1) * qn],
                       tmps[t][r0:r0 + 64, g * qn:(g + 1) * qn], ALU.mult, [B_tmp[to], B_tmp[t]], [obuf],
                       eng="pool")

        def run_swa(joblist):
            for i in range(0, len(joblist), 2):
                grp = joblist[i:i + 2]
                par = run_pair([g[0] for g in grp])
                for slot, g in enumerate(grp):
                    swa_finalize(obank(slot, par), LBK[slot], *g[1])

        jl = []
        for pb in range(2):
            for kv in range(4):
                p = kv // 2
                for gh in range(2):
                    q_ap = qTs[:, p * 4 + gh * 2:p * 4 + gh * 2 + 2, pb * 256:(pb + 1) * 256]
                    blocks = []
                    for j in range(2):
                        kc = pb * 256 + j * 128
                        blocks.append((kTs[:, kv, kc:kc + 128], [B_kTs],
                                       VS[:, pb * 2 + j, p * 128:(p + 1) * 128], [B_VS], None))
                    jl.append(((q_ap, 512, blocks, [B_qTs[0]]), (2, 256, kv, gh * 2, pb * 256, B_oS[0])))
        for qb in range(4):
            for kv in range(4):
                p = kv // 2
                q_ap = qTs[:, p * 4:p * 4 + 4, 512 + qb * 128:512 + (qb + 1) * 128]
                blocks = []
                for j in range(4):
                    kc = 1280 + j * 128
                    blocks.append((kTs[:, kv, kc:kc + 128], [B_kTs],
                                   VS[:, 10 + j, p * 128:(p + 1) * 128], [B_VS], None))
                for dj, mi in ((0, 2 * qb), (1, None), (2, 2 * qb + 1)):
                    w = qb + dj
                    kc = 512 + w * 128
                    blocks.append((kTs[:, kv, kc:kc + 128], [B_kTs],
                                   VS[:, 4 + w, p * 128:(p + 1) * 128], [B_VS], mi))
                jl.append(((q_ap, 512, blocks, [B_qTs[1]]), (4, 128, kv, 0, 512 + qb * 128, B_oS[1])))
        run_swa(jl)
        L1CH = [(0, 512, [B_xT[0]], 0, 0, B_oS[0]), (OWN0, 512, [B_xT[1], B_xT[2]], 1, 512, B_oS[1])]
        for (c0, n, xb, c, oc0, ob) in L1CH:
            pn = PostNorm(c0, n, xb, 1, 0, c)
            for half in range(2):
                si = load_w(wo1_d[half].rearrange("p k c -> p (k c)"), 4096)
                wv = slots[si][:, 0:4096].rearrange("p (k c) -> p k c", k=8)
                for dl in range(4):
                    bi = nbank()
                    linear_fm(bi, lambda k: wv[:, k, dl * 128:(dl + 1) * 128], lambda k: oS[:, k, oc0:oc0 + n], n,
                              [B_slot[si], ob])
                    pn.add(bi, half * 4 + dl)
            pn.finish()
        P.alias(B_aT, B_qTs + [B_kTs] + B_oS)
        if STAGE >= 3:
            ffn(1, [(0, 512, [B_xT[0]], B_hT[0], B_aT[0], 0),
                    (OWN0, 512, [B_xT[1], B_xT[2]], B_hT[1], B_aT[1], 1)])

    OWN0 = 640
    for t in range(8):
        c0 = t * 128 if t < 4 else OWN0 + (t - 4) * 128
        xb = [B_xT[0]] if t < 4 else [B_xT[1], B_xT[2]]
        xi = nxt("xin", 2)
        for half in range(2):
            bi = nbank()
            for j in range(4):
                cidx = half * 4 + j
                P.op("pe", lambda e, bi=bi, j=j, cidx=cidx, c0=c0: e.transpose(
                    banks[bi][:, j * 128:(j + 1) * 128], xT[:, cidx, c0:c0 + 128], ident[:, :]),
                    reads=xb + [B_const], writes=[Bbank[bi]])
            if half == 0:
                act(xin[xi][:, 0:512], banks[bi][:, :], AF.Copy, [Bbank[bi]], [B_xin[xi]])
            else:
                dve_copy(xin[xi][:, 512:1024], banks[bi][:, :], [Bbank[bi]], [B_xin[xi]])
        dst = yp_d[t * 128:(t + 1) * 128, :] if t < 4 else ys_d[(t - 4) * 128:(t - 3) * 128, :]
        dma_out(dst, xin[xi][:, :], B_xin[xi])

    P.finish()
    stats = P.emit()
    return nc, es, stats


def _rope_tables(pos):
    pos = np.asarray(pos)
    row = (pos // 64).astype(np.float32)
    col = (pos % 64).astype(np.float32)
    nf = 16
    inv = (np.float32(10000.0) ** (-np.arange(nf, dtype=np.float32) / np.float32(nf))).astype(np.float32)
    ar = row[:, None] * inv[None, :]
    ac = col[:, None] * inv[None, :]
    ang = np.concatenate([ar, ar, ac, ac], axis=-1).astype(np.float32)
    cos = np.cos(ang).astype(np.float32)
    sin = np.sin(ang).astype(np.float32)
    sign = np.concatenate([-np.ones(16), np.ones(16), -np.ones(16), np.ones(16)]).astype(np.float32)
    sin = sin * sign[None, :]
    cos2 = np.concatenate([cos, cos], axis=1).T
    sin2 = np.concatenate([sin, sin], axis=1).T
    return np.ascontiguousarray(cos2), np.ascontiguousarray(sin2)


_ROTSRC = np.concatenate([np.arange(16, 32), np.arange(0, 16), np.arange(48, 64), np.arange(32, 48)])


def _rot_cols(w):
    n = w.shape[1] // 64
    idx = (np.arange(n)[:, None] * 64 + _ROTSRC[None, :]).reshape(-1)
    return w[:, idx]


def _kmaj(w):
    return np.ascontiguousarray(w.reshape(8, 128, -1).transpose(1, 0, 2))


_PROG_CACHE = {}


def prep_inputs(x_prompt, x_sample, cache_diff_k, cache_diff_v, cache_swa_k, cache_swa_v, c, c_ctx,
                w_mod, b_mod, norm_g, w_qkv_diff, diff_lambda, diff_subln_g, w_o_diff,
                w_qkv_swa, swa_sink, w_o_swa, w_gate, w_up, w_down):
    f = np.float32
    A = lambda a: np.ascontiguousarray(np.asarray(a, dtype=f))
    x_prompt, x_sample = A(x_prompt), A(x_sample)
    cache_diff_k, cache_diff_v = A(cache_diff_k), A(cache_diff_v)
    cache_swa_k, cache_swa_v = A(cache_swa_k), A(cache_swa_v)
    c, c_ctx = A(c), A(c_ctx)
    w_mod, b_mod, norm_g = A(w_mod), A(b_mod), A(norm_g)
    w_qkv_diff, diff_lambda, diff_subln_g, w_o_diff = A(w_qkv_diff), A(diff_lambda), A(diff_subln_g), A(w_o_diff)
    w_qkv_swa, swa_sink, w_o_swa = A(w_qkv_swa), A(swa_sink), A(w_o_swa)
    w_gate, w_up, w_down = A(w_gate), A(w_up), A(w_down)

    wmod = np.ascontiguousarray(
        w_mod.reshape(2, 8, 128, 8, 768).transpose(0, 3, 2, 1, 4))
    wq, wk, wv = w_qkv_diff[0][:, 0:1024], w_qkv_diff[0][:, 1024:2048], w_qkv_diff[0][:, 2048:3072]
    wqr, wkr = _rot_cols(wq), _rot_cols(wk)
    wd0 = np.empty((8, 128, 8, 640), f)
    for h in range(8):
        s = slice(h * 128, (h + 1) * 128)
        wd0[h] = _kmaj(np.concatenate([wq[:, s], wqr[:, s], wkr[:, s], wk[:, s], wv[:, s]], axis=1))
    wo0 = np.stack([_kmaj(w_o_diff[0][:, 0:512]), _kmaj(w_o_diff[0][:, 512:1024])])
    ws = w_qkv_swa[0]
    sq_cols = []
    for p in range(2):
        for g in range(4):
            for kvl in range(2):
                hh = (2 * p + kvl) * 4 + g
                sq_cols.append(np.arange(hh * 64, (hh + 1) * 64))
    sq_cols = np.concatenate(sq_cols)
    wsq = ws[:, 0:1024]
    wsq_p = wsq[:, sq_cols]
    wsqr_p = _rot_cols(wsq)[:, sq_cols]
    ws1q = np.stack([_kmaj(wsq_p[:, 0:512]), _kmaj(wsq_p[:, 512:1024]),
                     _kmaj(wsqr_p[:, 0:512]), _kmaj(wsqr_p[:, 512:1024])])
    wsk, wsv = ws[:, 1024:1280], ws[:, 1280:1536]
    ws1k = _kmaj(np.concatenate([wsk, _rot_cols(wsk), wsv], axis=1))
    wos_p = w_o_swa[0][sq_cols, :]
    wo1 = np.stack([_kmaj(wos_p[:, 0:512]), _kmaj(wos_p[:, 512:1024])])
    wgu_f = np.zeros((2, 24, 128, 8, 256), f)
    for l in range(2):
        g_ = w_gate[l].reshape(8, 128, 22, 128).transpose(2, 1, 0, 3)
        u_ = w_up[l].reshape(8, 128, 22, 128).transpose(2, 1, 0, 3)
        wgu_f[l, 0:22, :, :, 0:128] = g_
        wgu_f[l, 0:22, :, :, 128:256] = u_
    wgu = np.ascontiguousarray(wgu_f.reshape(2, 8, 3, 128, 8, 256).transpose(0, 1, 3, 2, 4, 5))
    wdn = np.ascontiguousarray(w_down.reshape(2, 22, 128, 4, 2, 128).transpose(0, 3, 2, 4, 1, 5))
    ident = np.eye(128, dtype=f)
    cosf, sinf = _rope_tables(np.arange(TFULL))
    ropef = np.ascontiguousarray(np.stack([cosf, sinf], axis=1))

    normg_l = norm_g.reshape(8, 8, 128).transpose(2, 0, 1).reshape(128, 64)
    bmod_l = b_mod.reshape(2, 48, 128).transpose(2, 0, 1).reshape(128, 96)
    subg_l = diff_subln_g.reshape(128, 1)
    lam_l = np.broadcast_to(diff_lambda.reshape(1, 256), (128, 256))
    sink_l = np.broadcast_to(swa_sink.reshape(1, 16), (128, 16))

    in_maps = []
    for core in range(NCORES):
        b, ch = core // 4, core % 4
        xp = x_prompt[2 * core:2 * core + 2].reshape(512, D)
        pos = np.arange(ch * 512 - 128, ch * 512 + 640)
        valid = (pos >= 0) & (pos < 2048)
        xw = np.zeros((TW, D), f)
        xw[valid] = x_sample[b, pos[valid]]
        cosw, sinw = _rope_tables(np.clip(pos, 0, 2047))
        ropew = np.ascontiguousarray(np.stack([cosw, sinw], axis=1))
        maskb = np.zeros((128, 8, 128), f)
        for qb in range(4):
            for side in range(2):
                w = qb + (0 if side == 0 else 2)
                kpos = pos[w * 128:(w + 1) * 128]
                qpos = pos[(qb + 1) * 128:(qb + 2) * 128]
                ok = ((kpos[:, None] >= 0) & (kpos[:, None] < 2048)
                      & (np.abs(qpos[None, :] - kpos[:, None]) <= 128))
                maskb[:, 2 * qb + side, :] = np.where(ok, 0.0, -30000.0)
        cond_l = np.stack([c_ctx, c[b]], axis=-1).reshape(8, 128, 2).transpose(1, 0, 2).reshape(128, 16)
        sm = np.ascontiguousarray(np.concatenate([cond_l, normg_l, bmod_l, subg_l, lam_l, sink_l], axis=1), dtype=f)
        ckd = np.ascontiguousarray(cache_diff_k[b, 0].reshape(4, 128, 8, 128).transpose(2, 1, 0, 3))
        cvd = np.ascontiguousarray(cache_diff_v[b, 0].reshape(4, 128, 8, 128).transpose(2, 1, 0, 3))
        cks = np.ascontiguousarray(cache_swa_k[b, 0].reshape(4, 128, 256).transpose(1, 0, 2))
        cvs = np.ascontiguousarray(cache_swa_v[b, 0].reshape(4, 128, 256).transpose(1, 0, 2))
        in_maps.append(dict(xp=np.ascontiguousarray(xp), xw=xw, xf=x_sample[b], ckd=ckd, cvd=cvd, cks=cks, cvs=cvs,
                            sm=sm, ident=ident, ropew=ropew, ropef=ropef, maskb=maskb, wmod=wmod, wd0=wd0, wo0=wo0,
                            ws1q=ws1q, ws1k=ws1k, wo1=wo1, wgu=wgu, wdn=wdn))
    return in_maps


def kernel(**inputs):
    f = np.float32
    in_maps = prep_inputs(**inputs)
    if "nc" not in _PROG_CACHE:
        nc, es, stats = build_program()
        _PROG_CACHE["nc"] = (nc, es)
        if os.environ.get("KVERBOSE"):
            print("ops per engine:", stats)
    nc, _ = _PROG_CACHE["nc"]
    res = run_bass_kernel_spmd(nc, in_maps, core_ids=list(range(NCORES)))
    R = res.results
    y_prompt = np.concatenate([R[i]["yp"].reshape(2, 256, D) for i in range(NCORES)], axis=0)
    y_sample = np.stack([np.concatenate([R[b * 4 + ch]["ys"] for ch in range(4)], axis=0) for b in range(2)], axis=0)
    ndk = np.concatenate([R[i]["ndk"].reshape(2, 1, 256, 8, 128) for i in range(NCORES)], axis=0)
    ndv = np.concatenate([R[i]["ndv"].reshape(2, 1, 256, 8, 128) for i in range(NCORES)], axis=0)
    nsk = np.concatenate([R[i]["nsk"].reshape(2, 1, 256, 4, 64) for i in range(NCORES)], axis=0)
    nsv = np.concatenate([R[i]["nsv"].reshape(2, 1, 256, 4, 64) for i in range(NCORES)], axis=0)
    return (y_prompt.astype(f), y_sample.astype(f), ndk.astype(f), ndv.astype(f), nsk.astype(f), nsv.astype(f))
```

```python
import os
import numpy as np
import concourse.bass as bass
import concourse.mybir as mybir
from concourse.bass_utils import run_bass_kernel_spmd
from contextlib import ExitStack

F32 = mybir.dt.float32
BF16 = mybir.dt.bfloat16
AF = mybir.ActivationFunctionType
ALU = mybir.AluOpType

D = 1024
KC = 8
DFF = 2816
NF = 22
TP = 512
TW = 768
TT = 1280
TFULL = 2048
LC = 512
EPS = 1e-6
NCORES = 8
STAGE = int(os.environ.get("KSTAGE", "99"))
KHEADS = int(os.environ.get("KHEADS", "8"))
KSKIP = os.environ.get("KSKIP", "")


class Buf:
    __slots__ = ("name", "writers", "readers", "dsem", "ndma", "war", "excl")

    def __init__(self, name, excl=False):
        self.name = name
        self.excl = excl
        self.writers = []
        self.readers = []
        self.war = []
        self.dsem = None
        self.ndma = 0


class Op:
    __slots__ = ("eng", "idx", "fn", "deps", "dma", "buf", "ordinal", "flag", "count", "waits", "ring")

    def __init__(self, eng, idx, fn):
        self.eng = eng
        self.idx = idx
        self.fn = fn
        self.deps = []
        self.dma = False
        self.buf = None
        self.ordinal = 0
        self.flag = False
        self.count = 0
        self.waits = []
        self.ring = False


ENGS = ["pe", "act", "dve", "pool", "sp"]


class Prog:
    def __init__(self, nc, es):
        self.nc = nc
        self.es = es
        self.ops = {e: [] for e in ENGS}
        self.dma_bufs = []
        self.out_dmas = []

    def op(self, eng, fn, reads=(), writes=(), dma_buf=None, is_out=False, ring=False):
        o = Op(eng, len(self.ops[eng]), fn)
        o.ring = ring
        deps = o.deps
        for b in reads:
            for w in b.writers:
                deps.append((w, True))
            if b.excl:
                for r in b.readers:
                    if r.eng != eng:
                        deps.append((r, False))
            b.readers.append(o)
        for b in writes:
            if b.readers:
                b.war = [r for r in b.readers if r is not o]
                b.writers = [o]
                b.readers = []
            else:
                b.writers.append(o)
            for r in b.war:
                deps.append((r, False))
        if dma_buf is not None:
            o.dma = True
            o.buf = dma_buf
            if dma_buf.dsem is None:
                self.dma_bufs.append(dma_buf)
                dma_buf.dsem = True
            dma_buf.ndma += 1
            o.ordinal = dma_buf.ndma
            if is_out:
                self.out_dmas.append(o)
        self.ops[eng].append(o)
        return o

    def alias(self, new_bufs, old_bufs):
        pend = []
        for b in old_bufs:
            pend.extend(b.readers)
            pend.extend(b.writers)
        for nb in new_bufs:
            nb.readers.extend(pend)
            nb.war = []

    def finish(self):
        o = Op("sp", len(self.ops["sp"]), None)
        for d in self.out_dmas:
            o.deps.append((d, True))
        self.ops["sp"].append(o)

    def emit(self):
        nc = self.nc
        es = self.es
        esem = {e: es.enter_context(nc.semaphore("s_" + e)) for e in ["pe", "act", "dve", "pool"]}
        for i, b in enumerate(self.dma_bufs):
            b.dsem = es.enter_context(nc.semaphore("d%d" % i))
        for e in ENGS:
            waited = {}
            for o in self.ops[e]:
                need = {}
                for (d, raw) in o.deps:
                    if d.dma:
                        key = ("d", id(d.buf))
                        val = d.ordinal
                        if waited.get(key, 0) >= val:
                            continue
                        if need.get(key, (0, None))[0] < val:
                            need[key] = (val, d)
                    else:
                        if d.eng == e and e == "pe":
                            continue
                        key = ("e", d.eng)
                        val = d.idx + 1
                        if waited.get(key, 0) >= val:
                            continue
                        if need.get(key, (0, None))[0] < val:
                            need[key] = (val, d)
                for key, (val, d) in need.items():
                    waited[key] = val
                    if not d.dma:
                        d.flag = True
                    o.waits.append(d)
        for e in ["pe", "act", "dve", "pool"]:
            c = 0
            for o in self.ops[e]:
                if o.flag and not o.dma:
                    c += 1
                    o.count = c
        handles = {"pe": "tensor", "act": "scalar", "dve": "vector", "pool": "gpsimd", "sp": "sync"}
        stats = {}
        with nc.Block() as block:
            for e in ENGS:
                ops = self.ops[e]
                stats[e] = len(ops)

                def body(eng, ops=ops, e=e):
                    for o in ops:
                        for d in o.waits:
                            if d.dma:
                                eng.wait_ge(d.buf.dsem, 16 * d.ordinal)
                            else:
                                eng.wait_ge(esem[d.eng], d.count)
                        if o.fn is None:
                            continue
                        inst = o.fn(eng)
                        if o.ring:
                            assert not o.flag
                            inst.then_inc(self.ring_sem, 16)
                        elif o.dma:
                            inst.then_inc(o.buf.dsem, 16)
                        elif o.flag:
                            inst.then_inc(esem[e], 1)

                getattr(block, handles[e])(body)
        return stats


def build_program():
    nc = bass.Bass("TRN2", target_bir_lowering=False, monotonic_sem_count=0)
    es = ExitStack()
    P = Prog(nc, es)

    def din(name, shape):
        return nc.dram_tensor(name, list(shape), F32, kind="ExternalInput").ap()

    def dout(name, shape):
        return nc.dram_tensor(name, list(shape), F32, kind="ExternalOutput").ap()

    xp_d = din("xp", [TP, D])
    xw_d = din("xw", [TW, D])
    xf_d = din("xf", [TFULL, D])
    ckd_d = din("ckd", [8, 128, 4, 128])
    cvd_d = din("cvd", [8, 128, 4, 128])
    cks_d = din("cks", [128, 4, 256])
    cvs_d = din("cvs", [128, 4, 256])
    NSM = 16 + 64 + 96 + 1 + 256 + 16
    sm_d = din("sm", [128, NSM])
    ident_d = din("ident", [128, 128])
    ropew_d = din("ropew", [128, 2, TW])
    ropef_d = din("ropef", [128, 2, TFULL])
    maskb_d = din("maskb", [128, 8, 128])
    wmod_d = din("wmod", [2, 8, 128, 8, 768])
    wd0_d = din("wd0", [8, 128, 8, 640])
    wo0_d = din("wo0", [2, 128, 8, 512])
    ws1q_d = din("ws1q", [4, 128, 8, 512])
    ws1k_d = din("ws1k", [128, 8, 768])
    wo1_d = din("wo1", [2, 128, 8, 512])
    wgu_d = din("wgu", [2, 8, 128, 3, 8, 256])
    wdn_d = din("wdn", [2, 4, 128, 2, 22, 128])

    yp_d = dout("yp", [TP, D])
    ys_d = dout("ys", [512, D])
    ndk_d = dout("ndk", [TP, 1024])
    ndv_d = dout("ndv", [TP, 1024])
    nsk_d = dout("nsk", [TP, 256])
    nsv_d = dout("nsv", [TP, 256])

    def sb(name, shape, dt):
        return es.enter_context(nc.sbuf_tensor(name, list(shape), dt))

    xT = sb("xT", [128, KC, TT], F32)
    hT = sb("hT", [128, KC, TT], BF16)
    ytmp = hT.bitcast(F32)
    BIG = sb("BIG", [128, 28160], BF16)
    slots = [sb("wslot%d" % i, [128, 6144], BF16) for i in range(2)]
    kTh = sb("kTh", [128, 3072], BF16)
    Vh = sb("Vh", [128, 3584], BF16)
    qTh = sb("qTh", [128, 2, TT], BF16)
    Es = [sb("E%d" % i, [128, 2, 512], BF16) for i in range(2)]
    xin = [sb("xin%d" % i, [128, 1024], F32) for i in range(2)]
    cst = sb("cst", [128, 1024], F32)
    ropeW = sb("ropeW", [128, 2, TW], F32)
    ropeF = sb("ropeF", [128, 2, TFULL], BF16)
    maskb = sb("maskbs", [128, 8, 128], BF16)
    ident = sb("idents", [128, 128], F32)
    identb = sb("identb", [128, 128], BF16)
    ones = sb("ones", [128, 128], BF16)
    sm = sb("sms", [128, NSM], F32)
    modT = sb("modT", [128, 2, 48, 2], F32)
    der = sb("der", [128, 2, 4, 8, 2], F32)
    scb = sb("scb", [128, 8, 2], BF16)
    lamt = sb("lamt", [128, 8], F32)
    sinkE = sb("sinkE", [128, 16], F32)
    sqs = [sb("sq%d" % i, [128, 512], BF16) for i in range(2)]
    tmps = [sb("tmp%d" % i, [128, 512], F32) for i in range(4)]
    T1s = [sb("T1a", [128, 512], F32), sb("T1b", [128, 512], F32)]
    ostage = [xin[i] for i in range(2)]

    PS = es.enter_context(nc.psum_tensor("PS", [128, 8, 512], F32))

    class BankView:
        def __init__(self, i):
            self.i = i

        def __getitem__(self, idx):
            return PS[idx[0], self.i, idx[1]]

    banks = [BankView(i) for i in range(8)]
    Bbank = [Buf("bank%d" % i, excl=True) for i in range(8)]

    B_xT = [Buf("xT_p"), Buf("xT_w0"), Buf("xT_w1")]
    B_hT = [Buf("hT_p"), Buf("hT_w0"), Buf("hT_w1")]
    B_slot = [Buf("slot0"), Buf("slot1")]
    B_kTh = Buf("kTh_p"), Buf("kTh_f"), Buf("kTh_c")
    B_Vh = Buf("Vh_p"), Buf("Vh_f"), Buf("Vh_c")
    B_qTh = [Buf("qTh_p"), Buf("qTh_w")]
    B_E = [Buf("E%d" % i) for i in range(2)]
    B_xin = [Buf("xin0"), Buf("xin1")]
    B_cst = Buf("cst")
    B_cstv = Buf("cstv")
    B_const = Buf("consts")
    B_sm = Buf("sm")
    B_mod = [Buf("mod0"), Buf("mod1")]
    B_der = [Buf("der0"), Buf("der1")]
    B_scb = Buf("scb")
    B_lam = Buf("lam")
    B_sq = [Buf("sq0"), Buf("sq1")]
    B_tmp = [Buf("tmp%d" % i) for i in range(4)]
    B_T1s = [Buf("T1a"), Buf("T1b")]
    B_ost = B_xin
    B_hfT = [Buf("hfT%d" % i) for i in range(4)]
    B_oT = [Buf("oT_p"), Buf("oT_w0"), Buf("oT_w1")]
    B_aT = [Buf("aT0"), Buf("aT1"), Buf("aT2")]
    B_qTs = [Buf("qTs_p"), Buf("qTs_s")]
    B_kTs = Buf("kTs")
    B_oS = [Buf("oS_p"), Buf("oS_s")]

    hfT = BIG[:, 0:16384].rearrange("p (k t) -> p k t", k=8)
    oT = BIG[:, 16384:16384 + 10240].rearrange("p (k t) -> p k t", k=8)
    aT = BIG[:, 0:28160].rearrange("p (f t) -> p f t", f=22)
    qTs = BIG[:, 0:8192].rearrange("p (k t) -> p k t", k=8)
    kTs = BIG[:, 8192:8192 + 7168].rearrange("p (k t) -> p k t", k=4)
    oS = BIG[:, 15360:15360 + 8192].rearrange("p (h t) -> p h t", h=8)

    TCH = [(0, 512), (512, 512), (1024, 256)]

    rr = {"bank": 0, "tmp": 0, "sq": 0, "slot": 0, "xin": 0, "ost": 0, "E": 0}

    def nxt(kind, n):
        v = rr[kind]
        rr[kind] = (v + 1) % n
        return v

    reserved = set()

    def nbank():
        while True:
            b = nxt("bank", 8)
            if b not in reserved:
                return b

    open_grp = {}

    pe_work = {"v": 0.0}

    def mm(bi, out_ap, lhsT, rhs, start, stop, reads, wt=1.0):
        pe_work["v"] += wt
        if start and open_grp.get(bi):
            import traceback
            traceback.print_stack(limit=6)
            print("OPEN GROUP on bank", bi, "opened at:", open_grp[bi])
        if start:
            import traceback
            open_grp[bi] = "".join(traceback.format_stack(limit=5)[:-1])
        if stop:
            open_grp[bi] = None
        P.op("pe", lambda e: e.matmul(out_ap, lhsT, rhs, start=start, stop=stop),
             reads=reads, writes=[Bbank[bi]])

    def act(out_ap, in_ap, func, reads, writes, bias=None, scale=None):
        kw = {}
        if bias is not None:
            kw["bias"] = bias
        if scale is not None:
            kw["scale"] = scale
        P.op("act", lambda e: e.activation(out_ap, in_ap, func, **kw), reads=reads, writes=writes)

    def dve_tt(out_ap, a, b, op, reads, writes, eng="dve"):
        P.op(eng, lambda e: e.tensor_tensor(out_ap, a, b, op), reads=reads, writes=writes)

    def dve_stt(out_ap, a, scalar, b, op0, op1, reads, writes):
        P.op("dve", lambda e: e.scalar_tensor_tensor(out_ap, a, scalar, b, op0, op1), reads=reads, writes=writes)

    def dve_ts(out_ap, a, s1, s2, op0, op1, reads, writes, eng="dve"):
        if op1 is None:
            P.op(eng, lambda e: e.tensor_scalar(out_ap, a, s1, None, op0), reads=reads, writes=writes)
        else:
            P.op(eng, lambda e: e.tensor_scalar(out_ap, a, s1, s2, op0, op1), reads=reads, writes=writes)

    def dve_copy(out_ap, in_ap, reads, writes, eng="dve"):
        P.op(eng, lambda e: e.tensor_copy(out_ap, in_ap), reads=reads, writes=writes)

    def dve_recip(out_ap, in_ap, reads, writes):
        P.op("dve", lambda e: e.reciprocal(out_ap, in_ap), reads=reads, writes=writes)

    def dma_in(eng, out_ap, in_ap, buf, extra_writes=(), cast=False):
        if eng == "pool":
            P.op("pool", lambda e: e.dma_start(out=out_ap, in_=in_ap, max_dma_last_dim=4096),
                 reads=[], writes=[buf] + list(extra_writes), dma_buf=Buf("sw"))
        elif cast:
            P.op(eng, lambda e: e.dma_start(out=out_ap, in_=in_ap, max_dma_last_dim=4096),
                 reads=[], writes=[buf] + list(extra_writes), dma_buf=buf)
        else:
            P.op(eng, lambda e: e.dma_start(out=out_ap, in_=in_ap),
                 reads=[], writes=[buf] + list(extra_writes), dma_buf=buf)

    def dma_out(out_ap, in_ap, buf):
        P.op("sp", lambda e: e.dma_start(out=out_ap, in_=in_ap), reads=[buf], writes=[], dma_buf=buf, is_out=True)

    def load_w(srcs, nelem, view=None, parts=128):
        si = nxt("slot", 2)
        if not isinstance(srcs, (list, tuple)):
            srcs = [srcs]
        for i, src_ap in enumerate(srcs):
            dst = slots[si][0:parts, i * nelem:(i + 1) * nelem]
            dma_in("pool", dst, src_ap, B_slot[si], cast=True)
        return si

    dma_in("sp", sm[:, :], sm_d, B_sm)
    dma_in("sp", ident[:, :], ident_d, B_const)
    dma_in("sp", ropeW[:, :, :], ropew_d, B_const)
    dma_in("pool", ropeF[:, :, :].rearrange("p a t -> p (a t)"), ropef_d.rearrange("p a t -> p (a t)"), B_const, cast=True)
    dma_in("pool", maskb[:, :, :].rearrange("p a t -> p (a t)"), maskb_d.rearrange("p a t -> p (a t)"), B_const, cast=True)
    P.op("dve", lambda e: e.memset(ones[:, :], 1.0), reads=[], writes=[B_const])
    dve_copy(identb[:, :], ident[:, :], [B_const], [B_const])
    P.op("dve", lambda e: e.memset(qTh[:, :, :], 0.0), reads=[], writes=[B_qTh[0], B_qTh[1]])

    O_COND, O_NG, O_BM, O_SUBG, O_LAM, O_SINK = 0, 16, 80, 176, 177, 433
    cond_v = sm[:, O_COND:O_COND + 16].rearrange("p (k c) -> p k c", c=2)
    act(scb[:, :, :], cond_v, AF.Silu, [B_sm], [B_scb])
    LAM_INIT = 0.8 - 0.6 * float(np.exp(-0.3 * 0))
    lam_v = sm[:, O_LAM:O_LAM + 256].rearrange("p (a d) -> p a d", a=4)
    P.op("dve", lambda e: e.tensor_tensor(tmps[0][:, 0:64], lam_v[:, 0, :], lam_v[:, 1, :], ALU.mult),
         reads=[B_sm], writes=[B_tmp[0]])
    P.op("dve", lambda e: e.tensor_tensor(tmps[0][:, 64:128], lam_v[:, 2, :], lam_v[:, 3, :], ALU.mult),
         reads=[B_sm], writes=[B_tmp[0]])
    P.op("dve", lambda e: e.reduce_sum(lamt[:, 0:2], tmps[0][:, 0:128].rearrange("p (a d) -> p a d", a=2),
                                       mybir.AxisListType.X), reads=[B_tmp[0]], writes=[B_lam])
    act(lamt[:, 2:4], lamt[:, 0:2], AF.Exp, [B_lam], [B_lam])
    dve_tt(lamt[:, 4:5], lamt[:, 3:4], lamt[:, 2:3], ALU.subtract, [B_lam], [B_lam])
    dve_ts(lamt[:, 4:5], lamt[:, 4:5], -LAM_INIT, None, ALU.add, None, [B_lam], [B_lam])
    dve_ts(lamt[:, 5:6], sm[:, O_SUBG:O_SUBG + 1], 1.0 - LAM_INIT, None, ALU.mult, None, [B_sm, B_lam], [B_lam])
    act(sinkE[:, :], sm[:, O_SINK:O_SINK + 16], AF.Exp, [B_sm], [B_lam])

    def mod_piece(l, piece):
        bi = nbank()
        si = load_w(wmod_d[l, piece].rearrange("p k c -> p (k c)"), 6144)
        wv = slots[si][:, 0:6144].rearrange("p (k c) -> p k c", k=8)
        for oc in range(6):
            for k in range(KC):
                mm(bi, banks[bi][:, oc * 2:oc * 2 + 2], wv[:, k, oc * 128:(oc + 1) * 128], scb[:, k, :],
                   k == 0, k == KC - 1, [B_slot[si], B_scb])
        bv = banks[bi][:, 0:12].rearrange("p (o c) -> p o c", c=2)
        o0 = piece * 6
        for c in range(2):
            dve_tt(modT[:, l, o0:o0 + 6, c], bv[:, :, c], sm[:, O_BM + l * 48 + o0:O_BM + l * 48 + o0 + 6], ALU.add,
                   [Bbank[bi], B_sm], [B_mod[l]])

    def mod_finish(l):
        for c in range(2):
            def g(n):
                return sm[:, O_NG + (l * 4 + n) * 8: O_NG + (l * 4 + n) * 8 + 8]
            dve_stt(der[:, l, 0, :, c], modT[:, l, 8:16, c], 1.0, g(0), ALU.add, ALU.mult, [B_mod[l], B_sm], [B_der[l]])
            dve_tt(der[:, l, 1, :, c], modT[:, l, 16:24, c], g(1), ALU.mult, [B_mod[l], B_sm], [B_der[l]])
            dve_stt(der[:, l, 2, :, c], modT[:, l, 32:40, c], 1.0, g(2), ALU.add, ALU.mult, [B_mod[l], B_sm], [B_der[l]])
            dve_tt(der[:, l, 3, :, c], modT[:, l, 40:48, c], g(3), ALU.mult, [B_mod[l], B_sm], [B_der[l]])

    def modulation(l):
        for piece in range(8):
            mod_piece(l, piece)
        mod_finish(l)

    def mod_scalars(l, which, c):
        def gs(k):
            return der[:, l, 2 * which, k, c:c + 1]

        def sh(k):
            return modT[:, l, 24 * which + k, c:c + 1]

        def gg(k):
            return der[:, l, 2 * which + 1, k, c:c + 1]
        return gs, sh, gg

    def load_T(src_d, row0, dst, dcol0, dbufs):
        xi = nxt("xin", 2)
        dma_in("sp", xin[xi][:, :], src_d[row0:row0 + 128, :], B_xin[xi])
        for half in range(2):
            bi = nbank()
            for j in range(4):
                cidx = half * 4 + j
                P.op("pe", lambda e, bi=bi, j=j, cidx=cidx, xi=xi: e.transpose(
                    banks[bi][:, j * 128:(j + 1) * 128], xin[xi][:, cidx * 128:(cidx + 1) * 128], ident[:, :]),
                    reads=[B_xin[xi], B_const], writes=[Bbank[bi]])
            src = banks[bi][:, :].rearrange("p (j t) -> p j t", j=4)
            dsta = dst[:, half * 4:half * 4 + 4, dcol0:dcol0 + 128]
            if half == 0:
                act(dsta, src, AF.Copy, [Bbank[bi]], dbufs)
            else:
                dve_copy(dsta, src, [Bbank[bi]], dbufs)

    def rstd_from_bank(bs, n, nfeat, br):
        t = nxt("tmp", 4)
        act(tmps[t][:, 0:n], banks[bs][:, 0:n], AF.Ln, [Bbank[bs]], [B_tmp[t]], bias=EPS, scale=1.0 / nfeat)
        act(banks[br][:, 0:n], tmps[t][:, 0:n], AF.Exp, [B_tmp[t]], [Bbank[br]], scale=-0.5)

    def prenorm(src, sbufs, c0, n, dst, dbufs, dc0, l, which, c):
        gs, sh, _ = mod_scalars(l, which, c)
        bs = nbank()
        for k in range(KC):
            q = nxt("sq", 2)
            act(sqs[q][:, 0:n], src[:, k, c0:c0 + n], AF.Square, sbufs, [B_sq[q]])
            mm(bs, banks[bs][:, 0:n], ones[:, :], sqs[q][:, 0:n], k == 0, k == KC - 1, [B_sq[q], B_const])
        br = nbank()
        rstd_from_bank(bs, n, D, br)
        for k in range(KC):
            t = nxt("tmp", 4)
            dve_tt(tmps[t][:, 0:n], src[:, k, c0:c0 + n], banks[br][:, 0:n], ALU.mult, sbufs + [Bbank[br]], [B_tmp[t]])
            dve_ts(dst[:, k, dc0:dc0 + n], tmps[t][:, 0:n], gs(k), sh(k), ALU.mult, ALU.add,
                   [B_tmp[t], B_mod[l], B_der[l]], dbufs, eng="pool")

    class PostNorm:
        def __init__(self, c0, n, xbufs, l, which, c):
            self.c0, self.n, self.xbufs, self.l, self.which, self.c = c0, n, xbufs, l, which, c
            self.bs = nbank()
            reserved.add(self.bs)
            self.cnt = 0

        def add(self, bi, dch):
            n = self.n
            yv = ytmp[:, dch, 0:n]
            act(yv, banks[bi][:, 0:n], AF.Copy, [Bbank[bi]], B_hT)
            q = nxt("sq", 2)
            act(sqs[q][:, 0:n], banks[bi][:, 0:n], AF.Square, [Bbank[bi]], [B_sq[q]])
            mm(self.bs, banks[self.bs][:, 0:n], ones[:, :], sqs[q][:, 0:n], self.cnt == 0, self.cnt == KC - 1,
               [B_sq[q], B_const])
            self.cnt += 1

        def finish(self):
            n, c0 = self.n, self.c0
            _, _, gg = mod_scalars(self.l, self.which, self.c)
            br = nbank()
            reserved.discard(self.bs)
            rstd_from_bank(self.bs, n, D, br)
            for dch in range(KC):
                t = nxt("tmp", 4)
                dve_tt(tmps[t][:, 0:n], ytmp[:, dch, 0:n], banks[br][:, 0:n], ALU.mult, B_hT + [Bbank[br]], [B_tmp[t]])
                xa = xT[:, dch, c0:c0 + n]
                dve_stt(xa, tmps[t][:, 0:n], gg(dch), xa, ALU.mult, ALU.add,
                        [B_tmp[t], B_der[self.l]] + self.xbufs, self.xbufs)

    def linear_fm(bi, wfun, rhsfun, n, reads, nk=KC):
        for k in range(nk):
            mm(bi, banks[bi][:, 0:n], wfun(k), rhsfun(k), k == 0, k == nk - 1, reads, wt=n / 512.0)
        maybe_flush()

    def rope_epilogue(ba, bb, n, cos_ap, sin_ap, out_ap, obufs, out_hi=None):
        t1 = nxt("tmp", 4)
        dve_tt(tmps[t1][:, 0:n], banks[ba][:, 0:n], cos_ap, ALU.mult, [Bbank[ba], B_const], [B_tmp[t1]])
        t2 = nxt("tmp", 4)
        dve_tt(tmps[t2][:, 0:n], banks[bb][:, 0:n], sin_ap, ALU.mult, [Bbank[bb], B_const], [B_tmp[t2]])
        if out_hi is None:
            dve_tt(out_ap, tmps[t1][:, 0:n], tmps[t2][:, 0:n], ALU.add, [B_tmp[t1], B_tmp[t2]], obufs, eng="pool")
        else:
            dve_tt(out_ap, tmps[t1][0:64, 0:n], tmps[t2][0:64, 0:n], ALU.add, [B_tmp[t1], B_tmp[t2]], obufs,
                   eng="pool")
            dve_tt(out_hi, tmps[t1][64:128, 0:n], tmps[t2][64:128, 0:n], ALU.add, [B_tmp[t1], B_tmp[t2]], obufs,
                   eng="pool")

    SCALE = 0.125

    OBK = [4, 5]
    LBK = [6, 7]
    pstate = {"par": 0, "pending": {}}
    TAIL_DELAY = 60.0

    def maybe_flush(force_par=None):
        for par in list(pstate["pending"].keys()):
            created, fn = pstate["pending"][par]
            if par == force_par or pe_work["v"] - created >= TAIL_DELAY:
                del pstate["pending"][par]
                fn()

    def obank(slot, par):
        return OBK[slot]

    def run_pair(jobs, hook_after=2):
        nj = len(jobs)
        n = jobs[0][1]
        nb = len(jobs[0][2])

        def qk(ji, j):
            q_ap, _, blocks, qreads = jobs[ji]
            k_ap, kreads, _, _, mi = blocks[j]
            s_ = (j % 2) * 2 + ji
            mm(s_, banks[s_][:, 0:n], k_ap, q_ap, True, mi is None, qreads + kreads, wt=n / 512.0)
            if mi is not None:
                for g in range(n // 128):
                    mm(s_, banks[s_][:, g * 128:(g + 1) * 128], identb[:, :], maskb[:, mi, :], False,
                       g == n // 128 - 1, [B_const])

        for ji in range(nj):
            qk(ji, 0)
        for j in range(nb):
            if j + 1 < nb:
                for ji in range(nj):
                    qk(ji, j + 1)
            sp = (j % 2) * 2
            ei = j % 2
            act(Es[ei][:, 0:nj, 0:n], PS[:, sp:sp + nj, 0:n], AF.Exp, [Bbank[sp + ji] for ji in range(nj)],
                [B_E[ei]], scale=SCALE)
            for ji in range(nj):
                _, _, v_ap, vreads, _ = jobs[ji][2][j]
                bo, bl = OBK[ji], LBK[ji]
                mm(bo, banks[bo][:, 0:n], v_ap, Es[ei][:, ji, 0:n], j == 0, j == nb - 1, [B_E[ei]] + vreads,
                   wt=n / 512.0)
                mm(bl, banks[bl][:, 0:n], ones[:, :], Es[ei][:, ji, 0:n], j == 0, j == nb - 1, [B_E[ei], B_const],
                   wt=n / 512.0)
            if j % 2 == 1 or j == nb - 1:
                maybe_flush()
        return 0

    def flush_pending():
        for par in list(pstate["pending"].keys()):
            maybe_flush(force_par=par)

    for t in range(4):
        load_T(xp_d, t * 128, xT, t * 128, [B_xT[0]])
    for t in range(6):
        load_T(xw_d, t * 128, xT, 512 + t * 128, [B_xT[1] if t < 4 else B_xT[2]])

    if STAGE >= -4:
        modulation(0)

    xfT = ytmp
    for fc in range(4 if STAGE >= -3 else 0):
        for t in range(4):
            load_T(xf_d, fc * 512 + t * 128, xfT, t * 128, B_hT)
        prenorm(xfT, B_hT, 0, 512, hfT, [B_hfT[fc]], fc * 512, 0, 0, 1)
    for ci, (c0, n) in enumerate(TCH if STAGE >= -3 else []):
        prenorm(xT, [B_xT[ci]], c0, n, hT, [B_hT[ci]], c0, 0, 0, 0 if ci == 0 else 1)

    for hd in range(KHEADS if STAGE >= -2 else 0):
        si = load_w(wd0_d[hd].rearrange("p k c -> p (k c)"), 5120)
        wv = slots[si][:, 0:5120].rearrange("p (k c) -> p k c", k=8)
        WQ, WQR, WKR, WK, WV = 0, 128, 256, 384, 512
        rs = [B_slot[si]]
        if "c" not in KSKIP:
            dma_in("sp", cst[:, 0:512].rearrange("p (j c) -> p j c", j=4), ckd_d[hd], B_cst)
            bi = nbank()
            for j in range(4):
                P.op("pe", lambda e, bi=bi, j=j: e.transpose(banks[bi][:, j * 128:(j + 1) * 128],
                                                            cst[:, j * 128:(j + 1) * 128], ident[:, :]),
                     reads=[B_cst, B_const], writes=[Bbank[bi]])
            act(kTh[:, 2560:3072], banks[bi][:, :], AF.Copy, [Bbank[bi]], [B_kTh[2]])
            dma_in("sp", cst[:, 512:1024], cvd_d[hd].rearrange("p j c -> p (j c)"), B_cstv)
            dve_copy(Vh[:, 2560:3072], cst[:, 512:1024], [B_cstv], [B_Vh[2]])
        bi = nbank()
        linear_fm(bi, lambda k: wv[:, k, WQ:WQ + 128], lambda k: hT[:, k, 0:512], 512, rs + [B_hT[0]])
        act(qTh[0:64, 0, 0:512], banks[bi][0:64, :], AF.Copy, [Bbank[bi]], [B_qTh[0]])
        act(qTh[64:128, 1, 0:512], banks[bi][64:128, :], AF.Copy, [Bbank[bi]], [B_qTh[0]])
        bi = nbank()
        linear_fm(bi, lambda k: wv[:, k, WK:WK + 128], lambda k: hT[:, k, 0:512], 512, rs + [B_hT[0]])
        act(kTh[:, 0:512], banks[bi][:, :], AF.Copy, [Bbank[bi]], [B_kTh[0]])
        for t in range(0 if "t" in KSKIP else 4):
            bi = nbank()
            linear_fm(bi, lambda k, t=t: hT[:, k, t * 128:(t + 1) * 128], lambda k: wv[:, k, WK:WK + 256], 256,
                      rs + [B_hT[0]])
            oi = nxt("ost", 2)
            act(ostage[oi][:, 0:256], banks[bi][:, 0:256], AF.Copy, [Bbank[bi]], [B_ost[oi]])
            dve_copy(Vh[:, t * 128:(t + 1) * 128], banks[bi][:, 128:256], [Bbank[bi]], [B_Vh[0]])
            if "o" not in KSKIP:
                dma_out(ndk_d[t * 128:(t + 1) * 128, hd * 128:(hd + 1) * 128], ostage[oi][:, 0:128], B_ost[oi])
                dma_out(ndv_d[t * 128:(t + 1) * 128, hd * 128:(hd + 1) * 128], ostage[oi][:, 128:256], B_ost[oi])
        for ci in (() if "r" in KSKIP else (1, 2)):
            c0, n = TCH[ci]
            ba = nbank()
            linear_fm(ba, lambda k: wv[:, k, WQ:WQ + 128], lambda k: hT[:, k, c0:c0 + n], n, rs + [B_hT[ci]])
            bb = nbank()
            linear_fm(bb, lambda k: wv[:, k, WQR:WQR + 128], lambda k: hT[:, k, c0:c0 + n], n, rs + [B_hT[ci]])
            rope_epilogue(ba, bb, n, ropeW[:, 0, c0 - 512:c0 - 512 + n], ropeW[:, 1, c0 - 512:c0 - 512 + n],
                          qTh[0:64, 0, c0:c0 + n], [B_qTh[1]], out_hi=qTh[64:128, 1, c0:c0 + n])
        for fc in range(0 if "f" in KSKIP else 4):
            f0 = fc * 512
            ba = nbank()
            linear_fm(ba, lambda k: wv[:, k, WK:WK + 128], lambda k: hfT[:, k, f0:f0 + 512], 512, rs + [B_hfT[fc]])
            bb = nbank()
            linear_fm(bb, lambda k: wv[:, k, WKR:WKR + 128], lambda k: hfT[:, k, f0:f0 + 512], 512, rs + [B_hfT[fc]])
            rope_epilogue(ba, bb, 512, ropeF[:, 0, f0:f0 + 512], ropeF[:, 1, f0:f0 + 512],
                          kTh[:, 512 + f0:512 + f0 + 512], [B_kTh[1]])
            bi = nbank()
            for t in range(4):
                for k in range(KC):
                    mm(bi, banks[bi][:, t * 128:(t + 1) * 128], hfT[:, k, f0 + t * 128:f0 + (t + 1) * 128],
                       wv[:, k, WV:WV + 128], k == 0, k == KC - 1, rs + [B_hfT[fc]])
            dve_copy(Vh[:, 512 + f0:512 + f0 + 512], banks[bi][:, :], [Bbank[bi]], [B_Vh[1]])

        def diff_pair(q0_ap, q1_ap, n, blocks, qreads, out_ap, obufs):
            run_pair([(q0_ap, n, blocks, qreads), (q1_ap, n, blocks, qreads)])
            par = pstate["par"]
            pstate["par"] ^= 1
            maybe_flush(force_par=par)
            T1, B_T1 = T1s[par], B_T1s[par]
            oA, oB = OBK
            ta = nxt("tmp", 4)
            act(tmps[ta][:, 0:n], banks[LBK[0]][:, 0:n], AF.Copy, [Bbank[LBK[0]]], [B_tmp[ta]])
            tb = nxt("tmp", 4)
            act(tmps[tb][:, 0:n], banks[LBK[1]][:, 0:n], AF.Copy, [Bbank[LBK[1]]], [B_tmp[tb]])
            tc = nxt("tmp", 4)
            dve_copy(tmps[tc][:, 0:n], banks[oA][:, 0:n], [Bbank[oA]], [B_tmp[tc]])
            td = nxt("tmp", 4)
            dve_copy(tmps[td][:, 0:n], banks[oB][:, 0:n], [Bbank[oB]], [B_tmp[td]])
            dve_recip(tmps[ta][:, 0:n], tmps[ta][:, 0:n], [B_tmp[ta]], [B_tmp[ta]])
            dve_tt(T1[:, 0:n], tmps[tc][:, 0:n], tmps[ta][:, 0:n], ALU.mult, [B_tmp[tc], B_tmp[ta]], [B_T1])
            dve_recip(tmps[tb][:, 0:n], tmps[tb][:, 0:n], [B_tmp[tb]], [B_tmp[tb]])
            dve_tt(tmps[td][:, 0:n], tmps[td][:, 0:n], tmps[tb][:, 0:n], ALU.mult, [B_tmp[td], B_tmp[tb]],
                   [B_tmp[td]])
            dve_stt(T1[:, 0:n], tmps[td][:, 0:n], lamt[:, 4:5], T1[:, 0:n], ALU.mult, ALU.add,
                    [B_tmp[td], B_lam, B_T1], [B_T1])

            def tail():
                q = nxt("sq", 2)
                act(sqs[q][:, 0:n], T1[:, 0:n], AF.Square, [B_T1], [B_sq[q]])
                bs = 2
                mm(bs, banks[bs][:, 0:n], ones[:, :], sqs[q][:, 0:n], True, True, [B_sq[q], B_const])
                br = 3
                rstd_from_bank(bs, n, 128, br)
                dve_stt(out_ap, T1[:, 0:n], lamt[:, 5:6], banks[br][:, 0:n], ALU.mult, ALU.mult,
                        [B_T1, B_lam, Bbank[br]], obufs)
            pstate["pending"][par] = (pe_work["v"], tail)

        for pb in range(0 if "p" in KSKIP else 2):
            blocks = []
            for j in range(2):
                kc = pb * 256 + j * 128
                blocks.append((kTh[:, kc:kc + 128], [B_kTh[0]], Vh[:, kc:kc + 128], [B_Vh[0]], None))
            diff_pair(qTh[:, 0, pb * 256:(pb + 1) * 256], qTh[:, 1, pb * 256:(pb + 1) * 256], 256, blocks,
                      [B_qTh[0]], oT[:, hd, pb * 256:(pb + 1) * 256], [B_oT[0]])
        for ci in (() if "w" in KSKIP else (1, 2)):
            c0, n = TCH[ci]
            blocks = []
            for j in range(16):
                kc = 512 + j * 128
                blocks.append((kTh[:, kc:kc + 128], [B_kTh[1]], Vh[:, kc:kc + 128], [B_Vh[1]], None))
            for j in range(4):
                kc = 2560 + j * 128
                blocks.append((kTh[:, kc:kc + 128], [B_kTh[2]], Vh[:, kc:kc + 128], [B_Vh[2]], None))
            diff_pair(qTh[:, 0, c0:c0 + n], qTh[:, 1, c0:c0 + n], n, blocks, [B_qTh[1]],
                      oT[:, hd, c0:c0 + n], [B_oT[ci]])
        if STAGE >= 1:
            mod_piece(1, hd)
    flush_pending()

    def out_proj_l0(l):
        for ci, (c0, n) in enumerate(TCH):
            pn = PostNorm(c0, n, [B_xT[ci]], l, 0, 0 if ci == 0 else 1)
            for half in range(2):
                si = load_w(wo0_d[half].rearrange("p k c -> p (k c)"), 4096)
                wv = slots[si][:, 0:4096].rearrange("p (k c) -> p k c", k=8)
                for dl in range(4):
                    bi = nbank()
                    linear_fm(bi, lambda k: wv[:, k, dl * 128:(dl + 1) * 128], lambda k: oT[:, k, c0:c0 + n], n,
                              [B_slot[si], B_oT[ci]])
                    pn.add(bi, half * 4 + dl)
            pn.finish()

    def ffn(l, chunks):
        for (c0, n, xb, hb, ab, c) in chunks:
            prenorm(xT, xb, c0, n, hT, [hb], c0, l, 1, c)
        for piece in range(8):
            nf = min(3, NF - 3 * piece)
            si = load_w(wgu_d[l, piece, :, 0:nf].rearrange("p f k c -> p (f k c)"), nf * 2048)
            wv = slots[si][:, 0:nf * 2048].rearrange("p (f k c) -> p f k c", f=nf, k=8)
            for fl in range(nf):
                f = 3 * piece + fl
                for (c0, n, xb, hb, ab, c) in chunks:
                    bg = nbank()
                    linear_fm(bg, lambda k: wv[:, fl, k, 0:128], lambda k: hT[:, k, c0:c0 + n], n, [B_slot[si], hb])
                    bu = nbank()
                    linear_fm(bu, lambda k: wv[:, fl, k, 128:256], lambda k: hT[:, k, c0:c0 + n], n, [B_slot[si], hb])
                    t = nxt("tmp", 4)
                    act(tmps[t][:, 0:n], banks[bg][:, 0:n], AF.Silu, [Bbank[bg]], [B_tmp[t]])
                    dve_tt(aT[:, f, c0:c0 + n], tmps[t][:, 0:n], banks[bu][:, 0:n], ALU.mult,
                           [B_tmp[t], Bbank[bu]], [ab])
        for (c0, n, xb, hb, ab, c) in chunks:
            pn = PostNorm(c0, n, xb, l, 1, c)
            for j in range(4):
                si = load_w(wdn_d[l, j].rearrange("p d f c -> p (d f c)"), 5632)
                wv = slots[si][:, 0:5632].rearrange("p (d f c) -> p d f c", d=2, f=22)
                for dl in range(2):
                    bi = nbank()
                    for f in range(NF):
                        mm(bi, banks[bi][:, 0:n], wv[:, dl, f, :], aT[:, f, c0:c0 + n], f == 0, f == NF - 1,
                           [B_slot[si], ab])
                    pn.add(bi, 2 * j + dl)
            pn.finish()

    if STAGE >= -1:
        out_proj_l0(0)
    if STAGE >= 1:
        mod_finish(1)
    P.alias(B_aT, B_hfT + B_oT)
    if STAGE >= 0:
        ffn(0, [(TCH[i][0], TCH[i][1], [B_xT[i]], B_hT[i], B_aT[i], 0 if i == 0 else 1) for i in range(3)])

    if STAGE >= 2:
        P.alias(B_qTs + [B_kTs] + B_oS, B_aT)
        for ci, (c0, n) in enumerate(TCH):
            prenorm(xT, [B_xT[ci]], c0, n, hT, [B_hT[ci]], c0, 1, 0, 0 if ci == 0 else 1)
        OWN0 = 640
        B_hown = [B_hT[1], B_hT[2]]
        P.op("dve", lambda e: e.memset(kTs[:, :, :], 0.0), reads=[], writes=[B_kTs])
        VS = Vh[:, 0:3584].rearrange("p (t c) -> p t c", t=14)
        B_VS = Buf("VS")
        P.alias([B_VS], list(B_Vh))
        for half in range(2):
            sa = load_w(ws1q_d[half].rearrange("p k c -> p (k c)"), 4096)
            sr = load_w(ws1q_d[2 + half].rearrange("p k c -> p (k c)"), 4096)
            wa = slots[sa][:, 0:4096].rearrange("p (k c) -> p k c", k=8)
            wr = slots[sr][:, 0:4096].rearrange("p (k c) -> p k c", k=8)
            for cl in range(4):
                cq = half * 4 + cl
                bi = nbank()
                linear_fm(bi, lambda k: wa[:, k, cl * 128:(cl + 1) * 128], lambda k: hT[:, k, 0:512], 512,
                          [B_slot[sa], B_hT[0]])
                act(qTs[:, cq, 0:512], banks[bi][:, :], AF.Copy, [Bbank[bi]], [B_qTs[0]])
                ba = nbank()
                linear_fm(ba, lambda k: wa[:, k, cl * 128:(cl + 1) * 128], lambda k: hT[:, k, OWN0:OWN0 + 512], 512,
                          [B_slot[sa]] + B_hown)
                bb = nbank()
                linear_fm(bb, lambda k: wr[:, k, cl * 128:(cl + 1) * 128], lambda k: hT[:, k, OWN0:OWN0 + 512], 512,
                          [B_slot[sr]] + B_hown)
                rope_epilogue(ba, bb, 512, ropeW[:, 0, 128:640], ropeW[:, 1, 128:640], qTs[:, cq, 512:1024],
                              [B_qTs[1]])
        sk = load_w(ws1k_d.rearrange("p k c -> p (k c)"), 6144)
        wk = slots[sk][:, 0:6144].rearrange("p (k c) -> p k c", k=8)
        rsk = [B_slot[sk]]
        for p in range(2):
            bi = nbank()
            linear_fm(bi, lambda k: wk[:, k, p * 128:(p + 1) * 128], lambda k: hT[:, k, 0:512], 512, rsk + [B_hT[0]])
            act(kTs[0:64, 2 * p, 0:512], banks[bi][0:64, :], AF.Copy, [Bbank[bi]], [B_kTs])
            act(kTs[64:128, 2 * p + 1, 0:512], banks[bi][64:128, :], AF.Copy, [Bbank[bi]], [B_kTs])
            for ci in (1, 2):
                c0, n = TCH[ci]
                ba = nbank()
                linear_fm(ba, lambda k: wk[:, k, p * 128:(p + 1) * 128], lambda k: hT[:, k, c0:c0 + n], n,
                          rsk + [B_hT[ci]])
                bb = nbank()
                linear_fm(bb, lambda k: wk[:, k, 256 + p * 128:256 + (p + 1) * 128], lambda k: hT[:, k, c0:c0 + n], n,
                          rsk + [B_hT[ci]])
                rope_epilogue(ba, bb, n, ropeW[:, 0, c0 - 512:c0 - 512 + n], ropeW[:, 1, c0 - 512:c0 - 512 + n],
                              kTs[0:64, 2 * p, c0:c0 + n], [B_kTs], out_hi=kTs[64:128, 2 * p + 1, c0:c0 + n])
        for t in range(4):
            bi = nbank()
            for k in range(KC):
                mm(bi, banks[bi][:, 0:256], hT[:, k, t * 128:(t + 1) * 128], wk[:, k, 0:256], k == 0, k == KC - 1,
                   rsk + [B_hT[0]])
            for k in range(KC):
                mm(bi, banks[bi][:, 256:512], hT[:, k, t * 128:(t + 1) * 128], wk[:, k, 512:768], k == 0, k == KC - 1,
                   rsk + [B_hT[0]])
            oi = nxt("ost", 2)
            act(ostage[oi][:, 0:512], banks[bi][:, :], AF.Copy, [Bbank[bi]], [B_ost[oi]])
            dve_copy(VS[:, t, :], banks[bi][:, 256:512], [Bbank[bi]], [B_VS])
            dma_out(nsk_d[t * 128:(t + 1) * 128, :], ostage[oi][:, 0:256], B_ost[oi])
            dma_out(nsv_d[t * 128:(t + 1) * 128, :], ostage[oi][:, 256:512], B_ost[oi])
        for t in range(6):
            bi = nbank()
            c0 = 512 + t * 128
            for k in range(KC):
                mm(bi, banks[bi][:, 0:256], hT[:, k, c0:c0 + 128], wk[:, k, 512:768], k == 0, k == KC - 1,
                   rsk + [B_hT[1] if t < 4 else B_hT[2]])
            dve_copy(VS[:, 4 + t, :], banks[bi][:, 0:256], [Bbank[bi]], [B_VS])
        dma_in("sp", cst[:, :].rearrange("p (j c) -> p j c", j=4), cks_d, B_cst)
        for p in range(2):
            bi = nbank()
            for j in range(4):
                P.op("pe", lambda e, bi=bi, j=j, p=p: e.transpose(
                    banks[bi][:, j * 128:(j + 1) * 128], cst[:, j * 256 + p * 128:j * 256 + (p + 1) * 128], ident[:, :]),
                    reads=[B_cst, B_const], writes=[Bbank[bi]])
            act(kTs[0:64, 2 * p, 1280:1792], banks[bi][0:64, :], AF.Copy, [Bbank[bi]], [B_kTs])
            act(kTs[64:128, 2 * p + 1, 1280:1792], banks[bi][64:128, :], AF.Copy, [Bbank[bi]], [B_kTs])
        xi = nxt("xin", 2)
        dma_in("sp", xin[xi][:, :], cvs_d.rearrange("p j c -> p (j c)"), B_xin[xi])
        dve_copy(Vh[:, 2560:3584], xin[xi][:, :], [B_xin[xi]], [B_VS])

        def swa_finalize(bo, bl, ng, qn, kv, g0, oS_col0, obuf):
            r0 = (kv % 2) * 64
            pr = kv // 2
            n = ng * qn
            hh0 = kv * 4 + g0
            t = nxt("tmp", 4)
            dve_copy(tmps[t][r0:r0 + 64, 0:n], banks[bl][r0:r0 + 64, 0:n], [Bbank[bl]], [B_tmp[t]])
            to = nxt("tmp", 4)
            dve_copy(tmps[to][r0:r0 + 64, 0:n], banks[bo][r0:r0 + 64, 0:n], [Bbank[bo]], [B_tmp[to]])
            lv = tmps[t][r0:r0 + 64, 0:n].rearrange("p (g q) -> p g q", g=ng)
            sk = sinkE[r0:r0 + 64, hh0:hh0 + ng]
            sk_b = bass.AP(sk.tensor, sk.offset, [list(sk.ap[0]), list(sk.ap[1]), [0, qn]])
            dve_tt(lv, lv, sk_b, ALU.add, [B_tmp[t], B_lam], [B_tmp[t]])
            act(tmps[t][r0:r0 + 64, 0:n], tmps[t][r0:r0 + 64, 0:n], AF.Ln, [B_tmp[t]], [B_tmp[t]])
            act(tmps[t][r0:r0 + 64, 0:n], tmps[t][r0:r0 + 64, 0:n], AF.Exp, [B_tmp[t]], [B_tmp[t]], scale=-1.0)
            ov = tmps[to][r0:r0 + 64, 0:n].rearrange("p (g q) -> p g q", g=ng)
            dve_tt(oS[r0:r0 + 64, pr * 4 + g0:pr * 4 + g0 + ng, oS_col0:oS_col0 + qn], ov, lv, ALU.mult,
                   [B_tmp[to], B_tmp[t]], [obuf], eng="pool")

        def run_swa(joblist):
            for i in range(0, len(joblist), 2):
                grp = joblist[i:i + 2]
                par = run_pair([g[0] for g in grp])
                for slot, g in enumerate(grp):
                    swa_finalize(obank(slot, par), LBK[slot], *g[1])

        jl = []
        for pb in range(2):
            for kv in range(4):
                p = kv // 2
                for gh in range(2):
                    q_ap = qTs[:, p * 4 + gh * 2:p * 4 + gh * 2 + 2, pb * 256:(pb + 1) * 256]
                    blocks = []
                    for j in range(2):
                        kc = pb * 256 + j * 128
                        blocks.append((kTs[:, kv, kc:kc + 128], [B_kTs],
                                       VS[:, pb * 2 + j, p * 128:(p + 1) * 128], [B_VS], None))
                    jl.append(((q_ap, 512, blocks, [B_qTs[0]]), (2, 256, kv, gh * 2, pb * 256, B_oS[0])))
        for qb in range(4):
            for kv in range(4):
                p = kv // 2
                q_ap = qTs[:, p * 4:p * 4 + 4, 512 + qb * 128:512 + (qb + 1) * 128]
                blocks = []
                for j in range(4):
                    kc = 1280 + j * 128
                    blocks.append((kTs[:, kv, kc:kc + 128], [B_kTs],
                                   VS[:, 10 + j, p * 128:(p + 1) * 128], [B_VS], None))
                for dj, mi in ((0, 2 * qb), (1, None), (2, 2 * qb + 1)):
                    w = qb + dj
                    kc = 512 + w * 128
                    blocks.append((kTs[:, kv, kc:kc + 128], [B_kTs],
                                   VS[:, 4 + w, p * 128:(p + 1) * 128], [B_VS], mi))
                jl.append(((q_ap, 512, blocks, [B_qTs[1]]), (4, 128, kv, 0, 512 + qb * 128, B_oS[1])))
        run_swa(jl)
        L1CH = [(0, 512, [B_xT[0]], 0, 0, B_oS[0]), (OWN0, 512, [B_xT[1], B_xT[2]], 1, 512, B_oS[1])]
        for (c0, n, xb, c, oc0, ob) in L1CH:
            pn = PostNorm(c0, n, xb, 1, 0, c)
            for half in range(2):
                si = load_w(wo1_d[half].rearrange("p k c -> p (k c)"), 4096)
                wv = slots[si][:, 0:4096].rearrange("p (k c) -> p k c", k=8)
                for dl in range(4):
                    bi = nbank()
                    linear_fm(bi, lambda k: wv[:, k, dl * 128:(dl + 1) * 128], lambda k: oS[:, k, oc0:oc0 + n], n,
                              [B_slot[si], ob])
                    pn.add(bi, half * 4 + dl)
            pn.finish()
        P.alias(B_aT, B_qTs + [B_kTs] + B_oS)
        if STAGE >= 3:
            ffn(1, [(0, 512, [B_xT[0]], B_hT[0], B_aT[0], 0),
                    (OWN0, 512, [B_xT[1], B_xT[2]], B_hT[1], B_aT[1], 1)])

    OWN0 = 640
    for t in range(8):
        c0 = t * 128 if t < 4 else OWN0 + (t - 4) * 128
        xb = [B_xT[0]] if t < 4 else [B_xT[1], B_xT[2]]
        xi = nxt("xin", 2)
        for half in range(2):
            bi = nbank()
            for j in range(4):
                cidx = half * 4 + j
                P.op("pe", lambda e, bi=bi, j=j, cidx=cidx, c0=c0: e.transpose(
                    banks[bi][:, j * 128:(j + 1) * 128], xT[:, cidx, c0:c0 + 128], ident[:, :]),
                    reads=xb + [B_const], writes=[Bbank[bi]])
            if half == 0:
                act(xin[xi][:, 0:512], banks[bi][:, :], AF.Copy, [Bbank[bi]], [B_xin[xi]])
            else:
                dve_copy(xin[xi][:, 512:1024], banks[bi][:, :], [Bbank[bi]], [B_xin[xi]])
        dst = yp_d[t * 128:(t + 1) * 128, :] if t < 4 else ys_d[(t - 4) * 128:(t - 3) * 128, :]
        dma_out(dst, xin[xi][:, :], B_xin[xi])

    P.finish()
    stats = P.emit()
    return nc, es, stats


def _rope_tables(pos):
    pos = np.asarray(pos)
    row = (pos // 64).astype(np.float32)
    col = (pos % 64).astype(np.float32)
    nf = 16
    inv = (np.float32(10000.0) ** (-np.arange(nf, dtype=np.float32) / np.float32(nf))).astype(np.float32)
    ar = row[:, None] * inv[None, :]
    ac = col[:, None] * inv[None, :]
    ang = np.concatenate([ar, ar, ac, ac], axis=-1).astype(np.float32)
    cos = np.cos(ang).astype(np.float32)
    sin = np.sin(ang).astype(np.float32)
    sign = np.concatenate([-np.ones(16), np.ones(16), -np.ones(16), np.ones(16)]).astype(np.float32)
    sin = sin * sign[None, :]
    cos2 = np.concatenate([cos, cos], axis=1).T
    sin2 = np.concatenate([sin, sin], axis=1).T
    return np.ascontiguousarray(cos2), np.ascontiguousarray(sin2)


_ROTSRC = np.concatenate([np.arange(16, 32), np.arange(0, 16), np.arange(48, 64), np.arange(32, 48)])


def _rot_cols(w):
    n = w.shape[1] // 64
    idx = (np.arange(n)[:, None] * 64 + _ROTSRC[None, :]).reshape(-1)
    return w[:, idx]


def _kmaj(w):
    return np.ascontiguousarray(w.reshape(8, 128, -1).transpose(1, 0, 2))


_PROG_CACHE = {}


def prep_inputs(x_prompt, x_sample, cache_diff_k, cache_diff_v, cache_swa_k, cache_swa_v, c, c_ctx,
                w_mod, b_mod, norm_g, w_qkv_diff, diff_lambda, diff_subln_g, w_o_diff,
                w_qkv_swa, swa_sink, w_o_swa, w_gate, w_up, w_down):
    f = np.float32
    A = lambda a: np.ascontiguousarray(np.asarray(a, dtype=f))
    x_prompt, x_sample = A(x_prompt), A(x_sample)
    cache_diff_k, cache_diff_v = A(cache_diff_k), A(cache_diff_v)
    cache_swa_k, cache_swa_v = A(cache_swa_k), A(cache_swa_v)
    c, c_ctx = A(c), A(c_ctx)
    w_mod, b_mod, norm_g = A(w_mod), A(b_mod), A(norm_g)
    w_qkv_diff, diff_lambda, diff_subln_g, w_o_diff = A(w_qkv_diff), A(diff_lambda), A(diff_subln_g), A(w_o_diff)
    w_qkv_swa, swa_sink, w_o_swa = A(w_qkv_swa), A(swa_sink), A(w_o_swa)
    w_gate, w_up, w_down = A(w_gate), A(w_up), A(w_down)

    wmod = np.ascontiguousarray(
        w_mod.reshape(2, 8, 128, 8, 768).transpose(0, 3, 2, 1, 4))
    wq, wk, wv = w_qkv_diff[0][:, 0:1024], w_qkv_diff[0][:, 1024:2048], w_qkv_diff[0][:, 2048:3072]
    wqr, wkr = _rot_cols(wq), _rot_cols(wk)
    wd0 = np.empty((8, 128, 8, 640), f)
    for h in range(8):
        s = slice(h * 128, (h + 1) * 128)
        wd0[h] = _kmaj(np.concatenate([wq[:, s], wqr[:, s], wkr[:, s], wk[:, s], wv[:, s]], axis=1))
    wo0 = np.stack([_kmaj(w_o_diff[0][:, 0:512]), _kmaj(w_o_diff[0][:, 512:1024])])
    ws = w_qkv_swa[0]
    sq_cols = []
    for p in range(2):
        for g in range(4):
            for kvl in range(2):
                hh = (2 * p + kvl) * 4 + g
                sq_cols.append(np.arange(hh * 64, (hh + 1) * 64))
    sq_cols = np.concatenate(sq_cols)
    wsq = ws[:, 0:1024]
    wsq_p = wsq[:, sq_cols]
    wsqr_p = _rot_cols(wsq)[:, sq_cols]
    ws1q = np.stack([_kmaj(wsq_p[:, 0:512]), _kmaj(wsq_p[:, 512:1024]),
                     _kmaj(wsqr_p[:, 0:512]), _kmaj(wsqr_p[:, 512:1024])])
    wsk, wsv = ws[:, 1024:1280], ws[:, 1280:1536]
    ws1k = _kmaj(np.concatenate([wsk, _rot_cols(wsk), wsv], axis=1))
    wos_p = w_o_swa[0][sq_cols, :]
    wo1 = np.stack([_kmaj(wos_p[:, 0:512]), _kmaj(wos_p[:, 512:1024])])
    wgu_f = np.zeros((2, 24, 128, 8, 256), f)
    for l in range(2):
        g_ = w_gate[l].reshape(8, 128, 22, 128).transpose(2, 1, 0, 3)
        u_ = w_up[l].reshape(8, 128, 22, 128).transpose(2, 1, 0, 3)
        wgu_f[l, 0:22, :, :, 0:128] = g_
        wgu_f[l, 0:22, :, :, 128:256] = u_
    wgu = np.ascontiguousarray(wgu_f.reshape(2, 8, 3, 128, 8, 256).transpose(0, 1, 3, 2, 4, 5))
    wdn = np.ascontiguousarray(w_down.reshape(2, 22, 128, 4, 2, 128).transpose(0, 3, 2, 4, 1, 5))
    ident = np.eye(128, dtype=f)
    cosf, sinf = _rope_tables(np.arange(TFULL))
    ropef = np.ascontiguousarray(np.stack([cosf, sinf], axis=1))

    normg_l = norm_g.reshape(8, 8, 128).transpose(2, 0, 1).reshape(128, 64)
    bmod_l = b_mod.reshape(2, 48, 128).transpose(2, 0, 1).reshape(128, 96)
    subg_l = diff_subln_g.reshape(128, 1)
    lam_l = np.broadcast_to(diff_lambda.reshape(1, 256), (128, 256))
    sink_l = np.broadcast_to(swa_sink.reshape(1, 16), (128, 16))

    in_maps = []
    for core in range(NCORES):
        b, ch = core // 4, core % 4
        xp = x_prompt[2 * core:2 * core + 2].reshape(512, D)
        pos = np.arange(ch * 512 - 128, ch * 512 + 640)
        valid = (pos >= 0) & (pos < 2048)
        xw = np.zeros((TW, D), f)
        xw[valid] = x_sample[b, pos[valid]]
        cosw, sinw = _rope_tables(np.clip(pos, 0, 2047))
        ropew = np.ascontiguousarray(np.stack([cosw, sinw], axis=1))
        maskb = np.zeros((128, 8, 128), f)
        for qb in range(4):
            for side in range(2):
                w = qb + (0 if side == 0 else 2)
                kpos = pos[w * 128:(w + 1) * 128]
                qpos = pos[(qb + 1) * 128:(qb + 2) * 128]
                ok = ((kpos[:, None] >= 0) & (kpos[:, None] < 2048)
                      & (np.abs(qpos[None, :] - kpos[:, None]) <= 128))
                maskb[:, 2 * qb + side, :] = np.where(ok, 0.0, -30000.0)
        cond_l = np.stack([c_ctx, c[b]], axis=-1).reshape(8, 128, 2).transpose(1, 0, 2).reshape(128, 16)
        sm = np.ascontiguousarray(np.concatenate([cond_l, normg_l, bmod_l, subg_l, lam_l, sink_l], axis=1), dtype=f)
        ckd = np.ascontiguousarray(cache_diff_k[b, 0].reshape(4, 128, 8, 128).transpose(2, 1, 0, 3))
        cvd = np.ascontiguousarray(cache_diff_v[b, 0].reshape(4, 128, 8, 128).transpose(2, 1, 0, 3))
        cks = np.ascontiguousarray(cache_swa_k[b, 0].reshape(4, 128, 256).transpose(1, 0, 2))
        cvs = np.ascontiguousarray(cache_swa_v[b, 0].reshape(4, 128, 256).transpose(1, 0, 2))
        in_maps.append(dict(xp=np.ascontiguousarray(xp), xw=xw, xf=x_sample[b], ckd=ckd, cvd=cvd, cks=cks, cvs=cvs,
                            sm=sm, ident=ident, ropew=ropew, ropef=ropef, maskb=maskb, wmod=wmod, wd0=wd0, wo0=wo0,
                            ws1q=ws1q, ws1k=ws1k, wo1=wo1, wgu=wgu, wdn=wdn))
    return in_maps


def kernel(**inputs):
    f = np.float32
    in_maps = prep_inputs(**inputs)
    if "nc" not in _PROG_CACHE:
        nc, es, stats = build_program()
        _PROG_CACHE["nc"] = (nc, es)
        if os.environ.get("KVERBOSE"):
            print("ops per engine:", stats)
    nc, _ = _PROG_CACHE["nc"]
    res = run_bass_kernel_spmd(nc, in_maps, core_ids=list(range(NCORES)))
    R = res.results
    y_prompt = np.concatenate([R[i]["yp"].reshape(2, 256, D) for i in range(NCORES)], axis=0)
    y_sample = np.stack([np.concatenate([R[b * 4 + ch]["ys"] for ch in range(4)], axis=0) for b in range(2)], axis=0)
    ndk = np.concatenate([R[i]["ndk"].reshape(2, 1, 256, 8, 128) for i in range(NCORES)], axis=0)
    ndv = np.concatenate([R[i]["ndv"].reshape(2, 1, 256, 8, 128) for i in range(NCORES)], axis=0)
    nsk = np.concatenate([R[i]["nsk"].reshape(2, 1, 256, 4, 64) for i in range(NCORES)], axis=0)
    nsv = np.concatenate([R[i]["nsv"].reshape(2, 1, 256, 4, 64) for i in range(NCORES)], axis=0)
    return (y_prompt.astype(f), y_sample.astype(f), ndk.astype(f), ndv.astype(f), nsk.astype(f), nsv.astype(f))
```

```python
import os
import numpy as np
import concourse.bass as bass
import concourse.mybir as mybir
from concourse.bass_utils import run_bass_kernel_spmd
from contextlib import ExitStack

F32 = mybir.dt.float32
BF16 = mybir.dt.bfloat16
AF = mybir.ActivationFunctionType
ALU = mybir.AluOpType

D = 1024
KC = 8
DFF = 2816
NF = 22
TP = 512
TW = 768
TT = 1280
TFULL = 2048
LC = 512
EPS = 1e-6
NCORES = 8
STAGE = int(os.environ.get("KSTAGE", "99"))
KHEADS = int(os.environ.get("KHEADS", "8"))
KSKIP = os.environ.get("KSKIP", "")


class Buf:
    __slots__ = ("name", "writers", "readers", "dsem", "ndma", "war", "excl")

    def __init__(self, name, excl=False):
        self.name = name
        self.excl = excl
        self.writers = []
        self.readers = []
        self.war = []
        self.dsem = None
        self.ndma = 0


class Op:
    __slots__ = ("eng", "idx", "fn", "deps", "dma", "buf", "ordinal", "flag", "count", "waits", "ring")

    def __init__(self, eng, idx, fn):
        self.eng = eng
        self.idx = idx
        self.fn = fn
        self.deps = []
        self.dma = False
        self.buf = None
        self.ordinal = 0
        self.flag = False
        self.count = 0
        self.waits = []
        self.ring = False


ENGS = ["pe", "act", "dve", "pool", "sp"]


class Prog:
    def __init__(self, nc, es):
        self.nc = nc
        self.es = es
        self.ops = {e: [] for e in ENGS}
        self.dma_bufs = []
        self.out_dmas = []

    def op(self, eng, fn, reads=(), writes=(), dma_buf=None, is_out=False, ring=False):
        o = Op(eng, len(self.ops[eng]), fn)
        o.ring = ring
        deps = o.deps
        for b in reads:
            for w in b.writers:
                deps.append((w, True))
            if b.excl:
                for r in b.readers:
                    if r.eng != eng:
                        deps.append((r, False))
            b.readers.append(o)
        for b in writes:
            if b.readers:
                b.war = [r for r in b.readers if r is not o]
                b.writers = [o]
                b.readers = []
            else:
                b.writers.append(o)
            for r in b.war:
                deps.append((r, False))
        if dma_buf is not None:
            o.dma = True
            o.buf = dma_buf
            if dma_buf.dsem is None:
                self.dma_bufs.append(dma_buf)
                dma_buf.dsem = True
            dma_buf.ndma += 1
            o.ordinal = dma_buf.ndma
            if is_out:
                self.out_dmas.append(o)
        self.ops[eng].append(o)
        return o

    def alias(self, new_bufs, old_bufs):
        pend = []
        for b in old_bufs:
            pend.extend(b.readers)
            pend.extend(b.writers)
        for nb in new_bufs:
            nb.readers.extend(pend)
            nb.war = []

    def finish(self):
        o = Op("sp", len(self.ops["sp"]), None)
        for d in self.out_dmas:
            o.deps.append((d, True))
        self.ops["sp"].append(o)

    def emit(self):
        nc = self.nc
        es = self.es
        esem = {e: es.enter_context(nc.semaphore("s_" + e)) for e in ["pe", "act", "dve", "pool"]}
        for i, b in enumerate(self.dma_bufs):
            b.dsem = es.enter_context(nc.semaphore("d%d" % i))
        for e in ENGS:
            waited = {}
            for o in self.ops[e]:
                need = {}
                for (d, raw) in o.deps:
                    if d.dma:
                        key = ("d", id(d.buf))
                        val = d.ordinal
                        if waited.get(key, 0) >= val:
                            continue
                        if need.get(key, (0, None))[0] < val:
                            need[key] = (val, d)
                    else:
                        if d.eng == e and e == "pe":
                            continue
                        key = ("e", d.eng)
                        val = d.idx + 1
                        if waited.get(key, 0) >= val:
                            continue
                        if need.get(key, (0, None))[0] < val:
                            need[key] = (val, d)
                for key, (val, d) in need.items():
                    waited[key] = val
                    if not d.dma:
                        d.flag = True
                    o.waits.append(d)
        for e in ["pe", "act", "dve", "pool"]:
            c = 0
            for o in self.ops[e]:
                if o.flag and not o.dma:
                    c += 1
                    o.count = c
        handles = {"pe": "tensor", "act": "scalar", "dve": "vector", "pool": "gpsimd", "sp": "sync"}
        stats = {}
        with nc.Block() as block:
            for e in ENGS:
                ops = self.ops[e]
                stats[e] = len(ops)

                def body(eng, ops=ops, e=e):
                    for o in ops:
                        for d in o.waits:
                            if d.dma:
                                eng.wait_ge(d.buf.dsem, 16 * d.ordinal)
                            else:
                                eng.wait_ge(esem[d.eng], d.count)
                        if o.fn is None:
                            continue
                        inst = o.fn(eng)
                        if o.ring:
                            assert not o.flag
                            inst.then_inc(self.ring_sem, 16)
                        elif o.dma:
                            inst.then_inc(o.buf.dsem, 16)
                        elif o.flag:
                            inst.then_inc(esem[e], 1)

                getattr(block, handles[e])(body)
        return stats


def build_program():
    nc = bass.Bass("TRN2", target_bir_lowering=False, monotonic_sem_count=0)
    es = ExitStack()
    P = Prog(nc, es)

    def din(name, shape):
        return nc.dram_tensor(name, list(shape), F32, kind="ExternalInput").ap()

    def dout(name, shape):
        return nc.dram_tensor(name, list(shape), F32, kind="ExternalOutput").ap()

    xp_d = din("xp", [TP, D])
    xw_d = din("xw", [TW, D])
    xf_d = din("xf", [TFULL, D])
    ckd_d = din("ckd", [8, 128, 4, 128])
    cvd_d = din("cvd", [8, 128, 4, 128])
    cks_d = din("cks", [128, 4, 256])
    cvs_d = din("cvs", [128, 4, 256])
    NSM = 16 + 64 + 96 + 1 + 256 + 16
    sm_d = din("sm", [128, NSM])
    ident_d = din("ident", [128, 128])
    ropew_d = din("ropew", [128, 2, TW])
    ropef_d = din("ropef", [128, 2, TFULL])
    maskb_d = din("maskb", [128, 8, 128])
    wmod_d = din("wmod", [2, 8, 128, 8, 768])
    wd0_d = din("wd0", [8, 128, 8, 640])
    wo0_d = din("wo0", [2, 128, 8, 512])
    ws1q_d = din("ws1q", [4, 128, 8, 512])
    ws1k_d = din("ws1k", [128, 8, 768])
    wo1_d = din("wo1", [2, 128, 8, 512])
    wgu_d = din("wgu", [2, 8, 128, 3, 8, 256])
    wdn_d = din("wdn", [2, 4, 128, 2, 22, 128])

    yp_d = dout("yp", [TP, D])
    ys_d = dout("ys", [512, D])
    ndk_d = dout("ndk", [TP, 1024])
    ndv_d = dout("ndv", [TP, 1024])
    nsk_d = dout("nsk", [TP, 256])
    nsv_d = dout("nsv", [TP, 256])

    def sb(name, shape, dt):
        return es.enter_context(nc.sbuf_tensor(name, list(shape), dt))

    xT = sb("xT", [128, KC, TT], F32)
    hT = sb("hT", [128, KC, TT], BF16)
    ytmp = hT.bitcast(F32)
    BIG = sb("BIG", [128, 28160], BF16)
    slots = [sb("wslot%d" % i, [128, 6144], BF16) for i in range(2)]
    kTh = sb("kTh", [128, 3072], BF16)
    Vh = sb("Vh", [128, 3584], BF16)
    qTh = sb("qTh", [128, 2, TT], BF16)
    Es = [sb("E%d" % i, [128, 2, 512], BF16) for i in range(2)]
    xin = [sb("xin%d" % i, [128, 1024], F32) for i in range(2)]
    cst = sb("cst", [128, 1024], F32)
    ropeW = sb("ropeW", [128, 2, TW], F32)
    ropeF = sb("ropeF", [128, 2, TFULL], BF16)
    maskb = sb("maskbs", [128, 8, 128], BF16)
    ident = sb("idents", [128, 128], F32)
    identb = sb("identb", [128, 128], BF16)
    ones = sb("ones", [128, 128], BF16)
    sm = sb("sms", [128, NSM], F32)
    modT = sb("modT", [128, 2, 48, 2], F32)
    der = sb("der", [128, 2, 4, 8, 2], F32)
    scb = sb("scb", [128, 8, 2], BF16)
    lamt = sb("lamt", [128, 8], F32)
    sinkE = sb("sinkE", [128, 16], F32)
    sqs = [sb("sq%d" % i, [128, 512], BF16) for i in range(2)]
    tmps = [sb("tmp%d" % i, [128, 512], F32) for i in range(4)]
    T1s = [sb("T1a", [128, 512], F32), sb("T1b", [128, 512], F32)]
    ostage = [xin[i] for i in range(2)]

    PS = es.enter_context(nc.psum_tensor("PS", [128, 8, 512], F32))

    class BankView:
        def __init__(self, i):
            self.i = i

        def __getitem__(self, idx):
            return PS[idx[0], self.i, idx[1]]

    banks = [BankView(i) for i in range(8)]
    Bbank = [Buf("bank%d" % i, excl=True) for i in range(8)]

    B_xT = [Buf("xT_p"), Buf("xT_w0"), Buf("xT_w1")]
    B_hT = [Buf("hT_p"), Buf("hT_w0"), Buf("hT_w1")]
    B_slot = [Buf("slot0"), Buf("slot1")]
    B_kTh = Buf("kTh_p"), Buf("kTh_f"), Buf("kTh_c")
    B_Vh = Buf("Vh_p"), Buf("Vh_f"), Buf("Vh_c")
    B_qTh = [Buf("qTh_p"), Buf("qTh_w")]
    B_E = [Buf("E%d" % i) for i in range(2)]
    B_xin = [Buf("xin0"), Buf("xin1")]
    B_cst = Buf("cst")
    B_cstv = Buf("cstv")
    B_const = Buf("consts")
    B_sm = Buf("sm")
    B_mod = [Buf("mod0"), Buf("mod1")]
    B_der = [Buf("der0"), Buf("der1")]
    B_scb = Buf("scb")
    B_lam = Buf("lam")
    B_sq = [Buf("sq0"), Buf("sq1")]
    B_tmp = [Buf("tmp%d" % i) for i in range(4)]
    B_T1s = [Buf("T1a"), Buf("T1b")]
    B_ost = B_xin
    B_hfT = [Buf("hfT%d" % i) for i in range(4)]
    B_oT = [Buf("oT_p"), Buf("oT_w0"), Buf("oT_w1")]
    B_aT = [Buf("aT0"), Buf("aT1"), Buf("aT2")]
    B_qTs = [Buf("qTs_p"), Buf("qTs_s")]
    B_kTs = Buf("kTs")
    B_oS = [Buf("oS_p"), Buf("oS_s")]

    hfT = BIG[:, 0:16384].rearrange("p (k t) -> p k t", k=8)
    oT = BIG[:, 16384:16384 + 10240].rearrange("p (k t) -> p k t", k=8)
    aT = BIG[:, 0:28160].rearrange("p (f t) -> p f t", f=22)
    qTs = BIG[:, 0:8192].rearrange("p (k t) -> p k t", k=8)
    kTs = BIG[:, 8192:8192 + 7168].rearrange("p (k t) -> p k t", k=4)
    oS = BIG[:, 15360:15360 + 8192].rearrange("p (h t) -> p h t", h=8)

    BIGf = BIG.bitcast(F32)
    yA = BIGf[:, 0:4096].rearrange("p (d t) -> p d t", d=8)
    yA2 = BIGf[:, 4096:8192].rearrange("p (d t) -> p d t", d=8)
    B_yA, B_yA2 = Buf("yA"), Buf("yA2")
    TCH = [(0, 512), (512, 512), (1024, 256)]

    rr = {"bank": 0, "tmp": 0, "sq": 0, "slot": 0, "xin": 0, "ost": 0, "E": 0}

    def nxt(kind, n):
        v = rr[kind]
        rr[kind] = (v + 1) % n
        return v

    reserved = set()

    def nbank():
        while True:
            b = nxt("bank", 8)
            if b not in reserved:
                return b

    open_grp = {}

    pe_work = {"v": 0.0}

    def mm(bi, out_ap, lhsT, rhs, start, stop, reads, wt=1.0):
        pe_work["v"] += wt
        if start and open_grp.get(bi):
            import traceback
            traceback.print_stack(limit=6)
            print("OPEN GROUP on bank", bi, "opened at:", open_grp[bi])
        if start:
            import traceback
            open_grp[bi] = "".join(traceback.format_stack(limit=5)[:-1])
        if stop:
            open_grp[bi] = None
        P.op("pe", lambda e: e.matmul(out_ap, lhsT, rhs, start=start, stop=stop),
             reads=reads, writes=[Bbank[bi]])

    def act(out_ap, in_ap, func, reads, writes, bias=None, scale=None):
        kw = {}
        if bias is not None:
            kw["bias"] = bias
        if scale is not None:
            kw["scale"] = scale
        P.op("act", lambda e: e.activation(out_ap, in_ap, func, **kw), reads=reads, writes=writes)

    def dve_tt(out_ap, a, b, op, reads, writes, eng="dve"):
        P.op(eng, lambda e: e.tensor_tensor(out_ap, a, b, op), reads=reads, writes=writes)

    def dve_stt(out_ap, a, scalar, b, op0, op1, reads, writes):
        P.op("dve", lambda e: e.scalar_tensor_tensor(out_ap, a, scalar, b, op0, op1), reads=reads, writes=writes)

    def dve_ts(out_ap, a, s1, s2, op0, op1, reads, writes, eng="dve"):
        if op1 is None:
            P.op(eng, lambda e: e.tensor_scalar(out_ap, a, s1, None, op0), reads=reads, writes=writes)
        else:
            P.op(eng, lambda e: e.tensor_scalar(out_ap, a, s1, s2, op0, op1), reads=reads, writes=writes)

    def dve_copy(out_ap, in_ap, reads, writes, eng="dve"):
        P.op(eng, lambda e: e.tensor_copy(out_ap, in_ap), reads=reads, writes=writes)

    def dve_recip(out_ap, in_ap, reads, writes):
        P.op("dve", lambda e: e.reciprocal(out_ap, in_ap), reads=reads, writes=writes)

    def dma_in(eng, out_ap, in_ap, buf, extra_writes=(), cast=False):
        if eng == "pool":
            P.op("pool", lambda e: e.dma_start(out=out_ap, in_=in_ap, max_dma_last_dim=4096),
                 reads=[], writes=[buf] + list(extra_writes), dma_buf=Buf("sw"))
        elif cast:
            P.op(eng, lambda e: e.dma_start(out=out_ap, in_=in_ap, max_dma_last_dim=4096),
                 reads=[], writes=[buf] + list(extra_writes), dma_buf=buf)
        else:
            P.op(eng, lambda e: e.dma_start(out=out_ap, in_=in_ap),
                 reads=[], writes=[buf] + list(extra_writes), dma_buf=buf)

    def dma_out(out_ap, in_ap, buf):
        P.op("sp", lambda e: e.dma_start(out=out_ap, in_=in_ap), reads=[buf], writes=[], dma_buf=buf, is_out=True)

    def load_w(srcs, nelem, view=None, parts=128):
        si = nxt("slot", 2)
        if not isinstance(srcs, (list, tuple)):
            srcs = [srcs]
        for i, src_ap in enumerate(srcs):
            dst = slots[si][0:parts, i * nelem:(i + 1) * nelem]
            dma_in("pool", dst, src_ap, B_slot[si], cast=True)
        return si

    dma_in("sp", sm[:, :], sm_d, B_sm)
    dma_in("sp", ident[:, :], ident_d, B_const)
    dma_in("sp", ropeW[:, :, :], ropew_d, B_const)
    dma_in("pool", ropeF[:, :, :].rearrange("p a t -> p (a t)"), ropef_d.rearrange("p a t -> p (a t)"), B_const, cast=True)
    dma_in("pool", maskb[:, :, :].rearrange("p a t -> p (a t)"), maskb_d.rearrange("p a t -> p (a t)"), B_const, cast=True)
    P.op("dve", lambda e: e.memset(ones[:, :], 1.0), reads=[], writes=[B_const])
    dve_copy(identb[:, :], ident[:, :], [B_const], [B_const])
    P.op("dve", lambda e: e.memset(qTh[:, :, :], 0.0), reads=[], writes=[B_qTh[0], B_qTh[1]])

    O_COND, O_NG, O_BM, O_SUBG, O_LAM, O_SINK = 0, 16, 80, 176, 177, 433
    cond_v = sm[:, O_COND:O_COND + 16].rearrange("p (k c) -> p k c", c=2)
    act(scb[:, :, :], cond_v, AF.Silu, [B_sm], [B_scb])
    LAM_INIT = 0.8 - 0.6 * float(np.exp(-0.3 * 0))
    lam_v = sm[:, O_LAM:O_LAM + 256].rearrange("p (a d) -> p a d", a=4)
    P.op("dve", lambda e: e.tensor_tensor(tmps[0][:, 0:64], lam_v[:, 0, :], lam_v[:, 1, :], ALU.mult),
         reads=[B_sm], writes=[B_tmp[0]])
    P.op("dve", lambda e: e.tensor_tensor(tmps[0][:, 64:128], lam_v[:, 2, :], lam_v[:, 3, :], ALU.mult),
         reads=[B_sm], writes=[B_tmp[0]])
    P.op("dve", lambda e: e.reduce_sum(lamt[:, 0:2], tmps[0][:, 0:128].rearrange("p (a d) -> p a d", a=2),
                                       mybir.AxisListType.X), reads=[B_tmp[0]], writes=[B_lam])
    act(lamt[:, 2:4], lamt[:, 0:2], AF.Exp, [B_lam], [B_lam])
    dve_tt(lamt[:, 4:5], lamt[:, 3:4], lamt[:, 2:3], ALU.subtract, [B_lam], [B_lam])
    dve_ts(lamt[:, 4:5], lamt[:, 4:5], -LAM_INIT, None, ALU.add, None, [B_lam], [B_lam])
    dve_ts(lamt[:, 5:6], sm[:, O_SUBG:O_SUBG + 1], 1.0 - LAM_INIT, None, ALU.mult, None, [B_sm, B_lam], [B_lam])
    act(sinkE[:, :], sm[:, O_SINK:O_SINK + 16], AF.Exp, [B_sm], [B_lam])

    def mod_piece(l, piece):
        bi = nbank()
        si = load_w(wmod_d[l, piece].rearrange("p k c -> p (k c)"), 6144)
        wv = slots[si][:, 0:6144].rearrange("p (k c) -> p k c", k=8)
        for oc in range(6):
            for k in range(KC):
                mm(bi, banks[bi][:, oc * 2:oc * 2 + 2], wv[:, k, oc * 128:(oc + 1) * 128], scb[:, k, :],
                   k == 0, k == KC - 1, [B_slot[si], B_scb], wt=0.15)
        bv = banks[bi][:, 0:12].rearrange("p (o c) -> p o c", c=2)
        o0 = piece * 6
        for c in range(2):
            dve_tt(modT[:, l, o0:o0 + 6, c], bv[:, :, c], sm[:, O_BM + l * 48 + o0:O_BM + l * 48 + o0 + 6], ALU.add,
                   [Bbank[bi], B_sm], [B_mod[l]])

    def mod_finish(l, parts=(0, 1, 2, 3)):
        for c in range(2):
            def g(n):
                return sm[:, O_NG + (l * 4 + n) * 8: O_NG + (l * 4 + n) * 8 + 8]
            if 0 in parts:
                dve_stt(der[:, l, 0, :, c], modT[:, l, 8:16, c], 1.0, g(0), ALU.add, ALU.mult, [B_mod[l], B_sm],
                        [B_der[l]])
            if 1 in parts:
                dve_tt(der[:, l, 1, :, c], modT[:, l, 16:24, c], g(1), ALU.mult, [B_mod[l], B_sm], [B_der[l]])
            if 2 in parts:
                dve_stt(der[:, l, 2, :, c], modT[:, l, 32:40, c], 1.0, g(2), ALU.add, ALU.mult, [B_mod[l], B_sm],
                        [B_der[l]])
            if 3 in parts:
                dve_tt(der[:, l, 3, :, c], modT[:, l, 40:48, c], g(3), ALU.mult, [B_mod[l], B_sm], [B_der[l]])

    def modulation(l):
        for piece in range(8):
            mod_piece(l, piece)
        mod_finish(l)

    def mod_scalars(l, which, c):
        def gs(k):
            return der[:, l, 2 * which, k, c:c + 1]

        def sh(k):
            return modT[:, l, 24 * which + k, c:c + 1]

        def gg(k):
            return der[:, l, 2 * which + 1, k, c:c + 1]
        return gs, sh, gg

    def load_T(src_d, row0, dst, dcol0, dbufs):
        xi = nxt("xin", 2)
        dma_in("sp", xin[xi][:, :], src_d[row0:row0 + 128, :], B_xin[xi])
        for half in range(2):
            bi = nbank()
            for j in range(4):
                cidx = half * 4 + j
                P.op("pe", lambda e, bi=bi, j=j, cidx=cidx, xi=xi: e.transpose(
                    banks[bi][:, j * 128:(j + 1) * 128], xin[xi][:, cidx * 128:(cidx + 1) * 128], ident[:, :]),
                    reads=[B_xin[xi], B_const], writes=[Bbank[bi]])
            src = banks[bi][:, :].rearrange("p (j t) -> p j t", j=4)
            dsta = dst[:, half * 4:half * 4 + 4, dcol0:dcol0 + 128]
            if half == 0:
                act(dsta, src, AF.Copy, [Bbank[bi]], dbufs)
            else:
                dve_copy(dsta, src, [Bbank[bi]], dbufs)

    def rstd_from_bank(bs, n, nfeat, br):
        t = nxt("tmp", 4)
        act(tmps[t][:, 0:n], banks[bs][:, 0:n], AF.Ln, [Bbank[bs]], [B_tmp[t]], bias=EPS, scale=1.0 / nfeat)
        act(banks[br][:, 0:n], tmps[t][:, 0:n], AF.Exp, [B_tmp[t]], [Bbank[br]], scale=-0.5)

    def prenorm(src, sbufs, c0, n, dst, dbufs, dc0, l, which, c):
        gs, sh, _ = mod_scalars(l, which, c)
        bs = nbank()
        for k in range(KC):
            q = nxt("sq", 2)
            act(sqs[q][:, 0:n], src[:, k, c0:c0 + n], AF.Square, sbufs, [B_sq[q]])
            mm(bs, banks[bs][:, 0:n], ones[:, :], sqs[q][:, 0:n], k == 0, k == KC - 1, [B_sq[q], B_const])
        br = nbank()
        rstd_from_bank(bs, n, D, br)
        for k in range(KC):
            t = nxt("tmp", 4)
            dve_tt(tmps[t][:, 0:n], src[:, k, c0:c0 + n], banks[br][:, 0:n], ALU.mult, sbufs + [Bbank[br]], [B_tmp[t]])
            dve_ts(dst[:, k, dc0:dc0 + n], tmps[t][:, 0:n], gs(k), sh(k), ALU.mult, ALU.add,
                   [B_tmp[t], B_mod[l], B_der[l]], dbufs, eng="pool")

    class PostNorm:
        def __init__(self, c0, n, xbufs, l, which, c, ybuf=None):
            self.c0, self.n, self.xbufs, self.l, self.which, self.c = c0, n, xbufs, l, which, c
            self.yv, self.yb = ybuf if ybuf is not None else (ytmp, B_hT)
            self.bs = nbank()
            reserved.add(self.bs)
            self.cnt = 0

        def add(self, bi, dch):
            n = self.n
            yv = self.yv[:, dch, 0:n]
            act(yv, banks[bi][:, 0:n], AF.Copy, [Bbank[bi]], self.yb)
            q = nxt("sq", 2)
            act(sqs[q][:, 0:n], banks[bi][:, 0:n], AF.Square, [Bbank[bi]], [B_sq[q]])
            mm(self.bs, banks[self.bs][:, 0:n], ones[:, :], sqs[q][:, 0:n], self.cnt == 0, self.cnt == KC - 1,
               [B_sq[q], B_const])
            self.cnt += 1

        def finish(self):
            n, c0 = self.n, self.c0
            _, _, gg = mod_scalars(self.l, self.which, self.c)
            br = nbank()
            reserved.discard(self.bs)
            rstd_from_bank(self.bs, n, D, br)
            for dch in range(KC):
                t = nxt("tmp", 4)
                dve_tt(tmps[t][:, 0:n], self.yv[:, dch, 0:n], banks[br][:, 0:n], ALU.mult,
                       self.yb + [Bbank[br]], [B_tmp[t]])
                xa = xT[:, dch, c0:c0 + n]
                dve_stt(xa, tmps[t][:, 0:n], gg(dch), xa, ALU.mult, ALU.add,
                        [B_tmp[t], B_der[self.l]] + self.xbufs, self.xbufs)

    def linear_fm(bi, wfun, rhsfun, n, reads, nk=KC):
        for k in range(nk):
            mm(bi, banks[bi][:, 0:n], wfun(k), rhsfun(k), k == 0, k == nk - 1, reads, wt=n / 512.0)

    def rope_epilogue(ba, bb, n, cos_ap, sin_ap, out_ap, obufs, out_hi=None):
        t1 = nxt("tmp", 4)
        dve_tt(tmps[t1][:, 0:n], banks[ba][:, 0:n], cos_ap, ALU.mult, [Bbank[ba], B_const], [B_tmp[t1]])
        t2 = nxt("tmp", 4)
        dve_tt(tmps[t2][:, 0:n], banks[bb][:, 0:n], sin_ap, ALU.mult, [Bbank[bb], B_const], [B_tmp[t2]])
        if out_hi is None:
            dve_tt(out_ap, tmps[t1][:, 0:n], tmps[t2][:, 0:n], ALU.add, [B_tmp[t1], B_tmp[t2]], obufs, eng="pool")
        else:
            dve_tt(out_ap, tmps[t1][0:64, 0:n], tmps[t2][0:64, 0:n], ALU.add, [B_tmp[t1], B_tmp[t2]], obufs,
                   eng="pool")
            dve_tt(out_hi, tmps[t1][64:128, 0:n], tmps[t2][64:128, 0:n], ALU.add, [B_tmp[t1], B_tmp[t2]], obufs,
                   eng="pool")

    SCALE = 0.125

    OBK = [4, 5]
    LBK = [6, 7]
    pstate = {"par": 0, "pending": {}}
    TAIL_DELAY = 60.0

    def maybe_flush(force_par=None):
        for par in list(pstate["pending"].keys()):
            created, fn = pstate["pending"][par]
            if par == force_par or pe_work["v"] - created >= TAIL_DELAY:
                del pstate["pending"][par]
                fn()

    def obank(slot, par):
        return OBK[slot]

    def run_pair(jobs, hook_after=2):
        nj = len(jobs)
        n = jobs[0][1]
        nb = len(jobs[0][2])

        def qk(ji, j):
            q_ap, _, blocks, qreads = jobs[ji]
            k_ap, kreads, _, _, mi = blocks[j]
            s_ = (j % 2) * 2 + ji
            mm(s_, banks[s_][:, 0:n], k_ap, q_ap, True, mi is None, qreads + kreads, wt=n / 512.0)
            if mi is not None:
                for g in range(n // 128):
                    mm(s_, banks[s_][:, g * 128:(g + 1) * 128], identb[:, :], maskb[:, mi, :], False,
                       g == n // 128 - 1, [B_const])

        for ji in range(nj):
            qk(ji, 0)
        for j in range(nb):
            if j + 1 < nb:
                for ji in range(nj):
                    qk(ji, j + 1)
            sp = (j % 2) * 2
            ei = j % 2
            act(Es[ei][:, 0:nj, 0:n], PS[:, sp:sp + nj, 0:n], AF.Exp, [Bbank[sp + ji] for ji in range(nj)],
                [B_E[ei]], scale=SCALE)
            for ji in range(nj):
                _, _, v_ap, vreads, _ = jobs[ji][2][j]
                bo, bl = OBK[ji], LBK[ji]
                mm(bo, banks[bo][:, 0:n], v_ap, Es[ei][:, ji, 0:n], j == 0, j == nb - 1, [B_E[ei]] + vreads,
                   wt=n / 512.0)
                mm(bl, banks[bl][:, 0:n], ones[:, :], Es[ei][:, ji, 0:n], j == 0, j == nb - 1, [B_E[ei], B_const],
                   wt=n / 512.0)
            if j % 2 == 1 or j == nb - 1:
                maybe_flush()
        return 0

    def flush_pending():
        for par in list(pstate["pending"].keys()):
            maybe_flush(force_par=par)

    for t in range(4):
        load_T(xp_d, t * 128, xT, t * 128, [B_xT[0]])
    for t in range(6):
        load_T(xw_d, t * 128, xT, 512 + t * 128, [B_xT[1] if t < 4 else B_xT[2]])

    if STAGE >= -4:
        for piece in range(3):
            mod_piece(0, piece)
        mod_finish(0, parts=(0,))

    xfT = ytmp
    for fc in range(4 if STAGE >= -3 else 0):
        for t in range(4):
            load_T(xf_d, fc * 512 + t * 128, xfT, t * 128, B_hT)
        prenorm(xfT, B_hT, 0, 512, hfT, [B_hfT[fc]], fc * 512, 0, 0, 1)
    for ci, (c0, n) in enumerate(TCH if STAGE >= -3 else []):
        prenorm(xT, [B_xT[ci]], c0, n, hT, [B_hT[ci]], c0, 0, 0, 0 if ci == 0 else 1)

    for hd in range(KHEADS if STAGE >= -2 else 0):
        si = load_w(wd0_d[hd].rearrange("p k c -> p (k c)"), 5120)
        wv = slots[si][:, 0:5120].rearrange("p (k c) -> p k c", k=8)
        WQ, WQR, WKR, WK, WV = 0, 128, 256, 384, 512
        rs = [B_slot[si]]
        if "c" not in KSKIP:
            dma_in("sp", cst[:, 0:512].rearrange("p (j c) -> p j c", j=4), ckd_d[hd], B_cst)
            bi = nbank()
            for j in range(4):
                P.op("pe", lambda e, bi=bi, j=j: e.transpose(banks[bi][:, j * 128:(j + 1) * 128],
                                                            cst[:, j * 128:(j + 1) * 128], ident[:, :]),
                     reads=[B_cst, B_const], writes=[Bbank[bi]])
            act(kTh[:, 2560:3072], banks[bi][:, :], AF.Copy, [Bbank[bi]], [B_kTh[2]])
            dma_in("sp", cst[:, 512:1024], cvd_d[hd].rearrange("p j c -> p (j c)"), B_cstv)
            dve_copy(Vh[:, 2560:3072], cst[:, 512:1024], [B_cstv], [B_Vh[2]])
        bi = nbank()
        linear_fm(bi, lambda k: wv[:, k, WQ:WQ + 128], lambda k: hT[:, k, 0:512], 512, rs + [B_hT[0]])
        act(qTh[0:64, 0, 0:512], banks[bi][0:64, :], AF.Copy, [Bbank[bi]], [B_qTh[0]])
        act(qTh[64:128, 1, 0:512], banks[bi][64:128, :], AF.Copy, [Bbank[bi]], [B_qTh[0]])
        bi = nbank()
        linear_fm(bi, lambda k: wv[:, k, WK:WK + 128], lambda k: hT[:, k, 0:512], 512, rs + [B_hT[0]])
        act(kTh[:, 0:512], banks[bi][:, :], AF.Copy, [Bbank[bi]], [B_kTh[0]])
        maybe_flush()
        for t in range(0 if "t" in KSKIP else 4):
            bi = nbank()
            linear_fm(bi, lambda k, t=t: hT[:, k, t * 128:(t + 1) * 128], lambda k: wv[:, k, WK:WK + 256], 256,
                      rs + [B_hT[0]])
            oi = nxt("ost", 2)
            act(ostage[oi][:, 0:256], banks[bi][:, 0:256], AF.Copy, [Bbank[bi]], [B_ost[oi]])
            dve_copy(Vh[:, t * 128:(t + 1) * 128], banks[bi][:, 128:256], [Bbank[bi]], [B_Vh[0]])
            if "o" not in KSKIP:
                dma_out(ndk_d[t * 128:(t + 1) * 128, hd * 128:(hd + 1) * 128], ostage[oi][:, 0:128], B_ost[oi])
                dma_out(ndv_d[t * 128:(t + 1) * 128, hd * 128:(hd + 1) * 128], ostage[oi][:, 128:256], B_ost[oi])
        for ci in (() if "r" in KSKIP else (1, 2)):
            c0, n = TCH[ci]
            ba = nbank()
            linear_fm(ba, lambda k: wv[:, k, WQ:WQ + 128], lambda k: hT[:, k, c0:c0 + n], n, rs + [B_hT[ci]])
            bb = nbank()
            linear_fm(bb, lambda k: wv[:, k, WQR:WQR + 128], lambda k: hT[:, k, c0:c0 + n], n, rs + [B_hT[ci]])
            rope_epilogue(ba, bb, n, ropeW[:, 0, c0 - 512:c0 - 512 + n], ropeW[:, 1, c0 - 512:c0 - 512 + n],
                          qTh[0:64, 0, c0:c0 + n], [B_qTh[1]], out_hi=qTh[64:128, 1, c0:c0 + n])
            maybe_flush()
        for fc in range(0 if "f" in KSKIP else 4):
            f0 = fc * 512
            ba = nbank()
            linear_fm(ba, lambda k: wv[:, k, WK:WK + 128], lambda k: hfT[:, k, f0:f0 + 512], 512, rs + [B_hfT[fc]])
            bb = nbank()
            linear_fm(bb, lambda k: wv[:, k, WKR:WKR + 128], lambda k: hfT[:, k, f0:f0 + 512], 512, rs + [B_hfT[fc]])
            rope_epilogue(ba, bb, 512, ropeF[:, 0, f0:f0 + 512], ropeF[:, 1, f0:f0 + 512],
                          kTh[:, 512 + f0:512 + f0 + 512], [B_kTh[1]])
            bi = nbank()
            for t in range(4):
                for k in range(KC):
                    mm(bi, banks[bi][:, t * 128:(t + 1) * 128], hfT[:, k, f0 + t * 128:f0 + (t + 1) * 128],
                       wv[:, k, WV:WV + 128], k == 0, k == KC - 1, rs + [B_hfT[fc]])
            dve_copy(Vh[:, 512 + f0:512 + f0 + 512], banks[bi][:, :], [Bbank[bi]], [B_Vh[1]])
            maybe_flush()

        def diff_pair(q0_ap, q1_ap, n, blocks, qreads, out_ap, obufs):
            run_pair([(q0_ap, n, blocks, qreads), (q1_ap, n, blocks, qreads)])
            par = pstate["par"]
            pstate["par"] ^= 1
            maybe_flush(force_par=par)
            T1, B_T1 = T1s[par], B_T1s[par]
            oA, oB = OBK
            ta = nxt("tmp", 4)
            act(tmps[ta][:, 0:n], banks[LBK[0]][:, 0:n], AF.Copy, [Bbank[LBK[0]]], [B_tmp[ta]])
            tb = nxt("tmp", 4)
            act(tmps[tb][:, 0:n], banks[LBK[1]][:, 0:n], AF.Copy, [Bbank[LBK[1]]], [B_tmp[tb]])
            tc = nxt("tmp", 4)
            dve_copy(tmps[tc][:, 0:n], banks[oA][:, 0:n], [Bbank[oA]], [B_tmp[tc]])
            td = nxt("tmp", 4)
            dve_copy(tmps[td][:, 0:n], banks[oB][:, 0:n], [Bbank[oB]], [B_tmp[td]])
            dve_recip(tmps[ta][:, 0:n], tmps[ta][:, 0:n], [B_tmp[ta]], [B_tmp[ta]])
            dve_tt(T1[:, 0:n], tmps[tc][:, 0:n], tmps[ta][:, 0:n], ALU.mult, [B_tmp[tc], B_tmp[ta]], [B_T1])
            dve_recip(tmps[tb][:, 0:n], tmps[tb][:, 0:n], [B_tmp[tb]], [B_tmp[tb]])
            dve_tt(tmps[td][:, 0:n], tmps[td][:, 0:n], tmps[tb][:, 0:n], ALU.mult, [B_tmp[td], B_tmp[tb]],
                   [B_tmp[td]])
            dve_stt(T1[:, 0:n], tmps[td][:, 0:n], lamt[:, 4:5], T1[:, 0:n], ALU.mult, ALU.add,
                    [B_tmp[td], B_lam, B_T1], [B_T1])

            def tail():
                q = nxt("sq", 2)
                act(sqs[q][:, 0:n], T1[:, 0:n], AF.Square, [B_T1], [B_sq[q]])
                bs = 2
                mm(bs, banks[bs][:, 0:n], ones[:, :], sqs[q][:, 0:n], True, True, [B_sq[q], B_const])
                br = 3
                rstd_from_bank(bs, n, 128, br)
                dve_stt(out_ap, T1[:, 0:n], lamt[:, 5:6], banks[br][:, 0:n], ALU.mult, ALU.mult,
                        [B_T1, B_lam, Bbank[br]], obufs)
            pstate["pending"][par] = (pe_work["v"], tail)

        for pb in range(0 if "p" in KSKIP else 2):
            blocks = []
            for j in range(2):
                kc = pb * 256 + j * 128
                blocks.append((kTh[:, kc:kc + 128], [B_kTh[0]], Vh[:, kc:kc + 128], [B_Vh[0]], None))
            diff_pair(qTh[:, 0, pb * 256:(pb + 1) * 256], qTh[:, 1, pb * 256:(pb + 1) * 256], 256, blocks,
                      [B_qTh[0]], oT[:, hd, pb * 256:(pb + 1) * 256], [B_oT[0]])
        for ci in (() if "w" in KSKIP else (1, 2)):
            c0, n = TCH[ci]
            blocks = []
            for j in range(16):
                kc = 512 + j * 128
                blocks.append((kTh[:, kc:kc + 128], [B_kTh[1]], Vh[:, kc:kc + 128], [B_Vh[1]], None))
            for j in range(4):
                kc = 2560 + j * 128
                blocks.append((kTh[:, kc:kc + 128], [B_kTh[2]], Vh[:, kc:kc + 128], [B_Vh[2]], None))
            diff_pair(qTh[:, 0, c0:c0 + n], qTh[:, 1, c0:c0 + n], n, blocks, [B_qTh[1]],
                      oT[:, hd, c0:c0 + n], [B_oT[ci]])
        if hd < 5:
            mod_piece(0, 3 + hd)
        if STAGE >= 1:
            mod_piece(1, hd)
    flush_pending()
    if KHEADS < 5:
        for piece in range(3 + KHEADS, 8):
            mod_piece(0, piece)
    mod_finish(0, parts=(1, 2, 3))

    def out_proj_l0(l):
        P.alias([B_yA, B_yA2], B_hfT)
        ybs = [(yA, [B_yA]), (yA2, [B_yA2]), (yA, [B_yA])]
        for ci, (c0, n) in enumerate(TCH):
            pn = PostNorm(c0, n, [B_xT[ci]], l, 0, 0 if ci == 0 else 1, ybuf=ybs[ci])
            for half in range(2):
                si = load_w(wo0_d[half].rearrange("p k c -> p (k c)"), 4096)
                wv = slots[si][:, 0:4096].rearrange("p (k c) -> p k c", k=8)
                for dl in range(4):
                    bi = nbank()
                    linear_fm(bi, lambda k: wv[:, k, dl * 128:(dl + 1) * 128], lambda k: oT[:, k, c0:c0 + n], n,
                              [B_slot[si], B_oT[ci]])
                    pn.add(bi, half * 4 + dl)
            pn.finish()

    def ffn(l, chunks):
        for (c0, n, xb, hb, ab, c) in chunks:
            prenorm(xT, xb, c0, n, hT, [hb], c0, l, 1, c)
        for piece in range(8):
            nf = min(3, NF - 3 * piece)
            si = load_w(wgu_d[l, piece, :, 0:nf].rearrange("p f k c -> p (f k c)"), nf * 2048)
            wv = slots[si][:, 0:nf * 2048].rearrange("p (f k c) -> p f k c", f=nf, k=8)
            for fl in range(nf):
                f = 3 * piece + fl
                for (c0, n, xb, hb, ab, c) in chunks:
                    bg = nbank()
                    linear_fm(bg, lambda k: wv[:, fl, k, 0:128], lambda k: hT[:, k, c0:c0 + n], n, [B_slot[si], hb])
                    bu = nbank()
                    linear_fm(bu, lambda k: wv[:, fl, k, 128:256], lambda k: hT[:, k, c0:c0 + n], n, [B_slot[si], hb])
                    t = nxt("tmp", 4)
                    act(tmps[t][:, 0:n], banks[bg][:, 0:n], AF.Silu, [Bbank[bg]], [B_tmp[t]])
                    dve_tt(aT[:, f, c0:c0 + n], tmps[t][:, 0:n], banks[bu][:, 0:n], ALU.mult,
                           [B_tmp[t], Bbank[bu]], [ab])
        for (c0, n, xb, hb, ab, c) in chunks:
            pn = PostNorm(c0, n, xb, l, 1, c)
            for j in range(4):
                si = load_w(wdn_d[l, j].rearrange("p d f c -> p (d f c)"), 5632)
                wv = slots[si][:, 0:5632].rearrange("p (d f c) -> p d f c", d=2, f=22)
                for dl in range(2):
                    bi = nbank()
                    for f in range(NF):
                        mm(bi, banks[bi][:, 0:n], wv[:, dl, f, :], aT[:, f, c0:c0 + n], f == 0, f == NF - 1,
                           [B_slot[si], ab])
                    pn.add(bi, 2 * j + dl)
            pn.finish()

    if STAGE >= -1:
        out_proj_l0(0)
    if STAGE >= 1:
        mod_finish(1)
    P.alias(B_aT, B_hfT + B_oT + [B_yA, B_yA2])
    if STAGE >= 0:
        ffn(0, [(TCH[i][0], TCH[i][1], [B_xT[i]], B_hT[i], B_aT[i], 0 if i == 0 else 1) for i in range(3)])

    if STAGE >= 2:
        P.alias(B_qTs + [B_kTs] + B_oS, B_aT)
        for ci, (c0, n) in enumerate(TCH):
            prenorm(xT, [B_xT[ci]], c0, n, hT, [B_hT[ci]], c0, 1, 0, 0 if ci == 0 else 1)
        OWN0 = 640
        B_hown = [B_hT[1], B_hT[2]]
        P.op("dve", lambda e: e.memset(kTs[:, :, :], 0.0), reads=[], writes=[B_kTs])
        VS = Vh[:, 0:3584].rearrange("p (t c) -> p t c", t=14)
        B_VS = Buf("VS")
        P.alias([B_VS], list(B_Vh))
        for half in range(2):
            sa = load_w(ws1q_d[half].rearrange("p k c -> p (k c)"), 4096)
            sr = load_w(ws1q_d[2 + half].rearrange("p k c -> p (k c)"), 4096)
            wa = slots[sa][:, 0:4096].rearrange("p (k c) -> p k c", k=8)
            wr = slots[sr][:, 0:4096].rearrange("p (k c) -> p k c", k=8)
            for cl in range(4):
                cq = half * 4 + cl
                bi = nbank()
                linear_fm(bi, lambda k: wa[:, k, cl * 128:(cl + 1) * 128], lambda k: hT[:, k, 0:512], 512,
                          [B_slot[sa], B_hT[0]])
                act(qTs[:, cq, 0:512], banks[bi][:, :], AF.Copy, [Bbank[bi]], [B_qTs[0]])
                ba = nbank()
                linear_fm(ba, lambda k: wa[:, k, cl * 128:(cl + 1) * 128], lambda k: hT[:, k, OWN0:OWN0 + 512], 512,
                          [B_slot[sa]] + B_hown)
                bb = nbank()
                linear_fm(bb, lambda k: wr[:, k, cl * 128:(cl + 1) * 128], lambda k: hT[:, k, OWN0:OWN0 + 512], 512,
                          [B_slot[sr]] + B_hown)
                rope_epilogue(ba, bb, 512, ropeW[:, 0, 128:640], ropeW[:, 1, 128:640], qTs[:, cq, 512:1024],
                              [B_qTs[1]])
        sk = load_w(ws1k_d.rearrange("p k c -> p (k c)"), 6144)
        wk = slots[sk][:, 0:6144].rearrange("p (k c) -> p k c", k=8)
        rsk = [B_slot[sk]]
        for p in range(2):
            bi = nbank()
            linear_fm(bi, lambda k: wk[:, k, p * 128:(p + 1) * 128], lambda k: hT[:, k, 0:512], 512, rsk + [B_hT[0]])
            act(kTs[0:64, 2 * p, 0:512], banks[bi][0:64, :], AF.Copy, [Bbank[bi]], [B_kTs])
            act(kTs[64:128, 2 * p + 1, 0:512], banks[bi][64:128, :], AF.Copy, [Bbank[bi]], [B_kTs])
            for ci in (1, 2):
                c0, n = TCH[ci]
                ba = nbank()
                linear_fm(ba, lambda k: wk[:, k, p * 128:(p + 1) * 128], lambda k: hT[:, k, c0:c0 + n], n,
                          rsk + [B_hT[ci]])
                bb = nbank()
                linear_fm(bb, lambda k: wk[:, k, 256 + p * 128:256 + (p + 1) * 128], lambda k: hT[:, k, c0:c0 + n], n,
                          rsk + [B_hT[ci]])
                rope_epilogue(ba, bb, n, ropeW[:, 0, c0 - 512:c0 - 512 + n], ropeW[:, 1, c0 - 512:c0 - 512 + n],
                              kTs[0:64, 2 * p, c0:c0 + n], [B_kTs], out_hi=kTs[64:128, 2 * p + 1, c0:c0 + n])
        for t in range(4):
            bi = nbank()
            for k in range(KC):
                mm(bi, banks[bi][:, 0:256], hT[:, k, t * 128:(t + 1) * 128], wk[:, k, 0:256], k == 0, k == KC - 1,
                   rsk + [B_hT[0]])
            for k in range(KC):
                mm(bi, banks[bi][:, 256:512], hT[:, k, t * 128:(t + 1) * 128], wk[:, k, 512:768], k == 0, k == KC - 1,
                   rsk + [B_hT[0]])
            oi = nxt("ost", 2)
            act(ostage[oi][:, 0:512], banks[bi][:, :], AF.Copy, [Bbank[bi]], [B_ost[oi]])
            dve_copy(VS[:, t, :], banks[bi][:, 256:512], [Bbank[bi]], [B_VS])
            dma_out(nsk_d[t * 128:(t + 1) * 128, :], ostage[oi][:, 0:256], B_ost[oi])
            dma_out(nsv_d[t * 128:(t + 1) * 128, :], ostage[oi][:, 256:512], B_ost[oi])
        for t in range(6):
            bi = nbank()
            c0 = 512 + t * 128
            for k in range(KC):
                mm(bi, banks[bi][:, 0:256], hT[:, k, c0:c0 + 128], wk[:, k, 512:768], k == 0, k == KC - 1,
                   rsk + [B_hT[1] if t < 4 else B_hT[2]])
            dve_copy(VS[:, 4 + t, :], banks[bi][:, 0:256], [Bbank[bi]], [B_VS])
        dma_in("sp", cst[:, :].rearrange("p (j c) -> p j c", j=4), cks_d, B_cst)
        for p in range(2):
            bi = nbank()
            for j in range(4):
                P.op("pe", lambda e, bi=bi, j=j, p=p: e.transpose(
                    banks[bi][:, j * 128:(j + 1) * 128], cst[:, j * 256 + p * 128:j * 256 + (p + 1) * 128], ident[:, :]),
                    reads=[B_cst, B_const], writes=[Bbank[bi]])
            act(kTs[0:64, 2 * p, 1280:1792], banks[bi][0:64, :], AF.Copy, [Bbank[bi]], [B_kTs])
            act(kTs[64:128, 2 * p + 1, 1280:1792], banks[bi][64:128, :], AF.Copy, [Bbank[bi]], [B_kTs])
        xi = nxt("xin", 2)
        dma_in("sp", xin[xi][:, :], cvs_d.rearrange("p j c -> p (j c)"), B_xin[xi])
        dve_copy(Vh[:, 2560:3584], xin[xi][:, :], [B_xin[xi]], [B_VS])

        def swa_finalize(bo, bl, ng, qn, kv, g0, oS_col0, obuf):
            r0 = (kv % 2) * 64
            pr = kv // 2
            n = ng * qn
            hh0 = kv * 4 + g0
            t = nxt("tmp", 4)
            dve_copy(tmps[t][r0:r0 + 64, 0:n], banks[bl][r0:r0 + 64, 0:n], [Bbank[bl]], [B_tmp[t]])
            to = nxt("tmp", 4)
            dve_copy(tmps[to][r0:r0 + 64, 0:n], banks[bo][r0:r0 + 64, 0:n], [Bbank[bo]], [B_tmp[to]])
            lv = tmps[t][r0:r0 + 64, 0:n].rearrange("p (g q) -> p g q", g=ng)
            sk = sinkE[r0:r0 + 64, hh0:hh0 + ng]
            sk_b = bass.AP(sk.tensor, sk.offset, [list(sk.ap[0]), list(sk.ap[1]), [0, qn]])
            dve_tt(lv, lv, sk_b, ALU.add, [B_tmp[t], B_lam], [B_tmp[t]])
            act(tmps[t][r0:r0 + 64, 0:n], tmps[t][r0:r0 + 64, 0:n], AF.Ln, [B_tmp[t]], [B_tmp[t]])
            act(tmps[t][r0:r0 + 64, 0:n], tmps[t][r0:r0 + 64, 0:n], AF.Exp, [B_tmp[t]], [B_tmp[t]], scale=-1.0)
            ov = tmps[to][r0:r0 + 64, 0:n].rearrange("p (g q) -> p g q", g=ng)
            dve_tt(oS[r0:r0 + 64, pr * 4 + g0:pr * 4 + g0 + ng, oS_col0:oS_col0 + qn], ov, lv, ALU.mult,
                   [B_tmp[to], B_tmp[t]], [obuf], eng="pool")

        def run_swa(joblist):
            for i in range(0, len(joblist), 2):
                grp = joblist[i:i + 2]
                par = run_pair([g[0] for g in grp])
                for slot, g in enumerate(grp):
                    swa_finalize(obank(slot, par), LBK[slot], *g[1])

        jl = []
        for pb in range(2):
            for kv in range(4):
                p = kv // 2
                for gh in range(2):
                    q_ap = qTs[:, p * 4 + gh * 2:p * 4 + gh * 2 + 2, pb * 256:(pb + 1) * 256]
                    blocks = []
                    for j in range(2):
                        kc = pb * 256 + j * 128
                        blocks.append((kTs[:, kv, kc:kc + 128], [B_kTs],
                                       VS[:, pb * 2 + j, p * 128:(p + 1) * 128], [B_VS], None))
                    jl.append(((q_ap, 512, blocks, [B_qTs[0]]), (2, 256, kv, gh * 2, pb * 256, B_oS[0])))
        for qb in range(4):
            for kv in range(4):
                p = kv // 2
                q_ap = qTs[:, p * 4:p * 4 + 4, 512 + qb * 128:512 + (qb + 1) * 128]
                blocks = []
                for j in range(4):
                    kc = 1280 + j * 128
                    blocks.append((kTs[:, kv, kc:kc + 128], [B_kTs],
                                   VS[:, 10 + j, p * 128:(p + 1) * 128], [B_VS], None))
                for dj, mi in ((0, 2 * qb), (1, None), (2, 2 * qb + 1)):
                    w = qb + dj
                    kc = 512 + w * 128
                    blocks.append((kTs[:, kv, kc:kc + 128], [B_kTs],
                                   VS[:, 4 + w, p * 128:(p + 1) * 128], [B_VS], mi))
                jl.append(((q_ap, 512, blocks, [B_qTs[1]]), (4, 128, kv, 0, 512 + qb * 128, B_oS[1])))
        run_swa(jl)
        L1CH = [(0, 512, [B_xT[0]], 0, 0, B_oS[0]), (OWN0, 512, [B_xT[1], B_xT[2]], 1, 512, B_oS[1])]
        P.alias([B_yA], B_qTs)
        for (c0, n, xb, c, oc0, ob) in L1CH:
            pn = PostNorm(c0, n, xb, 1, 0, c, ybuf=(yA, [B_yA]) if c0 == 0 else None)
            for half in range(2):
                si = load_w(wo1_d[half].rearrange("p k c -> p (k c)"), 4096)
                wv = slots[si][:, 0:4096].rearrange("p (k c) -> p k c", k=8)
                for dl in range(4):
                    bi = nbank()
                    linear_fm(bi, lambda k: wv[:, k, dl * 128:(dl + 1) * 128], lambda k: oS[:, k, oc0:oc0 + n], n,
                              [B_slot[si], ob])
                    pn.add(bi, half * 4 + dl)
            pn.finish()
        P.alias(B_aT, B_qTs + [B_kTs] + B_oS + [B_yA])
        if STAGE >= 3:
            ffn(1, [(0, 512, [B_xT[0]], B_hT[0], B_aT[0], 0),
                    (OWN0, 512, [B_xT[1], B_xT[2]], B_hT[1], B_aT[1], 1)])

    OWN0 = 640
    for t in range(8):
        c0 = t * 128 if t < 4 else OWN0 + (t - 4) * 128
        xb = [B_xT[0]] if t < 4 else [B_xT[1], B_xT[2]]
        xi = nxt("xin", 2)
        for half in range(2):
            bi = nbank()
            for j in range(4):
                cidx = half * 4 + j
                P.op("pe", lambda e, bi=bi, j=j, cidx=cidx, c0=c0: e.transpose(
                    banks[bi][:, j * 128:(j + 1) * 128], xT[:, cidx, c0:c0 + 128], ident[:, :]),
                    reads=xb + [B_const], writes=[Bbank[bi]])
            if half == 0:
                act(xin[xi][:, 0:512], banks[bi][:, :], AF.Copy, [Bbank[bi]], [B_xin[xi]])
            else:
                dve_copy(xin[xi][:, 512:1024], banks[bi][:, :], [Bbank[bi]], [B_xin[xi]])
        dst = yp_d[t * 128:(t + 1) * 128, :] if t < 4 else ys_d[(t - 4) * 128:(t - 3) * 128, :]
        dma_out(dst, xin[xi][:, :], B_xin[xi])

    P.finish()
    stats = P.emit()
    return nc, es, stats


def _rope_tables(pos):
    pos = np.asarray(pos)
    row = (pos // 64).astype(np.float32)
    col = (pos % 64).astype(np.float32)
    nf = 16
    inv = (np.float32(10000.0) ** (-np.arange(nf, dtype=np.float32) / np.float32(nf))).astype(np.float32)
    ar = row[:, None] * inv[None, :]
    ac = col[:, None] * inv[None, :]
    ang = np.concatenate([ar, ar, ac, ac], axis=-1).astype(np.float32)
    cos = np.cos(ang).astype(np.float32)
    sin = np.sin(ang).astype(np.float32)
    sign = np.concatenate([-np.ones(16), np.ones(16), -np.ones(16), np.ones(16)]).astype(np.float32)
    sin = sin * sign[None, :]
    cos2 = np.concatenate([cos, cos], axis=1).T
    sin2 = np.concatenate([sin, sin], axis=1).T
    return np.ascontiguousarray(cos2), np.ascontiguousarray(sin2)


_ROTSRC = np.concatenate([np.arange(16, 32), np.arange(0, 16), np.arange(48, 64), np.arange(32, 48)])


def _rot_cols(w):
    n = w.shape[1] // 64
    idx = (np.arange(n)[:, None] * 64 + _ROTSRC[None, :]).reshape(-1)
    return w[:, idx]


def _kmaj(w):
    return np.ascontiguousarray(w.reshape(8, 128, -1).transpose(1, 0, 2))


_PROG_CACHE = {}


def prep_inputs(x_prompt, x_sample, cache_diff_k, cache_diff_v, cache_swa_k, cache_swa_v, c, c_ctx,
                w_mod, b_mod, norm_g, w_qkv_diff, diff_lambda, diff_subln_g, w_o_diff,
                w_qkv_swa, swa_sink, w_o_swa, w_gate, w_up, w_down):
    f = np.float32
    A = lambda a: np.ascontiguousarray(np.asarray(a, dtype=f))
    x_prompt, x_sample = A(x_prompt), A(x_sample)
    cache_diff_k, cache_diff_v = A(cache_diff_k), A(cache_diff_v)
    cache_swa_k, cache_swa_v = A(cache_swa_k), A(cache_swa_v)
    c, c_ctx = A(c), A(c_ctx)
    w_mod, b_mod, norm_g = A(w_mod), A(b_mod), A(norm_g)
    w_qkv_diff, diff_lambda, diff_subln_g, w_o_diff = A(w_qkv_diff), A(diff_lambda), A(diff_subln_g), A(w_o_diff)
    w_qkv_swa, swa_sink, w_o_swa = A(w_qkv_swa), A(swa_sink), A(w_o_swa)
    w_gate, w_up, w_down = A(w_gate), A(w_up), A(w_down)

    wmod = np.ascontiguousarray(
        w_mod.reshape(2, 8, 128, 8, 768).transpose(0, 3, 2, 1, 4))
    wq, wk, wv = w_qkv_diff[0][:, 0:1024], w_qkv_diff[0][:, 1024:2048], w_qkv_diff[0][:, 2048:3072]
    wqr, wkr = _rot_cols(wq), _rot_cols(wk)
    wd0 = np.empty((8, 128, 8, 640), f)
    for h in range(8):
        s = slice(h * 128, (h + 1) * 128)
        wd0[h] = _kmaj(np.concatenate([wq[:, s], wqr[:, s], wkr[:, s], wk[:, s], wv[:, s]], axis=1))
    wo0 = np.stack([_kmaj(w_o_diff[0][:, 0:512]), _kmaj(w_o_diff[0][:, 512:1024])])
    ws = w_qkv_swa[0]
    sq_cols = []
    for p in range(2):
        for g in range(4):
            for kvl in range(2):
                hh = (2 * p + kvl) * 4 + g
                sq_cols.append(np.arange(hh * 64, (hh + 1) * 64))
    sq_cols = np.concatenate(sq_cols)
    wsq = ws[:, 0:1024]
    wsq_p = wsq[:, sq_cols]
    wsqr_p = _rot_cols(wsq)[:, sq_cols]
    ws1q = np.stack([_kmaj(wsq_p[:, 0:512]), _kmaj(wsq_p[:, 512:1024]),
                     _kmaj(wsqr_p[:, 0:512]), _kmaj(wsqr_p[:, 512:1024])])
    wsk, wsv = ws[:, 1024:1280], ws[:, 1280:1536]
    ws1k = _kmaj(np.concatenate([wsk, _rot_cols(wsk), wsv], axis=1))
    wos_p = w_o_swa[0][sq_cols, :]
    wo1 = np.stack([_kmaj(wos_p[:, 0:512]), _kmaj(wos_p[:, 512:1024])])
    wgu_f = np.zeros((2, 24, 128, 8, 256), f)
    for l in range(2):
        g_ = w_gate[l].reshape(8, 128, 22, 128).transpose(2, 1, 0, 3)
        u_ = w_up[l].reshape(8, 128, 22, 128).transpose(2, 1, 0, 3)
        wgu_f[l, 0:22, :, :, 0:128] = g_
        wgu_f[l, 0:22, :, :, 128:256] = u_
    wgu = np.ascontiguousarray(wgu_f.reshape(2, 8, 3, 128, 8, 256).transpose(0, 1, 3, 2, 4, 5))
    wdn = np.ascontiguousarray(w_down.reshape(2, 22, 128, 4, 2, 128).transpose(0, 3, 2, 4, 1, 5))
    ident = np.eye(128, dtype=f)
    cosf, sinf = _rope_tables(np.arange(TFULL))
    ropef = np.ascontiguousarray(np.stack([cosf, sinf], axis=1))

    normg_l = norm_g.reshape(8, 8, 128).transpose(2, 0, 1).reshape(128, 64)
    bmod_l = b_mod.reshape(2, 48, 128).transpose(2, 0, 1).reshape(128, 96)
    subg_l = diff_subln_g.reshape(128, 1)
    lam_l = np.broadcast_to(diff_lambda.reshape(1, 256), (128, 256))
    sink_l = np.broadcast_to(swa_sink.reshape(1, 16), (128, 16))

    in_maps = []
    for core in range(NCORES):
        b, ch = core // 4, core % 4
        xp = x_prompt[2 * core:2 * core + 2].reshape(512, D)
        pos = np.arange(ch * 512 - 128, ch * 512 + 640)
        valid = (pos >= 0) & (pos < 2048)
        xw = np.zeros((TW, D), f)
        xw[valid] = x_sample[b, pos[valid]]
        cosw, sinw = _rope_tables(np.clip(pos, 0, 2047))
        ropew = np.ascontiguousarray(np.stack([cosw, sinw], axis=1))
        maskb = np.zeros((128, 8, 128), f)
        for qb in range(4):
            for side in range(2):
                w = qb + (0 if side == 0 else 2)
                kpos = pos[w * 128:(w + 1) * 128]
                qpos = pos[(qb + 1) * 128:(qb + 2) * 128]
                ok = ((kpos[:, None] >= 0) & (kpos[:, None] < 2048)
                      & (np.abs(qpos[None, :] - kpos[:, None]) <= 128))
                maskb[:, 2 * qb + side, :] = np.where(ok, 0.0, -30000.0)
        cond_l = np.stack([c_ctx, c[b]], axis=-1).reshape(8, 128, 2).transpose(1, 0, 2).reshape(128, 16)
        sm = np.ascontiguousarray(np.concatenate([cond_l, normg_l, bmod_l, subg_l, lam_l, sink_l], axis=1), dtype=f)
        ckd = np.ascontiguousarray(cache_diff_k[b, 0].reshape(4, 128, 8, 128).transpose(2, 1, 0, 3))
        cvd = np.ascontiguousarray(cache_diff_v[b, 0].reshape(4, 128, 8, 128).transpose(2, 1, 0, 3))
        cks = np.ascontiguousarray(cache_swa_k[b, 0].reshape(4, 128, 256).transpose(1, 0, 2))
        cvs = np.ascontiguousarray(cache_swa_v[b, 0].reshape(4, 128, 256).transpose(1, 0, 2))
        in_maps.append(dict(xp=np.ascontiguousarray(xp), xw=xw, xf=x_sample[b], ckd=ckd, cvd=cvd, cks=cks, cvs=cvs,
                            sm=sm, ident=ident, ropew=ropew, ropef=ropef, maskb=maskb, wmod=wmod, wd0=wd0, wo0=wo0,
                            ws1q=ws1q, ws1k=ws1k, wo1=wo1, wgu=wgu, wdn=wdn))
    return in_maps


def kernel(**inputs):
    f = np.float32
    in_maps = prep_inputs(**inputs)
    if "nc" not in _PROG_CACHE:
        nc, es, stats = build_program()
        _PROG_CACHE["nc"] = (nc, es)
        if os.environ.get("KVERBOSE"):
            print("ops per engine:", stats)
    nc, _ = _PROG_CACHE["nc"]
    res = run_bass_kernel_spmd(nc, in_maps, core_ids=list(range(NCORES)))
    R = res.results
    y_prompt = np.concatenate([R[i]["yp"].reshape(2, 256, D) for i in range(NCORES)], axis=0)
    y_sample = np.stack([np.concatenate([R[b * 4 + ch]["ys"] for ch in range(4)], axis=0) for b in range(2)], axis=0)
    ndk = np.concatenate([R[i]["ndk"].reshape(2, 1, 256, 8, 128) for i in range(NCORES)], axis=0)
    ndv = np.concatenate([R[i]["ndv"].reshape(2, 1, 256, 8, 128) for i in range(NCORES)], axis=0)
    nsk = np.concatenate([R[i]["nsk"].reshape(2, 1, 256, 4, 64) for i in range(NCORES)], axis=0)
    nsv = np.concatenate([R[i]["nsv"].reshape(2, 1, 256, 4, 64) for i in range(NCORES)], axis=0)
    return (y_prompt.astype(f), y_sample.astype(f), ndk.astype(f), ndv.astype(f), nsk.astype(f), nsv.astype(f))
```

```python
import os
import numpy as np
import concourse.bass as bass
import concourse.mybir as mybir
from concourse.bass_utils import run_bass_kernel_spmd
from contextlib import ExitStack

F32 = mybir.dt.float32
BF16 = mybir.dt.bfloat16
AF = mybir.ActivationFunctionType
ALU = mybir.AluOpType

D = 1024
KC = 8
DFF = 2816
NF = 22
TP = 512
TW = 768
TT = 1280
TFULL = 2048
LC = 512
EPS = 1e-6
NCORES = 8
STAGE = int(os.environ.get("KSTAGE", "99"))
KHEADS = int(os.environ.get("KHEADS", "8"))
KSKIP = os.environ.get("KSKIP", "")


class Buf:
    __slots__ = ("name", "writers", "readers", "dsem", "ndma", "war", "excl")

    def __init__(self, name, excl=False):
        self.name = name
        self.excl = excl
        self.writers = []
        self.readers = []
        self.war = []
        self.dsem = None
        self.ndma = 0


class Op:
    __slots__ = ("eng", "idx", "fn", "deps", "dma", "buf", "ordinal", "flag", "count", "waits", "ring")

    def __init__(self, eng, idx, fn):
        self.eng = eng
        self.idx = idx
        self.fn = fn
        self.deps = []
        self.dma = False
        self.buf = None
        self.ordinal = 0
        self.flag = False
        self.count = 0
        self.waits = []
        self.ring = False


ENGS = ["pe", "act", "dve", "pool", "sp"]


class Prog:
    def __init__(self, nc, es):
        self.nc = nc
        self.es = es
        self.ops = {e: [] for e in ENGS}
        self.dma_bufs = []
        self.out_dmas = []

    def op(self, eng, fn, reads=(), writes=(), dma_buf=None, is_out=False, ring=False):
        o = Op(eng, len(self.ops[eng]), fn)
        o.ring = ring
        deps = o.deps
        for b in reads:
            for w in b.writers:
                deps.append((w, True))
            if b.excl:
                for r in b.readers:
                    if r.eng != eng:
                        deps.append((r, False))
            b.readers.append(o)
        for b in writes:
            if b.readers:
                b.war = [r for r in b.readers if r is not o]
                b.writers = [o]
                b.readers = []
            else:
                b.writers.append(o)
            for r in b.war:
                deps.append((r, False))
        if dma_buf is not None:
            o.dma = True
            o.buf = dma_buf
            if dma_buf.dsem is None:
                self.dma_bufs.append(dma_buf)
                dma_buf.dsem = True
            dma_buf.ndma += 1
            o.ordinal = dma_buf.ndma
            if is_out:
                self.out_dmas.append(o)
        self.ops[eng].append(o)
        return o

    def alias(self, new_bufs, old_bufs):
        pend = []
        for b in old_bufs:
            pend.extend(b.readers)
            pend.extend(b.writers)
        for nb in new_bufs:
            nb.readers.extend(pend)
            nb.war = []

    def finish(self):
        o = Op("sp", len(self.ops["sp"]), None)
        for d in self.out_dmas:
            o.deps.append((d, True))
        self.ops["sp"].append(o)

    def emit(self):
        nc = self.nc
        es = self.es
        esem = {e: es.enter_context(nc.semaphore("s_" + e)) for e in ["pe", "act", "dve", "pool"]}
        for i, b in enumerate(self.dma_bufs):
            b.dsem = es.enter_context(nc.semaphore("d%d" % i))
        for e in ENGS:
            waited = {}
            for o in self.ops[e]:
                need = {}
                for (d, raw) in o.deps:
                    if d.dma:
                        key = ("d", id(d.buf))
                        val = d.ordinal
                        if waited.get(key, 0) >= val:
                            continue
                        if need.get(key, (0, None))[0] < val:
                            need[key] = (val, d)
                    else:
                        if d.eng == e and e == "pe":
                            continue
                        key = ("e", d.eng)
                        val = d.idx + 1
                        if waited.get(key, 0) >= val:
                            continue
                        if need.get(key, (0, None))[0] < val:
                            need[key] = (val, d)
                for key, (val, d) in need.items():
                    waited[key] = val
                    if not d.dma:
                        d.flag = True
                    o.waits.append(d)
        for e in ["pe", "act", "dve", "pool"]:
            c = 0
            for o in self.ops[e]:
                if o.flag and not o.dma:
                    c += 1
                    o.count = c
        handles = {"pe": "tensor", "act": "scalar", "dve": "vector", "pool": "gpsimd", "sp": "sync"}
        stats = {}
        with nc.Block() as block:
            for e in ENGS:
                ops = self.ops[e]
                stats[e] = len(ops)

                def body(eng, ops=ops, e=e):
                    for o in ops:
                        for d in o.waits:
                            if d.dma:
                                eng.wait_ge(d.buf.dsem, 16 * d.ordinal)
                            else:
                                eng.wait_ge(esem[d.eng], d.count)
                        if o.fn is None:
                            continue
                        inst = o.fn(eng)
                        if o.ring:
                            assert not o.flag
                            inst.then_inc(self.ring_sem, 16)
                        elif o.dma:
                            inst.then_inc(o.buf.dsem, 16)
                        elif o.flag:
                            inst.then_inc(esem[e], 1)

                getattr(block, handles[e])(body)
        return stats


def build_program():
    nc = bass.Bass("TRN2", target_bir_lowering=False, monotonic_sem_count=0)
    es = ExitStack()
    P = Prog(nc, es)

    def din(name, shape):
        return nc.dram_tensor(name, list(shape), F32, kind="ExternalInput").ap()

    def dout(name, shape):
        return nc.dram_tensor(name, list(shape), F32, kind="ExternalOutput").ap()

    xp_d = din("xp", [TP, D])
    xw_d = din("xw", [TW, D])
    xf_d = din("xf", [TFULL, D])
    ckd_d = din("ckd", [8, 128, 4, 128])
    cvd_d = din("cvd", [8, 128, 4, 128])
    cks_d = din("cks", [128, 4, 256])
    cvs_d = din("cvs", [128, 4, 256])
    NSM = 16 + 64 + 96 + 1 + 256 + 16
    sm_d = din("sm", [128, NSM])
    ident_d = din("ident", [128, 128])
    ropew_d = din("ropew", [128, 2, TW])
    ropef_d = din("ropef", [128, 2, TFULL])
    maskb_d = din("maskb", [128, 8, 128])
    wmod_d = din("wmod", [2, 8, 128, 8, 768])
    wd0_d = din("wd0", [8, 128, 8, 384])
    permT_d = din("permT", [128, 128])
    wo0_d = din("wo0", [2, 128, 8, 512])
    ws1q_d = din("ws1q", [2, 128, 8, 512])
    ws1k_d = din("ws1k", [128, 8, 512])
    wo1_d = din("wo1", [2, 128, 8, 512])
    wgu_d = din("wgu", [2, 8, 128, 3, 8, 256])
    wdn_d = din("wdn", [2, 4, 128, 2, 22, 128])

    yp_d = dout("yp", [TP, D])
    ys_d = dout("ys", [512, D])
    ndk_d = dout("ndk", [TP, 1024])
    ndv_d = dout("ndv", [TP, 1024])
    nsk_d = dout("nsk", [TP, 256])
    nsv_d = dout("nsv", [TP, 256])

    def sb(name, shape, dt):
        return es.enter_context(nc.sbuf_tensor(name, list(shape), dt))

    xT = sb("xT", [128, KC, TT], F32)
    hT = sb("hT", [128, KC, TT], BF16)
    ytmp = hT.bitcast(F32)
    BIG = sb("BIG", [128, 28160], BF16)
    slots = [sb("wslot%d" % i, [128, 6144], BF16) for i in range(2)]
    kTh = sb("kTh", [128, 3072], BF16)
    Vh = sb("Vh", [128, 3584], BF16)
    qTh = sb("qTh", [128, 2, TT], BF16)
    Es = [sb("E%d" % i, [128, 2, 512], BF16) for i in range(2)]
    xin = [sb("xin%d" % i, [128, 1024], F32) for i in range(2)]
    cst = sb("cst", [128, 1024], F32)
    ropeW = sb("ropeW", [128, 2, TW], F32)
    ropeF = sb("ropeF", [128, 2, TFULL], BF16)
    maskb = sb("maskbs", [128, 8, 128], BF16)
    ident = sb("idents", [128, 128], F32)
    identb = sb("identb", [128, 128], BF16)
    ones = sb("ones", [128, 128], BF16)
    permb = sb("permb", [128, 128], BF16)
    sm = sb("sms", [128, NSM], F32)
    modT = sb("modT", [128, 2, 48, 2], F32)
    der = sb("der", [128, 2, 4, 8, 2], F32)
    scb = sb("scb", [128, 8, 2], BF16)
    lamt = sb("lamt", [128, 8], F32)
    sinkE = sb("sinkE", [128, 16], F32)
    sqs = [sb("sq%d" % i, [128, 512], BF16) for i in range(2)]
    tmps = [sb("tmp%d" % i, [128, 512], F32) for i in range(4)]
    T1s = [sb("T1a", [128, 512], F32), sb("T1b", [128, 512], F32)]
    ostage = [xin[i] for i in range(2)]

    PS = es.enter_context(nc.psum_tensor("PS", [128, 8, 512], F32))

    class BankView:
        def __init__(self, i):
            self.i = i

        def __getitem__(self, idx):
            return PS[idx[0], self.i, idx[1]]

    banks = [BankView(i) for i in range(8)]
    Bbank = [Buf("bank%d" % i, excl=True) for i in range(8)]

    B_xT = [Buf("xT_p"), Buf("xT_w0"), Buf("xT_w1")]
    B_hT = [Buf("hT_p"), Buf("hT_w0"), Buf("hT_w1")]
    B_slot = [Buf("slot0"), Buf("slot1")]
    B_kTh = Buf("kTh_p"), Buf("kTh_f"), Buf("kTh_c")
    B_Vh = Buf("Vh_p"), Buf("Vh_f"), Buf("Vh_c")
    B_qTh = [Buf("qTh_p"), Buf("qTh_w")]
    B_E = [Buf("E%d" % i) for i in range(2)]
    B_xin = [Buf("xin0"), Buf("xin1")]
    B_cst = Buf("cst")
    B_cstv = Buf("cstv")
    B_const = Buf("consts")
    B_sm = Buf("sm")
    B_mod = [Buf("mod0"), Buf("mod1")]
    B_der = [Buf("der0"), Buf("der1")]
    B_scb = Buf("scb")
    B_lam = Buf("lam")
    B_sq = [Buf("sq0"), Buf("sq1")]
    B_tmp = [Buf("tmp%d" % i) for i in range(4)]
    B_T1s = [Buf("T1a"), Buf("T1b")]
    B_ost = B_xin
    B_hfT = [Buf("hfT%d" % i) for i in range(4)]
    B_oT = [Buf("oT_p"), Buf("oT_w0"), Buf("oT_w1")]
    B_aT = [Buf("aT0"), Buf("aT1"), Buf("aT2")]
    B_qTs = [Buf("qTs_p"), Buf("qTs_s")]
    B_kTs = Buf("kTs")
    B_oS = [Buf("oS_p"), Buf("oS_s")]

    hfT = BIG[:, 0:16384].rearrange("p (k t) -> p k t", k=8)
    oT = BIG[:, 16384:16384 + 10240].rearrange("p (k t) -> p k t", k=8)
    aT = BIG[:, 0:28160].rearrange("p (f t) -> p f t", f=22)
    qTs = BIG[:, 0:8192].rearrange("p (k t) -> p k t", k=8)
    kTs = BIG[:, 8192:8192 + 7168].rearrange("p (k t) -> p k t", k=4)
    oS = BIG[:, 15360:15360 + 8192].rearrange("p (h t) -> p h t", h=8)

    BIGf = BIG.bitcast(F32)
    yA = BIGf[:, 0:4096].rearrange("p (d t) -> p d t", d=8)
    yA2 = BIGf[:, 4096:8192].rearrange("p (d t) -> p d t", d=8)
    B_yA, B_yA2 = Buf("yA"), Buf("yA2")
    TCH = [(0, 512), (512, 512), (1024, 256)]

    rr = {"bank": 0, "tmp": 0, "sq": 0, "slot": 0, "xin": 0, "ost": 0, "E": 0}

    def nxt(kind, n):
        v = rr[kind]
        rr[kind] = (v + 1) % n
        return v

    reserved = set()

    def nbank():
        while True:
            b = nxt("bank", 8)
            if b not in reserved:
                return b

    open_grp = {}

    pe_work = {"v": 0.0}

    def mm(bi, out_ap, lhsT, rhs, start, stop, reads, wt=1.0):
        pe_work["v"] += wt
        if start and open_grp.get(bi):
            import traceback
            traceback.print_stack(limit=6)
            print("OPEN GROUP on bank", bi, "opened at:", open_grp[bi])
        if start:
            import traceback
            open_grp[bi] = "".join(traceback.format_stack(limit=5)[:-1])
        if stop:
            open_grp[bi] = None
        P.op("pe", lambda e: e.matmul(out_ap, lhsT, rhs, start=start, stop=stop),
             reads=reads, writes=[Bbank[bi]])

    def act(out_ap, in_ap, func, reads, writes, bias=None, scale=None):
        kw = {}
        if bias is not None:
            kw["bias"] = bias
        if scale is not None:
            kw["scale"] = scale
        P.op("act", lambda e: e.activation(out_ap, in_ap, func, **kw), reads=reads, writes=writes)

    def dve_tt(out_ap, a, b, op, reads, writes, eng="dve"):
        P.op(eng, lambda e: e.tensor_tensor(out_ap, a, b, op), reads=reads, writes=writes)

    def dve_stt(out_ap, a, scalar, b, op0, op1, reads, writes):
        P.op("dve", lambda e: e.scalar_tensor_tensor(out_ap, a, scalar, b, op0, op1), reads=reads, writes=writes)

    def dve_ts(out_ap, a, s1, s2, op0, op1, reads, writes, eng="dve"):
        if op1 is None:
            P.op(eng, lambda e: e.tensor_scalar(out_ap, a, s1, None, op0), reads=reads, writes=writes)
        else:
            P.op(eng, lambda e: e.tensor_scalar(out_ap, a, s1, s2, op0, op1), reads=reads, writes=writes)

    def dve_copy(out_ap, in_ap, reads, writes, eng="dve"):
        P.op(eng, lambda e: e.tensor_copy(out_ap, in_ap), reads=reads, writes=writes)

    def dve_recip(out_ap, in_ap, reads, writes):
        P.op("dve", lambda e: e.reciprocal(out_ap, in_ap), reads=reads, writes=writes)

    def dma_in(eng, out_ap, in_ap, buf, extra_writes=(), cast=False):
        if eng == "pool":
            P.op("pool", lambda e: e.dma_start(out=out_ap, in_=in_ap, max_dma_last_dim=4096),
                 reads=[], writes=[buf] + list(extra_writes), dma_buf=Buf("sw"))
        elif cast:
            P.op(eng, lambda e: e.dma_start(out=out_ap, in_=in_ap, max_dma_last_dim=4096),
                 reads=[], writes=[buf] + list(extra_writes), dma_buf=buf)
        else:
            P.op(eng, lambda e: e.dma_start(out=out_ap, in_=in_ap),
                 reads=[], writes=[buf] + list(extra_writes), dma_buf=buf)

    def dma_out(out_ap, in_ap, buf):
        P.op("sp", lambda e: e.dma_start(out=out_ap, in_=in_ap), reads=[buf], writes=[], dma_buf=buf, is_out=True)

    def load_w(srcs, nelem, view=None, parts=128):
        si = nxt("slot", 2)
        if not isinstance(srcs, (list, tuple)):
            srcs = [srcs]
        for i, src_ap in enumerate(srcs):
            dst = slots[si][0:parts, i * nelem:(i + 1) * nelem]
            dma_in("pool", dst, src_ap, B_slot[si], cast=True)
        return si

    dma_in("sp", sm[:, :], sm_d, B_sm)
    dma_in("sp", ident[:, :], ident_d, B_const)
    dma_in("sp", ropeW[:, :, :], ropew_d, B_const)
    dma_in("pool", ropeF[:, :, :].rearrange("p a t -> p (a t)"), ropef_d.rearrange("p a t -> p (a t)"), B_const, cast=True)
    dma_in("pool", maskb[:, :, :].rearrange("p a t -> p (a t)"), maskb_d.rearrange("p a t -> p (a t)"), B_const, cast=True)
    P.op("dve", lambda e: e.memset(ones[:, :], 1.0), reads=[], writes=[B_const])
    dve_copy(identb[:, :], ident[:, :], [B_const], [B_const])
    P.op("dve", lambda e: e.memset(qTh[:, :, :], 0.0), reads=[], writes=[B_qTh[0], B_qTh[1]])
    dma_in("sp", cst[:, 0:128], permT_d, B_cst)
    dve_copy(permb[:, :], cst[:, 0:128], [B_cst, B_const], [B_const])

    O_COND, O_NG, O_BM, O_SUBG, O_LAM, O_SINK = 0, 16, 80, 176, 177, 433
    cond_v = sm[:, O_COND:O_COND + 16].rearrange("p (k c) -> p k c", c=2)
    act(scb[:, :, :], cond_v, AF.Silu, [B_sm], [B_scb])
    LAM_INIT = 0.8 - 0.6 * float(np.exp(-0.3 * 0))
    lam_v = sm[:, O_LAM:O_LAM + 256].rearrange("p (a d) -> p a d", a=4)
    P.op("dve", lambda e: e.tensor_tensor(tmps[0][:, 0:64], lam_v[:, 0, :], lam_v[:, 1, :], ALU.mult),
         reads=[B_sm], writes=[B_tmp[0]])
    P.op("dve", lambda e: e.tensor_tensor(tmps[0][:, 64:128], lam_v[:, 2, :], lam_v[:, 3, :], ALU.mult),
         reads=[B_sm], writes=[B_tmp[0]])
    P.op("dve", lambda e: e.reduce_sum(lamt[:, 0:2], tmps[0][:, 0:128].rearrange("p (a d) -> p a d", a=2),
                                       mybir.AxisListType.X), reads=[B_tmp[0]], writes=[B_lam])
    act(lamt[:, 2:4], lamt[:, 0:2], AF.Exp, [B_lam], [B_lam])
    dve_tt(lamt[:, 4:5], lamt[:, 3:4], lamt[:, 2:3], ALU.subtract, [B_lam], [B_lam])
    dve_ts(lamt[:, 4:5], lamt[:, 4:5], -LAM_INIT, None, ALU.add, None, [B_lam], [B_lam])
    dve_ts(lamt[:, 5:6], sm[:, O_SUBG:O_SUBG + 1], 1.0 - LAM_INIT, None, ALU.mult, None, [B_sm, B_lam], [B_lam])
    act(sinkE[:, :], sm[:, O_SINK:O_SINK + 16], AF.Exp, [B_sm], [B_lam])

    def mod_piece(l, piece):
        bi = nbank()
        si = load_w(wmod_d[l, piece].rearrange("p k c -> p (k c)"), 6144)
        wv = slots[si][:, 0:6144].rearrange("p (k c) -> p k c", k=8)
        for oc in range(6):
            for k in range(KC):
                mm(bi, banks[bi][:, oc * 2:oc * 2 + 2], wv[:, k, oc * 128:(oc + 1) * 128], scb[:, k, :],
                   k == 0, k == KC - 1, [B_slot[si], B_scb], wt=0.15)
        bv = banks[bi][:, 0:12].rearrange("p (o c) -> p o c", c=2)
        o0 = piece * 6
        for c in range(2):
            dve_tt(modT[:, l, o0:o0 + 6, c], bv[:, :, c], sm[:, O_BM + l * 48 + o0:O_BM + l * 48 + o0 + 6], ALU.add,
                   [Bbank[bi], B_sm], [B_mod[l]])

    def mod_finish(l, parts=(0, 1, 2, 3)):
        for c in range(2):
            def g(n):
                return sm[:, O_NG + (l * 4 + n) * 8: O_NG + (l * 4 + n) * 8 + 8]
            if 0 in parts:
                dve_stt(der[:, l, 0, :, c], modT[:, l, 8:16, c], 1.0, g(0), ALU.add, ALU.mult, [B_mod[l], B_sm],
                        [B_der[l]])
            if 1 in parts:
                dve_tt(der[:, l, 1, :, c], modT[:, l, 16:24, c], g(1), ALU.mult, [B_mod[l], B_sm], [B_der[l]])
            if 2 in parts:
                dve_stt(der[:, l, 2, :, c], modT[:, l, 32:40, c], 1.0, g(2), ALU.add, ALU.mult, [B_mod[l], B_sm],
                        [B_der[l]])
            if 3 in parts:
                dve_tt(der[:, l, 3, :, c], modT[:, l, 40:48, c], g(3), ALU.mult, [B_mod[l], B_sm], [B_der[l]])

    def modulation(l):
        for piece in range(8):
            mod_piece(l, piece)
        mod_finish(l)

    def mod_scalars(l, which, c):
        def gs(k):
            return der[:, l, 2 * which, k, c:c + 1]

        def sh(k):
            return modT[:, l, 24 * which + k, c:c + 1]

        def gg(k):
            return der[:, l, 2 * which + 1, k, c:c + 1]
        return gs, sh, gg

    def load_T(src_d, row0, dst, dcol0, dbufs):
        xi = nxt("xin", 2)
        dma_in("sp", xin[xi][:, :], src_d[row0:row0 + 128, :], B_xin[xi])
        for half in range(2):
            bi = nbank()
            for j in range(4):
                cidx = half * 4 + j
                P.op("pe", lambda e, bi=bi, j=j, cidx=cidx, xi=xi: e.transpose(
                    banks[bi][:, j * 128:(j + 1) * 128], xin[xi][:, cidx * 128:(cidx + 1) * 128], ident[:, :]),
                    reads=[B_xin[xi], B_const], writes=[Bbank[bi]])
            src = banks[bi][:, :].rearrange("p (j t) -> p j t", j=4)
            dsta = dst[:, half * 4:half * 4 + 4, dcol0:dcol0 + 128]
            if half == 0:
                act(dsta, src, AF.Copy, [Bbank[bi]], dbufs)
            else:
                dve_copy(dsta, src, [Bbank[bi]], dbufs)

    def rstd_from_bank(bs, n, nfeat, br):
        t = nxt("tmp", 4)
        act(tmps[t][:, 0:n], banks[bs][:, 0:n], AF.Ln, [Bbank[bs]], [B_tmp[t]], bias=EPS, scale=1.0 / nfeat)
        act(banks[br][:, 0:n], tmps[t][:, 0:n], AF.Exp, [B_tmp[t]], [Bbank[br]], scale=-0.5)

    def prenorm(src, sbufs, c0, n, dst, dbufs, dc0, l, which, c):
        gs, sh, _ = mod_scalars(l, which, c)
        bs = nbank()
        for k in range(KC):
            q = nxt("sq", 2)
            act(sqs[q][:, 0:n], src[:, k, c0:c0 + n], AF.Square, sbufs, [B_sq[q]])
            mm(bs, banks[bs][:, 0:n], ones[:, :], sqs[q][:, 0:n], k == 0, k == KC - 1, [B_sq[q], B_const])
        br = nbank()
        rstd_from_bank(bs, n, D, br)
        for k in range(KC):
            t = nxt("tmp", 4)
            dve_tt(tmps[t][:, 0:n], src[:, k, c0:c0 + n], banks[br][:, 0:n], ALU.mult, sbufs + [Bbank[br]], [B_tmp[t]])
            dve_ts(dst[:, k, dc0:dc0 + n], tmps[t][:, 0:n], gs(k), sh(k), ALU.mult, ALU.add,
                   [B_tmp[t], B_mod[l], B_der[l]], dbufs, eng="pool")

    class PostNorm:
        def __init__(self, c0, n, xbufs, l, which, c, ybuf=None):
            self.c0, self.n, self.xbufs, self.l, self.which, self.c = c0, n, xbufs, l, which, c
            self.yv, self.yb = ybuf if ybuf is not None else (ytmp, B_hT)
            self.bs = nbank()
            reserved.add(self.bs)
            self.cnt = 0

        def add(self, bi, dch):
            n = self.n
            yv = self.yv[:, dch, 0:n]
            act(yv, banks[bi][:, 0:n], AF.Copy, [Bbank[bi]], self.yb)
            q = nxt("sq", 2)
            act(sqs[q][:, 0:n], banks[bi][:, 0:n], AF.Square, [Bbank[bi]], [B_sq[q]])
            mm(self.bs, banks[self.bs][:, 0:n], ones[:, :], sqs[q][:, 0:n], self.cnt == 0, self.cnt == KC - 1,
               [B_sq[q], B_const])
            self.cnt += 1

        def finish(self):
            n, c0 = self.n, self.c0
            _, _, gg = mod_scalars(self.l, self.which, self.c)
            br = nbank()
            reserved.discard(self.bs)
            rstd_from_bank(self.bs, n, D, br)
            for dch in range(KC):
                t = nxt("tmp", 4)
                dve_tt(tmps[t][:, 0:n], self.yv[:, dch, 0:n], banks[br][:, 0:n], ALU.mult,
                       self.yb + [Bbank[br]], [B_tmp[t]])
                xa = xT[:, dch, c0:c0 + n]
                dve_stt(xa, tmps[t][:, 0:n], gg(dch), xa, ALU.mult, ALU.add,
                        [B_tmp[t], B_der[self.l]] + self.xbufs, self.xbufs)

    def linear_fm(bi, wfun, rhsfun, n, reads, nk=KC):
        for k in range(nk):
            mm(bi, banks[bi][:, 0:n], wfun(k), rhsfun(k), k == 0, k == nk - 1, reads, wt=n / 512.0)

    def rope_epilogue(ba, bb, n, cos_ap, sin_ap, out_ap, obufs, out_hi=None):
        t1 = nxt("tmp", 4)
        dve_tt(tmps[t1][:, 0:n], banks[ba][:, 0:n], cos_ap, ALU.mult, [Bbank[ba], B_const], [B_tmp[t1]])
        t2 = nxt("tmp", 4)
        dve_tt(tmps[t2][:, 0:n], banks[bb][:, 0:n], sin_ap, ALU.mult, [Bbank[bb], B_const], [B_tmp[t2]])
        if out_hi is None:
            dve_tt(out_ap, tmps[t1][:, 0:n], tmps[t2][:, 0:n], ALU.add, [B_tmp[t1], B_tmp[t2]], obufs, eng="pool")
        else:
            dve_tt(out_ap, tmps[t1][0:64, 0:n], tmps[t2][0:64, 0:n], ALU.add, [B_tmp[t1], B_tmp[t2]], obufs,
                   eng="pool")
            dve_tt(out_hi, tmps[t1][64:128, 0:n], tmps[t2][64:128, 0:n], ALU.add, [B_tmp[t1], B_tmp[t2]], obufs,
                   eng="pool")

    def rope_tile(wfun, rhsfun, n, reads, cos_ap, sin_ap, out_ap, obufs, out_hi=None, filler=None):
        ba = nbank()
        linear_fm(ba, wfun, rhsfun, n, reads)
        q = nxt("sq", 2)
        act(sqs[q][:, 0:n], banks[ba][:, 0:n], AF.Copy, [Bbank[ba]], [B_sq[q]])
        if filler is not None:
            filler()
        bb = nbank()
        mm(bb, banks[bb][:, 0:n], permb[:, :], sqs[q][:, 0:n], True, True, [B_sq[q], B_const], wt=n / 512.0)
        rope_epilogue(ba, bb, n, cos_ap, sin_ap, out_ap, obufs, out_hi=out_hi)

    SCALE = 0.125

    OBK = [4, 5]
    LBK = [6, 7]
    pstate = {"par": 0, "pending": {}}
    TAIL_DELAY = 60.0

    def maybe_flush(force_par=None):
        for par in list(pstate["pending"].keys()):
            created, fn = pstate["pending"][par]
            if par == force_par or pe_work["v"] - created >= TAIL_DELAY:
                del pstate["pending"][par]
                fn()

    def obank(slot, par):
        return OBK[slot]

    def run_pair(jobs, hook_after=2):
        nj = len(jobs)
        n = jobs[0][1]
        nb = len(jobs[0][2])

        def qk(ji, j):
            q_ap, _, blocks, qreads = jobs[ji]
            k_ap, kreads, _, _, mi = blocks[j]
            s_ = (j % 2) * 2 + ji
            mm(s_, banks[s_][:, 0:n], k_ap, q_ap, True, mi is None, qreads + kreads, wt=n / 512.0)
            if mi is not None:
                for g in range(n // 128):
                    mm(s_, banks[s_][:, g * 128:(g + 1) * 128], identb[:, :], maskb[:, mi, :], False,
                       g == n // 128 - 1, [B_const])

        for ji in range(nj):
            qk(ji, 0)
        for j in range(nb):
            if j + 1 < nb:
                for ji in range(nj):
                    qk(ji, j + 1)
            sp = (j % 2) * 2
            ei = j % 2
            act(Es[ei][:, 0:nj, 0:n], PS[:, sp:sp + nj, 0:n], AF.Exp, [Bbank[sp + ji] for ji in range(nj)],
                [B_E[ei]], scale=SCALE)
            for ji in range(nj):
                _, _, v_ap, vreads, _ = jobs[ji][2][j]
                bo, bl = OBK[ji], LBK[ji]
                mm(bo, banks[bo][:, 0:n], v_ap, Es[ei][:, ji, 0:n], j == 0, j == nb - 1, [B_E[ei]] + vreads,
                   wt=n / 512.0)
                mm(bl, banks[bl][:, 0:n], ones[:, :], Es[ei][:, ji, 0:n], j == 0, j == nb - 1, [B_E[ei], B_const],
                   wt=n / 512.0)
            if j % 2 == 1 or j == nb - 1:
                maybe_flush()
        return 0

    def flush_pending():
        for par in list(pstate["pending"].keys()):
            maybe_flush(force_par=par)

    for t in range(4):
        load_T(xp_d, t * 128, xT, t * 128, [B_xT[0]])
    for t in range(6):
        load_T(xw_d, t * 128, xT, 512 + t * 128, [B_xT[1] if t < 4 else B_xT[2]])

    xfT = ytmp
    for fc in range(4 if STAGE >= -3 else 0):
        for t in range(4):
            load_T(xf_d, fc * 512 + t * 128, xfT, t * 128, B_hT)
        if fc == 0:
            for piece in range(3):
                mod_piece(0, piece)
            mod_finish(0, parts=(0,))
        prenorm(xfT, B_hT, 0, 512, hfT, [B_hfT[fc]], fc * 512, 0, 0, 1)
    for ci, (c0, n) in enumerate(TCH if STAGE >= -3 else []):
        prenorm(xT, [B_xT[ci]], c0, n, hT, [B_hT[ci]], c0, 0, 0, 0 if ci == 0 else 1)

    for hd in range(KHEADS if STAGE >= -2 else 0):
        si = load_w(wd0_d[hd].rearrange("p k c -> p (k c)"), 3072)
        wv = slots[si][:, 0:3072].rearrange("p (k c) -> p k c", k=8)
        WQ, WK, WV = 0, 128, 256
        rs = [B_slot[si]]
        if "c" not in KSKIP:
            dma_in("sp", cst[:, 0:512].rearrange("p (j c) -> p j c", j=4), ckd_d[hd], B_cst)
            bi = nbank()
            for j in range(4):
                P.op("pe", lambda e, bi=bi, j=j: e.transpose(banks[bi][:, j * 128:(j + 1) * 128],
                                                            cst[:, j * 128:(j + 1) * 128], ident[:, :]),
                     reads=[B_cst, B_const], writes=[Bbank[bi]])
            act(kTh[:, 2560:3072], banks[bi][:, :], AF.Copy, [Bbank[bi]], [B_kTh[2]])
            dma_in("sp", cst[:, 512:1024], cvd_d[hd].rearrange("p j c -> p (j c)"), B_cstv)
            dve_copy(Vh[:, 2560:3072], cst[:, 512:1024], [B_cstv], [B_Vh[2]])
        bi = nbank()
        linear_fm(bi, lambda k: wv[:, k, WQ:WQ + 128], lambda k: hT[:, k, 0:512], 512, rs + [B_hT[0]])
        act(qTh[0:64, 0, 0:512], banks[bi][0:64, :], AF.Copy, [Bbank[bi]], [B_qTh[0]])
        act(qTh[64:128, 1, 0:512], banks[bi][64:128, :], AF.Copy, [Bbank[bi]], [B_qTh[0]])
        bi = nbank()
        linear_fm(bi, lambda k: wv[:, k, WK:WK + 128], lambda k: hT[:, k, 0:512], 512, rs + [B_hT[0]])
        act(kTh[:, 0:512], banks[bi][:, :], AF.Copy, [Bbank[bi]], [B_kTh[0]])
        maybe_flush()
        for t in range(0 if "t" in KSKIP else 4):
            bi = nbank()
            linear_fm(bi, lambda k, t=t: hT[:, k, t * 128:(t + 1) * 128], lambda k: wv[:, k, WK:WK + 256], 256,
                      rs + [B_hT[0]])
            oi = nxt("ost", 2)
            act(ostage[oi][:, 0:256], banks[bi][:, 0:256], AF.Copy, [Bbank[bi]], [B_ost[oi]])
            dve_copy(Vh[:, t * 128:(t + 1) * 128], banks[bi][:, 128:256], [Bbank[bi]], [B_Vh[0]])
            if "o" not in KSKIP:
                dma_out(ndk_d[t * 128:(t + 1) * 128, hd * 128:(hd + 1) * 128], ostage[oi][:, 0:128], B_ost[oi])
                dma_out(ndv_d[t * 128:(t + 1) * 128, hd * 128:(hd + 1) * 128], ostage[oi][:, 128:256], B_ost[oi])
        for ci in (() if "r" in KSKIP else (1, 2)):
            c0, n = TCH[ci]
            rope_tile(lambda k: wv[:, k, WQ:WQ + 128], lambda k: hT[:, k, c0:c0 + n], n, rs + [B_hT[ci]],
                      ropeW[:, 0, c0 - 512:c0 - 512 + n], ropeW[:, 1, c0 - 512:c0 - 512 + n],
                      qTh[0:64, 0, c0:c0 + n], [B_qTh[1]], out_hi=qTh[64:128, 1, c0:c0 + n])
            maybe_flush()
        for fc in range(0 if "f" in KSKIP else 4):
            f0 = fc * 512

            def vfill(f0=f0, fc=fc):
                bi = nbank()
                for t in range(4):
                    for k in range(KC):
                        mm(bi, banks[bi][:, t * 128:(t + 1) * 128], hfT[:, k, f0 + t * 128:f0 + (t + 1) * 128],
                           wv[:, k, WV:WV + 128], k == 0, k == KC - 1, rs + [B_hfT[fc]], wt=0.25)
                dve_copy(Vh[:, 512 + f0:512 + f0 + 512], banks[bi][:, :], [Bbank[bi]], [B_Vh[1]])

            rope_tile(lambda k: wv[:, k, WK:WK + 128], lambda k: hfT[:, k, f0:f0 + 512], 512, rs + [B_hfT[fc]],
                      ropeF[:, 0, f0:f0 + 512], ropeF[:, 1, f0:f0 + 512],
                      kTh[:, 512 + f0:512 + f0 + 512], [B_kTh[1]], filler=vfill)
            maybe_flush()

        def diff_pair(q0_ap, q1_ap, n, blocks, qreads, out_ap, obufs):
            run_pair([(q0_ap, n, blocks, qreads), (q1_ap, n, blocks, qreads)])
            par = pstate["par"]
            pstate["par"] ^= 1
            maybe_flush(force_par=par)
            T1, B_T1 = T1s[par], B_T1s[par]
            oA, oB = OBK
            ta = nxt("tmp", 4)
            act(tmps[ta][:, 0:n], banks[LBK[0]][:, 0:n], AF.Copy, [Bbank[LBK[0]]], [B_tmp[ta]])
            tb = nxt("tmp", 4)
            act(tmps[tb][:, 0:n], banks[LBK[1]][:, 0:n], AF.Copy, [Bbank[LBK[1]]], [B_tmp[tb]])
            tc = nxt("tmp", 4)
            dve_copy(tmps[tc][:, 0:n], banks[oA][:, 0:n], [Bbank[oA]], [B_tmp[tc]])
            td = nxt("tmp", 4)
            dve_copy(tmps[td][:, 0:n], banks[oB][:, 0:n], [Bbank[oB]], [B_tmp[td]])
            dve_recip(tmps[ta][:, 0:n], tmps[ta][:, 0:n], [B_tmp[ta]], [B_tmp[ta]])
            dve_tt(T1[:, 0:n], tmps[tc][:, 0:n], tmps[ta][:, 0:n], ALU.mult, [B_tmp[tc], B_tmp[ta]], [B_T1])
            dve_recip(tmps[tb][:, 0:n], tmps[tb][:, 0:n], [B_tmp[tb]], [B_tmp[tb]])
            dve_tt(tmps[td][:, 0:n], tmps[td][:, 0:n], tmps[tb][:, 0:n], ALU.mult, [B_tmp[td], B_tmp[tb]],
                   [B_tmp[td]])
            dve_stt(T1[:, 0:n], tmps[td][:, 0:n], lamt[:, 4:5], T1[:, 0:n], ALU.mult, ALU.add,
                    [B_tmp[td], B_lam, B_T1], [B_T1])

            def tail():
                q = nxt("sq", 2)
                act(sqs[q][:, 0:n], T1[:, 0:n], AF.Square, [B_T1], [B_sq[q]])
                bs = 2
                mm(bs, banks[bs][:, 0:n], ones[:, :], sqs[q][:, 0:n], True, True, [B_sq[q], B_const])
                br = 3
                rstd_from_bank(bs, n, 128, br)
                dve_stt(out_ap, T1[:, 0:n], lamt[:, 5:6], banks[br][:, 0:n], ALU.mult, ALU.mult,
                        [B_T1, B_lam, Bbank[br]], obufs)
            pstate["pending"][par] = (pe_work["v"], tail)

        for pb in range(0 if "p" in KSKIP else 2):
            blocks = []
            for j in range(2):
                kc = pb * 256 + j * 128
                blocks.append((kTh[:, kc:kc + 128], [B_kTh[0]], Vh[:, kc:kc + 128], [B_Vh[0]], None))
            diff_pair(qTh[:, 0, pb * 256:(pb + 1) * 256], qTh[:, 1, pb * 256:(pb + 1) * 256], 256, blocks,
                      [B_qTh[0]], oT[:, hd, pb * 256:(pb + 1) * 256], [B_oT[0]])
        for ci in (() if "w" in KSKIP else (1, 2)):
            c0, n = TCH[ci]
            blocks = []
            for j in range(16):
                kc = 512 + j * 128
                blocks.append((kTh[:, kc:kc + 128], [B_kTh[1]], Vh[:, kc:kc + 128], [B_Vh[1]], None))
            for j in range(4):
                kc = 2560 + j * 128
                blocks.append((kTh[:, kc:kc + 128], [B_kTh[2]], Vh[:, kc:kc + 128], [B_Vh[2]], None))
            diff_pair(qTh[:, 0, c0:c0 + n], qTh[:, 1, c0:c0 + n], n, blocks, [B_qTh[1]],
                      oT[:, hd, c0:c0 + n], [B_oT[ci]])
        if hd < 5:
            mod_piece(0, 3 + hd)
        if STAGE >= 1:
            mod_piece(1, hd)
    flush_pending()
    if KHEADS < 5:
        for piece in range(3 + KHEADS, 8):
            mod_piece(0, piece)
    mod_finish(0, parts=(1, 2, 3))

    def out_proj_l0(l):
        P.alias([B_yA, B_yA2], B_hfT)
        ybs = [(yA, [B_yA]), (yA2, [B_yA2]), (yA, [B_yA])]
        for ci, (c0, n) in enumerate(TCH):
            pn = PostNorm(c0, n, [B_xT[ci]], l, 0, 0 if ci == 0 else 1, ybuf=ybs[ci])
            for half in range(2):
                si = load_w(wo0_d[half].rearrange("p k c -> p (k c)"), 4096)
                wv = slots[si][:, 0:4096].rearrange("p (k c) -> p k c", k=8)
                for dl in range(4):
                    bi = nbank()
                    linear_fm(bi, lambda k: wv[:, k, dl * 128:(dl + 1) * 128], lambda k: oT[:, k, c0:c0 + n], n,
                              [B_slot[si], B_oT[ci]])
                    pn.add(bi, half * 4 + dl)
            pn.finish()

    def ffn(l, chunks):
        for (c0, n, xb, hb, ab, c) in chunks:
            prenorm(xT, xb, c0, n, hT, [hb], c0, l, 1, c)
        for piece in range(8):
            nf = min(3, NF - 3 * piece)
            si = load_w(wgu_d[l, piece, :, 0:nf].rearrange("p f k c -> p (f k c)"), nf * 2048)
            wv = slots[si][:, 0:nf * 2048].rearrange("p (f k c) -> p f k c", f=nf, k=8)
            for fl in range(nf):
                f = 3 * piece + fl
                for (c0, n, xb, hb, ab, c) in chunks:
                    bg = nbank()
                    linear_fm(bg, lambda k: wv[:, fl, k, 0:128], lambda k: hT[:, k, c0:c0 + n], n, [B_slot[si], hb])
                    bu = nbank()
                    linear_fm(bu, lambda k: wv[:, fl, k, 128:256], lambda k: hT[:, k, c0:c0 + n], n, [B_slot[si], hb])
                    t = nxt("tmp", 4)
                    act(tmps[t][:, 0:n], banks[bg][:, 0:n], AF.Silu, [Bbank[bg]], [B_tmp[t]])
                    dve_tt(aT[:, f, c0:c0 + n], tmps[t][:, 0:n], banks[bu][:, 0:n], ALU.mult,
                           [B_tmp[t], Bbank[bu]], [ab])
        for (c0, n, xb, hb, ab, c) in chunks:
            pn = PostNorm(c0, n, xb, l, 1, c)
            for j in range(4):
                si = load_w(wdn_d[l, j].rearrange("p d f c -> p (d f c)"), 5632)
                wv = slots[si][:, 0:5632].rearrange("p (d f c) -> p d f c", d=2, f=22)
                for dl in range(2):
                    bi = nbank()
                    for f in range(NF):
                        mm(bi, banks[bi][:, 0:n], wv[:, dl, f, :], aT[:, f, c0:c0 + n], f == 0, f == NF - 1,
                           [B_slot[si], ab])
                    pn.add(bi, 2 * j + dl)
            pn.finish()

    if STAGE >= -1:
        out_proj_l0(0)
    if STAGE >= 1:
        mod_finish(1)
    P.alias(B_aT, B_hfT + B_oT + [B_yA, B_yA2])
    if STAGE >= 0:
        ffn(0, [(TCH[i][0], TCH[i][1], [B_xT[i]], B_hT[i], B_aT[i], 0 if i == 0 else 1) for i in range(3)])

    if STAGE >= 2:
        P.alias(B_qTs + [B_kTs] + B_oS, B_aT)
        for ci, (c0, n) in enumerate(TCH):
            prenorm(xT, [B_xT[ci]], c0, n, hT, [B_hT[ci]], c0, 1, 0, 0 if ci == 0 else 1)
        OWN0 = 640
        B_hown = [B_hT[1], B_hT[2]]
        P.op("dve", lambda e: e.memset(kTs[:, :, :], 0.0), reads=[], writes=[B_kTs])
        VS = Vh[:, 0:3584].rearrange("p (t c) -> p t c", t=14)
        B_VS = Buf("VS")
        P.alias([B_VS], list(B_Vh))
        for half in range(2):
            sa = load_w(ws1q_d[half].rearrange("p k c -> p (k c)"), 4096)
            wa = slots[sa][:, 0:4096].rearrange("p (k c) -> p k c", k=8)
            for cl in range(4):
                cq = half * 4 + cl

                def pfill(cl=cl, cq=cq):
                    bi = nbank()
                    linear_fm(bi, lambda k: wa[:, k, cl * 128:(cl + 1) * 128], lambda k: hT[:, k, 0:512], 512,
                              [B_slot[sa], B_hT[0]])
                    act(qTs[:, cq, 0:512], banks[bi][:, :], AF.Copy, [Bbank[bi]], [B_qTs[0]])

                rope_tile(lambda k: wa[:, k, cl * 128:(cl + 1) * 128], lambda k: hT[:, k, OWN0:OWN0 + 512], 512,
                          [B_slot[sa]] + B_hown, ropeW[:, 0, 128:640], ropeW[:, 1, 128:640],
                          qTs[:, cq, 512:1024], [B_qTs[1]], filler=pfill)
        sk = load_w(ws1k_d.rearrange("p k c -> p (k c)"), 4096)
        wk = slots[sk][:, 0:4096].rearrange("p (k c) -> p k c", k=8)
        rsk = [B_slot[sk]]
        for p in range(2):
            bi = nbank()
            linear_fm(bi, lambda k: wk[:, k, p * 128:(p + 1) * 128], lambda k: hT[:, k, 0:512], 512, rsk + [B_hT[0]])
            act(kTs[0:64, 2 * p, 0:512], banks[bi][0:64, :], AF.Copy, [Bbank[bi]], [B_kTs])
            act(kTs[64:128, 2 * p + 1, 0:512], banks[bi][64:128, :], AF.Copy, [Bbank[bi]], [B_kTs])
            for ci in (1, 2):
                c0, n = TCH[ci]
                rope_tile(lambda k: wk[:, k, p * 128:(p + 1) * 128], lambda k: hT[:, k, c0:c0 + n], n,
                          rsk + [B_hT[ci]], ropeW[:, 0, c0 - 512:c0 - 512 + n], ropeW[:, 1, c0 - 512:c0 - 512 + n],
                          kTs[0:64, 2 * p, c0:c0 + n], [B_kTs], out_hi=kTs[64:128, 2 * p + 1, c0:c0 + n])
        for t in range(4):
            bi = nbank()
            for k in range(KC):
                mm(bi, banks[bi][:, 0:256], hT[:, k, t * 128:(t + 1) * 128], wk[:, k, 0:256], k == 0, k == KC - 1,
                   rsk + [B_hT[0]])
            for k in range(KC):
                mm(bi, banks[bi][:, 256:512], hT[:, k, t * 128:(t + 1) * 128], wk[:, k, 256:512], k == 0, k == KC - 1,
                   rsk + [B_hT[0]])
            oi = nxt("ost", 2)
            act(ostage[oi][:, 0:512], banks[bi][:, :], AF.Copy, [Bbank[bi]], [B_ost[oi]])
            dve_copy(VS[:, t, :], banks[bi][:, 256:512], [Bbank[bi]], [B_VS])
            dma_out(nsk_d[t * 128:(t + 1) * 128, :], ostage[oi][:, 0:256], B_ost[oi])
            dma_out(nsv_d[t * 128:(t + 1) * 128, :], ostage[oi][:, 256:512], B_ost[oi])
        for t in range(6):
            bi = nbank()
            c0 = 512 + t * 128
            for k in range(KC):
                mm(bi, banks[bi][:, 0:256], hT[:, k, c0:c0 + 128], wk[:, k, 256:512], k == 0, k == KC - 1,
                   rsk + [B_hT[1] if t < 4 else B_hT[2]])
            dve_copy(VS[:, 4 + t, :], banks[bi][:, 0:256], [Bbank[bi]], [B_VS])
        dma_in("sp", cst[:, :].rearrange("p (j c) -> p j c", j=4), cks_d, B_cst)
        for p in range(2):
            bi = nbank()
            for j in range(4):
                P.op("pe", lambda e, bi=bi, j=j, p=p: e.transpose(
                    banks[bi][:, j * 128:(j + 1) * 128], cst[:, j * 256 + p * 128:j * 256 + (p + 1) * 128], ident[:, :]),
                    reads=[B_cst, B_const], writes=[Bbank[bi]])
            act(kTs[0:64, 2 * p, 1280:1792], banks[bi][0:64, :], AF.Copy, [Bbank[bi]], [B_kTs])
            act(kTs[64:128, 2 * p + 1, 1280:1792], banks[bi][64:128, :], AF.Copy, [Bbank[bi]], [B_kTs])
        xi = nxt("xin", 2)
        dma_in("sp", xin[xi][:, :], cvs_d.rearrange("p j c -> p (j c)"), B_xin[xi])
        dve_copy(Vh[:, 2560:3584], xin[xi][:, :], [B_xin[xi]], [B_VS])

        def swa_finalize(bo, bl, ng, qn, kv, g0, oS_col0, obuf):
            r0 = (kv % 2) * 64
            pr = kv // 2
            n = ng * qn
            hh0 = kv * 4 + g0
            t = nxt("tmp", 4)
            dve_copy(tmps[t][r0:r0 + 64, 0:n], banks[bl][r0:r0 + 64, 0:n], [Bbank[bl]], [B_tmp[t]])
            to = nxt("tmp", 4)
            dve_copy(tmps[to][r0:r0 + 64, 0:n], banks[bo][r0:r0 + 64, 0:n], [Bbank[bo]], [B_tmp[to]])
            lv = tmps[t][r0:r0 + 64, 0:n].rearrange("p (g q) -> p g q", g=ng)
            sk = sinkE[r0:r0 + 64, hh0:hh0 + ng]
            sk_b = bass.AP(sk.tensor, sk.offset, [list(sk.ap[0]), list(sk.ap[1]), [0, qn]])
            dve_tt(lv, lv, sk_b, ALU.add, [B_tmp[t], B_lam], [B_tmp[t]])
            act(tmps[t][r0:r0 + 64, 0:n], tmps[t][r0:r0 + 64, 0:n], AF.Ln, [B_tmp[t]], [B_tmp[t]])
            act(tmps[t][r0:r0 + 64, 0:n], tmps[t][r0:r0 + 64, 0:n], AF.Exp, [B_tmp[t]], [B_tmp[t]], scale=-1.0)
            ov = tmps[to][r0:r0 + 64, 0:n].rearrange("p (g q) -> p g q", g=ng)
            dve_tt(oS[r0:r0 + 64, pr * 4 + g0:pr * 4 + g0 + ng, oS_col0:oS_col0 + qn], ov, lv, ALU.mult,
                   [B_tmp[to], B_tmp[t]], [obuf], eng="pool")

        def run_swa(joblist):
            for i in range(0, len(joblist), 2):
                grp = joblist[i:i + 2]
                par = run_pair([g[0] for g in grp])
                for slot, g in enumerate(grp):
                    swa_finalize(obank(slot, par), LBK[slot], *g[1])

        jl = []
        for pb in range(2):
            for kv in range(4):
                p = kv // 2
                for gh in range(2):
                    q_ap = qTs[:, p * 4 + gh * 2:p * 4 + gh * 2 + 2, pb * 256:(pb + 1) * 256]
                    blocks = []
                    for j in range(2):
                        kc = pb * 256 + j * 128
                        blocks.append((kTs[:, kv, kc:kc + 128], [B_kTs],
                                       VS[:, pb * 2 + j, p * 128:(p + 1) * 128], [B_VS], None))
                    jl.append(((q_ap, 512, blocks, [B_qTs[0]]), (2, 256, kv, gh * 2, pb * 256, B_oS[0])))
        for qb in range(4):
            for kv in range(4):
                p = kv // 2
                q_ap = qTs[:, p * 4:p * 4 + 4, 512 + qb * 128:512 + (qb + 1) * 128]
                blocks = []
                for j in range(4):
                    kc = 1280 + j * 128
                    blocks.append((kTs[:, kv, kc:kc + 128], [B_kTs],
                                   VS[:, 10 + j, p * 128:(p + 1) * 128], [B_VS], None))
                for dj, mi in ((0, 2 * qb), (1, None), (2, 2 * qb + 1)):
                    w = qb + dj
                    kc = 512 + w * 128
                    blocks.append((kTs[:, kv, kc:kc + 128], [B_kTs],
                                   VS[:, 4 + w, p * 128:(p + 1) * 128], [B_VS], mi))
                jl.append(((q_ap, 512, blocks, [B_qTs[1]]), (4, 128, kv, 0, 512 + qb * 128, B_oS[1])))
        run_swa(jl)
        L1CH = [(0, 512, [B_xT[0]], 0, 0, B_oS[0]), (OWN0, 512, [B_xT[1], B_xT[2]], 1, 512, B_oS[1])]
        P.alias([B_yA], B_qTs)
        for (c0, n, xb, c, oc0, ob) in L1CH:
            pn = PostNorm(c0, n, xb, 1, 0, c, ybuf=(yA, [B_yA]) if c0 == 0 else None)
            for half in range(2):
                si = load_w(wo1_d[half].rearrange("p k c -> p (k c)"), 4096)
                wv = slots[si][:, 0:4096].rearrange("p (k c) -> p k c", k=8)
                for dl in range(4):
                    bi = nbank()
                    linear_fm(bi, lambda k: wv[:, k, dl * 128:(dl + 1) * 128], lambda k: oS[:, k, oc0:oc0 + n], n,
                              [B_slot[si], ob])
                    pn.add(bi, half * 4 + dl)
            pn.finish()
        P.alias(B_aT, B_qTs + [B_kTs] + B_oS + [B_yA])
        if STAGE >= 3:
            ffn(1, [(0, 512, [B_xT[0]], B_hT[0], B_aT[0], 0),
                    (OWN0, 512, [B_xT[1], B_xT[2]], B_hT[1], B_aT[1], 1)])

    OWN0 = 640
    for t in range(8):
        c0 = t * 128 if t < 4 else OWN0 + (t - 4) * 128
        xb = [B_xT[0]] if t < 4 else [B_xT[1], B_xT[2]]
        xi = nxt("xin", 2)
        for half in range(2):
            bi = nbank()
            for j in range(4):
                cidx = half * 4 + j
                P.op("pe", lambda e, bi=bi, j=j, cidx=cidx, c0=c0: e.transpose(
                    banks[bi][:, j * 128:(j + 1) * 128], xT[:, cidx, c0:c0 + 128], ident[:, :]),
                    reads=xb + [B_const], writes=[Bbank[bi]])
            if half == 0:
                act(xin[xi][:, 0:512], banks[bi][:, :], AF.Copy, [Bbank[bi]], [B_xin[xi]])
            else:
                dve_copy(xin[xi][:, 512:1024], banks[bi][:, :], [Bbank[bi]], [B_xin[xi]])
        dst = yp_d[t * 128:(t + 1) * 128, :] if t < 4 else ys_d[(t - 4) * 128:(t - 3) * 128, :]
        dma_out(dst, xin[xi][:, :], B_xin[xi])

    P.finish()
    stats = P.emit()
    return nc, es, stats


def _rope_tables(pos):
    pos = np.asarray(pos)
    row = (pos // 64).astype(np.float32)
    col = (pos % 64).astype(np.float32)
    nf = 16
    inv = (np.float32(10000.0) ** (-np.arange(nf, dtype=np.float32) / np.float32(nf))).astype(np.float32)
    ar = row[:, None] * inv[None, :]
    ac = col[:, None] * inv[None, :]
    ang = np.concatenate([ar, ar, ac, ac], axis=-1).astype(np.float32)
    cos = np.cos(ang).astype(np.float32)
    sin = np.sin(ang).astype(np.float32)
    sign = np.concatenate([-np.ones(16), np.ones(16), -np.ones(16), np.ones(16)]).astype(np.float32)
    sin = sin * sign[None, :]
    cos2 = np.concatenate([cos, cos], axis=1).T
    sin2 = np.concatenate([sin, sin], axis=1).T
    return np.ascontiguousarray(cos2), np.ascontiguousarray(sin2)


_ROTSRC = np.concatenate([np.arange(16, 32), np.arange(0, 16), np.arange(48, 64), np.arange(32, 48)])


def _rot_cols(w):
    n = w.shape[1] // 64
    idx = (np.arange(n)[:, None] * 64 + _ROTSRC[None, :]).reshape(-1)
    return w[:, idx]


def _kmaj(w):
    return np.ascontiguousarray(w.reshape(8, 128, -1).transpose(1, 0, 2))


_PROG_CACHE = {}


def prep_inputs(x_prompt, x_sample, cache_diff_k, cache_diff_v, cache_swa_k, cache_swa_v, c, c_ctx,
                w_mod, b_mod, norm_g, w_qkv_diff, diff_lambda, diff_subln_g, w_o_diff,
                w_qkv_swa, swa_sink, w_o_swa, w_gate, w_up, w_down):
    f = np.float32
    A = lambda a: np.ascontiguousarray(np.asarray(a, dtype=f))
    x_prompt, x_sample = A(x_prompt), A(x_sample)
    cache_diff_k, cache_diff_v = A(cache_diff_k), A(cache_diff_v)
    cache_swa_k, cache_swa_v = A(cache_swa_k), A(cache_swa_v)
    c, c_ctx = A(c), A(c_ctx)
    w_mod, b_mod, norm_g = A(w_mod), A(b_mod), A(norm_g)
    w_qkv_diff, diff_lambda, diff_subln_g, w_o_diff = A(w_qkv_diff), A(diff_lambda), A(diff_subln_g), A(w_o_diff)
    w_qkv_swa, swa_sink, w_o_swa = A(w_qkv_swa), A(swa_sink), A(w_o_swa)
    w_gate, w_up, w_down = A(w_gate), A(w_up), A(w_down)

    wmod = np.ascontiguousarray(
        w_mod.reshape(2, 8, 128, 8, 768).transpose(0, 3, 2, 1, 4))
    wq, wk, wv = w_qkv_diff[0][:, 0:1024], w_qkv_diff[0][:, 1024:2048], w_qkv_diff[0][:, 2048:3072]
    wd0 = np.empty((8, 128, 8, 384), f)
    for h in range(8):
        s = slice(h * 128, (h + 1) * 128)
        wd0[h] = _kmaj(np.concatenate([wq[:, s], wk[:, s], wv[:, s]], axis=1))
    permT = np.zeros((128, 128), f)
    for blk in range(2):
        for o in range(64):
            permT[blk * 64 + _ROTSRC[o], blk * 64 + o] = 1.0
    wo0 = np.stack([_kmaj(w_o_diff[0][:, 0:512]), _kmaj(w_o_diff[0][:, 512:1024])])
    ws = w_qkv_swa[0]
    sq_cols = []
    for p in range(2):
        for g in range(4):
            for kvl in range(2):
                hh = (2 * p + kvl) * 4 + g
                sq_cols.append(np.arange(hh * 64, (hh + 1) * 64))
    sq_cols = np.concatenate(sq_cols)
    wsq = ws[:, 0:1024]
    wsq_p = wsq[:, sq_cols]
    ws1q = np.stack([_kmaj(wsq_p[:, 0:512]), _kmaj(wsq_p[:, 512:1024])])
    wsk, wsv = ws[:, 1024:1280], ws[:, 1280:1536]
    ws1k = _kmaj(np.concatenate([wsk, wsv], axis=1))
    wos_p = w_o_swa[0][sq_cols, :]
    wo1 = np.stack([_kmaj(wos_p[:, 0:512]), _kmaj(wos_p[:, 512:1024])])
    wgu_f = np.zeros((2, 24, 128, 8, 256), f)
    for l in range(2):
        g_ = w_gate[l].reshape(8, 128, 22, 128).transpose(2, 1, 0, 3)
        u_ = w_up[l].reshape(8, 128, 22, 128).transpose(2, 1, 0, 3)
        wgu_f[l, 0:22, :, :, 0:128] = g_
        wgu_f[l, 0:22, :, :, 128:256] = u_
    wgu = np.ascontiguousarray(wgu_f.reshape(2, 8, 3, 128, 8, 256).transpose(0, 1, 3, 2, 4, 5))
    wdn = np.ascontiguousarray(w_down.reshape(2, 22, 128, 4, 2, 128).transpose(0, 3, 2, 4, 1, 5))
    ident = np.eye(128, dtype=f)
    cosf, sinf = _rope_tables(np.arange(TFULL))
    ropef = np.ascontiguousarray(np.stack([cosf, sinf], axis=1))

    normg_l = norm_g.reshape(8, 8, 128).transpose(2, 0, 1).reshape(128, 64)
    bmod_l = b_mod.reshape(2, 48, 128).transpose(2, 0, 1).reshape(128, 96)
    subg_l = diff_subln_g.reshape(128, 1)
    lam_l = np.broadcast_to(diff_lambda.reshape(1, 256), (128, 256))
    sink_l = np.broadcast_to(swa_sink.reshape(1, 16), (128, 16))

    in_maps = []
    for core in range(NCORES):
        b, ch = core // 4, core % 4
        xp = x_prompt[2 * core:2 * core + 2].reshape(512, D)
        pos = np.arange(ch * 512 - 128, ch * 512 + 640)
        valid = (pos >= 0) & (pos < 2048)
        xw = np.zeros((TW, D), f)
        xw[valid] = x_sample[b, pos[valid]]
        cosw, sinw = _rope_tables(np.clip(pos, 0, 2047))
        ropew = np.ascontiguousarray(np.stack([cosw, sinw], axis=1))
        maskb = np.zeros((128, 8, 128), f)
        for qb in range(4):
            for side in range(2):
                w = qb + (0 if side == 0 else 2)
                kpos = pos[w * 128:(w + 1) * 128]
                qpos = pos[(qb + 1) * 128:(qb + 2) * 128]
                ok = ((kpos[:, None] >= 0) & (kpos[:, None] < 2048)
                      & (np.abs(qpos[None, :] - kpos[:, None]) <= 128))
                maskb[:, 2 * qb + side, :] = np.where(ok, 0.0, -30000.0)
        cond_l = np.stack([c_ctx, c[b]], axis=-1).reshape(8, 128, 2).transpose(1, 0, 2).reshape(128, 16)
        sm = np.ascontiguousarray(np.concatenate([cond_l, normg_l, bmod_l, subg_l, lam_l, sink_l], axis=1), dtype=f)
        ckd = np.ascontiguousarray(cache_diff_k[b, 0].reshape(4, 128, 8, 128).transpose(2, 1, 0, 3))
        cvd = np.ascontiguousarray(cache_diff_v[b, 0].reshape(4, 128, 8, 128).transpose(2, 1, 0, 3))
        cks = np.ascontiguousarray(cache_swa_k[b, 0].reshape(4, 128, 256).transpose(1, 0, 2))
        cvs = np.ascontiguousarray(cache_swa_v[b, 0].reshape(4, 128, 256).transpose(1, 0, 2))
        in_maps.append(dict(xp=np.ascontiguousarray(xp), xw=xw, xf=x_sample[b], ckd=ckd, cvd=cvd, cks=cks, cvs=cvs,
                            sm=sm, ident=ident, permT=permT, ropew=ropew, ropef=ropef, maskb=maskb, wmod=wmod, wd0=wd0,
                            wo0=wo0,
                            ws1q=ws1q, ws1k=ws1k, wo1=wo1, wgu=wgu, wdn=wdn))
    return in_maps


def kernel(**inputs):
    f = np.float32
    in_maps = prep_inputs(**inputs)
    if "nc" not in _PROG_CACHE:
        nc, es, stats = build_program()
        _PROG_CACHE["nc"] = (nc, es)
        if os.environ.get("KVERBOSE"):
            print("ops per engine:", stats)
    nc, _ = _PROG_CACHE["nc"]
    res = run_bass_kernel_spmd(nc, in_maps, core_ids=list(range(NCORES)))
    R = res.results
    y_prompt = np.concatenate([R[i]["yp"].reshape(2, 256, D) for i in range(NCORES)], axis=0)
    y_sample = np.stack([np.concatenate([R[b * 4 + ch]["ys"] for ch in range(4)], axis=0) for b in range(2)], axis=0)
    ndk = np.concatenate([R[i]["ndk"].reshape(2, 1, 256, 8, 128) for i in range(NCORES)], axis=0)
    ndv = np.concatenate([R[i]["ndv"].reshape(2, 1, 256, 8, 128) for i in range(NCORES)], axis=0)
    nsk = np.concatenate([R[i]["nsk"].reshape(2, 1, 256, 4, 64) for i in range(NCORES)], axis=0)
    nsv = np.concatenate([R[i]["nsv"].reshape(2, 1, 256, 4, 64) for i in range(NCORES)], axis=0)
    return (y_prompt.astype(f), y_sample.astype(f), ndk.astype(f), ndv.astype(f), nsk.astype(f), nsv.astype(f))
```

```python
import os
import numpy as np
import concourse.bass as bass
import concourse.mybir as mybir
from concourse.bass_utils import run_bass_kernel_spmd
from contextlib import ExitStack

F32 = mybir.dt.float32
BF16 = mybir.dt.bfloat16
AF = mybir.ActivationFunctionType
ALU = mybir.AluOpType

D = 1024
KC = 8
DFF = 2816
NF = 22
TP = 512
TW = 768
TT = 1280
TFULL = 2048
LC = 512
EPS = 1e-6
NCORES = 8
STAGE = int(os.environ.get("KSTAGE", "99"))
KHEADS = int(os.environ.get("KHEADS", "8"))
KSKIP = os.environ.get("KSKIP", "")


class Buf:
    __slots__ = ("name", "writers", "readers", "dsem", "ndma", "war", "excl")

    def __init__(self, name, excl=False):
        self.name = name
        self.excl = excl
        self.writers = []
        self.readers = []
        self.war = []
        self.dsem = None
        self.ndma = 0


class Op:
    __slots__ = ("eng", "idx", "fn", "deps", "dma", "buf", "ordinal", "flag", "count", "waits", "ring")

    def __init__(self, eng, idx, fn):
        self.eng = eng
        self.idx = idx
        self.fn = fn
        self.deps = []
        self.dma = False
        self.buf = None
        self.ordinal = 0
        self.flag = False
        self.count = 0
        self.waits = []
        self.ring = False


ENGS = ["pe", "act", "dve", "pool", "sp"]


class Prog:
    def __init__(self, nc, es):
        self.nc = nc
        self.es = es
        self.ops = {e: [] for e in ENGS}
        self.dma_bufs = []
        self.out_dmas = []

    def op(self, eng, fn, reads=(), writes=(), dma_buf=None, is_out=False, ring=False):
        o = Op(eng, len(self.ops[eng]), fn)
        o.ring = ring
        deps = o.deps
        for b in reads:
            for w in b.writers:
                deps.append((w, True))
            if b.excl:
                for r in b.readers:
                    if r.eng != eng:
                        deps.append((r, False))
            b.readers.append(o)
        for b in writes:
            if b.readers:
                b.war = [r for r in b.readers if r is not o]
                b.writers = [o]
                b.readers = []
            else:
                b.writers.append(o)
            for r in b.war:
                deps.append((r, False))
        if dma_buf is not None:
            o.dma = True
            o.buf = dma_buf
            if dma_buf.dsem is None:
                self.dma_bufs.append(dma_buf)
                dma_buf.dsem = True
            dma_buf.ndma += 1
            o.ordinal = dma_buf.ndma
            if is_out:
                self.out_dmas.append(o)
        self.ops[eng].append(o)
        return o

    def alias(self, new_bufs, old_bufs):
        pend = []
        for b in old_bufs:
            pend.extend(b.readers)
            pend.extend(b.writers)
        for nb in new_bufs:
            nb.readers.extend(pend)
            nb.war = []

    def finish(self):
        o = Op("sp", len(self.ops["sp"]), None)
        for d in self.out_dmas:
            o.deps.append((d, True))
        self.ops["sp"].append(o)

    def emit(self):
        nc = self.nc
        es = self.es
        esem = {e: es.enter_context(nc.semaphore("s_" + e)) for e in ["pe", "act", "dve", "pool"]}
        for i, b in enumerate(self.dma_bufs):
            b.dsem = es.enter_context(nc.semaphore("d%d" % i))
        for e in ENGS:
            waited = {}
            for o in self.ops[e]:
                need = {}
                for (d, raw) in o.deps:
                    if d.dma:
                        key = ("d", id(d.buf))
                        val = d.ordinal
                        if waited.get(key, 0) >= val:
                            continue
                        if need.get(key, (0, None))[0] < val:
                            need[key] = (val, d)
                    else:
                        if d.eng == e and e == "pe":
                            continue
                        key = ("e", d.eng)
                        val = d.idx + 1
                        if waited.get(key, 0) >= val:
                            continue
                        if need.get(key, (0, None))[0] < val:
                            need[key] = (val, d)
                for key, (val, d) in need.items():
                    waited[key] = val
                    if not d.dma:
                        d.flag = True
                    o.waits.append(d)
        for e in ["pe", "act", "dve", "pool"]:
            c = 0
            for o in self.ops[e]:
                if o.flag and not o.dma:
                    c += 1
                    o.count = c
        handles = {"pe": "tensor", "act": "scalar", "dve": "vector", "pool": "gpsimd", "sp": "sync"}
        stats = {}
        with nc.Block() as block:
            for e in ENGS:
                ops = self.ops[e]
                stats[e] = len(ops)

                def body(eng, ops=ops, e=e):
                    for o in ops:
                        for d in o.waits:
                            if d.dma:
                                eng.wait_ge(d.buf.dsem, 16 * d.ordinal)
                            else:
                                eng.wait_ge(esem[d.eng], d.count)
                        if o.fn is None:
                            continue
                        inst = o.fn(eng)
                        if o.ring:
                            assert not o.flag
                            inst.then_inc(self.ring_sem, 16)
                        elif o.dma:
                            inst.then_inc(o.buf.dsem, 16)
                        elif o.flag:
                            inst.then_inc(esem[e], 1)

                getattr(block, handles[e])(body)
        return stats


def build_program():
    nc = bass.Bass("TRN2", target_bir_lowering=False, monotonic_sem_count=0)
    es = ExitStack()
    P = Prog(nc, es)

    def din(name, shape):
        return nc.dram_tensor(name, list(shape), F32, kind="ExternalInput").ap()

    def dout(name, shape):
        return nc.dram_tensor(name, list(shape), F32, kind="ExternalOutput").ap()

    xp_d = din("xp", [TP, D])
    xw_d = din("xw", [TW, D])
    xf_d = din("xf", [TFULL, D])
    ckd_d = din("ckd", [8, 128, 4, 128])
    cvd_d = din("cvd", [8, 128, 4, 128])
    cks_d = din("cks", [128, 4, 256])
    cvs_d = din("cvs", [128, 4, 256])
    NSM = 16 + 64 + 96 + 1 + 256 + 16
    sm_d = din("sm", [128, NSM])
    ident_d = din("ident", [128, 128])
    ropew_d = din("ropew", [128, 2, TW])
    ropef_d = din("ropef", [128, 2, TFULL])
    maskb_d = din("maskb", [128, 8, 128])
    wmod_d = din("wmod", [2, 8, 128, 8, 768])
    wd0_d = din("wd0", [8, 128, 8, 384])
    permT_d = din("permT", [128, 128])
    wo0_d = din("wo0", [2, 128, 8, 512])
    ws1q_d = din("ws1q", [2, 128, 8, 512])
    ws1k_d = din("ws1k", [128, 8, 512])
    wo1_d = din("wo1", [2, 128, 8, 512])
    wgu_d = din("wgu", [2, 8, 128, 3, 8, 256])
    wdn_d = din("wdn", [2, 4, 128, 2, 22, 128])

    yp_d = dout("yp", [TP, D])
    ys_d = dout("ys", [512, D])
    ndk_d = dout("ndk", [TP, 1024])
    ndv_d = dout("ndv", [TP, 1024])
    nsk_d = dout("nsk", [TP, 256])
    nsv_d = dout("nsv", [TP, 256])

    def sb(name, shape, dt):
        return es.enter_context(nc.sbuf_tensor(name, list(shape), dt))

    xT = sb("xT", [128, KC, TT], F32)
    hT = sb("hT", [128, KC, TT], BF16)
    ytmp = hT.bitcast(F32)
    BIG = sb("BIG", [128, 28160], BF16)
    slots = [sb("wslot%d" % i, [128, 6144], BF16) for i in range(2)]
    kTh = sb("kTh", [128, 3072], BF16)
    Vh = sb("Vh", [128, 3584], BF16)
    qTh = sb("qTh", [128, 2, TT], BF16)
    Es = [sb("E%d" % i, [128, 2, 512], BF16) for i in range(2)]
    xin = [sb("xin%d" % i, [128, 1024], F32) for i in range(2)]
    cst = sb("cst", [128, 1024], F32)
    ropeW = sb("ropeW", [128, 2, TW], F32)
    ropeF = sb("ropeF", [128, 2, TFULL], BF16)
    maskb = sb("maskbs", [128, 8, 128], BF16)
    ident = sb("idents", [128, 128], F32)
    identb = sb("identb", [128, 128], BF16)
    ones = sb("ones", [128, 128], BF16)
    permb = sb("permb", [128, 128], BF16)
    sm = sb("sms", [128, NSM], F32)
    modT = sb("modT", [128, 2, 48, 2], F32)
    der = sb("der", [128, 2, 4, 8, 2], F32)
    scb = sb("scb", [128, 8, 2], BF16)
    lamt = sb("lamt", [128, 8], F32)
    sinkE = sb("sinkE", [128, 16], F32)
    sqs = [sb("sq%d" % i, [128, 512], BF16) for i in range(2)]
    tmps = [sb("tmp%d" % i, [128, 512], F32) for i in range(4)]
    T1s = [sb("T1a", [128, 512], F32), sb("T1b", [128, 512], F32)]
    ostage = [xin[i] for i in range(2)]

    PS = es.enter_context(nc.psum_tensor("PS", [128, 8, 512], F32))

    class BankView:
        def __init__(self, i):
            self.i = i

        def __getitem__(self, idx):
            return PS[idx[0], self.i, idx[1]]

    banks = [BankView(i) for i in range(8)]
    Bbank = [Buf("bank%d" % i, excl=True) for i in range(8)]

    B_xT = [Buf("xT_p"), Buf("xT_w0"), Buf("xT_w1")]
    B_hT = [Buf("hT_p"), Buf("hT_w0"), Buf("hT_w1")]
    B_slot = [Buf("slot0"), Buf("slot1")]
    B_kTh = Buf("kTh_p"), Buf("kTh_f"), Buf("kTh_c")
    B_Vh = Buf("Vh_p"), Buf("Vh_f"), Buf("Vh_c")
    B_qTh = [Buf("qTh_p"), Buf("qTh_w")]
    B_E = [Buf("E%d" % i) for i in range(2)]
    B_xin = [Buf("xin0"), Buf("xin1")]
    B_cst = Buf("cst")
    B_cstv = Buf("cstv")
    B_const = Buf("consts")
    B_sm = Buf("sm")
    B_mod = [Buf("mod0"), Buf("mod1")]
    B_der = [Buf("der0"), Buf("der1")]
    B_scb = Buf("scb")
    B_lam = Buf("lam")
    B_sq = [Buf("sq0"), Buf("sq1")]
    B_tmp = [Buf("tmp%d" % i) for i in range(4)]
    B_T1s = [Buf("T1a"), Buf("T1b")]
    B_ost = B_xin
    B_hfT = [Buf("hfT%d" % i) for i in range(4)]
    B_oT = [Buf("oT_p"), Buf("oT_w0"), Buf("oT_w1")]
    B_aT = [Buf("aT0"), Buf("aT1"), Buf("aT2")]
    B_qTs = [Buf("qTs_p"), Buf("qTs_s")]
    B_kTs = Buf("kTs")
    B_oS = [Buf("oS_p"), Buf("oS_s")]

    hfT = BIG[:, 0:16384].rearrange("p (k t) -> p k t", k=8)
    oT = BIG[:, 16384:16384 + 10240].rearrange("p (k t) -> p k t", k=8)
    aT = BIG[:, 0:28160].rearrange("p (f t) -> p f t", f=22)
    qTs = BIG[:, 0:8192].rearrange("p (k t) -> p k t", k=8)
    kTs = BIG[:, 8192:8192 + 7168].rearrange("p (k t) -> p k t", k=4)
    oS = BIG[:, 15360:15360 + 8192].rearrange("p (h t) -> p h t", h=8)

    BIGf = BIG.bitcast(F32)
    yA = BIGf[:, 0:4096].rearrange("p (d t) -> p d t", d=8)
    yA2 = BIGf[:, 4096:8192].rearrange("p (d t) -> p d t", d=8)
    B_yA, B_yA2 = Buf("yA"), Buf("yA2")
    TCH = [(0, 512), (512, 512), (1024, 256)]

    rr = {"bank": 0, "tmp": 0, "sq": 0, "slot": 0, "xin": 0, "ost": 0, "E": 0}

    def nxt(kind, n):
        v = rr[kind]
        rr[kind] = (v + 1) % n
        return v

    reserved = set()

    def nbank():
        while True:
            b = nxt("bank", 8)
            if b not in reserved:
                return b

    open_grp = {}

    pe_work = {"v": 0.0}

    def mm(bi, out_ap, lhsT, rhs, start, stop, reads, wt=1.0):
        pe_work["v"] += wt
        if start and open_grp.get(bi):
            import traceback
            traceback.print_stack(limit=6)
            print("OPEN GROUP on bank", bi, "opened at:", open_grp[bi])
        if start:
            import traceback
            open_grp[bi] = "".join(traceback.format_stack(limit=5)[:-1])
        if stop:
            open_grp[bi] = None
        P.op("pe", lambda e: e.matmul(out_ap, lhsT, rhs, start=start, stop=stop),
             reads=reads, writes=[Bbank[bi]])

    def act(out_ap, in_ap, func, reads, writes, bias=None, scale=None):
        kw = {}
        if bias is not None:
            kw["bias"] = bias
        if scale is not None:
            kw["scale"] = scale
        P.op("act", lambda e: e.activation(out_ap, in_ap, func, **kw), reads=reads, writes=writes)

    def dve_tt(out_ap, a, b, op, reads, writes, eng="dve"):
        P.op(eng, lambda e: e.tensor_tensor(out_ap, a, b, op), reads=reads, writes=writes)

    def dve_stt(out_ap, a, scalar, b, op0, op1, reads, writes):
        P.op("dve", lambda e: e.scalar_tensor_tensor(out_ap, a, scalar, b, op0, op1), reads=reads, writes=writes)

    def dve_ts(out_ap, a, s1, s2, op0, op1, reads, writes, eng="dve"):
        if op1 is None:
            P.op(eng, lambda e: e.tensor_scalar(out_ap, a, s1, None, op0), reads=reads, writes=writes)
        else:
            P.op(eng, lambda e: e.tensor_scalar(out_ap, a, s1, s2, op0, op1), reads=reads, writes=writes)

    def dve_copy(out_ap, in_ap, reads, writes, eng="dve"):
        P.op(eng, lambda e: e.tensor_copy(out_ap, in_ap), reads=reads, writes=writes)

    def dve_recip(out_ap, in_ap, reads, writes):
        P.op("dve", lambda e: e.reciprocal(out_ap, in_ap), reads=reads, writes=writes)

    def dma_in(eng, out_ap, in_ap, buf, extra_writes=(), cast=False):
        if eng == "pool":
            P.op("pool", lambda e: e.dma_start(out=out_ap, in_=in_ap, max_dma_last_dim=4096),
                 reads=[], writes=[buf] + list(extra_writes), dma_buf=Buf("sw"))
        elif cast:
            P.op(eng, lambda e: e.dma_start(out=out_ap, in_=in_ap, max_dma_last_dim=4096),
                 reads=[], writes=[buf] + list(extra_writes), dma_buf=buf)
        else:
            P.op(eng, lambda e: e.dma_start(out=out_ap, in_=in_ap),
                 reads=[], writes=[buf] + list(extra_writes), dma_buf=buf)

    def dma_out(out_ap, in_ap, buf):
        P.op("sp", lambda e: e.dma_start(out=out_ap, in_=in_ap), reads=[buf], writes=[], dma_buf=buf, is_out=True)

    def load_w(srcs, nelem, view=None, parts=128):
        si = nxt("slot", 2)
        if not isinstance(srcs, (list, tuple)):
            srcs = [srcs]
        for i, src_ap in enumerate(srcs):
            dst = slots[si][0:parts, i * nelem:(i + 1) * nelem]
            dma_in("pool", dst, src_ap, B_slot[si], cast=True)
        return si

    dma_in("sp", sm[:, :], sm_d, B_sm)
    dma_in("sp", ident[:, :], ident_d, B_const)
    dma_in("sp", ropeW[:, :, :], ropew_d, B_const)
    dma_in("pool", ropeF[:, :, :].rearrange("p a t -> p (a t)"), ropef_d.rearrange("p a t -> p (a t)"), B_const, cast=True)
    dma_in("pool", maskb[:, :, :].rearrange("p a t -> p (a t)"), maskb_d.rearrange("p a t -> p (a t)"), B_const, cast=True)
    P.op("dve", lambda e: e.memset(ones[:, :], 1.0), reads=[], writes=[B_const])
    dve_copy(identb[:, :], ident[:, :], [B_const], [B_const])
    P.op("dve", lambda e: e.memset(qTh[:, :, :], 0.0), reads=[], writes=[B_qTh[0], B_qTh[1]])
    dma_in("sp", cst[:, 0:128], permT_d, B_cst)
    dve_copy(permb[:, :], cst[:, 0:128], [B_cst, B_const], [B_const])

    O_COND, O_NG, O_BM, O_SUBG, O_LAM, O_SINK = 0, 16, 80, 176, 177, 433
    cond_v = sm[:, O_COND:O_COND + 16].rearrange("p (k c) -> p k c", c=2)
    act(scb[:, :, :], cond_v, AF.Silu, [B_sm], [B_scb])
    LAM_INIT = 0.8 - 0.6 * float(np.exp(-0.3 * 0))
    lam_v = sm[:, O_LAM:O_LAM + 256].rearrange("p (a d) -> p a d", a=4)
    P.op("dve", lambda e: e.tensor_tensor(tmps[0][:, 0:64], lam_v[:, 0, :], lam_v[:, 1, :], ALU.mult),
         reads=[B_sm], writes=[B_tmp[0]])
    P.op("dve", lambda e: e.tensor_tensor(tmps[0][:, 64:128], lam_v[:, 2, :], lam_v[:, 3, :], ALU.mult),
         reads=[B_sm], writes=[B_tmp[0]])
    P.op("dve", lambda e: e.reduce_sum(lamt[:, 0:2], tmps[0][:, 0:128].rearrange("p (a d) -> p a d", a=2),
                                       mybir.AxisListType.X), reads=[B_tmp[0]], writes=[B_lam])
    act(lamt[:, 2:4], lamt[:, 0:2], AF.Exp, [B_lam], [B_lam])
    dve_tt(lamt[:, 4:5], lamt[:, 3:4], lamt[:, 2:3], ALU.subtract, [B_lam], [B_lam])
    dve_ts(lamt[:, 4:5], lamt[:, 4:5], -LAM_INIT, None, ALU.add, None, [B_lam], [B_lam])
    dve_ts(lamt[:, 5:6], sm[:, O_SUBG:O_SUBG + 1], 1.0 - LAM_INIT, None, ALU.mult, None, [B_sm, B_lam], [B_lam])
    act(sinkE[:, :], sm[:, O_SINK:O_SINK + 16], AF.Exp, [B_sm], [B_lam])

    def mod_piece(l, piece):
        bi = nbank()
        si = load_w(wmod_d[l, piece].rearrange("p k c -> p (k c)"), 6144)
        wv = slots[si][:, 0:6144].rearrange("p (k c) -> p k c", k=8)
        for oc in range(6):
            for k in range(KC):
                mm(bi, banks[bi][:, oc * 2:oc * 2 + 2], wv[:, k, oc * 128:(oc + 1) * 128], scb[:, k, :],
                   k == 0, k == KC - 1, [B_slot[si], B_scb], wt=0.15)
        bv = banks[bi][:, 0:12].rearrange("p (o c) -> p o c", c=2)
        o0 = piece * 6
        for c in range(2):
            dve_tt(modT[:, l, o0:o0 + 6, c], bv[:, :, c], sm[:, O_BM + l * 48 + o0:O_BM + l * 48 + o0 + 6], ALU.add,
                   [Bbank[bi], B_sm], [B_mod[l]])

    def mod_finish(l, parts=(0, 1, 2, 3)):
        for c in range(2):
            def g(n):
                return sm[:, O_NG + (l * 4 + n) * 8: O_NG + (l * 4 + n) * 8 + 8]
            if 0 in parts:
                dve_stt(der[:, l, 0, :, c], modT[:, l, 8:16, c], 1.0, g(0), ALU.add, ALU.mult, [B_mod[l], B_sm],
                        [B_der[l]])
            if 1 in parts:
                dve_tt(der[:, l, 1, :, c], modT[:, l, 16:24, c], g(1), ALU.mult, [B_mod[l], B_sm], [B_der[l]])
            if 2 in parts:
                dve_stt(der[:, l, 2, :, c], modT[:, l, 32:40, c], 1.0, g(2), ALU.add, ALU.mult, [B_mod[l], B_sm],
                        [B_der[l]])
            if 3 in parts:
                dve_tt(der[:, l, 3, :, c], modT[:, l, 40:48, c], g(3), ALU.mult, [B_mod[l], B_sm], [B_der[l]])

    def modulation(l):
        for piece in range(8):
            mod_piece(l, piece)
        mod_finish(l)

    def mod_scalars(l, which, c):
        def gs(k):
            return der[:, l, 2 * which, k, c:c + 1]

        def sh(k):
            return modT[:, l, 24 * which + k, c:c + 1]

        def gg(k):
            return der[:, l, 2 * which + 1, k, c:c + 1]
        return gs, sh, gg

    def load_T(src_d, row0, dst, dcol0, dbufs):
        xi = nxt("xin", 2)
        dma_in("sp", xin[xi][:, :], src_d[row0:row0 + 128, :], B_xin[xi])
        for half in range(2):
            bi = nbank()
            for j in range(4):
                cidx = half * 4 + j
                P.op("pe", lambda e, bi=bi, j=j, cidx=cidx, xi=xi: e.transpose(
                    banks[bi][:, j * 128:(j + 1) * 128], xin[xi][:, cidx * 128:(cidx + 1) * 128], ident[:, :]),
                    reads=[B_xin[xi], B_const], writes=[Bbank[bi]])
            src = banks[bi][:, :].rearrange("p (j t) -> p j t", j=4)
            dsta = dst[:, half * 4:half * 4 + 4, dcol0:dcol0 + 128]
            if half == 0:
                act(dsta, src, AF.Copy, [Bbank[bi]], dbufs)
            else:
                dve_copy(dsta, src, [Bbank[bi]], dbufs)

    def rstd_from_bank(bs, n, nfeat, br):
        t = nxt("tmp", 4)
        act(tmps[t][:, 0:n], banks[bs][:, 0:n], AF.Ln, [Bbank[bs]], [B_tmp[t]], bias=EPS, scale=1.0 / nfeat)
        act(banks[br][:, 0:n], tmps[t][:, 0:n], AF.Exp, [B_tmp[t]], [Bbank[br]], scale=-0.5)

    def prenorm(src, sbufs, c0, n, dst, dbufs, dc0, l, which, c):
        gs, sh, _ = mod_scalars(l, which, c)
        bs = nbank()
        for k in range(KC):
            q = nxt("sq", 2)
            act(sqs[q][:, 0:n], src[:, k, c0:c0 + n], AF.Square, sbufs, [B_sq[q]])
            mm(bs, banks[bs][:, 0:n], ones[:, :], sqs[q][:, 0:n], k == 0, k == KC - 1, [B_sq[q], B_const])
        br = nbank()
        rstd_from_bank(bs, n, D, br)
        for k in range(KC):
            t = nxt("tmp", 4)
            dve_tt(tmps[t][:, 0:n], src[:, k, c0:c0 + n], banks[br][:, 0:n], ALU.mult, sbufs + [Bbank[br]], [B_tmp[t]])
            dve_ts(dst[:, k, dc0:dc0 + n], tmps[t][:, 0:n], gs(k), sh(k), ALU.mult, ALU.add,
                   [B_tmp[t], B_mod[l], B_der[l]], dbufs, eng="pool")

    class PostNorm:
        def __init__(self, c0, n, xbufs, l, which, c, ybuf=None):
            self.c0, self.n, self.xbufs, self.l, self.which, self.c = c0, n, xbufs, l, which, c
            self.yv, self.yb = ybuf if ybuf is not None else (ytmp, B_hT)
            self.bs = nbank()
            reserved.add(self.bs)
            self.cnt = 0
            self.pend = None

        def _stats(self):
            if self.pend is not None:
                q = self.pend
                n = self.n
                mm(self.bs, banks[self.bs][:, 0:n], ones[:, :], sqs[q][:, 0:n], self.cnt == 0, self.cnt == KC - 1,
                   [B_sq[q], B_const])
                self.cnt += 1
                self.pend = None

        def add(self, bi, dch):
            n = self.n
            self._stats()
            q = nxt("sq", 2)
            act(sqs[q][:, 0:n], banks[bi][:, 0:n], AF.Square, [Bbank[bi]], [B_sq[q]])
            yv = self.yv[:, dch, 0:n]
            act(yv, banks[bi][:, 0:n], AF.Copy, [Bbank[bi]], self.yb)
            self.pend = q

        def finish(self):
            n, c0 = self.n, self.c0
            self._stats()
            _, _, gg = mod_scalars(self.l, self.which, self.c)
            br = nbank()
            reserved.discard(self.bs)
            rstd_from_bank(self.bs, n, D, br)
            for dch in range(KC):
                t = nxt("tmp", 4)
                dve_tt(tmps[t][:, 0:n], self.yv[:, dch, 0:n], banks[br][:, 0:n], ALU.mult,
                       self.yb + [Bbank[br]], [B_tmp[t]])
                xa = xT[:, dch, c0:c0 + n]
                dve_stt(xa, tmps[t][:, 0:n], gg(dch), xa, ALU.mult, ALU.add,
                        [B_tmp[t], B_der[self.l]] + self.xbufs, self.xbufs)

    def linear_fm(bi, wfun, rhsfun, n, reads, nk=KC):
        for k in range(nk):
            mm(bi, banks[bi][:, 0:n], wfun(k), rhsfun(k), k == 0, k == nk - 1, reads, wt=n / 512.0)

    def rope_epilogue(ba, bb, n, cos_ap, sin_ap, out_ap, obufs, out_hi=None):
        t1 = nxt("tmp", 4)
        dve_tt(tmps[t1][:, 0:n], banks[ba][:, 0:n], cos_ap, ALU.mult, [Bbank[ba], B_const], [B_tmp[t1]])
        t2 = nxt("tmp", 4)
        dve_tt(tmps[t2][:, 0:n], banks[bb][:, 0:n], sin_ap, ALU.mult, [Bbank[bb], B_const], [B_tmp[t2]])
        if out_hi is None:
            dve_tt(out_ap, tmps[t1][:, 0:n], tmps[t2][:, 0:n], ALU.add, [B_tmp[t1], B_tmp[t2]], obufs, eng="pool")
        else:
            dve_tt(out_ap, tmps[t1][0:64, 0:n], tmps[t2][0:64, 0:n], ALU.add, [B_tmp[t1], B_tmp[t2]], obufs,
                   eng="pool")
            dve_tt(out_hi, tmps[t1][64:128, 0:n], tmps[t2][64:128, 0:n], ALU.add, [B_tmp[t1], B_tmp[t2]], obufs,
                   eng="pool")

    def rope_tile(wfun, rhsfun, n, reads, cos_ap, sin_ap, out_ap, obufs, out_hi=None, filler=None):
        ba = nbank()
        linear_fm(ba, wfun, rhsfun, n, reads)
        q = nxt("sq", 2)
        act(sqs[q][:, 0:n], banks[ba][:, 0:n], AF.Copy, [Bbank[ba]], [B_sq[q]])
        if filler is not None:
            filler()
        bb = nbank()
        mm(bb, banks[bb][:, 0:n], permb[:, :], sqs[q][:, 0:n], True, True, [B_sq[q], B_const], wt=n / 512.0)
        rope_epilogue(ba, bb, n, cos_ap, sin_ap, out_ap, obufs, out_hi=out_hi)

    SCALE = 0.125

    OBK = [4, 5]
    LBK = [6, 7]
    pstate = {"par": 0, "pending": {}}
    TAIL_DELAY = 60.0

    def maybe_flush(force_par=None):
        for par in list(pstate["pending"].keys()):
            created, fn = pstate["pending"][par]
            if par == force_par or pe_work["v"] - created >= TAIL_DELAY:
                del pstate["pending"][par]
                fn()

    def obank(slot, par):
        return OBK[slot]

    def run_pair(jobs, hook_after=2):
        nj = len(jobs)
        n = jobs[0][1]
        nb = len(jobs[0][2])

        def qk(ji, j):
            q_ap, _, blocks, qreads = jobs[ji]
            k_ap, kreads, _, _, mi = blocks[j]
            s_ = (j % 2) * 2 + ji
            mm(s_, banks[s_][:, 0:n], k_ap, q_ap, True, mi is None, qreads + kreads, wt=n / 512.0)
            if mi is not None:
                for g in range(n // 128):
                    mm(s_, banks[s_][:, g * 128:(g + 1) * 128], identb[:, :], maskb[:, mi, :], False,
                       g == n // 128 - 1, [B_const])

        for ji in range(nj):
            qk(ji, 0)
        for j in range(nb):
            if j + 1 < nb:
                for ji in range(nj):
                    qk(ji, j + 1)
            sp = (j % 2) * 2
            ei = j % 2
            act(Es[ei][:, 0:nj, 0:n], PS[:, sp:sp + nj, 0:n], AF.Exp, [Bbank[sp + ji] for ji in range(nj)],
                [B_E[ei]], scale=SCALE)
            for ji in range(nj):
                _, _, v_ap, vreads, _ = jobs[ji][2][j]
                bo, bl = OBK[ji], LBK[ji]
                mm(bo, banks[bo][:, 0:n], v_ap, Es[ei][:, ji, 0:n], j == 0, j == nb - 1, [B_E[ei]] + vreads,
                   wt=n / 512.0)
                mm(bl, banks[bl][:, 0:n], ones[:, :], Es[ei][:, ji, 0:n], j == 0, j == nb - 1, [B_E[ei], B_const],
                   wt=n / 512.0)
            if j % 2 == 1 or j == nb - 1:
                maybe_flush()
        return 0

    def flush_pending():
        for par in list(pstate["pending"].keys()):
            maybe_flush(force_par=par)

    for t in range(4):
        load_T(xp_d, t * 128, xT, t * 128, [B_xT[0]])
    for t in range(6):
        load_T(xw_d, t * 128, xT, 512 + t * 128, [B_xT[1] if t < 4 else B_xT[2]])

    xfT = ytmp
    for fc in range(4 if STAGE >= -3 else 0):
        for t in range(4):
            load_T(xf_d, fc * 512 + t * 128, xfT, t * 128, B_hT)
        if fc == 0:
            for piece in range(3):
                mod_piece(0, piece)
            mod_finish(0, parts=(0,))
        prenorm(xfT, B_hT, 0, 512, hfT, [B_hfT[fc]], fc * 512, 0, 0, 1)
    for ci, (c0, n) in enumerate(TCH if STAGE >= -3 else []):
        prenorm(xT, [B_xT[ci]], c0, n, hT, [B_hT[ci]], c0, 0, 0, 0 if ci == 0 else 1)

    for hd in range(KHEADS if STAGE >= -2 else 0):
        si = load_w(wd0_d[hd].rearrange("p k c -> p (k c)"), 3072)
        wv = slots[si][:, 0:3072].rearrange("p (k c) -> p k c", k=8)
        WQ, WK, WV = 0, 128, 256
        rs = [B_slot[si]]
        if "c" not in KSKIP:
            dma_in("sp", cst[:, 0:512].rearrange("p (j c) -> p j c", j=4), ckd_d[hd], B_cst)
            bi = nbank()
            for j in range(4):
                P.op("pe", lambda e, bi=bi, j=j: e.transpose(banks[bi][:, j * 128:(j + 1) * 128],
                                                            cst[:, j * 128:(j + 1) * 128], ident[:, :]),
                     reads=[B_cst, B_const], writes=[Bbank[bi]])
            act(kTh[:, 2560:3072], banks[bi][:, :], AF.Copy, [Bbank[bi]], [B_kTh[2]])
            dma_in("sp", cst[:, 512:1024], cvd_d[hd].rearrange("p j c -> p (j c)"), B_cstv)
            dve_copy(Vh[:, 2560:3072], cst[:, 512:1024], [B_cstv], [B_Vh[2]])
        bi = nbank()
        linear_fm(bi, lambda k: wv[:, k, WQ:WQ + 128], lambda k: hT[:, k, 0:512], 512, rs + [B_hT[0]])
        act(qTh[0:64, 0, 0:512], banks[bi][0:64, :], AF.Copy, [Bbank[bi]], [B_qTh[0]])
        act(qTh[64:128, 1, 0:512], banks[bi][64:128, :], AF.Copy, [Bbank[bi]], [B_qTh[0]])
        bi = nbank()
        linear_fm(bi, lambda k: wv[:, k, WK:WK + 128], lambda k: hT[:, k, 0:512], 512, rs + [B_hT[0]])
        act(kTh[:, 0:512], banks[bi][:, :], AF.Copy, [Bbank[bi]], [B_kTh[0]])
        maybe_flush()
        for t in range(0 if "t" in KSKIP else 4):
            bi = nbank()
            linear_fm(bi, lambda k, t=t: hT[:, k, t * 128:(t + 1) * 128], lambda k: wv[:, k, WK:WK + 256], 256,
                      rs + [B_hT[0]])
            oi = nxt("ost", 2)
            act(ostage[oi][:, 0:256], banks[bi][:, 0:256], AF.Copy, [Bbank[bi]], [B_ost[oi]])
            dve_copy(Vh[:, t * 128:(t + 1) * 128], banks[bi][:, 128:256], [Bbank[bi]], [B_Vh[0]])
            if "o" not in KSKIP:
                dma_out(ndk_d[t * 128:(t + 1) * 128, hd * 128:(hd + 1) * 128], ostage[oi][:, 0:128], B_ost[oi])
                dma_out(ndv_d[t * 128:(t + 1) * 128, hd * 128:(hd + 1) * 128], ostage[oi][:, 128:256], B_ost[oi])
        for ci in (() if "r" in KSKIP else (1, 2)):
            c0, n = TCH[ci]
            rope_tile(lambda k: wv[:, k, WQ:WQ + 128], lambda k: hT[:, k, c0:c0 + n], n, rs + [B_hT[ci]],
                      ropeW[:, 0, c0 - 512:c0 - 512 + n], ropeW[:, 1, c0 - 512:c0 - 512 + n],
                      qTh[0:64, 0, c0:c0 + n], [B_qTh[1]], out_hi=qTh[64:128, 1, c0:c0 + n])
            maybe_flush()
        for fc in range(0 if "f" in KSKIP else 4):
            f0 = fc * 512

            def vfill(f0=f0, fc=fc):
                bi = nbank()
                for t in range(4):
                    for k in range(KC):
                        mm(bi, banks[bi][:, t * 128:(t + 1) * 128], hfT[:, k, f0 + t * 128:f0 + (t + 1) * 128],
                           wv[:, k, WV:WV + 128], k == 0, k == KC - 1, rs + [B_hfT[fc]], wt=0.25)
                dve_copy(Vh[:, 512 + f0:512 + f0 + 512], banks[bi][:, :], [Bbank[bi]], [B_Vh[1]])

            rope_tile(lambda k: wv[:, k, WK:WK + 128], lambda k: hfT[:, k, f0:f0 + 512], 512, rs + [B_hfT[fc]],
                      ropeF[:, 0, f0:f0 + 512], ropeF[:, 1, f0:f0 + 512],
                      kTh[:, 512 + f0:512 + f0 + 512], [B_kTh[1]], filler=vfill)
            maybe_flush()

        def diff_pair(q0_ap, q1_ap, n, blocks, qreads, out_ap, obufs):
            run_pair([(q0_ap, n, blocks, qreads), (q1_ap, n, blocks, qreads)])
            par = pstate["par"]
            pstate["par"] ^= 1
            maybe_flush(force_par=par)
            T1, B_T1 = T1s[par], B_T1s[par]
            oA, oB = OBK
            ta = nxt("tmp", 4)
            act(tmps[ta][:, 0:n], banks[LBK[0]][:, 0:n], AF.Copy, [Bbank[LBK[0]]], [B_tmp[ta]])
            tb = nxt("tmp", 4)
            act(tmps[tb][:, 0:n], banks[LBK[1]][:, 0:n], AF.Copy, [Bbank[LBK[1]]], [B_tmp[tb]])
            tc = nxt("tmp", 4)
            dve_copy(tmps[tc][:, 0:n], banks[oA][:, 0:n], [Bbank[oA]], [B_tmp[tc]])
            td = nxt("tmp", 4)
            dve_copy(tmps[td][:, 0:n], banks[oB][:, 0:n], [Bbank[oB]], [B_tmp[td]])
            dve_recip(tmps[ta][:, 0:n], tmps[ta][:, 0:n], [B_tmp[ta]], [B_tmp[ta]])
            dve_tt(T1[:, 0:n], tmps[tc][:, 0:n], tmps[ta][:, 0:n], ALU.mult, [B_tmp[tc], B_tmp[ta]], [B_T1])
            dve_recip(tmps[tb][:, 0:n], tmps[tb][:, 0:n], [B_tmp[tb]], [B_tmp[tb]])
            dve_tt(tmps[td][:, 0:n], tmps[td][:, 0:n], tmps[tb][:, 0:n], ALU.mult, [B_tmp[td], B_tmp[tb]],
                   [B_tmp[td]])
            dve_stt(T1[:, 0:n], tmps[td][:, 0:n], lamt[:, 4:5], T1[:, 0:n], ALU.mult, ALU.add,
                    [B_tmp[td], B_lam, B_T1], [B_T1])

            def tail():
                q = nxt("sq", 2)
                act(sqs[q][:, 0:n], T1[:, 0:n], AF.Square, [B_T1], [B_sq[q]])
                bs = 2
                mm(bs, banks[bs][:, 0:n], ones[:, :], sqs[q][:, 0:n], True, True, [B_sq[q], B_const])
                br = 3
                rstd_from_bank(bs, n, 128, br)
                dve_stt(out_ap, T1[:, 0:n], lamt[:, 5:6], banks[br][:, 0:n], ALU.mult, ALU.mult,
                        [B_T1, B_lam, Bbank[br]], obufs)
            pstate["pending"][par] = (pe_work["v"], tail)

        for pb in range(0 if "p" in KSKIP else 2):
            blocks = []
            for j in range(2):
                kc = pb * 256 + j * 128
                blocks.append((kTh[:, kc:kc + 128], [B_kTh[0]], Vh[:, kc:kc + 128], [B_Vh[0]], None))
            diff_pair(qTh[:, 0, pb * 256:(pb + 1) * 256], qTh[:, 1, pb * 256:(pb + 1) * 256], 256, blocks,
                      [B_qTh[0]], oT[:, hd, pb * 256:(pb + 1) * 256], [B_oT[0]])
        for ci in (() if "w" in KSKIP else (1, 2)):
            c0, n = TCH[ci]
            blocks = []
            for j in range(16):
                kc = 512 + j * 128
                blocks.append((kTh[:, kc:kc + 128], [B_kTh[1]], Vh[:, kc:kc + 128], [B_Vh[1]], None))
            for j in range(4):
                kc = 2560 + j * 128
                blocks.append((kTh[:, kc:kc + 128], [B_kTh[2]], Vh[:, kc:kc + 128], [B_Vh[2]], None))
            diff_pair(qTh[:, 0, c0:c0 + n], qTh[:, 1, c0:c0 + n], n, blocks, [B_qTh[1]],
                      oT[:, hd, c0:c0 + n], [B_oT[ci]])
        if hd < 5:
            mod_piece(0, 3 + hd)
        if STAGE >= 1:
            mod_piece(1, hd)
    flush_pending()
    if KHEADS < 5:
        for piece in range(3 + KHEADS, 8):
            mod_piece(0, piece)
    mod_finish(0, parts=(1, 2, 3))

    def out_proj_l0(l):
        P.alias([B_yA, B_yA2], B_hfT)
        ybs = [(yA, [B_yA]), (yA2, [B_yA2]), (yA, [B_yA])]
        for ci, (c0, n) in enumerate(TCH):
            pn = PostNorm(c0, n, [B_xT[ci]], l, 0, 0 if ci == 0 else 1, ybuf=ybs[ci])
            for half in range(2):
                si = load_w(wo0_d[half].rearrange("p k c -> p (k c)"), 4096)
                wv = slots[si][:, 0:4096].rearrange("p (k c) -> p k c", k=8)
                for dl in range(4):
                    bi = nbank()
                    linear_fm(bi, lambda k: wv[:, k, dl * 128:(dl + 1) * 128], lambda k: oT[:, k, c0:c0 + n], n,
                              [B_slot[si], B_oT[ci]])
                    pn.add(bi, half * 4 + dl)
            pn.finish()

    def ffn(l, chunks):
        for (c0, n, xb, hb, ab, c) in chunks:
            prenorm(xT, xb, c0, n, hT, [hb], c0, l, 1, c)
        for piece in range(8):
            nf = min(3, NF - 3 * piece)
            si = load_w(wgu_d[l, piece, :, 0:nf].rearrange("p f k c -> p (f k c)"), nf * 2048)
            wv = slots[si][:, 0:nf * 2048].rearrange("p (f k c) -> p f k c", f=nf, k=8)
            for fl in range(nf):
                f = 3 * piece + fl
                for (c0, n, xb, hb, ab, c) in chunks:
                    bg = nbank()
                    linear_fm(bg, lambda k: wv[:, fl, k, 0:128], lambda k: hT[:, k, c0:c0 + n], n, [B_slot[si], hb])
                    bu = nbank()
                    linear_fm(bu, lambda k: wv[:, fl, k, 128:256], lambda k: hT[:, k, c0:c0 + n], n, [B_slot[si], hb])
                    t = nxt("tmp", 4)
                    act(tmps[t][:, 0:n], banks[bg][:, 0:n], AF.Silu, [Bbank[bg]], [B_tmp[t]])
                    dve_tt(aT[:, f, c0:c0 + n], tmps[t][:, 0:n], banks[bu][:, 0:n], ALU.mult,
                           [B_tmp[t], Bbank[bu]], [ab])
        for (c0, n, xb, hb, ab, c) in chunks:
            pn = PostNorm(c0, n, xb, l, 1, c)
            for j in range(4):
                si = load_w(wdn_d[l, j].rearrange("p d f c -> p (d f c)"), 5632)
                wv = slots[si][:, 0:5632].rearrange("p (d f c) -> p d f c", d=2, f=22)
                for dl in range(2):
                    bi = nbank()
                    for f in range(NF):
                        mm(bi, banks[bi][:, 0:n], wv[:, dl, f, :], aT[:, f, c0:c0 + n], f == 0, f == NF - 1,
                           [B_slot[si], ab])
                    pn.add(bi, 2 * j + dl)
            pn.finish()

    if STAGE >= -1:
        out_proj_l0(0)
    if STAGE >= 1:
        mod_finish(1)
    P.alias(B_aT, B_hfT + B_oT + [B_yA, B_yA2])
    if STAGE >= 0:
        ffn(0, [(TCH[i][0], TCH[i][1], [B_xT[i]], B_hT[i], B_aT[i], 0 if i == 0 else 1) for i in range(3)])

    if STAGE >= 2:
        P.alias(B_qTs + [B_kTs] + B_oS, B_aT)
        for ci, (c0, n) in enumerate(TCH):
            prenorm(xT, [B_xT[ci]], c0, n, hT, [B_hT[ci]], c0, 1, 0, 0 if ci == 0 else 1)
        OWN0 = 640
        B_hown = [B_hT[1], B_hT[2]]
        P.op("dve", lambda e: e.memset(kTs[:, :, :], 0.0), reads=[], writes=[B_kTs])
        VS = Vh[:, 0:3584].rearrange("p (t c) -> p t c", t=14)
        B_VS = Buf("VS")
        P.alias([B_VS], list(B_Vh))
        for half in range(2):
            sa = load_w(ws1q_d[half].rearrange("p k c -> p (k c)"), 4096)
            wa = slots[sa][:, 0:4096].rearrange("p (k c) -> p k c", k=8)
            for cl in range(4):
                cq = half * 4 + cl

                def pfill(cl=cl, cq=cq):
                    bi = nbank()
                    linear_fm(bi, lambda k: wa[:, k, cl * 128:(cl + 1) * 128], lambda k: hT[:, k, 0:512], 512,
                              [B_slot[sa], B_hT[0]])
                    act(qTs[:, cq, 0:512], banks[bi][:, :], AF.Copy, [Bbank[bi]], [B_qTs[0]])

                rope_tile(lambda k: wa[:, k, cl * 128:(cl + 1) * 128], lambda k: hT[:, k, OWN0:OWN0 + 512], 512,
                          [B_slot[sa]] + B_hown, ropeW[:, 0, 128:640], ropeW[:, 1, 128:640],
                          qTs[:, cq, 512:1024], [B_qTs[1]], filler=pfill)
        sk = load_w(ws1k_d.rearrange("p k c -> p (k c)"), 4096)
        wk = slots[sk][:, 0:4096].rearrange("p (k c) -> p k c", k=8)
        rsk = [B_slot[sk]]
        for p in range(2):
            bi = nbank()
            linear_fm(bi, lambda k: wk[:, k, p * 128:(p + 1) * 128], lambda k: hT[:, k, 0:512], 512, rsk + [B_hT[0]])
            act(kTs[0:64, 2 * p, 0:512], banks[bi][0:64, :], AF.Copy, [Bbank[bi]], [B_kTs])
            act(kTs[64:128, 2 * p + 1, 0:512], banks[bi][64:128, :], AF.Copy, [Bbank[bi]], [B_kTs])
            for ci in (1, 2):
                c0, n = TCH[ci]
                rope_tile(lambda k: wk[:, k, p * 128:(p + 1) * 128], lambda k: hT[:, k, c0:c0 + n], n,
                          rsk + [B_hT[ci]], ropeW[:, 0, c0 - 512:c0 - 512 + n], ropeW[:, 1, c0 - 512:c0 - 512 + n],
                          kTs[0:64, 2 * p, c0:c0 + n], [B_kTs], out_hi=kTs[64:128, 2 * p + 1, c0:c0 + n])
        for t in range(4):
            bi = nbank()
            for k in range(KC):
                mm(bi, banks[bi][:, 0:256], hT[:, k, t * 128:(t + 1) * 128], wk[:, k, 0:256], k == 0, k == KC - 1,
                   rsk + [B_hT[0]])
            for k in range(KC):
                mm(bi, banks[bi][:, 256:512], hT[:, k, t * 128:(t + 1) * 128], wk[:, k, 256:512], k == 0, k == KC - 1,
                   rsk + [B_hT[0]])
            oi = nxt("ost", 2)
            act(ostage[oi][:, 0:512], banks[bi][:, :], AF.Copy, [Bbank[bi]], [B_ost[oi]])
            dve_copy(VS[:, t, :], banks[bi][:, 256:512], [Bbank[bi]], [B_VS])
            dma_out(nsk_d[t * 128:(t + 1) * 128, :], ostage[oi][:, 0:256], B_ost[oi])
            dma_out(nsv_d[t * 128:(t + 1) * 128, :], ostage[oi][:, 256:512], B_ost[oi])
        for t in range(6):
            bi = nbank()
            c0 = 512 + t * 128
            for k in range(KC):
                mm(bi, banks[bi][:, 0:256], hT[:, k, c0:c0 + 128], wk[:, k, 256:512], k == 0, k == KC - 1,
                   rsk + [B_hT[1] if t < 4 else B_hT[2]])
            dve_copy(VS[:, 4 + t, :], banks[bi][:, 0:256], [Bbank[bi]], [B_VS])
        dma_in("sp", cst[:, :].rearrange("p (j c) -> p j c", j=4), cks_d, B_cst)
        for p in range(2):
            bi = nbank()
            for j in range(4):
                P.op("pe", lambda e, bi=bi, j=j, p=p: e.transpose(
                    banks[bi][:, j * 128:(j + 1) * 128], cst[:, j * 256 + p * 128:j * 256 + (p + 1) * 128], ident[:, :]),
                    reads=[B_cst, B_const], writes=[Bbank[bi]])
            act(kTs[0:64, 2 * p, 1280:1792], banks[bi][0:64, :], AF.Copy, [Bbank[bi]], [B_kTs])
            act(kTs[64:128, 2 * p + 1, 1280:1792], banks[bi][64:128, :], AF.Copy, [Bbank[bi]], [B_kTs])
        xi = nxt("xin", 2)
        dma_in("sp", xin[xi][:, :], cvs_d.rearrange("p j c -> p (j c)"), B_xin[xi])
        dve_copy(Vh[:, 2560:3584], xin[xi][:, :], [B_xin[xi]], [B_VS])

        def swa_finalize(bo, bl, ng, qn, kv, g0, oS_col0, obuf):
            r0 = (kv % 2) * 64
            pr = kv // 2
            n = ng * qn
            hh0 = kv * 4 + g0
            t = nxt("tmp", 4)
            dve_copy(tmps[t][r0:r0 + 64, 0:n], banks[bl][r0:r0 + 64, 0:n], [Bbank[bl]], [B_tmp[t]])
            to = nxt("tmp", 4)
            dve_copy(tmps[to][r0:r0 + 64, 0:n], banks[bo][r0:r0 + 64, 0:n], [Bbank[bo]], [B_tmp[to]])
            lv = tmps[t][r0:r0 + 64, 0:n].rearrange("p (g q) -> p g q", g=ng)
            sk = sinkE[r0:r0 + 64, hh0:hh0 + ng]
            sk_b = bass.AP(sk.tensor, sk.offset, [list(sk.ap[0]), list(sk.ap[1]), [0, qn]])
            dve_tt(lv, lv, sk_b, ALU.add, [B_tmp[t], B_lam], [B_tmp[t]])
            act(tmps[t][r0:r0 + 64, 0:n], tmps[t][r0:r0 + 64, 0:n], AF.Ln, [B_tmp[t]], [B_tmp[t]])
            act(tmps[t][r0:r0 + 64, 0:n], tmps[t][r0:r0 + 64, 0:n], AF.Exp, [B_tmp[t]], [B_tmp[t]], scale=-1.0)
            ov = tmps[to][r0:r0 + 64, 0:n].rearrange("p (g q) -> p g q", g=ng)
            dve_tt(oS[r0:r0 + 64, pr * 4 + g0:pr * 4 + g0 + ng, oS_col0:oS_col0 + qn], ov, lv, ALU.mult,
                   [B_tmp[to], B_tmp[t]], [obuf], eng="pool")

        def run_swa(joblist):
            for i in range(0, len(joblist), 2):
                grp = joblist[i:i + 2]
                par = run_pair([g[0] for g in grp])
                for slot, g in enumerate(grp):
                    swa_finalize(obank(slot, par), LBK[slot], *g[1])

        jl = []
        for pb in range(2):
            for kv in range(4):
                p = kv // 2
                for gh in range(2):
                    q_ap = qTs[:, p * 4 + gh * 2:p * 4 + gh * 2 + 2, pb * 256:(pb + 1) * 256]
                    blocks = []
                    for j in range(2):
                        kc = pb * 256 + j * 128
                        blocks.append((kTs[:, kv, kc:kc + 128], [B_kTs],
                                       VS[:, pb * 2 + j, p * 128:(p + 1) * 128], [B_VS], None))
                    jl.append(((q_ap, 512, blocks, [B_qTs[0]]), (2, 256, kv, gh * 2, pb * 256, B_oS[0])))
        for qb in range(4):
            for kv in range(4):
                p = kv // 2
                q_ap = qTs[:, p * 4:p * 4 + 4, 512 + qb * 128:512 + (qb + 1) * 128]
                blocks = []
                for j in range(4):
                    kc = 1280 + j * 128
                    blocks.append((kTs[:, kv, kc:kc + 128], [B_kTs],
                                   VS[:, 10 + j, p * 128:(p + 1) * 128], [B_VS], None))
                for dj, mi in ((0, 2 * qb), (1, None), (2, 2 * qb + 1)):
                    w = qb + dj
                    kc = 512 + w * 128
                    blocks.append((kTs[:, kv, kc:kc + 128], [B_kTs],
                                   VS[:, 4 + w, p * 128:(p + 1) * 128], [B_VS], mi))
                jl.append(((q_ap, 512, blocks, [B_qTs[1]]), (4, 128, kv, 0, 512 + qb * 128, B_oS[1])))
        run_swa(jl)
        L1CH = [(0, 512, [B_xT[0]], 0, 0, B_oS[0]), (OWN0, 512, [B_xT[1], B_xT[2]], 1, 512, B_oS[1])]
        P.alias([B_yA], B_qTs)
        for (c0, n, xb, c, oc0, ob) in L1CH:
            pn = PostNorm(c0, n, xb, 1, 0, c, ybuf=(yA, [B_yA]) if c0 == 0 else None)
            for half in range(2):
                si = load_w(wo1_d[half].rearrange("p k c -> p (k c)"), 4096)
                wv = slots[si][:, 0:4096].rearrange("p (k c) -> p k c", k=8)
                for dl in range(4):
                    bi = nbank()
                    linear_fm(bi, lambda k: wv[:, k, dl * 128:(dl + 1) * 128], lambda k: oS[:, k, oc0:oc0 + n], n,
                              [B_slot[si], ob])
                    pn.add(bi, half * 4 + dl)
            pn.finish()
        P.alias(B_aT, B_qTs + [B_kTs] + B_oS + [B_yA])
        if STAGE >= 3:
            ffn(1, [(0, 512, [B_xT[0]], B_hT[0], B_aT[0], 0),
                    (OWN0, 512, [B_xT[1], B_xT[2]], B_hT[1], B_aT[1], 1)])

    OWN0 = 640
    for t in range(8):
        c0 = t * 128 if t < 4 else OWN0 + (t - 4) * 128
        xb = [B_xT[0]] if t < 4 else [B_xT[1], B_xT[2]]
        xi = nxt("xin", 2)
        for half in range(2):
            bi = nbank()
            for j in range(4):
                cidx = half * 4 + j
                P.op("pe", lambda e, bi=bi, j=j, cidx=cidx, c0=c0: e.transpose(
                    banks[bi][:, j * 128:(j + 1) * 128], xT[:, cidx, c0:c0 + 128], ident[:, :]),
                    reads=xb + [B_const], writes=[Bbank[bi]])
            if half == 0:
                act(xin[xi][:, 0:512], banks[bi][:, :], AF.Copy, [Bbank[bi]], [B_xin[xi]])
            else:
                dve_copy(xin[xi][:, 512:1024], banks[bi][:, :], [Bbank[bi]], [B_xin[xi]])
        dst = yp_d[t * 128:(t + 1) * 128, :] if t < 4 else ys_d[(t - 4) * 128:(t - 3) * 128, :]
        dma_out(dst, xin[xi][:, :], B_xin[xi])

    P.finish()
    stats = P.emit()
    return nc, es, stats


def _rope_tables(pos):
    pos = np.asarray(pos)
    row = (pos // 64).astype(np.float32)
    col = (pos % 64).astype(np.float32)
    nf = 16
    inv = (np.float32(10000.0) ** (-np.arange(nf, dtype=np.float32) / np.float32(nf))).astype(np.float32)
    ar = row[:, None] * inv[None, :]
    ac = col[:, None] * inv[None, :]
    ang = np.concatenate([ar, ar, ac, ac], axis=-1).astype(np.float32)
    cos = np.cos(ang).astype(np.float32)
    sin = np.sin(ang).astype(np.float32)
    sign = np.concatenate([-np.ones(16), np.ones(16), -np.ones(16), np.ones(16)]).astype(np.float32)
    sin = sin * sign[None, :]
    cos2 = np.concatenate([cos, cos], axis=1).T
    sin2 = np.concatenate([sin, sin], axis=1).T
    return np.ascontiguousarray(cos2), np.ascontiguousarray(sin2)


_ROTSRC = np.concatenate([np.arange(16, 32), np.arange(0, 16), np.arange(48, 64), np.arange(32, 48)])


def _rot_cols(w):
    n = w.shape[1] // 64
    idx = (np.arange(n)[:, None] * 64 + _ROTSRC[None, :]).reshape(-1)
    return w[:, idx]


def _kmaj(w):
    return np.ascontiguousarray(w.reshape(8, 128, -1).transpose(1, 0, 2))


_PROG_CACHE = {}


def prep_inputs(x_prompt, x_sample, cache_diff_k, cache_diff_v, cache_swa_k, cache_swa_v, c, c_ctx,
                w_mod, b_mod, norm_g, w_qkv_diff, diff_lambda, diff_subln_g, w_o_diff,
                w_qkv_swa, swa_sink, w_o_swa, w_gate, w_up, w_down):
    f = np.float32
    A = lambda a: np.ascontiguousarray(np.asarray(a, dtype=f))
    x_prompt, x_sample = A(x_prompt), A(x_sample)
    cache_diff_k, cache_diff_v = A(cache_diff_k), A(cache_diff_v)
    cache_swa_k, cache_swa_v = A(cache_swa_k), A(cache_swa_v)
    c, c_ctx = A(c), A(c_ctx)
    w_mod, b_mod, norm_g = A(w_mod), A(b_mod), A(norm_g)
    w_qkv_diff, diff_lambda, diff_subln_g, w_o_diff = A(w_qkv_diff), A(diff_lambda), A(diff_subln_g), A(w_o_diff)
    w_qkv_swa, swa_sink, w_o_swa = A(w_qkv_swa), A(swa_sink), A(w_o_swa)
    w_gate, w_up, w_down = A(w_gate), A(w_up), A(w_down)

    wmod = np.ascontiguousarray(
        w_mod.reshape(2, 8, 128, 8, 768).transpose(0, 3, 2, 1, 4))
    wq, wk, wv = w_qkv_diff[0][:, 0:1024], w_qkv_diff[0][:, 1024:2048], w_qkv_diff[0][:, 2048:3072]
    wd0 = np.empty((8, 128, 8, 384), f)
    for h in range(8):
        s = slice(h * 128, (h + 1) * 128)
        wd0[h] = _kmaj(np.concatenate([wq[:, s], wk[:, s], wv[:, s]], axis=1))
    permT = np.zeros((128, 128), f)
    for blk in range(2):
        for o in range(64):
            permT[blk * 64 + _ROTSRC[o], blk * 64 + o] = 1.0
    wo0 = np.stack([_kmaj(w_o_diff[0][:, 0:512]), _kmaj(w_o_diff[0][:, 512:1024])])
    ws = w_qkv_swa[0]
    sq_cols = []
    for p in range(2):
        for g in range(4):
            for kvl in range(2):
                hh = (2 * p + kvl) * 4 + g
                sq_cols.append(np.arange(hh * 64, (hh + 1) * 64))
    sq_cols = np.concatenate(sq_cols)
    wsq = ws[:, 0:1024]
    wsq_p = wsq[:, sq_cols]
    ws1q = np.stack([_kmaj(wsq_p[:, 0:512]), _kmaj(wsq_p[:, 512:1024])])
    wsk, wsv = ws[:, 1024:1280], ws[:, 1280:1536]
    ws1k = _kmaj(np.concatenate([wsk, wsv], axis=1))
    wos_p = w_o_swa[0][sq_cols, :]
    wo1 = np.stack([_kmaj(wos_p[:, 0:512]), _kmaj(wos_p[:, 512:1024])])
    wgu_f = np.zeros((2, 24, 128, 8, 256), f)
    for l in range(2):
        g_ = w_gate[l].reshape(8, 128, 22, 128).transpose(2, 1, 0, 3)
        u_ = w_up[l].reshape(8, 128, 22, 128).transpose(2, 1, 0, 3)
        wgu_f[l, 0:22, :, :, 0:128] = g_
        wgu_f[l, 0:22, :, :, 128:256] = u_
    wgu = np.ascontiguousarray(wgu_f.reshape(2, 8, 3, 128, 8, 256).transpose(0, 1, 3, 2, 4, 5))
    wdn = np.ascontiguousarray(w_down.reshape(2, 22, 128, 4, 2, 128).transpose(0, 3, 2, 4, 1, 5))
    ident = np.eye(128, dtype=f)
    cosf, sinf = _rope_tables(np.arange(TFULL))
    ropef = np.ascontiguousarray(np.stack([cosf, sinf], axis=1))

    normg_l = norm_g.reshape(8, 8, 128).transpose(2, 0, 1).reshape(128, 64)
    bmod_l = b_mod.reshape(2, 48, 128).transpose(2, 0, 1).reshape(128, 96)
    subg_l = diff_subln_g.reshape(128, 1)
    lam_l = np.broadcast_to(diff_lambda.reshape(1, 256), (128, 256))
    sink_l = np.broadcast_to(swa_sink.reshape(1, 16), (128, 16))

    in_maps = []
    for core in range(NCORES):
        b, ch = core // 4, core % 4
        xp = x_prompt[2 * core:2 * core + 2].reshape(512, D)
        pos = np.arange(ch * 512 - 128, ch * 512 + 640)
        valid = (pos >= 0) & (pos < 2048)
        xw = np.zeros((TW, D), f)
        xw[valid] = x_sample[b, pos[valid]]
        cosw, sinw = _rope_tables(np.clip(pos, 0, 2047))
        ropew = np.ascontiguousarray(np.stack([cosw, sinw], axis=1))
        maskb = np.zeros((128, 8, 128), f)
        for qb in range(4):
            for side in range(2):
                w = qb + (0 if side == 0 else 2)
                kpos = pos[w * 128:(w + 1) * 128]
                qpos = pos[(qb + 1) * 128:(qb + 2) * 128]
                ok = ((kpos[:, None] >= 0) & (kpos[:, None] < 2048)
                      & (np.abs(qpos[None, :] - kpos[:, None]) <= 128))
                maskb[:, 2 * qb + side, :] = np.where(ok, 0.0, -30000.0)
        cond_l = np.stack([c_ctx, c[b]], axis=-1).reshape(8, 128, 2).transpose(1, 0, 2).reshape(128, 16)
        sm = np.ascontiguousarray(np.concatenate([cond_l, normg_l, bmod_l, subg_l, lam_l, sink_l], axis=1), dtype=f)
        ckd = np.ascontiguousarray(cache_diff_k[b, 0].reshape(4, 128, 8, 128).transpose(2, 1, 0, 3))
        cvd = np.ascontiguousarray(cache_diff_v[b, 0].reshape(4, 128, 8, 128).transpose(2, 1, 0, 3))
        cks = np.ascontiguousarray(cache_swa_k[b, 0].reshape(4, 128, 256).transpose(1, 0, 2))
        cvs = np.ascontiguousarray(cache_swa_v[b, 0].reshape(4, 128, 256).transpose(1, 0, 2))
        in_maps.append(dict(xp=np.ascontiguousarray(xp), xw=xw, xf=x_sample[b], ckd=ckd, cvd=cvd, cks=cks, cvs=cvs,
                            sm=sm, ident=ident, permT=permT, ropew=ropew, ropef=ropef, maskb=maskb, wmod=wmod, wd0=wd0,
                            wo0=wo0,
                            ws1q=ws1q, ws1k=ws1k, wo1=wo1, wgu=wgu, wdn=wdn))
    return in_maps


def kernel(**inputs):
    f = np.float32
    in_maps = prep_inputs(**inputs)
    if "nc" not in _PROG_CACHE:
        nc, es, stats = build_program()
        _PROG_CACHE["nc"] = (nc, es)
        if os.environ.get("KVERBOSE"):
            print("ops per engine:", stats)
    nc, _ = _PROG_CACHE["nc"]
    res = run_bass_kernel_spmd(nc, in_maps, core_ids=list(range(NCORES)))
    R = res.results
    y_prompt = np.concatenate([R[i]["yp"].reshape(2, 256, D) for i in range(NCORES)], axis=0)
    y_sample = np.stack([np.concatenate([R[b * 4 + ch]["ys"] for ch in range(4)], axis=0) for b in range(2)], axis=0)
    ndk = np.concatenate([R[i]["ndk"].reshape(2, 1, 256, 8, 128) for i in range(NCORES)], axis=0)
    ndv = np.concatenate([R[i]["ndv"].reshape(2, 1, 256, 8, 128) for i in range(NCORES)], axis=0)
    nsk = np.concatenate([R[i]["nsk"].reshape(2, 1, 256, 4, 64) for i in range(NCORES)], axis=0)
    nsv = np.concatenate([R[i]["nsv"].reshape(2, 1, 256, 4, 64) for i in range(NCORES)], axis=0)
    return (y_prompt.astype(f), y_sample.astype(f), ndk.astype(f), ndv.astype(f), nsk.astype(f), nsv.astype(f))
```

```python
import os
import numpy as np
import concourse.bass as bass
import concourse.mybir as mybir
from concourse.bass_utils import run_bass_kernel_spmd
from contextlib import ExitStack

F32 = mybir.dt.float32
BF16 = mybir.dt.bfloat16
AF = mybir.ActivationFunctionType
ALU = mybir.AluOpType

D = 1024
KC = 8
DFF = 2816
NF = 22
TP = 512
TW = 768
TT = 1280
TFULL = 2048
LC = 512
EPS = 1e-6
NCORES = 8
STAGE = int(os.environ.get("KSTAGE", "99"))
KHEADS = int(os.environ.get("KHEADS", "8"))
KSKIP = os.environ.get("KSKIP", "")


class Buf:
    __slots__ = ("name", "writers", "readers", "dsem", "ndma", "war", "excl")

    def __init__(self, name, excl=False):
        self.name = name
        self.excl = excl
        self.writers = []
        self.readers = []
        self.war = []
        self.dsem = None
        self.ndma = 0


class Op:
    __slots__ = ("eng", "idx", "fn", "deps", "dma", "buf", "ordinal", "flag", "count", "waits", "ring")

    def __init__(self, eng, idx, fn):
        self.eng = eng
        self.idx = idx
        self.fn = fn
        self.deps = []
        self.dma = False
        self.buf = None
        self.ordinal = 0
        self.flag = False
        self.count = 0
        self.waits = []
        self.ring = False


ENGS = ["pe", "act", "dve", "pool", "sp"]


class Prog:
    def __init__(self, nc, es):
        self.nc = nc
        self.es = es
        self.ops = {e: [] for e in ENGS}
        self.dma_bufs = []
        self.out_dmas = []

    def op(self, eng, fn, reads=(), writes=(), dma_buf=None, is_out=False, ring=False):
        o = Op(eng, len(self.ops[eng]), fn)
        o.ring = ring
        deps = o.deps
        for b in reads:
            for w in b.writers:
                deps.append((w, True))
            if b.excl:
                for r in b.readers:
                    if r.eng != eng:
                        deps.append((r, False))
            b.readers.append(o)
        for b in writes:
            if b.readers:
                b.war = [r for r in b.readers if r is not o]
                b.writers = [o]
                b.readers = []
            else:
                b.writers.append(o)
            for r in b.war:
                deps.append((r, False))
        if dma_buf is not None:
            o.dma = True
            o.buf = dma_buf
            if dma_buf.dsem is None:
                self.dma_bufs.append(dma_buf)
                dma_buf.dsem = True
            dma_buf.ndma += 1
            o.ordinal = dma_buf.ndma
            if is_out:
                self.out_dmas.append(o)
        self.ops[eng].append(o)
        return o

    def alias(self, new_bufs, old_bufs):
        pend = []
        for b in old_bufs:
            pend.extend(b.readers)
            pend.extend(b.writers)
        for nb in new_bufs:
            nb.readers.extend(pend)
            nb.war = []

    def finish(self):
        o = Op("sp", len(self.ops["sp"]), None)
        for d in self.out_dmas:
            o.deps.append((d, True))
        self.ops["sp"].append(o)

    def emit(self):
        nc = self.nc
        es = self.es
        esem = {e: es.enter_context(nc.semaphore("s_" + e)) for e in ["pe", "act", "dve", "pool"]}
        for i, b in enumerate(self.dma_bufs):
            b.dsem = es.enter_context(nc.semaphore("d%d" % i))
        for e in ENGS:
            waited = {}
            for o in self.ops[e]:
                need = {}
                for (d, raw) in o.deps:
                    if d.dma:
                        key = ("d", id(d.buf))
                        val = d.ordinal
                        if waited.get(key, 0) >= val:
                            continue
                        if need.get(key, (0, None))[0] < val:
                            need[key] = (val, d)
                    else:
                        if d.eng == e and e == "pe":
                            continue
                        key = ("e", d.eng)
                        val = d.idx + 1
                        if waited.get(key, 0) >= val:
                            continue
                        if need.get(key, (0, None))[0] < val:
                            need[key] = (val, d)
                for key, (val, d) in need.items():
                    waited[key] = val
                    if not d.dma:
                        d.flag = True
                    o.waits.append(d)
        for e in ["pe", "act", "dve", "pool"]:
            c = 0
            for o in self.ops[e]:
                if o.flag and not o.dma:
                    c += 1
                    o.count = c
        handles = {"pe": "tensor", "act": "scalar", "dve": "vector", "pool": "gpsimd", "sp": "sync"}
        stats = {}
        with nc.Block() as block:
            for e in ENGS:
                ops = self.ops[e]
                stats[e] = len(ops)

                def body(eng, ops=ops, e=e):
                    for o in ops:
                        for d in o.waits:
                            if d.dma:
                                eng.wait_ge(d.buf.dsem, 16 * d.ordinal)
                            else:
                                eng.wait_ge(esem[d.eng], d.count)
                        if o.fn is None:
                            continue
                        inst = o.fn(eng)
                        if o.ring:
                            assert not o.flag
                            inst.then_inc(self.ring_sem, 16)
                        elif o.dma:
                            inst.then_inc(o.buf.dsem, 16)
                        elif o.flag:
                            inst.then_inc(esem[e], 1)

                getattr(block, handles[e])(body)
        return stats


def build_program():
    nc = bass.Bass("TRN2", target_bir_lowering=False, monotonic_sem_count=0)
    es = ExitStack()
    P = Prog(nc, es)

    def din(name, shape):
        return nc.dram_tensor(name, list(shape), F32, kind="ExternalInput").ap()

    def dout(name, shape):
        return nc.dram_tensor(name, list(shape), F32, kind="ExternalOutput").ap()

    xp_d = din("xp", [TP, D])
    xw_d = din("xw", [TW, D])
    xf_d = din("xf", [TFULL, D])
    ckd_d = din("ckd", [8, 128, 4, 128])
    cvd_d = din("cvd", [8, 128, 4, 128])
    cks_d = din("cks", [128, 4, 256])
    cvs_d = din("cvs", [128, 4, 256])
    NSM = 16 + 64 + 96 + 1 + 256 + 16
    sm_d = din("sm", [128, NSM])
    ident_d = din("ident", [128, 128])
    ropew_d = din("ropew", [128, 2, TW])
    ropef_d = din("ropef", [128, 2, TFULL])
    maskb_d = din("maskb", [128, 8, 128])
    wmod_d = din("wmod", [2, 8, 128, 8, 768])
    wd0_d = din("wd0", [8, 128, 8, 384])
    permT_d = din("permT", [128, 128])
    wo0_d = din("wo0", [2, 128, 8, 512])
    ws1q_d = din("ws1q", [2, 128, 8, 512])
    ws1k_d = din("ws1k", [128, 8, 512])
    wo1_d = din("wo1", [2, 128, 8, 512])
    wgu_d = din("wgu", [2, 8, 128, 3, 8, 256])
    wdn_d = din("wdn", [2, 4, 128, 2, 22, 128])

    yp_d = dout("yp", [TP, D])
    ys_d = dout("ys", [512, D])
    ndk_d = dout("ndk", [TP, 1024])
    ndv_d = dout("ndv", [TP, 1024])
    nsk_d = dout("nsk", [TP, 256])
    nsv_d = dout("nsv", [TP, 256])

    def sb(name, shape, dt):
        return es.enter_context(nc.sbuf_tensor(name, list(shape), dt))

    xT = sb("xT", [128, KC, TT], F32)
    hT = sb("hT", [128, KC, TT], BF16)
    ytmp = hT.bitcast(F32)
    BIG = sb("BIG", [128, 28160], BF16)
    slots = [sb("wslot%d" % i, [128, 6144], BF16) for i in range(2)]
    kTh = sb("kTh", [128, 3072], BF16)
    Vh = sb("Vh", [128, 3584], BF16)
    qTh = sb("qTh", [128, 2, TT], BF16)
    Es = [sb("E%d" % i, [128, 2, 512], BF16) for i in range(2)]
    xin = [sb("xin%d" % i, [128, 1024], F32) for i in range(2)]
    cst = sb("cst", [128, 1024], F32)
    ropeW = sb("ropeW", [128, 2, TW], F32)
    ropeF = sb("ropeF", [128, 2, TFULL], BF16)
    maskb = sb("maskbs", [128, 8, 128], BF16)
    ident = sb("idents", [128, 128], F32)
    identb = sb("identb", [128, 128], BF16)
    ones = sb("ones", [128, 128], BF16)
    permb = sb("permb", [128, 128], BF16)
    sm = sb("sms", [128, NSM], F32)
    modT = sb("modT", [128, 2, 48, 2], F32)
    der = sb("der", [128, 2, 4, 8, 2], F32)
    scb = sb("scb", [128, 8, 2], BF16)
    lamt = sb("lamt", [128, 8], F32)
    sinkE = sb("sinkE", [128, 16], F32)
    sqs = [sb("sq%d" % i, [128, 512], BF16) for i in range(2)]
    tmps = [sb("tmp%d" % i, [128, 512], F32) for i in range(4)]
    T1s = [sb("T1a", [128, 512], F32), sb("T1b", [128, 512], F32)]
    ostage = [xin[i] for i in range(2)]

    PS = es.enter_context(nc.psum_tensor("PS", [128, 8, 512], F32))

    class BankView:
        def __init__(self, i):
            self.i = i

        def __getitem__(self, idx):
            return PS[idx[0], self.i, idx[1]]

    banks = [BankView(i) for i in range(8)]
    Bbank = [Buf("bank%d" % i, excl=True) for i in range(8)]

    B_xT = [Buf("xT_p"), Buf("xT_w0"), Buf("xT_w1")]
    B_hT = [Buf("hT_p"), Buf("hT_w0"), Buf("hT_w1")]
    B_slot = [Buf("slot0"), Buf("slot1")]
    B_kTh = Buf("kTh_p"), Buf("kTh_f"), Buf("kTh_c")
    B_Vh = Buf("Vh_p"), Buf("Vh_f"), Buf("Vh_c")
    B_qTh = [Buf("qTh_p"), Buf("qTh_w")]
    B_E = [Buf("E%d" % i) for i in range(2)]
    B_xin = [Buf("xin0"), Buf("xin1")]
    B_cst = Buf("cst")
    B_cstv = Buf("cstv")
    B_const = Buf("consts")
    B_sm = Buf("sm")
    B_mod = [Buf("mod0"), Buf("mod1")]
    B_der = [Buf("der0"), Buf("der1")]
    B_scb = Buf("scb")
    B_lam = Buf("lam")
    B_sq = [Buf("sq0"), Buf("sq1")]
    B_tmp = [Buf("tmp%d" % i) for i in range(4)]
    B_T1s = [Buf("T1a"), Buf("T1b")]
    B_ost = B_xin
    B_hfT = [Buf("hfT%d" % i) for i in range(4)]
    B_oT = [Buf("oT_p"), Buf("oT_w0"), Buf("oT_w1")]
    B_aT = [Buf("aT0"), Buf("aT1"), Buf("aT2")]
    B_qTs = [Buf("qTs_p"), Buf("qTs_s")]
    B_kTs = Buf("kTs")
    B_oS = [Buf("oS_p"), Buf("oS_s")]

    hfT = BIG[:, 0:16384].rearrange("p (k t) -> p k t", k=8)
    oT = BIG[:, 16384:16384 + 10240].rearrange("p (k t) -> p k t", k=8)
    aT = BIG[:, 0:28160].rearrange("p (f t) -> p f t", f=22)
    qTs = BIG[:, 0:8192].rearrange("p (k t) -> p k t", k=8)
    kTs = BIG[:, 8192:8192 + 7168].rearrange("p (k t) -> p k t", k=4)
    oS = BIG[:, 15360:15360 + 8192].rearrange("p (h t) -> p h t", h=8)

    BIGf = BIG.bitcast(F32)
    yA = BIGf[:, 0:4096].rearrange("p (d t) -> p d t", d=8)
    yA2 = BIGf[:, 4096:8192].rearrange("p (d t) -> p d t", d=8)
    B_yA, B_yA2 = Buf("yA"), Buf("yA2")
    TCH = [(0, 512), (512, 512), (1024, 256)]

    rr = {"bank": 0, "tmp": 0, "sq": 0, "slot": 0, "xin": 0, "ost": 0, "E": 0}

    def nxt(kind, n):
        v = rr[kind]
        rr[kind] = (v + 1) % n
        return v

    reserved = set()

    def nbank():
        while True:
            b = nxt("bank", 8)
            if b not in reserved:
                return b

    open_grp = {}

    pe_work = {"v": 0.0}

    def mm(bi, out_ap, lhsT, rhs, start, stop, reads, wt=1.0):
        pe_work["v"] += wt
        if start and open_grp.get(bi):
            import traceback
            traceback.print_stack(limit=6)
            print("OPEN GROUP on bank", bi, "opened at:", open_grp[bi])
        if start:
            import traceback
            open_grp[bi] = "".join(traceback.format_stack(limit=5)[:-1])
        if stop:
            open_grp[bi] = None
        P.op("pe", lambda e: e.matmul(out_ap, lhsT, rhs, start=start, stop=stop),
             reads=reads, writes=[Bbank[bi]])

    def act(out_ap, in_ap, func, reads, writes, bias=None, scale=None):
        kw = {}
        if bias is not None:
            kw["bias"] = bias
        if scale is not None:
            kw["scale"] = scale
        P.op("act", lambda e: e.activation(out_ap, in_ap, func, **kw), reads=reads, writes=writes)

    def dve_tt(out_ap, a, b, op, reads, writes, eng="dve"):
        P.op(eng, lambda e: e.tensor_tensor(out_ap, a, b, op), reads=reads, writes=writes)

    def dve_stt(out_ap, a, scalar, b, op0, op1, reads, writes):
        P.op("dve", lambda e: e.scalar_tensor_tensor(out_ap, a, scalar, b, op0, op1), reads=reads, writes=writes)

    def dve_ts(out_ap, a, s1, s2, op0, op1, reads, writes, eng="dve"):
        if op1 is None:
            P.op(eng, lambda e: e.tensor_scalar(out_ap, a, s1, None, op0), reads=reads, writes=writes)
        else:
            P.op(eng, lambda e: e.tensor_scalar(out_ap, a, s1, s2, op0, op1), reads=reads, writes=writes)

    def dve_copy(out_ap, in_ap, reads, writes, eng="dve"):
        P.op(eng, lambda e: e.tensor_copy(out_ap, in_ap), reads=reads, writes=writes)

    def dve_recip(out_ap, in_ap, reads, writes):
        P.op("dve", lambda e: e.reciprocal(out_ap, in_ap), reads=reads, writes=writes)

    def dma_in(eng, out_ap, in_ap, buf, extra_writes=(), cast=False):
        if eng == "pool":
            P.op("pool", lambda e: e.dma_start(out=out_ap, in_=in_ap, max_dma_last_dim=4096),
                 reads=[], writes=[buf] + list(extra_writes), dma_buf=Buf("sw"))
        elif cast:
            P.op(eng, lambda e: e.dma_start(out=out_ap, in_=in_ap, max_dma_last_dim=4096),
                 reads=[], writes=[buf] + list(extra_writes), dma_buf=buf)
        else:
            P.op(eng, lambda e: e.dma_start(out=out_ap, in_=in_ap),
                 reads=[], writes=[buf] + list(extra_writes), dma_buf=buf)

    def dma_out(out_ap, in_ap, buf):
        P.op("sp", lambda e: e.dma_start(out=out_ap, in_=in_ap), reads=[buf], writes=[], dma_buf=buf, is_out=True)

    def load_w(srcs, nelem, view=None, parts=128, force=None):
        if force is None:
            si, off, buf = nxt("slot", 2), 0, None
            buf = B_slot[si]
        else:
            si, off, buf = force
        if not isinstance(srcs, (list, tuple)):
            srcs = [srcs]
        for i, src_ap in enumerate(srcs):
            dst = slots[si][0:parts, off + i * nelem:off + (i + 1) * nelem]
            dma_in("pool", dst, src_ap, buf, cast=True)
        return si

    dma_in("sp", sm[:, :], sm_d, B_sm)
    dma_in("sp", ident[:, :], ident_d, B_const)
    dma_in("sp", ropeW[:, :, :], ropew_d, B_const)
    dma_in("pool", ropeF[:, :, :].rearrange("p a t -> p (a t)"), ropef_d.rearrange("p a t -> p (a t)"), B_const, cast=True)
    dma_in("pool", maskb[:, :, :].rearrange("p a t -> p (a t)"), maskb_d.rearrange("p a t -> p (a t)"), B_const, cast=True)
    P.op("dve", lambda e: e.memset(ones[:, :], 1.0), reads=[], writes=[B_const])
    dve_copy(identb[:, :], ident[:, :], [B_const], [B_const])
    P.op("dve", lambda e: e.memset(qTh[:, :, :], 0.0), reads=[], writes=[B_qTh[0], B_qTh[1]])
    dma_in("sp", cst[:, 0:128], permT_d, B_cst)
    dve_copy(permb[:, :], cst[:, 0:128], [B_cst, B_const], [B_const])

    O_COND, O_NG, O_BM, O_SUBG, O_LAM, O_SINK = 0, 16, 80, 176, 177, 433
    cond_v = sm[:, O_COND:O_COND + 16].rearrange("p (k c) -> p k c", c=2)
    act(scb[:, :, :], cond_v, AF.Silu, [B_sm], [B_scb])
    LAM_INIT = 0.8 - 0.6 * float(np.exp(-0.3 * 0))
    lam_v = sm[:, O_LAM:O_LAM + 256].rearrange("p (a d) -> p a d", a=4)
    P.op("dve", lambda e: e.tensor_tensor(tmps[0][:, 0:64], lam_v[:, 0, :], lam_v[:, 1, :], ALU.mult),
         reads=[B_sm], writes=[B_tmp[0]])
    P.op("dve", lambda e: e.tensor_tensor(tmps[0][:, 64:128], lam_v[:, 2, :], lam_v[:, 3, :], ALU.mult),
         reads=[B_sm], writes=[B_tmp[0]])
    P.op("dve", lambda e: e.reduce_sum(lamt[:, 0:2], tmps[0][:, 0:128].rearrange("p (a d) -> p a d", a=2),
                                       mybir.AxisListType.X), reads=[B_tmp[0]], writes=[B_lam])
    act(lamt[:, 2:4], lamt[:, 0:2], AF.Exp, [B_lam], [B_lam])
    dve_tt(lamt[:, 4:5], lamt[:, 3:4], lamt[:, 2:3], ALU.subtract, [B_lam], [B_lam])
    dve_ts(lamt[:, 4:5], lamt[:, 4:5], -LAM_INIT, None, ALU.add, None, [B_lam], [B_lam])
    dve_ts(lamt[:, 5:6], sm[:, O_SUBG:O_SUBG + 1], 1.0 - LAM_INIT, None, ALU.mult, None, [B_sm, B_lam], [B_lam])
    act(sinkE[:, :], sm[:, O_SINK:O_SINK + 16], AF.Exp, [B_sm], [B_lam])

    def mod_load(l, piece, force=None):
        si = load_w(wmod_d[l, piece].rearrange("p k c -> p (k c)"), 6144, force=force)
        return (l, piece, si, B_slot[si] if force is None else force[2])

    def mod_compute(h):
        l, piece, si, sbuf = h
        bi = nbank()
        wv = slots[si][:, 0:6144].rearrange("p (k c) -> p k c", k=8)
        for oc in range(6):
            for k in range(KC):
                mm(bi, banks[bi][:, oc * 2:oc * 2 + 2], wv[:, k, oc * 128:(oc + 1) * 128], scb[:, k, :],
                   k == 0, k == KC - 1, [sbuf, B_scb], wt=0.15)
        bv = banks[bi][:, 0:12].rearrange("p (o c) -> p o c", c=2)
        o0 = piece * 6
        for c in range(2):
            dve_tt(modT[:, l, o0:o0 + 6, c], bv[:, :, c], sm[:, O_BM + l * 48 + o0:O_BM + l * 48 + o0 + 6], ALU.add,
                   [Bbank[bi], B_sm], [B_mod[l]])

    def mod_piece(l, piece, force=None):
        mod_compute(mod_load(l, piece, force=force))

    def mod_finish(l, parts=(0, 1, 2, 3)):
        for c in range(2):
            def g(n):
                return sm[:, O_NG + (l * 4 + n) * 8: O_NG + (l * 4 + n) * 8 + 8]
            if 0 in parts:
                dve_stt(der[:, l, 0, :, c], modT[:, l, 8:16, c], 1.0, g(0), ALU.add, ALU.mult, [B_mod[l], B_sm],
                        [B_der[l]])
            if 1 in parts:
                dve_tt(der[:, l, 1, :, c], modT[:, l, 16:24, c], g(1), ALU.mult, [B_mod[l], B_sm], [B_der[l]])
            if 2 in parts:
                dve_stt(der[:, l, 2, :, c], modT[:, l, 32:40, c], 1.0, g(2), ALU.add, ALU.mult, [B_mod[l], B_sm],
                        [B_der[l]])
            if 3 in parts:
                dve_tt(der[:, l, 3, :, c], modT[:, l, 40:48, c], g(3), ALU.mult, [B_mod[l], B_sm], [B_der[l]])

    def modulation(l):
        for piece in range(8):
            mod_piece(l, piece)
        mod_finish(l)

    def mod_scalars(l, which, c):
        def gs(k):
            return der[:, l, 2 * which, k, c:c + 1]

        def sh(k):
            return modT[:, l, 24 * which + k, c:c + 1]

        def gg(k):
            return der[:, l, 2 * which + 1, k, c:c + 1]
        return gs, sh, gg

    def load_T(src_d, row0, dst, dcol0, dbufs):
        xi = nxt("xin", 2)
        dma_in("sp", xin[xi][:, :], src_d[row0:row0 + 128, :], B_xin[xi])
        for half in range(2):
            bi = nbank()
            for j in range(4):
                cidx = half * 4 + j
                P.op("pe", lambda e, bi=bi, j=j, cidx=cidx, xi=xi: e.transpose(
                    banks[bi][:, j * 128:(j + 1) * 128], xin[xi][:, cidx * 128:(cidx + 1) * 128], ident[:, :]),
                    reads=[B_xin[xi], B_const], writes=[Bbank[bi]])
            src = banks[bi][:, :].rearrange("p (j t) -> p j t", j=4)
            dsta = dst[:, half * 4:half * 4 + 4, dcol0:dcol0 + 128]
            if half == 0:
                act(dsta, src, AF.Copy, [Bbank[bi]], dbufs)
            else:
                dve_copy(dsta, src, [Bbank[bi]], dbufs)

    def rstd_from_bank(bs, n, nfeat, br):
        t = nxt("tmp", 4)
        act(tmps[t][:, 0:n], banks[bs][:, 0:n], AF.Ln, [Bbank[bs]], [B_tmp[t]], bias=EPS, scale=1.0 / nfeat)
        act(banks[br][:, 0:n], tmps[t][:, 0:n], AF.Exp, [B_tmp[t]], [Bbank[br]], scale=-0.5)

    def prenorm(src, sbufs, c0, n, dst, dbufs, dc0, l, which, c):
        gs, sh, _ = mod_scalars(l, which, c)
        bs = nbank()
        for k in range(KC):
            q = nxt("sq", 2)
            act(sqs[q][:, 0:n], src[:, k, c0:c0 + n], AF.Square, sbufs, [B_sq[q]])
            mm(bs, banks[bs][:, 0:n], ones[:, :], sqs[q][:, 0:n], k == 0, k == KC - 1, [B_sq[q], B_const])
        br = nbank()
        rstd_from_bank(bs, n, D, br)
        for k in range(KC):
            t = nxt("tmp", 4)
            dve_tt(tmps[t][:, 0:n], src[:, k, c0:c0 + n], banks[br][:, 0:n], ALU.mult, sbufs + [Bbank[br]], [B_tmp[t]])
            dve_ts(dst[:, k, dc0:dc0 + n], tmps[t][:, 0:n], gs(k), sh(k), ALU.mult, ALU.add,
                   [B_tmp[t], B_mod[l], B_der[l]], dbufs, eng="pool")

    class PostNorm:
        def __init__(self, c0, n, xbufs, l, which, c, ybuf=None):
            self.c0, self.n, self.xbufs, self.l, self.which, self.c = c0, n, xbufs, l, which, c
            self.yv, self.yb = ybuf if ybuf is not None else (ytmp, B_hT)
            self.bs = nbank()
            reserved.add(self.bs)
            self.cnt = 0
            self.pend = None

        def _stats(self):
            if self.pend is not None:
                q = self.pend
                n = self.n
                mm(self.bs, banks[self.bs][:, 0:n], ones[:, :], sqs[q][:, 0:n], self.cnt == 0, self.cnt == KC - 1,
                   [B_sq[q], B_const])
                self.cnt += 1
                self.pend = None

        def add(self, bi, dch):
            n = self.n
            self._stats()
            q = nxt("sq", 2)
            act(sqs[q][:, 0:n], banks[bi][:, 0:n], AF.Square, [Bbank[bi]], [B_sq[q]])
            yv = self.yv[:, dch, 0:n]
            act(yv, banks[bi][:, 0:n], AF.Copy, [Bbank[bi]], self.yb)
            self.pend = q

        def finish(self):
            n, c0 = self.n, self.c0
            self._stats()
            _, _, gg = mod_scalars(self.l, self.which, self.c)
            br = nbank()
            reserved.discard(self.bs)
            rstd_from_bank(self.bs, n, D, br)
            for dch in range(KC):
                t = nxt("tmp", 4)
                dve_tt(tmps[t][:, 0:n], self.yv[:, dch, 0:n], banks[br][:, 0:n], ALU.mult,
                       self.yb + [Bbank[br]], [B_tmp[t]])
                xa = xT[:, dch, c0:c0 + n]
                dve_stt(xa, tmps[t][:, 0:n], gg(dch), xa, ALU.mult, ALU.add,
                        [B_tmp[t], B_der[self.l]] + self.xbufs, self.xbufs)

    def linear_fm(bi, wfun, rhsfun, n, reads, nk=KC):
        for k in range(nk):
            mm(bi, banks[bi][:, 0:n], wfun(k), rhsfun(k), k == 0, k == nk - 1, reads, wt=n / 512.0)

    def rope_epilogue(ba, bb, n, cos_ap, sin_ap, out_ap, obufs, out_hi=None):
        t1 = nxt("tmp", 4)
        dve_tt(tmps[t1][:, 0:n], banks[ba][:, 0:n], cos_ap, ALU.mult, [Bbank[ba], B_const], [B_tmp[t1]])
        t2 = nxt("tmp", 4)
        dve_tt(tmps[t2][:, 0:n], banks[bb][:, 0:n], sin_ap, ALU.mult, [Bbank[bb], B_const], [B_tmp[t2]])
        if out_hi is None:
            dve_tt(out_ap, tmps[t1][:, 0:n], tmps[t2][:, 0:n], ALU.add, [B_tmp[t1], B_tmp[t2]], obufs, eng="pool")
        else:
            dve_tt(out_ap, tmps[t1][0:64, 0:n], tmps[t2][0:64, 0:n], ALU.add, [B_tmp[t1], B_tmp[t2]], obufs,
                   eng="pool")
            dve_tt(out_hi, tmps[t1][64:128, 0:n], tmps[t2][64:128, 0:n], ALU.add, [B_tmp[t1], B_tmp[t2]], obufs,
                   eng="pool")

    def rope_tile(wfun, rhsfun, n, reads, cos_ap, sin_ap, out_ap, obufs, out_hi=None, filler=None):
        ba = nbank()
        linear_fm(ba, wfun, rhsfun, n, reads)
        q = nxt("sq", 2)
        act(sqs[q][:, 0:n], banks[ba][:, 0:n], AF.Copy, [Bbank[ba]], [B_sq[q]])
        if filler is not None:
            filler()
        bb = nbank()
        mm(bb, banks[bb][:, 0:n], permb[:, :], sqs[q][:, 0:n], True, True, [B_sq[q], B_const], wt=n / 512.0)
        rope_epilogue(ba, bb, n, cos_ap, sin_ap, out_ap, obufs, out_hi=out_hi)

    SCALE = 0.125

    OBK = [4, 5]
    LBK = [6, 7]
    pstate = {"par": 0, "pending": {}}
    TAIL_DELAY = 60.0

    def maybe_flush(force_par=None):
        for par in list(pstate["pending"].keys()):
            created, fn = pstate["pending"][par]
            if par == force_par or pe_work["v"] - created >= TAIL_DELAY:
                del pstate["pending"][par]
                fn()

    def obank(slot, par):
        return OBK[slot]

    def run_pair(jobs, hook_after=2):
        nj = len(jobs)
        n = jobs[0][1]
        nb = len(jobs[0][2])

        def qk(ji, j):
            q_ap, _, blocks, qreads = jobs[ji]
            k_ap, kreads, _, _, mi = blocks[j]
            s_ = (j % 2) * 2 + ji
            mm(s_, banks[s_][:, 0:n], k_ap, q_ap, True, mi is None, qreads + kreads, wt=n / 512.0)
            if mi is not None:
                for g in range(n // 128):
                    mm(s_, banks[s_][:, g * 128:(g + 1) * 128], identb[:, :], maskb[:, mi, :], False,
                       g == n // 128 - 1, [B_const])

        for ji in range(nj):
            qk(ji, 0)
        for j in range(nb):
            if j + 1 < nb:
                for ji in range(nj):
                    qk(ji, j + 1)
            sp = (j % 2) * 2
            ei = j % 2
            act(Es[ei][:, 0:nj, 0:n], PS[:, sp:sp + nj, 0:n], AF.Exp, [Bbank[sp + ji] for ji in range(nj)],
                [B_E[ei]], scale=SCALE)
            for ji in range(nj):
                _, _, v_ap, vreads, _ = jobs[ji][2][j]
                bo, bl = OBK[ji], LBK[ji]
                mm(bo, banks[bo][:, 0:n], v_ap, Es[ei][:, ji, 0:n], j == 0, j == nb - 1, [B_E[ei]] + vreads,
                   wt=n / 512.0)
                mm(bl, banks[bl][:, 0:n], ones[:, :], Es[ei][:, ji, 0:n], j == 0, j == nb - 1, [B_E[ei], B_const],
                   wt=n / 512.0)
            if j % 2 == 1 or j == nb - 1:
                maybe_flush()
        return 0

    def flush_pending():
        for par in list(pstate["pending"].keys()):
            maybe_flush(force_par=par)

    for t in range(4):
        load_T(xp_d, t * 128, xT, t * 128, [B_xT[0]])
    for t in range(6):
        load_T(xw_d, t * 128, xT, 512 + t * 128, [B_xT[1] if t < 4 else B_xT[2]])

    xfT = ytmp
    for fc in range(4 if STAGE >= -3 else 0):
        for t in range(4):
            load_T(xf_d, fc * 512 + t * 128, xfT, t * 128, B_hT)
        if fc == 0:
            for piece in range(3):
                mod_piece(0, piece)
            mod_finish(0, parts=(0,))
        prenorm(xfT, B_hT, 0, 512, hfT, [B_hfT[fc]], fc * 512, 0, 0, 1)
    for ci, (c0, n) in enumerate(TCH if STAGE >= -3 else []):
        prenorm(xT, [B_xT[ci]], c0, n, hT, [B_hT[ci]], c0, 0, 0, 0 if ci == 0 else 1)

    B_half = [Buf("half0"), Buf("half1")]
    P.alias(B_half, [B_slot[1]])
    for hd in range(KHEADS if STAGE >= -2 else 0):
        hoff = (hd % 2) * 3072
        load_w(wd0_d[hd].rearrange("p k c -> p (k c)"), 3072, force=(1, hoff, B_half[hd % 2]))
        wv = slots[1][:, hoff:hoff + 3072].rearrange("p (k c) -> p k c", k=8)
        WQ, WK, WV = 0, 128, 256
        rs = [B_half[hd % 2]]
        mh = mod_load(0, 3 + hd, force=(0, 0, B_slot[0])) if hd < 5 else None
        if "c" not in KSKIP:
            dma_in("sp", cst[:, 0:512].rearrange("p (j c) -> p j c", j=4), ckd_d[hd], B_cst)
            bi = nbank()
            for j in range(4):
                P.op("pe", lambda e, bi=bi, j=j: e.transpose(banks[bi][:, j * 128:(j + 1) * 128],
                                                            cst[:, j * 128:(j + 1) * 128], ident[:, :]),
                     reads=[B_cst, B_const], writes=[Bbank[bi]])
            act(kTh[:, 2560:3072], banks[bi][:, :], AF.Copy, [Bbank[bi]], [B_kTh[2]])
            dma_in("sp", cst[:, 512:1024], cvd_d[hd].rearrange("p j c -> p (j c)"), B_cstv)
            dve_copy(Vh[:, 2560:3072], cst[:, 512:1024], [B_cstv], [B_Vh[2]])
        bi = nbank()
        linear_fm(bi, lambda k: wv[:, k, WQ:WQ + 128], lambda k: hT[:, k, 0:512], 512, rs + [B_hT[0]])
        act(qTh[0:64, 0, 0:512], banks[bi][0:64, :], AF.Copy, [Bbank[bi]], [B_qTh[0]])
        act(qTh[64:128, 1, 0:512], banks[bi][64:128, :], AF.Copy, [Bbank[bi]], [B_qTh[0]])
        bi = nbank()
        linear_fm(bi, lambda k: wv[:, k, WK:WK + 128], lambda k: hT[:, k, 0:512], 512, rs + [B_hT[0]])
        act(kTh[:, 0:512], banks[bi][:, :], AF.Copy, [Bbank[bi]], [B_kTh[0]])
        maybe_flush()
        for t in range(0 if "t" in KSKIP else 4):
            bi = nbank()
            linear_fm(bi, lambda k, t=t: hT[:, k, t * 128:(t + 1) * 128], lambda k: wv[:, k, WK:WK + 256], 256,
                      rs + [B_hT[0]])
            oi = nxt("ost", 2)
            act(ostage[oi][:, 0:256], banks[bi][:, 0:256], AF.Copy, [Bbank[bi]], [B_ost[oi]])
            dve_copy(Vh[:, t * 128:(t + 1) * 128], banks[bi][:, 128:256], [Bbank[bi]], [B_Vh[0]])
            if "o" not in KSKIP:
                dma_out(ndk_d[t * 128:(t + 1) * 128, hd * 128:(hd + 1) * 128], ostage[oi][:, 0:128], B_ost[oi])
                dma_out(ndv_d[t * 128:(t + 1) * 128, hd * 128:(hd + 1) * 128], ostage[oi][:, 128:256], B_ost[oi])
        for ci in (() if "r" in KSKIP else (1, 2)):
            c0, n = TCH[ci]
            rope_tile(lambda k: wv[:, k, WQ:WQ + 128], lambda k: hT[:, k, c0:c0 + n], n, rs + [B_hT[ci]],
                      ropeW[:, 0, c0 - 512:c0 - 512 + n], ropeW[:, 1, c0 - 512:c0 - 512 + n],
                      qTh[0:64, 0, c0:c0 + n], [B_qTh[1]], out_hi=qTh[64:128, 1, c0:c0 + n])
            maybe_flush()
        for fc in range(0 if "f" in KSKIP else 4):
            f0 = fc * 512

            def vfill(f0=f0, fc=fc):
                bi = nbank()
                for t in range(4):
                    for k in range(KC):
                        mm(bi, banks[bi][:, t * 128:(t + 1) * 128], hfT[:, k, f0 + t * 128:f0 + (t + 1) * 128],
                           wv[:, k, WV:WV + 128], k == 0, k == KC - 1, rs + [B_hfT[fc]], wt=0.25)
                dve_copy(Vh[:, 512 + f0:512 + f0 + 512], banks[bi][:, :], [Bbank[bi]], [B_Vh[1]])

            rope_tile(lambda k: wv[:, k, WK:WK + 128], lambda k: hfT[:, k, f0:f0 + 512], 512, rs + [B_hfT[fc]],
                      ropeF[:, 0, f0:f0 + 512], ropeF[:, 1, f0:f0 + 512],
                      kTh[:, 512 + f0:512 + f0 + 512], [B_kTh[1]], filler=vfill)
            maybe_flush()

        mh1 = None
        if mh is not None:
            mod_compute(mh)
        if STAGE >= 1:
            mh1 = mod_load(1, hd, force=(0, 0, B_slot[0]))
        def diff_pair(q0_ap, q1_ap, n, blocks, qreads, out_ap, obufs):
            run_pair([(q0_ap, n, blocks, qreads), (q1_ap, n, blocks, qreads)])
            par = pstate["par"]
            pstate["par"] ^= 1
            maybe_flush(force_par=par)
            T1, B_T1 = T1s[par], B_T1s[par]
            oA, oB = OBK
            ta = nxt("tmp", 4)
            act(tmps[ta][:, 0:n], banks[LBK[0]][:, 0:n], AF.Copy, [Bbank[LBK[0]]], [B_tmp[ta]])
            tb = nxt("tmp", 4)
            act(tmps[tb][:, 0:n], banks[LBK[1]][:, 0:n], AF.Copy, [Bbank[LBK[1]]], [B_tmp[tb]])
            tc = nxt("tmp", 4)
            dve_copy(tmps[tc][:, 0:n], banks[oA][:, 0:n], [Bbank[oA]], [B_tmp[tc]])
            td = nxt("tmp", 4)
            dve_copy(tmps[td][:, 0:n], banks[oB][:, 0:n], [Bbank[oB]], [B_tmp[td]])
            dve_recip(tmps[ta][:, 0:n], tmps[ta][:, 0:n], [B_tmp[ta]], [B_tmp[ta]])
            dve_tt(T1[:, 0:n], tmps[tc][:, 0:n], tmps[ta][:, 0:n], ALU.mult, [B_tmp[tc], B_tmp[ta]], [B_T1])
            dve_recip(tmps[tb][:, 0:n], tmps[tb][:, 0:n], [B_tmp[tb]], [B_tmp[tb]])
            dve_tt(tmps[td][:, 0:n], tmps[td][:, 0:n], tmps[tb][:, 0:n], ALU.mult, [B_tmp[td], B_tmp[tb]],
                   [B_tmp[td]])
            dve_stt(T1[:, 0:n], tmps[td][:, 0:n], lamt[:, 4:5], T1[:, 0:n], ALU.mult, ALU.add,
                    [B_tmp[td], B_lam, B_T1], [B_T1])

            def tail():
                q = nxt("sq", 2)
                act(sqs[q][:, 0:n], T1[:, 0:n], AF.Square, [B_T1], [B_sq[q]])
                bs = 2
                mm(bs, banks[bs][:, 0:n], ones[:, :], sqs[q][:, 0:n], True, True, [B_sq[q], B_const])
                br = 3
                rstd_from_bank(bs, n, 128, br)
                dve_stt(out_ap, T1[:, 0:n], lamt[:, 5:6], banks[br][:, 0:n], ALU.mult, ALU.mult,
                        [B_T1, B_lam, Bbank[br]], obufs)
            pstate["pending"][par] = (pe_work["v"], tail)

        for pb in range(0 if "p" in KSKIP else 2):
            blocks = []
            for j in range(2):
                kc = pb * 256 + j * 128
                blocks.append((kTh[:, kc:kc + 128], [B_kTh[0]], Vh[:, kc:kc + 128], [B_Vh[0]], None))
            diff_pair(qTh[:, 0, pb * 256:(pb + 1) * 256], qTh[:, 1, pb * 256:(pb + 1) * 256], 256, blocks,
                      [B_qTh[0]], oT[:, hd, pb * 256:(pb + 1) * 256], [B_oT[0]])
        for ci in (() if "w" in KSKIP else (1, 2)):
            c0, n = TCH[ci]
            blocks = []
            for j in range(16):
                kc = 512 + j * 128
                blocks.append((kTh[:, kc:kc + 128], [B_kTh[1]], Vh[:, kc:kc + 128], [B_Vh[1]], None))
            for j in range(4):
                kc = 2560 + j * 128
                blocks.append((kTh[:, kc:kc + 128], [B_kTh[2]], Vh[:, kc:kc + 128], [B_Vh[2]], None))
            diff_pair(qTh[:, 0, c0:c0 + n], qTh[:, 1, c0:c0 + n], n, blocks, [B_qTh[1]],
                      oT[:, hd, c0:c0 + n], [B_oT[ci]])
        if mh1 is not None:
            mod_compute(mh1)
    flush_pending()
    P.alias([B_slot[1]], B_half)
    if KHEADS < 5:
        for piece in range(3 + KHEADS, 8):
            mod_piece(0, piece)
    mod_finish(0, parts=(1, 2, 3))

    def out_proj_l0(l):
        P.alias([B_yA, B_yA2], B_hfT)
        ybs = [(yA, [B_yA]), (yA2, [B_yA2]), (yA, [B_yA])]
        for ci, (c0, n) in enumerate(TCH):
            pn = PostNorm(c0, n, [B_xT[ci]], l, 0, 0 if ci == 0 else 1, ybuf=ybs[ci])
            for half in range(2):
                si = load_w(wo0_d[half].rearrange("p k c -> p (k c)"), 4096)
                wv = slots[si][:, 0:4096].rearrange("p (k c) -> p k c", k=8)
                for dl in range(4):
                    bi = nbank()
                    linear_fm(bi, lambda k: wv[:, k, dl * 128:(dl + 1) * 128], lambda k: oT[:, k, c0:c0 + n], n,
                              [B_slot[si], B_oT[ci]])
                    pn.add(bi, half * 4 + dl)
            pn.finish()

    def ffn(l, chunks):
        for (c0, n, xb, hb, ab, c) in chunks:
            prenorm(xT, xb, c0, n, hT, [hb], c0, l, 1, c)
        for piece in range(8):
            nf = min(3, NF - 3 * piece)
            si = load_w(wgu_d[l, piece, :, 0:nf].rearrange("p f k c -> p (f k c)"), nf * 2048)
            wv = slots[si][:, 0:nf * 2048].rearrange("p (f k c) -> p f k c", f=nf, k=8)
            for fl in range(nf):
                f = 3 * piece + fl
                for (c0, n, xb, hb, ab, c) in chunks:
                    bg = nbank()
                    linear_fm(bg, lambda k: wv[:, fl, k, 0:128], lambda k: hT[:, k, c0:c0 + n], n, [B_slot[si], hb])
                    bu = nbank()
                    linear_fm(bu, lambda k: wv[:, fl, k, 128:256], lambda k: hT[:, k, c0:c0 + n], n, [B_slot[si], hb])
                    t = nxt("tmp", 4)
                    act(tmps[t][:, 0:n], banks[bg][:, 0:n], AF.Silu, [Bbank[bg]], [B_tmp[t]])
                    dve_tt(aT[:, f, c0:c0 + n], tmps[t][:, 0:n], banks[bu][:, 0:n], ALU.mult,
                           [B_tmp[t], Bbank[bu]], [ab])
        for (c0, n, xb, hb, ab, c) in chunks:
            pn = PostNorm(c0, n, xb, l, 1, c)
            for j in range(4):
                si = load_w(wdn_d[l, j].rearrange("p d f c -> p (d f c)"), 5632)
                wv = slots[si][:, 0:5632].rearrange("p (d f c) -> p d f c", d=2, f=22)
                for dl in range(2):
                    bi = nbank()
                    for f in range(NF):
                        mm(bi, banks[bi][:, 0:n], wv[:, dl, f, :], aT[:, f, c0:c0 + n], f == 0, f == NF - 1,
                           [B_slot[si], ab])
                    pn.add(bi, 2 * j + dl)
            pn.finish()

    if STAGE >= -1:
        out_proj_l0(0)
    if STAGE >= 1:
        mod_finish(1)
    P.alias(B_aT, B_hfT + B_oT + [B_yA, B_yA2])
    if STAGE >= 0:
        ffn(0, [(TCH[i][0], TCH[i][1], [B_xT[i]], B_hT[i], B_aT[i], 0 if i == 0 else 1) for i in range(3)])

    if STAGE >= 2:
        P.alias(B_qTs + [B_kTs] + B_oS, B_aT)
        for ci, (c0, n) in enumerate(TCH):
            prenorm(xT, [B_xT[ci]], c0, n, hT, [B_hT[ci]], c0, 1, 0, 0 if ci == 0 else 1)
        OWN0 = 640
        B_hown = [B_hT[1], B_hT[2]]
        P.op("dve", lambda e: e.memset(kTs[:, :, :], 0.0), reads=[], writes=[B_kTs])
        VS = Vh[:, 0:3584].rearrange("p (t c) -> p t c", t=14)
        B_VS = Buf("VS")
        P.alias([B_VS], list(B_Vh))
        for half in range(2):
            sa = load_w(ws1q_d[half].rearrange("p k c -> p (k c)"), 4096)
            wa = slots[sa][:, 0:4096].rearrange("p (k c) -> p k c", k=8)
            for cl in range(4):
                cq = half * 4 + cl

                def pfill(cl=cl, cq=cq):
                    bi = nbank()
                    linear_fm(bi, lambda k: wa[:, k, cl * 128:(cl + 1) * 128], lambda k: hT[:, k, 0:512], 512,
                              [B_slot[sa], B_hT[0]])
                    act(qTs[:, cq, 0:512], banks[bi][:, :], AF.Copy, [Bbank[bi]], [B_qTs[0]])

                rope_tile(lambda k: wa[:, k, cl * 128:(cl + 1) * 128], lambda k: hT[:, k, OWN0:OWN0 + 512], 512,
                          [B_slot[sa]] + B_hown, ropeW[:, 0, 128:640], ropeW[:, 1, 128:640],
                          qTs[:, cq, 512:1024], [B_qTs[1]], filler=pfill)
        sk = load_w(ws1k_d.rearrange("p k c -> p (k c)"), 4096)
        wk = slots[sk][:, 0:4096].rearrange("p (k c) -> p k c", k=8)
        rsk = [B_slot[sk]]
        for p in range(2):
            bi = nbank()
            linear_fm(bi, lambda k: wk[:, k, p * 128:(p + 1) * 128], lambda k: hT[:, k, 0:512], 512, rsk + [B_hT[0]])
            act(kTs[0:64, 2 * p, 0:512], banks[bi][0:64, :], AF.Copy, [Bbank[bi]], [B_kTs])
            act(kTs[64:128, 2 * p + 1, 0:512], banks[bi][64:128, :], AF.Copy, [Bbank[bi]], [B_kTs])
            for ci in (1, 2):
                c0, n = TCH[ci]
                rope_tile(lambda k: wk[:, k, p * 128:(p + 1) * 128], lambda k: hT[:, k, c0:c0 + n], n,
                          rsk + [B_hT[ci]], ropeW[:, 0, c0 - 512:c0 - 512 + n], ropeW[:, 1, c0 - 512:c0 - 512 + n],
                          kTs[0:64, 2 * p, c0:c0 + n], [B_kTs], out_hi=kTs[64:128, 2 * p + 1, c0:c0 + n])
        for t in range(4):
            bi = nbank()
            for k in range(KC):
                mm(bi, banks[bi][:, 0:256], hT[:, k, t * 128:(t + 1) * 128], wk[:, k, 0:256], k == 0, k == KC - 1,
                   rsk + [B_hT[0]])
            for k in range(KC):
                mm(bi, banks[bi][:, 256:512], hT[:, k, t * 128:(t + 1) * 128], wk[:, k, 256:512], k == 0, k == KC - 1,
                   rsk + [B_hT[0]])
            oi = nxt("ost", 2)
            act(ostage[oi][:, 0:512], banks[bi][:, :], AF.Copy, [Bbank[bi]], [B_ost[oi]])
            dve_copy(VS[:, t, :], banks[bi][:, 256:512], [Bbank[bi]], [B_VS])
            dma_out(nsk_d[t * 128:(t + 1) * 128, :], ostage[oi][:, 0:256], B_ost[oi])
            dma_out(nsv_d[t * 128:(t + 1) * 128, :], ostage[oi][:, 256:512], B_ost[oi])
        for t in range(6):
            bi = nbank()
            c0 = 512 + t * 128
            for k in range(KC):
                mm(bi, banks[bi][:, 0:256], hT[:, k, c0:c0 + 128], wk[:, k, 256:512], k == 0, k == KC - 1,
                   rsk + [B_hT[1] if t < 4 else B_hT[2]])
            dve_copy(VS[:, 4 + t, :], banks[bi][:, 0:256], [Bbank[bi]], [B_VS])
        dma_in("sp", cst[:, :].rearrange("p (j c) -> p j c", j=4), cks_d, B_cst)
        for p in range(2):
            bi = nbank()
            for j in range(4):
                P.op("pe", lambda e, bi=bi, j=j, p=p: e.transpose(
                    banks[bi][:, j * 128:(j + 1) * 128], cst[:, j * 256 + p * 128:j * 256 + (p + 1) * 128], ident[:, :]),
                    reads=[B_cst, B_const], writes=[Bbank[bi]])
            act(kTs[0:64, 2 * p, 1280:1792], banks[bi][0:64, :], AF.Copy, [Bbank[bi]], [B_kTs])
            act(kTs[64:128, 2 * p + 1, 1280:1792], banks[bi][64:128, :], AF.Copy, [Bbank[bi]], [B_kTs])
        xi = nxt("xin", 2)
        dma_in("sp", xin[xi][:, :], cvs_d.rearrange("p j c -> p (j c)"), B_xin[xi])
        dve_copy(Vh[:, 2560:3584], xin[xi][:, :], [B_xin[xi]], [B_VS])

        def swa_finalize(bo, bl, ng, qn, kv, g0, oS_col0, obuf):
            r0 = (kv % 2) * 64
            pr = kv // 2
            n = ng * qn
            hh0 = kv * 4 + g0
            t = nxt("tmp", 4)
            dve_copy(tmps[t][r0:r0 + 64, 0:n], banks[bl][r0:r0 + 64, 0:n], [Bbank[bl]], [B_tmp[t]])
            to = nxt("tmp", 4)
            dve_copy(tmps[to][r0:r0 + 64, 0:n], banks[bo][r0:r0 + 64, 0:n], [Bbank[bo]], [B_tmp[to]])
            lv = tmps[t][r0:r0 + 64, 0:n].rearrange("p (g q) -> p g q", g=ng)
            sk = sinkE[r0:r0 + 64, hh0:hh0 + ng]
            sk_b = bass.AP(sk.tensor, sk.offset, [list(sk.ap[0]), list(sk.ap[1]), [0, qn]])
            dve_tt(lv, lv, sk_b, ALU.add, [B_tmp[t], B_lam], [B_tmp[t]])
            act(tmps[t][r0:r0 + 64, 0:n], tmps[t][r0:r0 + 64, 0:n], AF.Ln, [B_tmp[t]], [B_tmp[t]])
            act(tmps[t][r0:r0 + 64, 0:n], tmps[t][r0:r0 + 64, 0:n], AF.Exp, [B_tmp[t]], [B_tmp[t]], scale=-1.0)
            ov = tmps[to][r0:r0 + 64, 0:n].rearrange("p (g q) -> p g q", g=ng)
            dve_tt(oS[r0:r0 + 64, pr * 4 + g0:pr * 4 + g0 + ng, oS_col0:oS_col0 + qn], ov, lv, ALU.mult,
                   [B_tmp[to], B_tmp[t]], [obuf], eng="pool")

        def run_swa(joblist):
            for i in range(0, len(joblist), 2):
                grp = joblist[i:i + 2]
                par = run_pair([g[0] for g in grp])
                for slot, g in enumerate(grp):
                    swa_finalize(obank(slot, par), LBK[slot], *g[1])

        jl = []
        for pb in range(2):
            for kv in range(4):
                p = kv // 2
                for gh in range(2):
                    q_ap = qTs[:, p * 4 + gh * 2:p * 4 + gh * 2 + 2, pb * 256:(pb + 1) * 256]
                    blocks = []
                    for j in range(2):
                        kc = pb * 256 + j * 128
                        blocks.append((kTs[:, kv, kc:kc + 128], [B_kTs],
                                       VS[:, pb * 2 + j, p * 128:(p + 1) * 128], [B_VS], None))
                    jl.append(((q_ap, 512, blocks, [B_qTs[0]]), (2, 256, kv, gh * 2, pb * 256, B_oS[0])))
        for qb in range(4):
            for kv in range(4):
                p = kv // 2
                q_ap = qTs[:, p * 4:p * 4 + 4, 512 + qb * 128:512 + (qb + 1) * 128]
                blocks = []
                for j in range(4):
                    kc = 1280 + j * 128
                    blocks.append((kTs[:, kv, kc:kc + 128], [B_kTs],
                                   VS[:, 10 + j, p * 128:(p + 1) * 128], [B_VS], None))
                for dj, mi in ((0, 2 * qb), (1, None), (2, 2 * qb + 1)):
                    w = qb + dj
                    kc = 512 + w * 128
                    blocks.append((kTs[:, kv, kc:kc + 128], [B_kTs],
                                   VS[:, 4 + w, p * 128:(p + 1) * 128], [B_VS], mi))
                jl.append(((q_ap, 512, blocks, [B_qTs[1]]), (4, 128, kv, 0, 512 + qb * 128, B_oS[1])))
        run_swa(jl)
        L1CH = [(0, 512, [B_xT[0]], 0, 0, B_oS[0]), (OWN0, 512, [B_xT[1], B_xT[2]], 1, 512, B_oS[1])]
        P.alias([B_yA], B_qTs)
        for (c0, n, xb, c, oc0, ob) in L1CH:
            pn = PostNorm(c0, n, xb, 1, 0, c, ybuf=(yA, [B_yA]) if c0 == 0 else None)
            for half in range(2):
                si = load_w(wo1_d[half].rearrange("p k c -> p (k c)"), 4096)
                wv = slots[si][:, 0:4096].rearrange("p (k c) -> p k c", k=8)
                for dl in range(4):
                    bi = nbank()
                    linear_fm(bi, lambda k: wv[:, k, dl * 128:(dl + 1) * 128], lambda k: oS[:, k, oc0:oc0 + n], n,
                              [B_slot[si], ob])
                    pn.add(bi, half * 4 + dl)
            pn.finish()
        P.alias(B_aT, B_qTs + [B_kTs] + B_oS + [B_yA])
        if STAGE >= 3:
            ffn(1, [(0, 512, [B_xT[0]], B_hT[0], B_aT[0], 0),
                    (OWN0, 512, [B_xT[1], B_xT[2]], B_hT[1], B_aT[1], 1)])

    OWN0 = 640
    for t in range(8):
        c0 = t * 128 if t < 4 else OWN0 + (t - 4) * 128
        xb = [B_xT[0]] if t < 4 else [B_xT[1], B_xT[2]]
        xi = nxt("xin", 2)
        for half in range(2):
            bi = nbank()
            for j in range(4):
                cidx = half * 4 + j
                P.op("pe", lambda e, bi=bi, j=j, cidx=cidx, c0=c0: e.transpose(
                    banks[bi][:, j * 128:(j + 1) * 128], xT[:, cidx, c0:c0 + 128], ident[:, :]),
                    reads=xb + [B_const], writes=[Bbank[bi]])
            if half == 0:
                act(xin[xi][:, 0:512], banks[bi][:, :], AF.Copy, [Bbank[bi]], [B_xin[xi]])
            else:
                dve_copy(xin[xi][:, 512:1024], banks[bi][:, :], [Bbank[bi]], [B_xin[xi]])
        dst = yp_d[t * 128:(t + 1) * 128, :] if t < 4 else ys_d[(t - 4) * 128:(t - 3) * 128, :]
        dma_out(dst, xin[xi][:, :], B_xin[xi])

    P.finish()
    stats = P.emit()
    return nc, es, stats


def _rope_tables(pos):
    pos = np.asarray(pos)
    row = (pos // 64).astype(np.float32)
    col = (pos % 64).astype(np.float32)
    nf = 16
    inv = (np.float32(10000.0) ** (-np.arange(nf, dtype=np.float32) / np.float32(nf))).astype(np.float32)
    ar = row[:, None] * inv[None, :]
    ac = col[:, None] * inv[None, :]
    ang = np.concatenate([ar, ar, ac, ac], axis=-1).astype(np.float32)
    cos = np.cos(ang).astype(np.float32)
    sin = np.sin(ang).astype(np.float32)
    sign = np.concatenate([-np.ones(16), np.ones(16), -np.ones(16), np.ones(16)]).astype(np.float32)
    sin = sin * sign[None, :]
    cos2 = np.concatenate([cos, cos], axis=1).T
    sin2 = np.concatenate([sin, sin], axis=1).T
    return np.ascontiguousarray(cos2), np.ascontiguousarray(sin2)


_ROTSRC = np.concatenate([np.arange(16, 32), np.arange(0, 16), np.arange(48, 64), np.arange(32, 48)])


def _rot_cols(w):
    n = w.shape[1] // 64
    idx = (np.arange(n)[:, None] * 64 + _ROTSRC[None, :]).reshape(-1)
    return w[:, idx]


def _kmaj(w):
    return np.ascontiguousarray(w.reshape(8, 128, -1).transpose(1, 0, 2))


_PROG_CACHE = {}


def prep_inputs(x_prompt, x_sample, cache_diff_k, cache_diff_v, cache_swa_k, cache_swa_v, c, c_ctx,
                w_mod, b_mod, norm_g, w_qkv_diff, diff_lambda, diff_subln_g, w_o_diff,
                w_qkv_swa, swa_sink, w_o_swa, w_gate, w_up, w_down):
    f = np.float32
    A = lambda a: np.ascontiguousarray(np.asarray(a, dtype=f))
    x_prompt, x_sample = A(x_prompt), A(x_sample)
    cache_diff_k, cache_diff_v = A(cache_diff_k), A(cache_diff_v)
    cache_swa_k, cache_swa_v = A(cache_swa_k), A(cache_swa_v)
    c, c_ctx = A(c), A(c_ctx)
    w_mod, b_mod, norm_g = A(w_mod), A(b_mod), A(norm_g)
    w_qkv_diff, diff_lambda, diff_subln_g, w_o_diff = A(w_qkv_diff), A(diff_lambda), A(diff_subln_g), A(w_o_diff)
    w_qkv_swa, swa_sink, w_o_swa = A(w_qkv_swa), A(swa_sink), A(w_o_swa)
    w_gate, w_up, w_down = A(w_gate), A(w_up), A(w_down)

    wmod = np.ascontiguousarray(
        w_mod.reshape(2, 8, 128, 8, 768).transpose(0, 3, 2, 1, 4))
    wq, wk, wv = w_qkv_diff[0][:, 0:1024], w_qkv_diff[0][:, 1024:2048], w_qkv_diff[0][:, 2048:3072]
    wd0 = np.empty((8, 128, 8, 384), f)
    for h in range(8):
        s = slice(h * 128, (h + 1) * 128)
        wd0[h] = _kmaj(np.concatenate([wq[:, s], wk[:, s], wv[:, s]], axis=1))
    permT = np.zeros((128, 128), f)
    for blk in range(2):
        for o in range(64):
            permT[blk * 64 + _ROTSRC[o], blk * 64 + o] = 1.0
    wo0 = np.stack([_kmaj(w_o_diff[0][:, 0:512]), _kmaj(w_o_diff[0][:, 512:1024])])
    ws = w_qkv_swa[0]
    sq_cols = []
    for p in range(2):
        for g in range(4):
            for kvl in range(2):
                hh = (2 * p + kvl) * 4 + g
                sq_cols.append(np.arange(hh * 64, (hh + 1) * 64))
    sq_cols = np.concatenate(sq_cols)
    wsq = ws[:, 0:1024]
    wsq_p = wsq[:, sq_cols]
    ws1q = np.stack([_kmaj(wsq_p[:, 0:512]), _kmaj(wsq_p[:, 512:1024])])
    wsk, wsv = ws[:, 1024:1280], ws[:, 1280:1536]
    ws1k = _kmaj(np.concatenate([wsk, wsv], axis=1))
    wos_p = w_o_swa[0][sq_cols, :]
    wo1 = np.stack([_kmaj(wos_p[:, 0:512]), _kmaj(wos_p[:, 512:1024])])
    wgu_f = np.zeros((2, 24, 128, 8, 256), f)
    for l in range(2):
        g_ = w_gate[l].reshape(8, 128, 22, 128).transpose(2, 1, 0, 3)
        u_ = w_up[l].reshape(8, 128, 22, 128).transpose(2, 1, 0, 3)
        wgu_f[l, 0:22, :, :, 0:128] = g_
        wgu_f[l, 0:22, :, :, 128:256] = u_
    wgu = np.ascontiguousarray(wgu_f.reshape(2, 8, 3, 128, 8, 256).transpose(0, 1, 3, 2, 4, 5))
    wdn = np.ascontiguousarray(w_down.reshape(2, 22, 128, 4, 2, 128).transpose(0, 3, 2, 4, 1, 5))
    ident = np.eye(128, dtype=f)
    cosf, sinf = _rope_tables(np.arange(TFULL))
    ropef = np.ascontiguousarray(np.stack([cosf, sinf], axis=1))

    normg_l = norm_g.reshape(8, 8, 128).transpose(2, 0, 1).reshape(128, 64)
    bmod_l = b_mod.reshape(2, 48, 128).transpose(2, 0, 1).reshape(128, 96)
    subg_l = diff_subln_g.reshape(128, 1)
    lam_l = np.broadcast_to(diff_lambda.reshape(1, 256), (128, 256))
    sink_l = np.broadcast_to(swa_sink.reshape(1, 16), (128, 16))

    in_maps = []
    for core in range(NCORES):
        b, ch = core // 4, core % 4
        xp = x_prompt[2 * core:2 * core + 2].reshape(512, D)
        pos = np.arange(ch * 512 - 128, ch * 512 + 640)
        valid = (pos >= 0) & (pos < 2048)
        xw = np.zeros((TW, D), f)
        xw[valid] = x_sample[b, pos[valid]]
        cosw, sinw = _rope_tables(np.clip(pos, 0, 2047))
        ropew = np.ascontiguousarray(np.stack([cosw, sinw], axis=1))
        maskb = np.zeros((128, 8, 128), f)
        for qb in range(4):
            for side in range(2):
                w = qb + (0 if side == 0 else 2)
                kpos = pos[w * 128:(w + 1) * 128]
                qpos = pos[(qb + 1) * 128:(qb + 2) * 128]
                ok = ((kpos[:, None] >= 0) & (kpos[:, None] < 2048)
                      & (np.abs(qpos[None, :] - kpos[:, None]) <= 128))
                maskb[:, 2 * qb + side, :] = np.where(ok, 0.0, -30000.0)
        cond_l = np.stack([c_ctx, c[b]], axis=-1).reshape(8, 128, 2).transpose(1, 0, 2).reshape(128, 16)
        sm = np.ascontiguousarray(np.concatenate([cond_l, normg_l, bmod_l, subg_l, lam_l, sink_l], axis=1), dtype=f)
        ckd = np.ascontiguousarray(cache_diff_k[b, 0].reshape(4, 128, 8, 128).transpose(2, 1, 0, 3))
        cvd = np.ascontiguousarray(cache_diff_v[b, 0].reshape(4, 128, 8, 128).transpose(2, 1, 0, 3))
        cks = np.ascontiguousarray(cache_swa_k[b, 0].reshape(4, 128, 256).transpose(1, 0, 2))
        cvs = np.ascontiguousarray(cache_swa_v[b, 0].reshape(4, 128, 256).transpose(1, 0, 2))
        in_maps.append(dict(xp=np.ascontiguousarray(xp), xw=xw, xf=x_sample[b], ckd=ckd, cvd=cvd, cks=cks, cvs=cvs,
                            sm=sm, ident=ident, permT=permT, ropew=ropew, ropef=ropef, maskb=maskb, wmod=wmod, wd0=wd0,
                            wo0=wo0,
                            ws1q=ws1q, ws1k=ws1k, wo1=wo1, wgu=wgu, wdn=wdn))
    return in_maps


def kernel(**inputs):
    f = np.float32
    in_maps = prep_inputs(**inputs)
    if "nc" not in _PROG_CACHE:
        nc, es, stats = build_program()
        _PROG_CACHE["nc"] = (nc, es)
        if os.environ.get("KVERBOSE"):
            print("ops per engine:", stats)
    nc, _ = _PROG_CACHE["nc"]
    res = run_bass_kernel_spmd(nc, in_maps, core_ids=list(range(NCORES)))
    R = res.results
    y_prompt = np.concatenate([R[i]["yp"].reshape(2, 256, D) for i in range(NCORES)], axis=0)
    y_sample = np.stack([np.concatenate([R[b * 4 + ch]["ys"] for ch in range(4)], axis=0) for b in range(2)], axis=0)
    ndk = np.concatenate([R[i]["ndk"].reshape(2, 1, 256, 8, 128) for i in range(NCORES)], axis=0)
    ndv = np.concatenate([R[i]["ndv"].reshape(2, 1, 256, 8, 128) for i in range(NCORES)], axis=0)
    nsk = np.concatenate([R[i]["nsk"].reshape(2, 1, 256, 4, 64) for i in range(NCORES)], axis=0)
    nsv = np.concatenate([R[i]["nsv"].reshape(2, 1, 256, 4, 64) for i in range(NCORES)], axis=0)
    return (y_prompt.astype(f), y_sample.astype(f), ndk.astype(f), ndv.astype(f), nsk.astype(f), nsv.astype(f))
```

```python
import os
import numpy as np
import concourse.bass as bass
import concourse.mybir as mybir
from concourse.bass_utils import run_bass_kernel_spmd
from contextlib import ExitStack

F32 = mybir.dt.float32
BF16 = mybir.dt.bfloat16
AF = mybir.ActivationFunctionType
ALU = mybir.AluOpType

D = 1024
KC = 8
DFF = 2816
NF = 22
TP = 512
TW = 768
TT = 1280
TFULL = 2048
LC = 512
EPS = 1e-6
NCORES = 8
STAGE = int(os.environ.get("KSTAGE", "99"))
KHEADS = int(os.environ.get("KHEADS", "8"))
KSKIP = os.environ.get("KSKIP", "")


class Buf:
    __slots__ = ("name", "writers", "readers", "dsem", "ndma", "war", "excl")

    def __init__(self, name, excl=False):
        self.name = name
        self.excl = excl
        self.writers = []
        self.readers = []
        self.war = []
        self.dsem = None
        self.ndma = 0


class Op:
    __slots__ = ("eng", "idx", "fn", "deps", "dma", "buf", "ordinal", "flag", "count", "waits", "ring")

    def __init__(self, eng, idx, fn):
        self.eng = eng
        self.idx = idx
        self.fn = fn
        self.deps = []
        self.dma = False
        self.buf = None
        self.ordinal = 0
        self.flag = False
        self.count = 0
        self.waits = []
        self.ring = False


ENGS = ["pe", "act", "dve", "pool", "sp"]


class Prog:
    def __init__(self, nc, es):
        self.nc = nc
        self.es = es
        self.ops = {e: [] for e in ENGS}
        self.dma_bufs = []
        self.out_dmas = []

    def op(self, eng, fn, reads=(), writes=(), dma_buf=None, is_out=False, ring=False):
        o = Op(eng, len(self.ops[eng]), fn)
        o.ring = ring
        deps = o.deps
        for b in reads:
            for w in b.writers:
                deps.append((w, True))
            if b.excl:
                for r in b.readers:
                    if r.eng != eng:
                        deps.append((r, False))
            b.readers.append(o)
        for b in writes:
            if b.readers:
                b.war = [r for r in b.readers if r is not o]
                b.writers = [o]
                b.readers = []
            else:
                b.writers.append(o)
            for r in b.war:
                deps.append((r, False))
        if dma_buf is not None:
            o.dma = True
            o.buf = dma_buf
            if dma_buf.dsem is None:
                self.dma_bufs.append(dma_buf)
                dma_buf.dsem = True
            dma_buf.ndma += 1
            o.ordinal = dma_buf.ndma
            if is_out:
                self.out_dmas.append(o)
        self.ops[eng].append(o)
        return o

    def alias(self, new_bufs, old_bufs):
        pend = []
        for b in old_bufs:
            pend.extend(b.readers)
            pend.extend(b.writers)
        for nb in new_bufs:
            nb.readers.extend(pend)
            nb.war = []

    def finish(self):
        o = Op("sp", len(self.ops["sp"]), None)
        for d in self.out_dmas:
            o.deps.append((d, True))
        self.ops["sp"].append(o)

    def emit(self):
        nc = self.nc
        es = self.es
        esem = {e: es.enter_context(nc.semaphore("s_" + e)) for e in ["pe", "act", "dve", "pool"]}
        for i, b in enumerate(self.dma_bufs):
            b.dsem = es.enter_context(nc.semaphore("d%d" % i))
        for e in ENGS:
            waited = {}
            for o in self.ops[e]:
                need = {}
                for (d, raw) in o.deps:
                    if d.dma:
                        key = ("d", id(d.buf))
                        val = d.ordinal
                        if waited.get(key, 0) >= val:
                            continue
                        if need.get(key, (0, None))[0] < val:
                            need[key] = (val, d)
                    else:
                        if d.eng == e and e == "pe":
                            continue
                        key = ("e", d.eng)
                        val = d.idx + 1
                        if waited.get(key, 0) >= val:
                            continue
                        if need.get(key, (0, None))[0] < val:
                            need[key] = (val, d)
                for key, (val, d) in need.items():
                    waited[key] = val
                    if not d.dma:
                        d.flag = True
                    o.waits.append(d)
        for e in ["pe", "act", "dve", "pool"]:
            c = 0
            for o in self.ops[e]:
                if o.flag and not o.dma:
                    c += 1
                    o.count = c
        handles = {"pe": "tensor", "act": "scalar", "dve": "vector", "pool": "gpsimd", "sp": "sync"}
        stats = {}
        with nc.Block() as block:
            for e in ENGS:
                ops = self.ops[e]
                stats[e] = len(ops)

                def body(eng, ops=ops, e=e):
                    for o in ops:
                        for d in o.waits:
                            if d.dma:
                                eng.wait_ge(d.buf.dsem, 16 * d.ordinal)
                            else:
                                eng.wait_ge(esem[d.eng], d.count)
                        if o.fn is None:
                            continue
                        inst = o.fn(eng)
                        if o.ring:
                            assert not o.flag
                            inst.then_inc(self.ring_sem, 16)
                        elif o.dma:
                            inst.then_inc(o.buf.dsem, 16)
                        elif o.flag:
                            inst.then_inc(esem[e], 1)

                getattr(block, handles[e])(body)
        return stats


def build_program():
    nc = bass.Bass("TRN2", target_bir_lowering=False, monotonic_sem_count=0)
    es = ExitStack()
    P = Prog(nc, es)

    def din(name, shape):
        return nc.dram_tensor(name, list(shape), F32, kind="ExternalInput").ap()

    def dout(name, shape):
        return nc.dram_tensor(name, list(shape), F32, kind="ExternalOutput").ap()

    xp_d = din("xp", [TP, D])
    xw_d = din("xw", [TW, D])
    xf_d = din("xf", [TFULL, D])
    ckd_d = din("ckd", [8, 128, 4, 128])
    cvd_d = din("cvd", [8, 128, 4, 128])
    cks_d = din("cks", [128, 4, 256])
    cvs_d = din("cvs", [128, 4, 256])
    NSM = 16 + 64 + 96 + 1 + 256 + 16
    sm_d = din("sm", [128, NSM])
    ident_d = din("ident", [128, 128])
    ropew_d = din("ropew", [128, 2, TW])
    ropef_d = din("ropef", [128, 2, TFULL])
    maskb_d = din("maskb", [128, 8, 128])
    wmod_d = din("wmod", [2, 8, 128, 8, 768])
    wd0_d = din("wd0", [8, 128, 8, 384])
    permT_d = din("permT", [128, 128])
    wo0_d = din("wo0", [2, 128, 8, 512])
    ws1q_d = din("ws1q", [2, 128, 8, 512])
    ws1k_d = din("ws1k", [128, 8, 512])
    wo1_d = din("wo1", [2, 128, 8, 512])
    wgu_d = din("wgu", [2, 8, 128, 3, 8, 256])
    wdn_d = din("wdn", [2, 4, 128, 2, 22, 128])

    yp_d = dout("yp", [TP, D])
    ys_d = dout("ys", [512, D])
    ndk_d = dout("ndk", [TP, 1024])
    ndv_d = dout("ndv", [TP, 1024])
    nsk_d = dout("nsk", [TP, 256])
    nsv_d = dout("nsv", [TP, 256])

    def sb(name, shape, dt):
        return es.enter_context(nc.sbuf_tensor(name, list(shape), dt))

    xT = sb("xT", [128, KC, TT], F32)
    hT = sb("hT", [128, KC, TT], BF16)
    ytmp = hT.bitcast(F32)
    BIG = sb("BIG", [128, 28160], BF16)
    slots = [sb("wslot%d" % i, [128, 6144], BF16) for i in range(2)]
    kTh = sb("kTh", [128, 3072], BF16)
    Vh = sb("Vh", [128, 3584], BF16)
    qTh = sb("qTh", [128, 2, TT], BF16)
    Es = [sb("E%d" % i, [128, 2, 512], BF16) for i in range(2)]
    xin = [sb("xin%d" % i, [128, 1024], F32) for i in range(2)]
    cst = sb("cst", [128, 1024], F32)
    ropeW = sb("ropeW", [128, 2, TW], F32)
    ropeF = sb("ropeF", [128, 2, TFULL], BF16)
    maskb = sb("maskbs", [128, 8, 128], BF16)
    ident = sb("idents", [128, 128], F32)
    identb = sb("identb", [128, 128], BF16)
    ones = sb("ones", [128, 128], BF16)
    permb = sb("permb", [128, 128], BF16)
    sm = sb("sms", [128, NSM], F32)
    modT = sb("modT", [128, 2, 48, 2], F32)
    der = sb("der", [128, 2, 4, 8, 2], F32)
    scb = sb("scb", [128, 8, 2], BF16)
    lamt = sb("lamt", [128, 8], F32)
    sinkE = sb("sinkE", [128, 16], F32)
    sqs = [sb("sq%d" % i, [128, 512], BF16) for i in range(2)]
    tmps = [sb("tmp%d" % i, [128, 512], F32) for i in range(4)]
    T1s = [sb("T1a", [128, 512], F32), sb("T1b", [128, 512], F32)]
    ostage = [xin[i] for i in range(2)]

    PS = es.enter_context(nc.psum_tensor("PS", [128, 8, 512], F32))

    class BankView:
        def __init__(self, i):
            self.i = i

        def __getitem__(self, idx):
            return PS[idx[0], self.i, idx[1]]

    banks = [BankView(i) for i in range(8)]
    Bbank = [Buf("bank%d" % i, excl=True) for i in range(8)]

    B_xT = [Buf("xT_p"), Buf("xT_w0"), Buf("xT_w1")]
    B_hT = [Buf("hT_p"), Buf("hT_w0"), Buf("hT_w1")]
    B_slot = [Buf("slot0"), Buf("slot1")]
    B_kTh = Buf("kTh_p"), Buf("kTh_f"), Buf("kTh_c")
    B_Vh = Buf("Vh_p"), Buf("Vh_f"), Buf("Vh_c")
    B_qTh = [Buf("qTh_p"), Buf("qTh_w")]
    B_E = [Buf("E%d" % i) for i in range(2)]
    B_xin = [Buf("xin0"), Buf("xin1")]
    B_cst = Buf("cst")
    B_cstv = Buf("cstv")
    B_const = Buf("consts")
    B_sm = Buf("sm")
    B_mod = [Buf("mod0"), Buf("mod1")]
    B_der = [Buf("der0"), Buf("der1")]
    B_scb = Buf("scb")
    B_lam = Buf("lam")
    B_sq = [Buf("sq0"), Buf("sq1")]
    B_tmp = [Buf("tmp%d" % i) for i in range(4)]
    B_T1s = [Buf("T1a"), Buf("T1b")]
    B_ost = B_xin
    B_hfT = [Buf("hfT%d" % i) for i in range(4)]
    B_oT = [Buf("oT_p"), Buf("oT_w0"), Buf("oT_w1")]
    B_aT = [Buf("aT0"), Buf("aT1"), Buf("aT2")]
    B_qTs = [Buf("qTs_p"), Buf("qTs_s")]
    B_kTs = Buf("kTs")
    B_oS = [Buf("oS_p"), Buf("oS_s")]

    hfT = BIG[:, 0:16384].rearrange("p (k t) -> p k t", k=8)
    oT = BIG[:, 16384:16384 + 10240].rearrange("p (k t) -> p k t", k=8)
    aT = BIG[:, 0:28160].rearrange("p (f t) -> p f t", f=22)
    qTs = BIG[:, 0:8192].rearrange("p (k t) -> p k t", k=8)
    kTs = BIG[:, 8192:8192 + 7168].rearrange("p (k t) -> p k t", k=4)
    oS = BIG[:, 15360:15360 + 8192].rearrange("p (h t) -> p h t", h=8)

    BIGf = BIG.bitcast(F32)
    yA = BIGf[:, 0:4096].rearrange("p (d t) -> p d t", d=8)
    yA2 = BIGf[:, 4096:8192].rearrange("p (d t) -> p d t", d=8)
    B_yA, B_yA2 = Buf("yA"), Buf("yA2")
    TCH = [(0, 512), (512, 512), (1024, 256)]

    rr = {"bank": 0, "tmp": 0, "sq": 0, "slot": 0, "xin": 0, "ost": 0, "E": 0}

    def nxt(kind, n):
        v = rr[kind]
        rr[kind] = (v + 1) % n
        return v

    reserved = set()

    def nbank():
        while True:
            b = nxt("bank", 8)
            if b not in reserved:
                return b

    open_grp = {}

    pe_work = {"v": 0.0}

    def mm(bi, out_ap, lhsT, rhs, start, stop, reads, wt=1.0):
        pe_work["v"] += wt
        if start and open_grp.get(bi):
            import traceback
            traceback.print_stack(limit=6)
            print("OPEN GROUP on bank", bi, "opened at:", open_grp[bi])
        if start:
            import traceback
            open_grp[bi] = "".join(traceback.format_stack(limit=5)[:-1])
        if stop:
            open_grp[bi] = None
        P.op("pe", lambda e: e.matmul(out_ap, lhsT, rhs, start=start, stop=stop),
             reads=reads, writes=[Bbank[bi]])

    def act(out_ap, in_ap, func, reads, writes, bias=None, scale=None):
        kw = {}
        if bias is not None:
            kw["bias"] = bias
        if scale is not None:
            kw["scale"] = scale
        P.op("act", lambda e: e.activation(out_ap, in_ap, func, **kw), reads=reads, writes=writes)

    def dve_tt(out_ap, a, b, op, reads, writes, eng="dve"):
        P.op(eng, lambda e: e.tensor_tensor(out_ap, a, b, op), reads=reads, writes=writes)

    def dve_stt(out_ap, a, scalar, b, op0, op1, reads, writes):
        P.op("dve", lambda e: e.scalar_tensor_tensor(out_ap, a, scalar, b, op0, op1), reads=reads, writes=writes)

    def dve_ts(out_ap, a, s1, s2, op0, op1, reads, writes, eng="dve"):
        if op1 is None:
            P.op(eng, lambda e: e.tensor_scalar(out_ap, a, s1, None, op0), reads=reads, writes=writes)
        else:
            P.op(eng, lambda e: e.tensor_scalar(out_ap, a, s1, s2, op0, op1), reads=reads, writes=writes)

    def dve_copy(out_ap, in_ap, reads, writes, eng="dve"):
        P.op(eng, lambda e: e.tensor_copy(out_ap, in_ap), reads=reads, writes=writes)

    def dve_recip(out_ap, in_ap, reads, writes):
        P.op("dve", lambda e: e.reciprocal(out_ap, in_ap), reads=reads, writes=writes)

    def dma_in(eng, out_ap, in_ap, buf, extra_writes=(), cast=False):
        if eng == "pool":
            P.op("pool", lambda e: e.dma_start(out=out_ap, in_=in_ap, max_dma_last_dim=4096),
                 reads=[], writes=[buf] + list(extra_writes), dma_buf=Buf("sw"))
        elif cast:
            P.op(eng, lambda e: e.dma_start(out=out_ap, in_=in_ap, max_dma_last_dim=4096),
                 reads=[], writes=[buf] + list(extra_writes), dma_buf=buf)
        else:
            P.op(eng, lambda e: e.dma_start(out=out_ap, in_=in_ap),
                 reads=[], writes=[buf] + list(extra_writes), dma_buf=buf)

    def dma_out(out_ap, in_ap, buf):
        P.op("sp", lambda e: e.dma_start(out=out_ap, in_=in_ap), reads=[buf], writes=[], dma_buf=buf, is_out=True)

    def load_w(srcs, nelem, view=None, parts=128, force=None):
        if force is None:
            si, off, buf = nxt("slot", 2), 0, None
            buf = B_slot[si]
        else:
            si, off, buf = force
        if not isinstance(srcs, (list, tuple)):
            srcs = [srcs]
        for i, src_ap in enumerate(srcs):
            dst = slots[si][0:parts, off + i * nelem:off + (i + 1) * nelem]
            dma_in("pool", dst, src_ap, buf, cast=True)
        return si

    dma_in("sp", sm[:, :], sm_d, B_sm)
    dma_in("sp", ident[:, :], ident_d, B_const)
    dma_in("sp", ropeW[:, :, :], ropew_d, B_const)
    dma_in("pool", ropeF[:, :, :].rearrange("p a t -> p (a t)"), ropef_d.rearrange("p a t -> p (a t)"), B_const, cast=True)
    dma_in("pool", maskb[:, :, :].rearrange("p a t -> p (a t)"), maskb_d.rearrange("p a t -> p (a t)"), B_const, cast=True)
    P.op("dve", lambda e: e.memset(ones[:, :], 1.0), reads=[], writes=[B_const])
    dve_copy(identb[:, :], ident[:, :], [B_const], [B_const])
    P.op("dve", lambda e: e.memset(qTh[:, :, :], 0.0), reads=[], writes=[B_qTh[0], B_qTh[1]])
    dma_in("sp", cst[:, 0:128], permT_d, B_cst)
    dve_copy(permb[:, :], cst[:, 0:128], [B_cst, B_const], [B_const])

    O_COND, O_NG, O_BM, O_SUBG, O_LAM, O_SINK = 0, 16, 80, 176, 177, 433
    cond_v = sm[:, O_COND:O_COND + 16].rearrange("p (k c) -> p k c", c=2)
    act(scb[:, :, :], cond_v, AF.Silu, [B_sm], [B_scb])
    LAM_INIT = 0.8 - 0.6 * float(np.exp(-0.3 * 0))
    lam_v = sm[:, O_LAM:O_LAM + 256].rearrange("p (a d) -> p a d", a=4)
    P.op("dve", lambda e: e.tensor_tensor(tmps[0][:, 0:64], lam_v[:, 0, :], lam_v[:, 1, :], ALU.mult),
         reads=[B_sm], writes=[B_tmp[0]])
    P.op("dve", lambda e: e.tensor_tensor(tmps[0][:, 64:128], lam_v[:, 2, :], lam_v[:, 3, :], ALU.mult),
         reads=[B_sm], writes=[B_tmp[0]])
    P.op("dve", lambda e: e.reduce_sum(lamt[:, 0:2], tmps[0][:, 0:128].rearrange("p (a d) -> p a d", a=2),
                                       mybir.AxisListType.X), reads=[B_tmp[0]], writes=[B_lam])
    act(lamt[:, 2:4], lamt[:, 0:2], AF.Exp, [B_lam], [B_lam])
    dve_tt(lamt[:, 4:5], lamt[:, 3:4], lamt[:, 2:3], ALU.subtract, [B_lam], [B_lam])
    dve_ts(lamt[:, 4:5], lamt[:, 4:5], -LAM_INIT, None, ALU.add, None, [B_lam], [B_lam])
    dve_ts(lamt[:, 5:6], sm[:, O_SUBG:O_SUBG + 1], 1.0 - LAM_INIT, None, ALU.mult, None, [B_sm, B_lam], [B_lam])
    act(sinkE[:, :], sm[:, O_SINK:O_SINK + 16], AF.Exp, [B_sm], [B_lam])

    def mod_load(l, piece, force=None):
        si = load_w(wmod_d[l, piece].rearrange("p k c -> p (k c)"), 6144, force=force)
        return (l, piece, si, B_slot[si] if force is None else force[2])

    def mod_compute(h):
        l, piece, si, sbuf = h
        bi = nbank()
        wv = slots[si][:, 0:6144].rearrange("p (k c) -> p k c", k=8)
        for oc in range(6):
            for k in range(KC):
                mm(bi, banks[bi][:, oc * 2:oc * 2 + 2], wv[:, k, oc * 128:(oc + 1) * 128], scb[:, k, :],
                   k == 0, k == KC - 1, [sbuf, B_scb], wt=0.15)
        bv = banks[bi][:, 0:12].rearrange("p (o c) -> p o c", c=2)
        o0 = piece * 6
        for c in range(2):
            dve_tt(modT[:, l, o0:o0 + 6, c], bv[:, :, c], sm[:, O_BM + l * 48 + o0:O_BM + l * 48 + o0 + 6], ALU.add,
                   [Bbank[bi], B_sm], [B_mod[l]])

    def mod_piece(l, piece, force=None):
        mod_compute(mod_load(l, piece, force=force))

    def mod_finish(l, parts=(0, 1, 2, 3)):
        for c in range(2):
            def g(n):
                return sm[:, O_NG + (l * 4 + n) * 8: O_NG + (l * 4 + n) * 8 + 8]
            if 0 in parts:
                dve_stt(der[:, l, 0, :, c], modT[:, l, 8:16, c], 1.0, g(0), ALU.add, ALU.mult, [B_mod[l], B_sm],
                        [B_der[l]])
            if 1 in parts:
                dve_tt(der[:, l, 1, :, c], modT[:, l, 16:24, c], g(1), ALU.mult, [B_mod[l], B_sm], [B_der[l]])
            if 2 in parts:
                dve_stt(der[:, l, 2, :, c], modT[:, l, 32:40, c], 1.0, g(2), ALU.add, ALU.mult, [B_mod[l], B_sm],
                        [B_der[l]])
            if 3 in parts:
                dve_tt(der[:, l, 3, :, c], modT[:, l, 40:48, c], g(3), ALU.mult, [B_mod[l], B_sm], [B_der[l]])

    def modulation(l):
        for piece in range(8):
            mod_piece(l, piece)
        mod_finish(l)

    def mod_scalars(l, which, c):
        def gs(k):
            return der[:, l, 2 * which, k, c:c + 1]

        def sh(k):
            return modT[:, l, 24 * which + k, c:c + 1]

        def gg(k):
            return der[:, l, 2 * which + 1, k, c:c + 1]
        return gs, sh, gg

    def load_T(src_d, row0, dst, dcol0, dbufs):
        xi = nxt("xin", 2)
        dma_in("sp", xin[xi][:, :], src_d[row0:row0 + 128, :], B_xin[xi])
        for half in range(2):
            bi = nbank()
            for j in range(4):
                cidx = half * 4 + j
                P.op("pe", lambda e, bi=bi, j=j, cidx=cidx, xi=xi: e.transpose(
                    banks[bi][:, j * 128:(j + 1) * 128], xin[xi][:, cidx * 128:(cidx + 1) * 128], ident[:, :]),
                    reads=[B_xin[xi], B_const], writes=[Bbank[bi]])
            src = banks[bi][:, :].rearrange("p (j t) -> p j t", j=4)
            dsta = dst[:, half * 4:half * 4 + 4, dcol0:dcol0 + 128]
            if half == 0:
                act(dsta, src, AF.Copy, [Bbank[bi]], dbufs)
            else:
                dve_copy(dsta, src, [Bbank[bi]], dbufs)

    def rstd_from_bank(bs, n, nfeat, br):
        t = nxt("tmp", 4)
        act(tmps[t][:, 0:n], banks[bs][:, 0:n], AF.Ln, [Bbank[bs]], [B_tmp[t]], bias=EPS, scale=1.0 / nfeat)
        act(banks[br][:, 0:n], tmps[t][:, 0:n], AF.Exp, [B_tmp[t]], [Bbank[br]], scale=-0.5)

    def prenorm(src, sbufs, c0, n, dst, dbufs, dc0, l, which, c):
        gs, sh, _ = mod_scalars(l, which, c)
        bs = nbank()
        for k in range(KC):
            q = nxt("sq", 2)
            act(sqs[q][:, 0:n], src[:, k, c0:c0 + n], AF.Square, sbufs, [B_sq[q]])
            mm(bs, banks[bs][:, 0:n], ones[:, :], sqs[q][:, 0:n], k == 0, k == KC - 1, [B_sq[q], B_const])
        br = nbank()
        rstd_from_bank(bs, n, D, br)
        for k in range(KC):
            t = nxt("tmp", 4)
            dve_tt(tmps[t][:, 0:n], src[:, k, c0:c0 + n], banks[br][:, 0:n], ALU.mult, sbufs + [Bbank[br]], [B_tmp[t]])
            dve_ts(dst[:, k, dc0:dc0 + n], tmps[t][:, 0:n], gs(k), sh(k), ALU.mult, ALU.add,
                   [B_tmp[t], B_mod[l], B_der[l]], dbufs, eng="pool")

    class PostNorm:
        def __init__(self, c0, n, xbufs, l, which, c, ybuf=None):
            self.c0, self.n, self.xbufs, self.l, self.which, self.c = c0, n, xbufs, l, which, c
            self.yv, self.yb = ybuf if ybuf is not None else (ytmp, B_hT)
            self.bs = nbank()
            reserved.add(self.bs)
            self.cnt = 0
            self.pend = None

        def _stats(self):
            if self.pend is not None:
                q = self.pend
                n = self.n
                mm(self.bs, banks[self.bs][:, 0:n], ones[:, :], sqs[q][:, 0:n], self.cnt == 0, self.cnt == KC - 1,
                   [B_sq[q], B_const])
                self.cnt += 1
                self.pend = None

        def add(self, bi, dch):
            n = self.n
            self._stats()
            q = nxt("sq", 2)
            act(sqs[q][:, 0:n], banks[bi][:, 0:n], AF.Square, [Bbank[bi]], [B_sq[q]])
            yv = self.yv[:, dch, 0:n]
            act(yv, banks[bi][:, 0:n], AF.Copy, [Bbank[bi]], self.yb)
            self.pend = q

        def finish(self):
            n, c0 = self.n, self.c0
            self._stats()
            _, _, gg = mod_scalars(self.l, self.which, self.c)
            br = nbank()
            reserved.discard(self.bs)
            rstd_from_bank(self.bs, n, D, br)
            for dch in range(KC):
                t = nxt("tmp", 4)
                dve_tt(tmps[t][:, 0:n], self.yv[:, dch, 0:n], banks[br][:, 0:n], ALU.mult,
                       self.yb + [Bbank[br]], [B_tmp[t]])
                xa = xT[:, dch, c0:c0 + n]
                dve_stt(xa, tmps[t][:, 0:n], gg(dch), xa, ALU.mult, ALU.add,
                        [B_tmp[t], B_der[self.l]] + self.xbufs, self.xbufs)

    def linear_fm(bi, wfun, rhsfun, n, reads, nk=KC):
        for k in range(nk):
            mm(bi, banks[bi][:, 0:n], wfun(k), rhsfun(k), k == 0, k == nk - 1, reads, wt=n / 512.0)

    def rope_epilogue(ba, bb, n, cos_ap, sin_ap, out_ap, obufs, out_hi=None):
        t1 = nxt("tmp", 4)
        dve_tt(tmps[t1][:, 0:n], banks[ba][:, 0:n], cos_ap, ALU.mult, [Bbank[ba], B_const], [B_tmp[t1]])
        t2 = nxt("tmp", 4)
        dve_tt(tmps[t2][:, 0:n], banks[bb][:, 0:n], sin_ap, ALU.mult, [Bbank[bb], B_const], [B_tmp[t2]])
        if out_hi is None:
            dve_tt(out_ap, tmps[t1][:, 0:n], tmps[t2][:, 0:n], ALU.add, [B_tmp[t1], B_tmp[t2]], obufs, eng="pool")
        else:
            dve_tt(out_ap, tmps[t1][0:64, 0:n], tmps[t2][0:64, 0:n], ALU.add, [B_tmp[t1], B_tmp[t2]], obufs,
                   eng="pool")
            dve_tt(out_hi, tmps[t1][64:128, 0:n], tmps[t2][64:128, 0:n], ALU.add, [B_tmp[t1], B_tmp[t2]], obufs,
                   eng="pool")

    def rope_tile(wfun, rhsfun, n, reads, cos_ap, sin_ap, out_ap, obufs, out_hi=None, filler=None):
        ba = nbank()
        linear_fm(ba, wfun, rhsfun, n, reads)
        q = nxt("sq", 2)
        act(sqs[q][:, 0:n], banks[ba][:, 0:n], AF.Copy, [Bbank[ba]], [B_sq[q]])
        if filler is not None:
            filler()
        bb = nbank()
        mm(bb, banks[bb][:, 0:n], permb[:, :], sqs[q][:, 0:n], True, True, [B_sq[q], B_const], wt=n / 512.0)
        rope_epilogue(ba, bb, n, cos_ap, sin_ap, out_ap, obufs, out_hi=out_hi)

    SCALE = 0.125

    OBK = [4, 5]
    LBK = [6, 7]
    pstate = {"par": 0, "pending": {}}
    TAIL_DELAY = 60.0

    def maybe_flush(force_par=None):
        for par in list(pstate["pending"].keys()):
            created, fn = pstate["pending"][par]
            if par == force_par or pe_work["v"] - created >= TAIL_DELAY:
                del pstate["pending"][par]
                fn()

    def obank(slot, par):
        return OBK[slot]

    def run_pair(jobs, hook_after=2):
        nj = len(jobs)
        n = jobs[0][1]
        nb = len(jobs[0][2])

        def qk(ji, j):
            q_ap, _, blocks, qreads = jobs[ji]
            k_ap, kreads, _, _, mi = blocks[j]
            s_ = (j % 2) * 2 + ji
            mm(s_, banks[s_][:, 0:n], k_ap, q_ap, True, mi is None, qreads + kreads, wt=n / 512.0)
            if mi is not None:
                for g in range(n // 128):
                    mm(s_, banks[s_][:, g * 128:(g + 1) * 128], identb[:, :], maskb[:, mi, :], False,
                       g == n // 128 - 1, [B_const])

        for ji in range(nj):
            qk(ji, 0)
        for j in range(nb):
            if j + 1 < nb:
                for ji in range(nj):
                    qk(ji, j + 1)
            sp = (j % 2) * 2
            ei = j % 2
            act(Es[ei][:, 0:nj, 0:n], PS[:, sp:sp + nj, 0:n], AF.Exp, [Bbank[sp + ji] for ji in range(nj)],
                [B_E[ei]], scale=SCALE)
            for ji in range(nj):
                _, _, v_ap, vreads, _ = jobs[ji][2][j]
                bo, bl = OBK[ji], LBK[ji]
                mm(bo, banks[bo][:, 0:n], v_ap, Es[ei][:, ji, 0:n], j == 0, j == nb - 1, [B_E[ei]] + vreads,
                   wt=n / 512.0)
                mm(bl, banks[bl][:, 0:n], ones[:, :], Es[ei][:, ji, 0:n], j == 0, j == nb - 1, [B_E[ei], B_const],
                   wt=n / 512.0)
            if j % 2 == 1 or j == nb - 1:
                maybe_flush()
        return 0

    def flush_pending():
        for par in list(pstate["pending"].keys()):
            maybe_flush(force_par=par)

    for t in range(4):
        load_T(xp_d, t * 128, xT, t * 128, [B_xT[0]])
    for t in range(6):
        load_T(xw_d, t * 128, xT, 512 + t * 128, [B_xT[1] if t < 4 else B_xT[2]])

    xfT = ytmp
    for fc in range(4 if STAGE >= -3 else 0):
        for t in range(4):
            load_T(xf_d, fc * 512 + t * 128, xfT, t * 128, B_hT)
        if fc == 0:
            for piece in range(3):
                mod_piece(0, piece)
            mod_finish(0, parts=(0,))
        prenorm(xfT, B_hT, 0, 512, hfT, [B_hfT[fc]], fc * 512, 0, 0, 1)
    for ci, (c0, n) in enumerate(TCH if STAGE >= -3 else []):
        prenorm(xT, [B_xT[ci]], c0, n, hT, [B_hT[ci]], c0, 0, 0, 0 if ci == 0 else 1)

    B_half = [Buf("half0"), Buf("half1")]
    P.alias(B_half, [B_slot[1]])
    for hd in range(KHEADS if STAGE >= -2 else 0):
        hoff = (hd % 2) * 3072
        load_w(wd0_d[hd].rearrange("p k c -> p (k c)"), 3072, force=(1, hoff, B_half[hd % 2]))
        wv = slots[1][:, hoff:hoff + 3072].rearrange("p (k c) -> p k c", k=8)
        WQ, WK, WV = 0, 128, 256
        rs = [B_half[hd % 2]]
        mh = mod_load(0, 3 + hd, force=(0, 0, B_slot[0])) if hd < 5 else None
        if "c" not in KSKIP:
            dma_in("sp", cst[:, 0:512].rearrange("p (j c) -> p j c", j=4), ckd_d[hd], B_cst)
            bi = nbank()
            for j in range(4):
                P.op("pe", lambda e, bi=bi, j=j: e.transpose(banks[bi][:, j * 128:(j + 1) * 128],
                                                            cst[:, j * 128:(j + 1) * 128], ident[:, :]),
                     reads=[B_cst, B_const], writes=[Bbank[bi]])
            act(kTh[:, 2560:3072], banks[bi][:, :], AF.Copy, [Bbank[bi]], [B_kTh[2]])
            dma_in("sp", cst[:, 512:1024], cvd_d[hd].rearrange("p j c -> p (j c)"), B_cstv)
            dve_copy(Vh[:, 2560:3072], cst[:, 512:1024], [B_cstv], [B_Vh[2]])
        bi = nbank()
        linear_fm(bi, lambda k: wv[:, k, WQ:WQ + 128], lambda k: hT[:, k, 0:512], 512, rs + [B_hT[0]])
        act(qTh[0:64, 0, 0:512], banks[bi][0:64, :], AF.Copy, [Bbank[bi]], [B_qTh[0]])
        act(qTh[64:128, 1, 0:512], banks[bi][64:128, :], AF.Copy, [Bbank[bi]], [B_qTh[0]])
        bi = nbank()
        linear_fm(bi, lambda k: wv[:, k, WK:WK + 128], lambda k: hT[:, k, 0:512], 512, rs + [B_hT[0]])
        act(kTh[:, 0:512], banks[bi][:, :], AF.Copy, [Bbank[bi]], [B_kTh[0]])
        maybe_flush()
        for t in range(0 if "t" in KSKIP else 4):
            bi = nbank()
            linear_fm(bi, lambda k, t=t: hT[:, k, t * 128:(t + 1) * 128], lambda k: wv[:, k, WK:WK + 256], 256,
                      rs + [B_hT[0]])
            oi = nxt("ost", 2)
            act(ostage[oi][:, 0:256], banks[bi][:, 0:256], AF.Copy, [Bbank[bi]], [B_ost[oi]])
            dve_copy(Vh[:, t * 128:(t + 1) * 128], banks[bi][:, 128:256], [Bbank[bi]], [B_Vh[0]])
            if "o" not in KSKIP:
                dma_out(ndk_d[t * 128:(t + 1) * 128, hd * 128:(hd + 1) * 128], ostage[oi][:, 0:128], B_ost[oi])
                dma_out(ndv_d[t * 128:(t + 1) * 128, hd * 128:(hd + 1) * 128], ostage[oi][:, 128:256], B_ost[oi])
        for ci in (() if "r" in KSKIP else (1, 2)):
            c0, n = TCH[ci]
            rope_tile(lambda k: wv[:, k, WQ:WQ + 128], lambda k: hT[:, k, c0:c0 + n], n, rs + [B_hT[ci]],
                      ropeW[:, 0, c0 - 512:c0 - 512 + n], ropeW[:, 1, c0 - 512:c0 - 512 + n],
                      qTh[0:64, 0, c0:c0 + n], [B_qTh[1]], out_hi=qTh[64:128, 1, c0:c0 + n])
            maybe_flush()
        for fc in range(0 if "f" in KSKIP else 4):
            f0 = fc * 512

            def vfill(f0=f0, fc=fc):
                bi = nbank()
                for t in range(4):
                    for k in range(KC):
                        mm(bi, banks[bi][:, t * 128:(t + 1) * 128], hfT[:, k, f0 + t * 128:f0 + (t + 1) * 128],
                           wv[:, k, WV:WV + 128], k == 0, k == KC - 1, rs + [B_hfT[fc]], wt=0.25)
                dve_copy(Vh[:, 512 + f0:512 + f0 + 512], banks[bi][:, :], [Bbank[bi]], [B_Vh[1]])

            rope_tile(lambda k: wv[:, k, WK:WK + 128], lambda k: hfT[:, k, f0:f0 + 512], 512, rs + [B_hfT[fc]],
                      ropeF[:, 0, f0:f0 + 512], ropeF[:, 1, f0:f0 + 512],
                      kTh[:, 512 + f0:512 + f0 + 512], [B_kTh[1]], filler=vfill)
            maybe_flush()

        mh1 = None
        if mh is not None:
            mod_compute(mh)
        if STAGE >= 1:
            mh1 = mod_load(1, hd, force=(0, 0, B_slot[0]))
        def diff_pair(q0_ap, q1_ap, n, blocks, qreads, out_ap, obufs):
            run_pair([(q0_ap, n, blocks, qreads), (q1_ap, n, blocks, qreads)])
            par = pstate["par"]
            pstate["par"] ^= 1
            maybe_flush(force_par=par)
            T1, B_T1 = T1s[par], B_T1s[par]
            oA, oB = OBK
            ta = nxt("tmp", 4)
            act(tmps[ta][:, 0:n], banks[LBK[0]][:, 0:n], AF.Copy, [Bbank[LBK[0]]], [B_tmp[ta]])
            tb = nxt("tmp", 4)
            act(tmps[tb][:, 0:n], banks[LBK[1]][:, 0:n], AF.Copy, [Bbank[LBK[1]]], [B_tmp[tb]])
            tc = nxt("tmp", 4)
            dve_copy(tmps[tc][:, 0:n], banks[oA][:, 0:n], [Bbank[oA]], [B_tmp[tc]])
            td = nxt("tmp", 4)
            dve_copy(tmps[td][:, 0:n], banks[oB][:, 0:n], [Bbank[oB]], [B_tmp[td]])
            dve_recip(tmps[ta][:, 0:n], tmps[ta][:, 0:n], [B_tmp[ta]], [B_tmp[ta]])
            dve_tt(T1[:, 0:n], tmps[tc][:, 0:n], tmps[ta][:, 0:n], ALU.mult, [B_tmp[tc], B_tmp[ta]], [B_T1])
            dve_recip(tmps[tb][:, 0:n], tmps[tb][:, 0:n], [B_tmp[tb]], [B_tmp[tb]])
            dve_tt(tmps[td][:, 0:n], tmps[td][:, 0:n], tmps[tb][:, 0:n], ALU.mult, [B_tmp[td], B_tmp[tb]],
                   [B_tmp[td]])
            dve_stt(T1[:, 0:n], tmps[td][:, 0:n], lamt[:, 4:5], T1[:, 0:n], ALU.mult, ALU.add,
                    [B_tmp[td], B_lam, B_T1], [B_T1])

            def tail():
                q = nxt("sq", 2)
                act(sqs[q][:, 0:n], T1[:, 0:n], AF.Square, [B_T1], [B_sq[q]])
                bs = 2
                mm(bs, banks[bs][:, 0:n], ones[:, :], sqs[q][:, 0:n], True, True, [B_sq[q], B_const])
                br = 3
                rstd_from_bank(bs, n, 128, br)
                dve_stt(out_ap, T1[:, 0:n], lamt[:, 5:6], banks[br][:, 0:n], ALU.mult, ALU.mult,
                        [B_T1, B_lam, Bbank[br]], obufs)
            pstate["pending"][par] = (pe_work["v"], tail)

        for pb in range(0 if "p" in KSKIP else 2):
            blocks = []
            for j in range(2):
                kc = pb * 256 + j * 128
                blocks.append((kTh[:, kc:kc + 128], [B_kTh[0]], Vh[:, kc:kc + 128], [B_Vh[0]], None))
            diff_pair(qTh[:, 0, pb * 256:(pb + 1) * 256], qTh[:, 1, pb * 256:(pb + 1) * 256], 256, blocks,
                      [B_qTh[0]], oT[:, hd, pb * 256:(pb + 1) * 256], [B_oT[0]])
        for ci in (() if "w" in KSKIP else (1, 2)):
            c0, n = TCH[ci]
            blocks = []
            for j in range(16):
                kc = 512 + j * 128
                blocks.append((kTh[:, kc:kc + 128], [B_kTh[1]], Vh[:, kc:kc + 128], [B_Vh[1]], None))
            for j in range(4):
                kc = 2560 + j * 128
                blocks.append((kTh[:, kc:kc + 128], [B_kTh[2]], Vh[:, kc:kc + 128], [B_Vh[2]], None))
            diff_pair(qTh[:, 0, c0:c0 + n], qTh[:, 1, c0:c0 + n], n, blocks, [B_qTh[1]],
                      oT[:, hd, c0:c0 + n], [B_oT[ci]])
        if mh1 is not None:
            mod_compute(mh1)
    flush_pending()
    P.alias([B_slot[1]], B_half)
    if KHEADS < 5:
        for piece in range(3 + KHEADS, 8):
            mod_piece(0, piece)
    mod_finish(0, parts=(1, 2, 3))

    def out_proj_l0(l):
        P.alias([B_yA, B_yA2], B_hfT)
        ybs = [(yA, [B_yA]), (yA2, [B_yA2]), (yA, [B_yA])]
        for ci, (c0, n) in enumerate(TCH):
            pn = PostNorm(c0, n, [B_xT[ci]], l, 0, 0 if ci == 0 else 1, ybuf=ybs[ci])
            for half in range(2):
                si = load_w(wo0_d[half].rearrange("p k c -> p (k c)"), 4096)
                wv = slots[si][:, 0:4096].rearrange("p (k c) -> p k c", k=8)
                for dl in range(4):
                    bi = nbank()
                    linear_fm(bi, lambda k: wv[:, k, dl * 128:(dl + 1) * 128], lambda k: oT[:, k, c0:c0 + n], n,
                              [B_slot[si], B_oT[ci]])
                    pn.add(bi, half * 4 + dl)
            pn.finish()

    def ffn(l, chunks, down_groups=None):
        for (c0, n, xb, hb, ab, c) in chunks:
            prenorm(xT, xb, c0, n, hT, [hb], c0, l, 1, c)
        for piece in range(8):
            nf = min(3, NF - 3 * piece)
            si = load_w(wgu_d[l, piece, :, 0:nf].rearrange("p f k c -> p (f k c)"), nf * 2048)
            wv = slots[si][:, 0:nf * 2048].rearrange("p (f k c) -> p f k c", f=nf, k=8)
            for fl in range(nf):
                f = 3 * piece + fl
                for (c0, n, xb, hb, ab, c) in chunks:
                    bg = nbank()
                    linear_fm(bg, lambda k: wv[:, fl, k, 0:128], lambda k: hT[:, k, c0:c0 + n], n, [B_slot[si], hb])
                    bu = nbank()
                    linear_fm(bu, lambda k: wv[:, fl, k, 128:256], lambda k: hT[:, k, c0:c0 + n], n, [B_slot[si], hb])
                    t = nxt("tmp", 4)
                    act(tmps[t][:, 0:n], banks[bg][:, 0:n], AF.Silu, [Bbank[bg]], [B_tmp[t]])
                    dve_tt(aT[:, f, c0:c0 + n], tmps[t][:, 0:n], banks[bu][:, 0:n], ALU.mult,
                           [B_tmp[t], Bbank[bu]], [ab])
        if down_groups is None:
            down_groups = [[(c0, n, xb, [ab], c, 0)] for (c0, n, xb, hb, ab, c) in chunks]
        for grp in down_groups:
            pns = [PostNorm(c0, n, xb, l, 1, c, ybuf=(ytmp[:, :, y0:y0 + n], B_hT))
                   for (c0, n, xb, abl, c, y0) in grp]
            for j in range(4):
                si = load_w(wdn_d[l, j].rearrange("p d f c -> p (d f c)"), 5632)
                wv = slots[si][:, 0:5632].rearrange("p (d f c) -> p d f c", d=2, f=22)
                for dl in range(2):
                    for gi, (c0, n, xb, abl, c, y0) in enumerate(grp):
                        bi = nbank()
                        for f in range(NF):
                            mm(bi, banks[bi][:, 0:n], wv[:, dl, f, :], aT[:, f, c0:c0 + n], f == 0, f == NF - 1,
                               [B_slot[si]] + abl, wt=n / 512.0)
                        pns[gi].add(bi, 2 * j + dl)
            for pn in pns:
                pn.finish()

    if STAGE >= -1:
        out_proj_l0(0)
    if STAGE >= 1:
        mod_finish(1)
    P.alias(B_aT, B_hfT + B_oT + [B_yA, B_yA2])
    if STAGE >= 0:
        ffn(0, [(TCH[i][0], TCH[i][1], [B_xT[i]], B_hT[i], B_aT[i], 0 if i == 0 else 1) for i in range(3)],
            down_groups=[[(0, 512, [B_xT[0]], [B_aT[0]], 0, 0), (512, 128, [B_xT[1]], [B_aT[1]], 1, 512)],
                         [(640, 512, [B_xT[1], B_xT[2]], [B_aT[1], B_aT[2]], 1, 0),
                          (1152, 128, [B_xT[2]], [B_aT[2]], 1, 512)]])

    if STAGE >= 2:
        P.alias(B_qTs + [B_kTs] + B_oS, B_aT)
        for ci, (c0, n) in enumerate(TCH):
            prenorm(xT, [B_xT[ci]], c0, n, hT, [B_hT[ci]], c0, 1, 0, 0 if ci == 0 else 1)
        OWN0 = 640
        B_hown = [B_hT[1], B_hT[2]]
        P.op("dve", lambda e: e.memset(kTs[:, :, :], 0.0), reads=[], writes=[B_kTs])
        VS = Vh[:, 0:3584].rearrange("p (t c) -> p t c", t=14)
        B_VS = Buf("VS")
        P.alias([B_VS], list(B_Vh))
        for half in range(2):
            sa = load_w(ws1q_d[half].rearrange("p k c -> p (k c)"), 4096)
            wa = slots[sa][:, 0:4096].rearrange("p (k c) -> p k c", k=8)
            for cl in range(4):
                cq = half * 4 + cl

                def pfill(cl=cl, cq=cq):
                    bi = nbank()
                    linear_fm(bi, lambda k: wa[:, k, cl * 128:(cl + 1) * 128], lambda k: hT[:, k, 0:512], 512,
                              [B_slot[sa], B_hT[0]])
                    act(qTs[:, cq, 0:512], banks[bi][:, :], AF.Copy, [Bbank[bi]], [B_qTs[0]])

                rope_tile(lambda k: wa[:, k, cl * 128:(cl + 1) * 128], lambda k: hT[:, k, OWN0:OWN0 + 512], 512,
                          [B_slot[sa]] + B_hown, ropeW[:, 0, 128:640], ropeW[:, 1, 128:640],
                          qTs[:, cq, 512:1024], [B_qTs[1]], filler=pfill)
        sk = load_w(ws1k_d.rearrange("p k c -> p (k c)"), 4096)
        wk = slots[sk][:, 0:4096].rearrange("p (k c) -> p k c", k=8)
        rsk = [B_slot[sk]]
        for p in range(2):
            bi = nbank()
            linear_fm(bi, lambda k: wk[:, k, p * 128:(p + 1) * 128], lambda k: hT[:, k, 0:512], 512, rsk + [B_hT[0]])
            act(kTs[0:64, 2 * p, 0:512], banks[bi][0:64, :], AF.Copy, [Bbank[bi]], [B_kTs])
            act(kTs[64:128, 2 * p + 1, 0:512], banks[bi][64:128, :], AF.Copy, [Bbank[bi]], [B_kTs])
            for ci in (1, 2):
                c0, n = TCH[ci]
                rope_tile(lambda k: wk[:, k, p * 128:(p + 1) * 128], lambda k: hT[:, k, c0:c0 + n], n,
                          rsk + [B_hT[ci]], ropeW[:, 0, c0 - 512:c0 - 512 + n], ropeW[:, 1, c0 - 512:c0 - 512 + n],
                          kTs[0:64, 2 * p, c0:c0 + n], [B_kTs], out_hi=kTs[64:128, 2 * p + 1, c0:c0 + n])
        for t in range(4):
            bi = nbank()
            for k in range(KC):
                mm(bi, banks[bi][:, 0:256], hT[:, k, t * 128:(t + 1) * 128], wk[:, k, 0:256], k == 0, k == KC - 1,
                   rsk + [B_hT[0]])
            for k in range(KC):
                mm(bi, banks[bi][:, 256:512], hT[:, k, t * 128:(t + 1) * 128], wk[:, k, 256:512], k == 0, k == KC - 1,
                   rsk + [B_hT[0]])
            oi = nxt("ost", 2)
            act(ostage[oi][:, 0:512], banks[bi][:, :], AF.Copy, [Bbank[bi]], [B_ost[oi]])
            dve_copy(VS[:, t, :], banks[bi][:, 256:512], [Bbank[bi]], [B_VS])
            dma_out(nsk_d[t * 128:(t + 1) * 128, :], ostage[oi][:, 0:256], B_ost[oi])
            dma_out(nsv_d[t * 128:(t + 1) * 128, :], ostage[oi][:, 256:512], B_ost[oi])
        for t in range(6):
            bi = nbank()
            c0 = 512 + t * 128
            for k in range(KC):
                mm(bi, banks[bi][:, 0:256], hT[:, k, c0:c0 + 128], wk[:, k, 256:512], k == 0, k == KC - 1,
                   rsk + [B_hT[1] if t < 4 else B_hT[2]])
            dve_copy(VS[:, 4 + t, :], banks[bi][:, 0:256], [Bbank[bi]], [B_VS])
        dma_in("sp", cst[:, :].rearrange("p (j c) -> p j c", j=4), cks_d, B_cst)
        for p in range(2):
            bi = nbank()
            for j in range(4):
                P.op("pe", lambda e, bi=bi, j=j, p=p: e.transpose(
                    banks[bi][:, j * 128:(j + 1) * 128], cst[:, j * 256 + p * 128:j * 256 + (p + 1) * 128], ident[:, :]),
                    reads=[B_cst, B_const], writes=[Bbank[bi]])
            act(kTs[0:64, 2 * p, 1280:1792], banks[bi][0:64, :], AF.Copy, [Bbank[bi]], [B_kTs])
            act(kTs[64:128, 2 * p + 1, 1280:1792], banks[bi][64:128, :], AF.Copy, [Bbank[bi]], [B_kTs])
        xi = nxt("xin", 2)
        dma_in("sp", xin[xi][:, :], cvs_d.rearrange("p j c -> p (j c)"), B_xin[xi])
        dve_copy(Vh[:, 2560:3584], xin[xi][:, :], [B_xin[xi]], [B_VS])

        def swa_finalize(bo, bl, ng, qn, kv, g0, oS_col0, obuf):
            r0 = (kv % 2) * 64
            pr = kv // 2
            n = ng * qn
            hh0 = kv * 4 + g0
            t = nxt("tmp", 4)
            dve_copy(tmps[t][r0:r0 + 64, 0:n], banks[bl][r0:r0 + 64, 0:n], [Bbank[bl]], [B_tmp[t]])
            to = nxt("tmp", 4)
            dve_copy(tmps[to][r0:r0 + 64, 0:n], banks[bo][r0:r0 + 64, 0:n], [Bbank[bo]], [B_tmp[to]])
            lv = tmps[t][r0:r0 + 64, 0:n].rearrange("p (g q) -> p g q", g=ng)
            sk = sinkE[r0:r0 + 64, hh0:hh0 + ng]
            sk_b = bass.AP(sk.tensor, sk.offset, [list(sk.ap[0]), list(sk.ap[1]), [0, qn]])
            dve_tt(lv, lv, sk_b, ALU.add, [B_tmp[t], B_lam], [B_tmp[t]])
            act(tmps[t][r0:r0 + 64, 0:n], tmps[t][r0:r0 + 64, 0:n], AF.Ln, [B_tmp[t]], [B_tmp[t]])
            act(tmps[t][r0:r0 + 64, 0:n], tmps[t][r0:r0 + 64, 0:n], AF.Exp, [B_tmp[t]], [B_tmp[t]], scale=-1.0)
            ov = tmps[to][r0:r0 + 64, 0:n].rearrange("p (g q) -> p g q", g=ng)
            dve_tt(oS[r0:r0 + 64, pr * 4 + g0:pr * 4 + g0 + ng, oS_col0:oS_col0 + qn], ov, lv, ALU.mult,
                   [B_tmp[to], B_tmp[t]], [obuf], eng="pool")

        def run_swa(joblist):
            for i in range(0, len(joblist), 2):
                grp = joblist[i:i + 2]
                par = run_pair([g[0] for g in grp])
                for slot, g in enumerate(grp):
                    swa_finalize(obank(slot, par), LBK[slot], *g[1])

        jl = []
        for pb in range(2):
            for kv in range(4):
                p = kv // 2
                for gh in range(2):
                    q_ap = qTs[:, p * 4 + gh * 2:p * 4 + gh * 2 + 2, pb * 256:(pb + 1) * 256]
                    blocks = []
                    for j in range(2):
                        kc = pb * 256 + j * 128
                        blocks.append((kTs[:, kv, kc:kc + 128], [B_kTs],
                                       VS[:, pb * 2 + j, p * 128:(p + 1) * 128], [B_VS], None))
                    jl.append(((q_ap, 512, blocks, [B_qTs[0]]), (2, 256, kv, gh * 2, pb * 256, B_oS[0])))
        for qb in range(4):
            for kv in range(4):
                p = kv // 2
                q_ap = qTs[:, p * 4:p * 4 + 4, 512 + qb * 128:512 + (qb + 1) * 128]
                blocks = []
                for j in range(4):
                    kc = 1280 + j * 128
                    blocks.append((kTs[:, kv, kc:kc + 128], [B_kTs],
                                   VS[:, 10 + j, p * 128:(p + 1) * 128], [B_VS], None))
                for dj, mi in ((0, 2 * qb), (1, None), (2, 2 * qb + 1)):
                    w = qb + dj
                    kc = 512 + w * 128
                    blocks.append((kTs[:, kv, kc:kc + 128], [B_kTs],
                                   VS[:, 4 + w, p * 128:(p + 1) * 128], [B_VS], mi))
                jl.append(((q_ap, 512, blocks, [B_qTs[1]]), (4, 128, kv, 0, 512 + qb * 128, B_oS[1])))
        run_swa(jl)
        L1CH = [(0, 512, [B_xT[0]], 0, 0, B_oS[0]), (OWN0, 512, [B_xT[1], B_xT[2]], 1, 512, B_oS[1])]
        P.alias([B_yA], B_qTs)
        for (c0, n, xb, c, oc0, ob) in L1CH:
            pn = PostNorm(c0, n, xb, 1, 0, c, ybuf=(yA, [B_yA]) if c0 == 0 else None)
            for half in range(2):
                si = load_w(wo1_d[half].rearrange("p k c -> p (k c)"), 4096)
                wv = slots[si][:, 0:4096].rearrange("p (k c) -> p k c", k=8)
                for dl in range(4):
                    bi = nbank()
                    linear_fm(bi, lambda k: wv[:, k, dl * 128:(dl + 1) * 128], lambda k: oS[:, k, oc0:oc0 + n], n,
                              [B_slot[si], ob])
                    pn.add(bi, half * 4 + dl)
            pn.finish()
        P.alias(B_aT, B_qTs + [B_kTs] + B_oS + [B_yA])
        if STAGE >= 3:
            ffn(1, [(0, 512, [B_xT[0]], B_hT[0], B_aT[0], 0),
                    (OWN0, 512, [B_xT[1], B_xT[2]], B_hT[1], B_aT[1], 1)])

    OWN0 = 640
    for t in range(8):
        c0 = t * 128 if t < 4 else OWN0 + (t - 4) * 128
        xb = [B_xT[0]] if t < 4 else [B_xT[1], B_xT[2]]
        xi = nxt("xin", 2)
        for half in range(2):
            bi = nbank()
            for j in range(4):
                cidx = half * 4 + j
                P.op("pe", lambda e, bi=bi, j=j, cidx=cidx, c0=c0: e.transpose(
                    banks[bi][:, j * 128:(j + 1) * 128], xT[:, cidx, c0:c0 + 128], ident[:, :]),
                    reads=xb + [B_const], writes=[Bbank[bi]])
            if half == 0:
                act(xin[xi][:, 0:512], banks[bi][:, :], AF.Copy, [Bbank[bi]], [B_xin[xi]])
            else:
                dve_copy(xin[xi][:, 512:1024], banks[bi][:, :], [Bbank[bi]], [B_xin[xi]])
        dst = yp_d[t * 128:(t + 1) * 128, :] if t < 4 else ys_d[(t - 4) * 128:(t - 3) * 128, :]
        dma_out(dst, xin[xi][:, :], B_xin[xi])

    P.finish()
    stats = P.emit()
    return nc, es, stats


def _rope_tables(pos):
    pos = np.asarray(pos)
    row = (pos // 64).astype(np.float32)
    col = (pos % 64).astype(np.float32)
    nf = 16
    inv = (np.float32(10000.0) ** (-np.arange(nf, dtype=np.float32) / np.float32(nf))).astype(np.float32)
    ar = row[:, None] * inv[None, :]
    ac = col[:, None] * inv[None, :]
    ang = np.concatenate([ar, ar, ac, ac], axis=-1).astype(np.float32)
    cos = np.cos(ang).astype(np.float32)
    sin = np.sin(ang).astype(np.float32)
    sign = np.concatenate([-np.ones(16), np.ones(16), -np.ones(16), np.ones(16)]).astype(np.float32)
    sin = sin * sign[None, :]
    cos2 = np.concatenate([cos, cos], axis=1).T
    sin2 = np.concatenate([sin, sin], axis=1).T
    return np.ascontiguousarray(cos2), np.ascontiguousarray(sin2)


_ROTSRC = np.concatenate([np.arange(16, 32), np.arange(0, 16), np.arange(48, 64), np.arange(32, 48)])


def _rot_cols(w):
    n = w.shape[1] // 64
    idx = (np.arange(n)[:, None] * 64 + _ROTSRC[None, :]).reshape(-1)
    return w[:, idx]


def _kmaj(w):
    return np.ascontiguousarray(w.reshape(8, 128, -1).transpose(1, 0, 2))


_PROG_CACHE = {}


def prep_inputs(x_prompt, x_sample, cache_diff_k, cache_diff_v, cache_swa_k, cache_swa_v, c, c_ctx,
                w_mod, b_mod, norm_g, w_qkv_diff, diff_lambda, diff_subln_g, w_o_diff,
                w_qkv_swa, swa_sink, w_o_swa, w_gate, w_up, w_down):
    f = np.float32
    A = lambda a: np.ascontiguousarray(np.asarray(a, dtype=f))
    x_prompt, x_sample = A(x_prompt), A(x_sample)
    cache_diff_k, cache_diff_v = A(cache_diff_k), A(cache_diff_v)
    cache_swa_k, cache_swa_v = A(cache_swa_k), A(cache_swa_v)
    c, c_ctx = A(c), A(c_ctx)
    w_mod, b_mod, norm_g = A(w_mod), A(b_mod), A(norm_g)
    w_qkv_diff, diff_lambda, diff_subln_g, w_o_diff = A(w_qkv_diff), A(diff_lambda), A(diff_subln_g), A(w_o_diff)
    w_qkv_swa, swa_sink, w_o_swa = A(w_qkv_swa), A(swa_sink), A(w_o_swa)
    w_gate, w_up, w_down = A(w_gate), A(w_up), A(w_down)

    wmod = np.ascontiguousarray(
        w_mod.reshape(2, 8, 128, 8, 768).transpose(0, 3, 2, 1, 4))
    wq, wk, wv = w_qkv_diff[0][:, 0:1024], w_qkv_diff[0][:, 1024:2048], w_qkv_diff[0][:, 2048:3072]
    wd0 = np.empty((8, 128, 8, 384), f)
    for h in range(8):
        s = slice(h * 128, (h + 1) * 128)
        wd0[h] = _kmaj(np.concatenate([wq[:, s], wk[:, s], wv[:, s]], axis=1))
    permT = np.zeros((128, 128), f)
    for blk in range(2):
        for o in range(64):
            permT[blk * 64 + _ROTSRC[o], blk * 64 + o] = 1.0
    wo0 = np.stack([_kmaj(w_o_diff[0][:, 0:512]), _kmaj(w_o_diff[0][:, 512:1024])])
    ws = w_qkv_swa[0]
    sq_cols = []
    for p in range(2):
        for g in range(4):
            for kvl in range(2):
                hh = (2 * p + kvl) * 4 + g
                sq_cols.append(np.arange(hh * 64, (hh + 1) * 64))
    sq_cols = np.concatenate(sq_cols)
    wsq = ws[:, 0:1024]
    wsq_p = wsq[:, sq_cols]
    ws1q = np.stack([_kmaj(wsq_p[:, 0:512]), _kmaj(wsq_p[:, 512:1024])])
    wsk, wsv = ws[:, 1024:1280], ws[:, 1280:1536]
    ws1k = _kmaj(np.concatenate([wsk, wsv], axis=1))
    wos_p = w_o_swa[0][sq_cols, :]
    wo1 = np.stack([_kmaj(wos_p[:, 0:512]), _kmaj(wos_p[:, 512:1024])])
    wgu_f = np.zeros((2, 24, 128, 8, 256), f)
    for l in range(2):
        g_ = w_gate[l].reshape(8, 128, 22, 128).transpose(2, 1, 0, 3)
        u_ = w_up[l].reshape(8, 128, 22, 128).transpose(2, 1, 0, 3)
        wgu_f[l, 0:22, :, :, 0:128] = g_
        wgu_f[l, 0:22, :, :, 128:256] = u_
    wgu = np.ascontiguousarray(wgu_f.reshape(2, 8, 3, 128, 8, 256).transpose(0, 1, 3, 2, 4, 5))
    wdn = np.ascontiguousarray(w_down.reshape(2, 22, 128, 4, 2, 128).transpose(0, 3, 2, 4, 1, 5))
    ident = np.eye(128, dtype=f)
    cosf, sinf = _rope_tables(np.arange(TFULL))
    ropef = np.ascontiguousarray(np.stack([cosf, sinf], axis=1))

    normg_l = norm_g.reshape(8, 8, 128).transpose(2, 0, 1).reshape(128, 64)
    bmod_l = b_mod.reshape(2, 48, 128).transpose(2, 0, 1).reshape(128, 96)
    subg_l = diff_subln_g.reshape(128, 1)
    lam_l = np.broadcast_to(diff_lambda.reshape(1, 256), (128, 256))
    sink_l = np.broadcast_to(swa_sink.reshape(1, 16), (128, 16))

    in_maps = []
    for core in range(NCORES):
        b, ch = core // 4, core % 4
        xp = x_prompt[2 * core:2 * core + 2].reshape(512, D)
        pos = np.arange(ch * 512 - 128, ch * 512 + 640)
        valid = (pos >= 0) & (pos < 2048)
        xw = np.zeros((TW, D), f)
        xw[valid] = x_sample[b, pos[valid]]
        cosw, sinw = _rope_tables(np.clip(pos, 0, 2047))
        ropew = np.ascontiguousarray(np.stack([cosw, sinw], axis=1))
        maskb = np.zeros((128, 8, 128), f)
        for qb in range(4):
            for side in range(2):
                w = qb + (0 if side == 0 else 2)
                kpos = pos[w * 128:(w + 1) * 128]
                qpos = pos[(qb + 1) * 128:(qb + 2) * 128]
                ok = ((kpos[:, None] >= 0) & (kpos[:, None] < 2048)
                      & (np.abs(qpos[None, :] - kpos[:, None]) <= 128))
                maskb[:, 2 * qb + side, :] = np.where(ok, 0.0, -30000.0)
        cond_l = np.stack([c_ctx, c[b]], axis=-1).reshape(8, 128, 2).transpose(1, 0, 2).reshape(128, 16)
        sm = np.ascontiguousarray(np.concatenate([cond_l, normg_l, bmod_l, subg_l, lam_l, sink_l], axis=1), dtype=f)
        ckd = np.ascontiguousarray(cache_diff_k[b, 0].reshape(4, 128, 8, 128).transpose(2, 1, 0, 3))
        cvd = np.ascontiguousarray(cache_diff_v[b, 0].reshape(4, 128, 8, 128).transpose(2, 1, 0, 3))
        cks = np.ascontiguousarray(cache_swa_k[b, 0].reshape(4, 128, 256).transpose(1, 0, 2))
        cvs = np.ascontiguousarray(cache_swa_v[b, 0].reshape(4, 128, 256).transpose(1, 0, 2))
        in_maps.append(dict(xp=np.ascontiguousarray(xp), xw=xw, xf=x_sample[b], ckd=ckd, cvd=cvd, cks=cks, cvs=cvs,
                            sm=sm, ident=ident, permT=permT, ropew=ropew, ropef=ropef, maskb=maskb, wmod=wmod, wd0=wd0,
                            wo0=wo0,
                            ws1q=ws1q, ws1k=ws1k, wo1=wo1, wgu=wgu, wdn=wdn))
    return in_maps


def kernel(**inputs):
    f = np.float32
    in_maps = prep_inputs(**inputs)
    if "nc" not in _PROG_CACHE:
        nc, es, stats = build_program()
        _PROG_CACHE["nc"] = (nc, es)
        if os.environ.get("KVERBOSE"):
            print("ops per engine:", stats)
    nc, _ = _PROG_CACHE["nc"]
    res = run_bass_kernel_spmd(nc, in_maps, core_ids=list(range(NCORES)))
    R = res.results
    y_prompt = np.concatenate([R[i]["yp"].reshape(2, 256, D) for i in range(NCORES)], axis=0)
    y_sample = np.stack([np.concatenate([R[b * 4 + ch]["ys"] for ch in range(4)], axis=0) for b in range(2)], axis=0)
    ndk = np.concatenate([R[i]["ndk"].reshape(2, 1, 256, 8, 128) for i in range(NCORES)], axis=0)
    ndv = np.concatenate([R[i]["ndv"].reshape(2, 1, 256, 8, 128) for i in range(NCORES)], axis=0)
    nsk = np.concatenate([R[i]["nsk"].reshape(2, 1, 256, 4, 64) for i in range(NCORES)], axis=0)
    nsv = np.concatenate([R[i]["nsv"].reshape(2, 1, 256, 4, 64) for i in range(NCORES)], axis=0)
    return (y_prompt.astype(f), y_sample.astype(f), ndk.astype(f), ndv.astype(f), nsk.astype(f), nsv.astype(f))
```

```python
import os
import numpy as np
import concourse.bass as bass
import concourse.mybir as mybir
from concourse.bass_utils import run_bass_kernel_spmd
from contextlib import ExitStack

F32 = mybir.dt.float32
BF16 = mybir.dt.bfloat16
AF = mybir.ActivationFunctionType
ALU = mybir.AluOpType

D = 1024
KC = 8
DFF = 2816
NF = 22
TP = 512
TW = 768
TT = 1280
TFULL = 2048
LC = 512
EPS = 1e-6
NCORES = 8
STAGE = int(os.environ.get("KSTAGE", "99"))
KHEADS = int(os.environ.get("KHEADS", "8"))
KSKIP = os.environ.get("KSKIP", "")


class Buf:
    __slots__ = ("name", "writers", "readers", "dsem", "ndma", "war", "excl")

    def __init__(self, name, excl=False):
        self.name = name
        self.excl = excl
        self.writers = []
        self.readers = []
        self.war = []
        self.dsem = None
        self.ndma = 0


class Op:
    __slots__ = ("eng", "idx", "fn", "deps", "dma", "buf", "ordinal", "flag", "count", "waits", "ring")

    def __init__(self, eng, idx, fn):
        self.eng = eng
        self.idx = idx
        self.fn = fn
        self.deps = []
        self.dma = False
        self.buf = None
        self.ordinal = 0
        self.flag = False
        self.count = 0
        self.waits = []
        self.ring = False


ENGS = ["pe", "act", "dve", "pool", "sp"]


class Prog:
    def __init__(self, nc, es):
        self.nc = nc
        self.es = es
        self.ops = {e: [] for e in ENGS}
        self.dma_bufs = []
        self.out_dmas = []

    def op(self, eng, fn, reads=(), writes=(), dma_buf=None, is_out=False, ring=False):
        o = Op(eng, len(self.ops[eng]), fn)
        o.ring = ring
        deps = o.deps
        for b in reads:
            for w in b.writers:
                deps.append((w, True))
            if b.excl:
                for r in b.readers:
                    if r.eng != eng:
                        deps.append((r, False))
            b.readers.append(o)
        for b in writes:
            if b.readers:
                b.war = [r for r in b.readers if r is not o]
                b.writers = [o]
                b.readers = []
            else:
                b.writers.append(o)
            for r in b.war:
                deps.append((r, False))
        if dma_buf is not None:
            o.dma = True
            o.buf = dma_buf
            if dma_buf.dsem is None:
                self.dma_bufs.append(dma_buf)
                dma_buf.dsem = True
            dma_buf.ndma += 1
            o.ordinal = dma_buf.ndma
            if is_out:
                self.out_dmas.append(o)
        self.ops[eng].append(o)
        return o

    def alias(self, new_bufs, old_bufs):
        pend = []
        for b in old_bufs:
            pend.extend(b.readers)
            pend.extend(b.writers)
        for nb in new_bufs:
            nb.readers.extend(pend)
            nb.war = []

    def finish(self):
        o = Op("sp", len(self.ops["sp"]), None)
        for d in self.out_dmas:
            o.deps.append((d, True))
        self.ops["sp"].append(o)

    def emit(self):
        nc = self.nc
        es = self.es
        esem = {e: es.enter_context(nc.semaphore("s_" + e)) for e in ["pe", "act", "dve", "pool"]}
        for i, b in enumerate(self.dma_bufs):
            b.dsem = es.enter_context(nc.semaphore("d%d" % i))
        for e in ENGS:
            waited = {}
            for o in self.ops[e]:
                need = {}
                for (d, raw) in o.deps:
                    if d.dma:
                        key = ("d", id(d.buf))
                        val = d.ordinal
                        if waited.get(key, 0) >= val:
                            continue
                        if need.get(key, (0, None))[0] < val:
                            need[key] = (val, d)
                    else:
                        if d.eng == e and e == "pe":
                            continue
                        key = ("e", d.eng)
                        val = d.idx + 1
                        if waited.get(key, 0) >= val:
                            continue
                        if need.get(key, (0, None))[0] < val:
                            need[key] = (val, d)
                for key, (val, d) in need.items():
                    waited[key] = val
                    if not d.dma:
                        d.flag = True
                    o.waits.append(d)
        for e in ["pe", "act", "dve", "pool"]:
            c = 0
            for o in self.ops[e]:
                if o.flag and not o.dma:
                    c += 1
                    o.count = c
        handles = {"pe": "tensor", "act": "scalar", "dve": "vector", "pool": "gpsimd", "sp": "sync"}
        stats = {}
        with nc.Block() as block:
            for e in ENGS:
                ops = self.ops[e]
                stats[e] = len(ops)

                def body(eng, ops=ops, e=e):
                    for o in ops:
                        for d in o.waits:
                            if d.dma:
                                eng.wait_ge(d.buf.dsem, 16 * d.ordinal)
                            else:
                                eng.wait_ge(esem[d.eng], d.count)
                        if o.fn is None:
                            continue
                        inst = o.fn(eng)
                        if o.ring:
                            assert not o.flag
                            inst.then_inc(self.ring_sem, 16)
                        elif o.dma:
                            inst.then_inc(o.buf.dsem, 16)
                        elif o.flag:
                            inst.then_inc(esem[e], 1)

                getattr(block, handles[e])(body)
        return stats


def build_program():
    nc = bass.Bass("TRN2", target_bir_lowering=False, monotonic_sem_count=0)
    es = ExitStack()
    P = Prog(nc, es)

    def din(name, shape):
        return nc.dram_tensor(name, list(shape), F32, kind="ExternalInput").ap()

    def dout(name, shape):
        return nc.dram_tensor(name, list(shape), F32, kind="ExternalOutput").ap()

    xp_d = din("xp", [TP, D])
    xw_d = din("xw", [TW, D])
    xf_d = din("xf", [TFULL, D])
    ckd_d = din("ckd", [8, 128, 4, 128])
    cvd_d = din("cvd", [8, 128, 4, 128])
    cks_d = din("cks", [128, 4, 256])
    cvs_d = din("cvs", [128, 4, 256])
    NSM = 16 + 64 + 96 + 1 + 256 + 16
    sm_d = din("sm", [128, NSM])
    ident_d = din("ident", [128, 128])
    ropew_d = din("ropew", [128, 2, TW])
    ropef_d = din("ropef", [128, 2, TFULL])
    maskb_d = din("maskb", [128, 8, 128])
    wmod_d = din("wmod", [2, 8, 128, 8, 768])
    wd0_d = din("wd0", [8, 128, 8, 384])
    permT_d = din("permT", [128, 128])
    wo0_d = din("wo0", [2, 128, 8, 512])
    ws1q_d = din("ws1q", [2, 128, 8, 512])
    ws1k_d = din("ws1k", [128, 8, 512])
    wo1_d = din("wo1", [2, 128, 8, 512])
    wgu_d = din("wgu", [2, 8, 128, 3, 8, 256])
    wdn_d = din("wdn", [2, 4, 128, 2, 22, 128])

    yp_d = dout("yp", [TP, D])
    ys_d = dout("ys", [512, D])
    ndk_d = dout("ndk", [TP, 1024])
    ndv_d = dout("ndv", [TP, 1024])
    nsk_d = dout("nsk", [TP, 256])
    nsv_d = dout("nsv", [TP, 256])

    def sb(name, shape, dt):
        return es.enter_context(nc.sbuf_tensor(name, list(shape), dt))

    xT = sb("xT", [128, KC, TT], F32)
    hT = sb("hT", [128, KC, TT], BF16)
    ytmp = hT.bitcast(F32)
    BIG = sb("BIG", [128, 28160], BF16)
    slots = [sb("wslot%d" % i, [128, 6144], BF16) for i in range(2)]
    kTh = sb("kTh", [128, 3072], BF16)
    Vh = sb("Vh", [128, 3584], BF16)
    qTh = sb("qTh", [128, 2, TT], BF16)
    Es = [sb("E%d" % i, [128, 2, 512], BF16) for i in range(2)]
    xin = [sb("xin%d" % i, [128, 1024], F32) for i in range(2)]
    cst = sb("cst", [128, 1024], F32)
    ropeW = sb("ropeW", [128, 2, TW], F32)
    ropeF = sb("ropeF", [128, 2, TFULL], BF16)
    maskb = sb("maskbs", [128, 8, 128], BF16)
    ident = sb("idents", [128, 128], F32)
    identb = sb("identb", [128, 128], BF16)
    ones = sb("ones", [128, 128], BF16)
    permb = sb("permb", [128, 128], BF16)
    sm = sb("sms", [128, NSM], F32)
    modT = sb("modT", [128, 2, 48, 2], F32)
    der = sb("der", [128, 2, 4, 8, 2], F32)
    scb = sb("scb", [128, 8, 2], BF16)
    lamt = sb("lamt", [128, 8], F32)
    sinkE = sb("sinkE", [128, 16], F32)
    sqs = [sb("sq%d" % i, [128, 512], BF16) for i in range(2)]
    tmps = [sb("tmp%d" % i, [128, 512], F32) for i in range(4)]
    T1s = [sb("T1a", [128, 512], F32), sb("T1b", [128, 512], F32)]
    ostage = [xin[i] for i in range(2)]

    PS = es.enter_context(nc.psum_tensor("PS", [128, 8, 512], F32))

    class BankView:
        def __init__(self, i):
            self.i = i

        def __getitem__(self, idx):
            return PS[idx[0], self.i, idx[1]]

    banks = [BankView(i) for i in range(8)]
    Bbank = [Buf("bank%d" % i, excl=True) for i in range(8)]

    B_xT = [Buf("xT_p"), Buf("xT_w0"), Buf("xT_w1")]
    B_hT = [Buf("hT_p"), Buf("hT_w0"), Buf("hT_w1")]
    B_slot = [Buf("slot0"), Buf("slot1")]
    B_kTh = Buf("kTh_p"), Buf("kTh_f"), Buf("kTh_c")
    B_Vh = Buf("Vh_p"), Buf("Vh_f"), Buf("Vh_c")
    B_qTh = [Buf("qTh_p"), Buf("qTh_w")]
    B_E = [Buf("E%d" % i) for i in range(2)]
    B_xin = [Buf("xin0"), Buf("xin1")]
    B_cst = Buf("cst")
    B_cstv = Buf("cstv")
    B_const = Buf("consts")
    B_sm = Buf("sm")
    B_mod = [Buf("mod0"), Buf("mod1")]
    B_der = [Buf("der0"), Buf("der1")]
    B_scb = Buf("scb")
    B_lam = Buf("lam")
    B_sq = [Buf("sq0"), Buf("sq1")]
    B_tmp = [Buf("tmp%d" % i) for i in range(4)]
    B_T1s = [Buf("T1a"), Buf("T1b")]
    B_ost = B_xin
    B_hfT = [Buf("hfT%d" % i) for i in range(4)]
    B_oT = [Buf("oT_p"), Buf("oT_w0"), Buf("oT_w1")]
    B_aT = [Buf("aT0"), Buf("aT1"), Buf("aT2")]
    B_qTs = [Buf("qTs_p"), Buf("qTs_s")]
    B_kTs = Buf("kTs")
    B_oS = [Buf("oS_p"), Buf("oS_s")]

    hfT = BIG[:, 0:16384].rearrange("p (k t) -> p k t", k=8)
    oT = BIG[:, 16384:16384 + 10240].rearrange("p (k t) -> p k t", k=8)
    aT = BIG[:, 0:28160].rearrange("p (f t) -> p f t", f=22)
    qTs = BIG[:, 0:8192].rearrange("p (k t) -> p k t", k=8)
    kTs = BIG[:, 8192:8192 + 7168].rearrange("p (k t) -> p k t", k=4)
    oS = BIG[:, 15360:15360 + 8192].rearrange("p (h t) -> p h t", h=8)

    BIGf = BIG.bitcast(F32)
    yA = BIGf[:, 0:4096].rearrange("p (d t) -> p d t", d=8)
    yA2 = BIGf[:, 4096:8192].rearrange("p (d t) -> p d t", d=8)
    B_yA, B_yA2 = Buf("yA"), Buf("yA2")
    TCH = [(0, 512), (512, 512), (1024, 256)]

    rr = {"bank": 0, "tmp": 0, "sq": 0, "slot": 0, "xin": 0, "ost": 0, "E": 0}

    def nxt(kind, n):
        v = rr[kind]
        rr[kind] = (v + 1) % n
        return v

    reserved = set()

    def nbank():
        while True:
            b = nxt("bank", 8)
            if b not in reserved:
                return b

    open_grp = {}

    pe_work = {"v": 0.0}

    def mm(bi, out_ap, lhsT, rhs, start, stop, reads, wt=1.0):
        pe_work["v"] += wt
        if start and open_grp.get(bi):
            import traceback
            traceback.print_stack(limit=6)
            print("OPEN GROUP on bank", bi, "opened at:", open_grp[bi])
        if start:
            import traceback
            open_grp[bi] = "".join(traceback.format_stack(limit=5)[:-1])
        if stop:
            open_grp[bi] = None
        P.op("pe", lambda e: e.matmul(out_ap, lhsT, rhs, start=start, stop=stop),
             reads=reads, writes=[Bbank[bi]])

    def act(out_ap, in_ap, func, reads, writes, bias=None, scale=None):
        kw = {}
        if bias is not None:
            kw["bias"] = bias
        if scale is not None:
            kw["scale"] = scale
        P.op("act", lambda e: e.activation(out_ap, in_ap, func, **kw), reads=reads, writes=writes)

    def dve_tt(out_ap, a, b, op, reads, writes, eng="dve"):
        P.op(eng, lambda e: e.tensor_tensor(out_ap, a, b, op), reads=reads, writes=writes)

    def dve_stt(out_ap, a, scalar, b, op0, op1, reads, writes):
        P.op("dve", lambda e: e.scalar_tensor_tensor(out_ap, a, scalar, b, op0, op1), reads=reads, writes=writes)

    def dve_ts(out_ap, a, s1, s2, op0, op1, reads, writes, eng="dve"):
        if op1 is None:
            P.op(eng, lambda e: e.tensor_scalar(out_ap, a, s1, None, op0), reads=reads, writes=writes)
        else:
            P.op(eng, lambda e: e.tensor_scalar(out_ap, a, s1, s2, op0, op1), reads=reads, writes=writes)

    def dve_copy(out_ap, in_ap, reads, writes, eng="dve"):
        P.op(eng, lambda e: e.tensor_copy(out_ap, in_ap), reads=reads, writes=writes)

    def dve_recip(out_ap, in_ap, reads, writes):
        P.op("dve", lambda e: e.reciprocal(out_ap, in_ap), reads=reads, writes=writes)

    def dma_in(eng, out_ap, in_ap, buf, extra_writes=(), cast=False):
        if eng == "pool":
            P.op("pool", lambda e: e.dma_start(out=out_ap, in_=in_ap, max_dma_last_dim=4096),
                 reads=[], writes=[buf] + list(extra_writes), dma_buf=Buf("sw"))
        elif cast:
            P.op(eng, lambda e: e.dma_start(out=out_ap, in_=in_ap, max_dma_last_dim=4096),
                 reads=[], writes=[buf] + list(extra_writes), dma_buf=buf)
        else:
            P.op(eng, lambda e: e.dma_start(out=out_ap, in_=in_ap),
                 reads=[], writes=[buf] + list(extra_writes), dma_buf=buf)

    def dma_out(out_ap, in_ap, buf):
        P.op("sp", lambda e: e.dma_start(out=out_ap, in_=in_ap), reads=[buf], writes=[], dma_buf=buf, is_out=True)

    def load_w(srcs, nelem, view=None, parts=128, force=None):
        if force is None:
            si, off, buf = nxt("slot", 2), 0, None
            buf = B_slot[si]
        else:
            si, off, buf = force
        if not isinstance(srcs, (list, tuple)):
            srcs = [srcs]
        for i, src_ap in enumerate(srcs):
            dst = slots[si][0:parts, off + i * nelem:off + (i + 1) * nelem]
            dma_in("pool", dst, src_ap, buf, cast=True)
        return si

    dma_in("sp", sm[:, :], sm_d, B_sm)
    dma_in("sp", ident[:, :], ident_d, B_const)
    dma_in("sp", ropeW[:, :, :], ropew_d, B_const)
    dma_in("pool", ropeF[:, :, :].rearrange("p a t -> p (a t)"), ropef_d.rearrange("p a t -> p (a t)"), B_const, cast=True)
    dma_in("pool", maskb[:, :, :].rearrange("p a t -> p (a t)"), maskb_d.rearrange("p a t -> p (a t)"), B_const, cast=True)
    P.op("dve", lambda e: e.memset(ones[:, :], 1.0), reads=[], writes=[B_const])
    dve_copy(identb[:, :], ident[:, :], [B_const], [B_const])
    P.op("dve", lambda e: e.memset(qTh[:, :, :], 0.0), reads=[], writes=[B_qTh[0], B_qTh[1]])
    dma_in("sp", cst[:, 0:128], permT_d, B_cst)
    dve_copy(permb[:, :], cst[:, 0:128], [B_cst, B_const], [B_const])

    O_COND, O_NG, O_BM, O_SUBG, O_LAM, O_SINK = 0, 16, 80, 176, 177, 433
    cond_v = sm[:, O_COND:O_COND + 16].rearrange("p (k c) -> p k c", c=2)
    act(scb[:, :, :], cond_v, AF.Silu, [B_sm], [B_scb])
    LAM_INIT = 0.8 - 0.6 * float(np.exp(-0.3 * 0))
    lam_v = sm[:, O_LAM:O_LAM + 256].rearrange("p (a d) -> p a d", a=4)
    P.op("dve", lambda e: e.tensor_tensor(tmps[0][:, 0:64], lam_v[:, 0, :], lam_v[:, 1, :], ALU.mult),
         reads=[B_sm], writes=[B_tmp[0]])
    P.op("dve", lambda e: e.tensor_tensor(tmps[0][:, 64:128], lam_v[:, 2, :], lam_v[:, 3, :], ALU.mult),
         reads=[B_sm], writes=[B_tmp[0]])
    P.op("dve", lambda e: e.reduce_sum(lamt[:, 0:2], tmps[0][:, 0:128].rearrange("p (a d) -> p a d", a=2),
                                       mybir.AxisListType.X), reads=[B_tmp[0]], writes=[B_lam])
    act(lamt[:, 2:4], lamt[:, 0:2], AF.Exp, [B_lam], [B_lam])
    dve_tt(lamt[:, 4:5], lamt[:, 3:4], lamt[:, 2:3], ALU.subtract, [B_lam], [B_lam])
    dve_ts(lamt[:, 4:5], lamt[:, 4:5], -LAM_INIT, None, ALU.add, None, [B_lam], [B_lam])
    dve_ts(lamt[:, 5:6], sm[:, O_SUBG:O_SUBG + 1], 1.0 - LAM_INIT, None, ALU.mult, None, [B_sm, B_lam], [B_lam])
    act(sinkE[:, :], sm[:, O_SINK:O_SINK + 16], AF.Exp, [B_sm], [B_lam])

    def mod_load(l, piece, force=None):
        si = load_w(wmod_d[l, piece].rearrange("p k c -> p (k c)"), 6144, force=force)
        return (l, piece, si, B_slot[si] if force is None else force[2])

    def mod_compute(h):
        l, piece, si, sbuf = h
        bi = nbank()
        wv = slots[si][:, 0:6144].rearrange("p (k c) -> p k c", k=8)
        for oc in range(6):
            for k in range(KC):
                mm(bi, banks[bi][:, oc * 2:oc * 2 + 2], wv[:, k, oc * 128:(oc + 1) * 128], scb[:, k, :],
                   k == 0, k == KC - 1, [sbuf, B_scb], wt=0.15)
        bv = banks[bi][:, 0:12].rearrange("p (o c) -> p o c", c=2)
        o0 = piece * 6
        for c in range(2):
            dve_tt(modT[:, l, o0:o0 + 6, c], bv[:, :, c], sm[:, O_BM + l * 48 + o0:O_BM + l * 48 + o0 + 6], ALU.add,
                   [Bbank[bi], B_sm], [B_mod[l]])

    def mod_piece(l, piece, force=None):
        mod_compute(mod_load(l, piece, force=force))

    def mod_finish(l, parts=(0, 1, 2, 3)):
        for c in range(2):
            def g(n):
                return sm[:, O_NG + (l * 4 + n) * 8: O_NG + (l * 4 + n) * 8 + 8]
            if 0 in parts:
                dve_stt(der[:, l, 0, :, c], modT[:, l, 8:16, c], 1.0, g(0), ALU.add, ALU.mult, [B_mod[l], B_sm],
                        [B_der[l]])
            if 1 in parts:
                dve_tt(der[:, l, 1, :, c], modT[:, l, 16:24, c], g(1), ALU.mult, [B_mod[l], B_sm], [B_der[l]])
            if 2 in parts:
                dve_stt(der[:, l, 2, :, c], modT[:, l, 32:40, c], 1.0, g(2), ALU.add, ALU.mult, [B_mod[l], B_sm],
                        [B_der[l]])
            if 3 in parts:
                dve_tt(der[:, l, 3, :, c], modT[:, l, 40:48, c], g(3), ALU.mult, [B_mod[l], B_sm], [B_der[l]])

    def modulation(l):
        for piece in range(8):
            mod_piece(l, piece)
        mod_finish(l)

    def mod_scalars(l, which, c):
        def gs(k):
            return der[:, l, 2 * which, k, c:c + 1]

        def sh(k):
            return modT[:, l, 24 * which + k, c:c + 1]

        def gg(k):
            return der[:, l, 2 * which + 1, k, c:c + 1]
        return gs, sh, gg

    def load_T(src_d, row0, dst, dcol0, dbufs):
        xi = nxt("xin", 2)
        dma_in("sp", xin[xi][:, :], src_d[row0:row0 + 128, :], B_xin[xi])
        for half in range(2):
            bi = nbank()
            for j in range(4):
                cidx = half * 4 + j
                P.op("pe", lambda e, bi=bi, j=j, cidx=cidx, xi=xi: e.transpose(
                    banks[bi][:, j * 128:(j + 1) * 128], xin[xi][:, cidx * 128:(cidx + 1) * 128], ident[:, :]),
                    reads=[B_xin[xi], B_const], writes=[Bbank[bi]])
            src = banks[bi][:, :].rearrange("p (j t) -> p j t", j=4)
            dsta = dst[:, half * 4:half * 4 + 4, dcol0:dcol0 + 128]
            if half == 0:
                act(dsta, src, AF.Copy, [Bbank[bi]], dbufs)
            else:
                dve_copy(dsta, src, [Bbank[bi]], dbufs)

    def rstd_from_bank(bs, n, nfeat, br):
        t = nxt("tmp", 4)
        act(tmps[t][:, 0:n], banks[bs][:, 0:n], AF.Ln, [Bbank[bs]], [B_tmp[t]], bias=EPS, scale=1.0 / nfeat)
        act(banks[br][:, 0:n], tmps[t][:, 0:n], AF.Exp, [B_tmp[t]], [Bbank[br]], scale=-0.5)

    def prenorm(src, sbufs, c0, n, dst, dbufs, dc0, l, which, c):
        gs, sh, _ = mod_scalars(l, which, c)
        bs = nbank()
        for k in range(KC):
            q = nxt("sq", 2)
            act(sqs[q][:, 0:n], src[:, k, c0:c0 + n], AF.Square, sbufs, [B_sq[q]])
            mm(bs, banks[bs][:, 0:n], ones[:, :], sqs[q][:, 0:n], k == 0, k == KC - 1, [B_sq[q], B_const])
        br = nbank()
        rstd_from_bank(bs, n, D, br)
        for k in range(KC):
            t = nxt("tmp", 4)
            dve_tt(tmps[t][:, 0:n], src[:, k, c0:c0 + n], banks[br][:, 0:n], ALU.mult, sbufs + [Bbank[br]], [B_tmp[t]])
            dve_ts(dst[:, k, dc0:dc0 + n], tmps[t][:, 0:n], gs(k), sh(k), ALU.mult, ALU.add,
                   [B_tmp[t], B_mod[l], B_der[l]], dbufs, eng="pool")

    class PostNorm:
        def __init__(self, c0, n, xbufs, l, which, c, ybuf=None):
            self.c0, self.n, self.xbufs, self.l, self.which, self.c = c0, n, xbufs, l, which, c
            self.yv, self.yb = ybuf if ybuf is not None else (ytmp, B_hT)
            self.bs = nbank()
            reserved.add(self.bs)
            self.cnt = 0
            self.pend = None

        def _stats(self):
            if self.pend is not None:
                q = self.pend
                n = self.n
                mm(self.bs, banks[self.bs][:, 0:n], ones[:, :], sqs[q][:, 0:n], self.cnt == 0, self.cnt == KC - 1,
                   [B_sq[q], B_const])
                self.cnt += 1
                self.pend = None

        def add(self, bi, dch):
            n = self.n
            self._stats()
            q = nxt("sq", 2)
            act(sqs[q][:, 0:n], banks[bi][:, 0:n], AF.Square, [Bbank[bi]], [B_sq[q]])
            yv = self.yv[:, dch, 0:n]
            act(yv, banks[bi][:, 0:n], AF.Copy, [Bbank[bi]], self.yb)
            self.pend = q

        def finish(self):
            n, c0 = self.n, self.c0
            self._stats()
            _, _, gg = mod_scalars(self.l, self.which, self.c)
            br = nbank()
            reserved.discard(self.bs)
            rstd_from_bank(self.bs, n, D, br)
            for dch in range(KC):
                t = nxt("tmp", 4)
                dve_tt(tmps[t][:, 0:n], self.yv[:, dch, 0:n], banks[br][:, 0:n], ALU.mult,
                       self.yb + [Bbank[br]], [B_tmp[t]])
                xa = xT[:, dch, c0:c0 + n]
                dve_stt(xa, tmps[t][:, 0:n], gg(dch), xa, ALU.mult, ALU.add,
                        [B_tmp[t], B_der[self.l]] + self.xbufs, self.xbufs)

    def linear_fm(bi, wfun, rhsfun, n, reads, nk=KC):
        for k in range(nk):
            mm(bi, banks[bi][:, 0:n], wfun(k), rhsfun(k), k == 0, k == nk - 1, reads, wt=n / 512.0)

    def rope_epilogue(ba, bb, n, cos_ap, sin_ap, out_ap, obufs, out_hi=None):
        t1 = nxt("tmp", 4)
        dve_tt(tmps[t1][:, 0:n], banks[ba][:, 0:n], cos_ap, ALU.mult, [Bbank[ba], B_const], [B_tmp[t1]])
        t2 = nxt("tmp", 4)
        dve_tt(tmps[t2][:, 0:n], banks[bb][:, 0:n], sin_ap, ALU.mult, [Bbank[bb], B_const], [B_tmp[t2]])
        if out_hi is None:
            dve_tt(out_ap, tmps[t1][:, 0:n], tmps[t2][:, 0:n], ALU.add, [B_tmp[t1], B_tmp[t2]], obufs, eng="pool")
        else:
            dve_tt(out_ap, tmps[t1][0:64, 0:n], tmps[t2][0:64, 0:n], ALU.add, [B_tmp[t1], B_tmp[t2]], obufs,
                   eng="pool")
            dve_tt(out_hi, tmps[t1][64:128, 0:n], tmps[t2][64:128, 0:n], ALU.add, [B_tmp[t1], B_tmp[t2]], obufs,
                   eng="pool")

    def rope_tile(wfun, rhsfun, n, reads, cos_ap, sin_ap, out_ap, obufs, out_hi=None, filler=None):
        ba = nbank()
        linear_fm(ba, wfun, rhsfun, n, reads)
        q = nxt("sq", 2)
        act(sqs[q][:, 0:n], banks[ba][:, 0:n], AF.Copy, [Bbank[ba]], [B_sq[q]])
        if filler is not None:
            filler()
        bb = nbank()
        mm(bb, banks[bb][:, 0:n], permb[:, :], sqs[q][:, 0:n], True, True, [B_sq[q], B_const], wt=n / 512.0)
        rope_epilogue(ba, bb, n, cos_ap, sin_ap, out_ap, obufs, out_hi=out_hi)

    SCALE = 0.125

    OBK = [4, 5]
    LBK = [6, 7]
    pstate = {"par": 0, "pending": {}}
    TAIL_DELAY = 60.0

    def maybe_flush(force_par=None):
        for par in list(pstate["pending"].keys()):
            created, fn = pstate["pending"][par]
            if par == force_par or pe_work["v"] - created >= TAIL_DELAY:
                del pstate["pending"][par]
                fn()

    def obank(slot, par):
        return OBK[slot]

    def run_pair(jobs, hook_after=2):
        nj = len(jobs)
        n = jobs[0][1]
        nb = len(jobs[0][2])

        def qk(ji, j):
            q_ap, _, blocks, qreads = jobs[ji]
            k_ap, kreads, _, _, mi = blocks[j]
            s_ = (j % 2) * 2 + ji
            mm(s_, banks[s_][:, 0:n], k_ap, q_ap, True, mi is None, qreads + kreads, wt=n / 512.0)
            if mi is not None:
                for g in range(n // 128):
                    mm(s_, banks[s_][:, g * 128:(g + 1) * 128], identb[:, :], maskb[:, mi, :], False,
                       g == n // 128 - 1, [B_const])

        for ji in range(nj):
            qk(ji, 0)
        for j in range(nb):
            if j + 1 < nb:
                for ji in range(nj):
                    qk(ji, j + 1)
            sp = (j % 2) * 2
            ei = j % 2
            act(Es[ei][:, 0:nj, 0:n], PS[:, sp:sp + nj, 0:n], AF.Exp, [Bbank[sp + ji] for ji in range(nj)],
                [B_E[ei]], scale=SCALE)
            for ji in range(nj):
                _, _, v_ap, vreads, _ = jobs[ji][2][j]
                bo, bl = OBK[ji], LBK[ji]
                mm(bo, banks[bo][:, 0:n], v_ap, Es[ei][:, ji, 0:n], j == 0, j == nb - 1, [B_E[ei]] + vreads,
                   wt=n / 512.0)
                mm(bl, banks[bl][:, 0:n], ones[:, :], Es[ei][:, ji, 0:n], j == 0, j == nb - 1, [B_E[ei], B_const],
                   wt=n / 512.0)
            if j % 2 == 1 or j == nb - 1:
                maybe_flush()
        return 0

    def flush_pending():
        for par in list(pstate["pending"].keys()):
            maybe_flush(force_par=par)

    for t in range(4):
        load_T(xp_d, t * 128, xT, t * 128, [B_xT[0]])
    for t in range(6):
        load_T(xw_d, t * 128, xT, 512 + t * 128, [B_xT[1] if t < 4 else B_xT[2]])

    xfT = ytmp
    for fc in range(4 if STAGE >= -3 else 0):
        for t in range(4):
            load_T(xf_d, fc * 512 + t * 128, xfT, t * 128, B_hT)
        if fc == 0:
            for piece in range(3):
                mod_piece(0, piece)
            mod_finish(0, parts=(0,))
        prenorm(xfT, B_hT, 0, 512, hfT, [B_hfT[fc]], fc * 512, 0, 0, 1)
    for ci, (c0, n) in enumerate(TCH if STAGE >= -3 else []):
        prenorm(xT, [B_xT[ci]], c0, n, hT, [B_hT[ci]], c0, 0, 0, 0 if ci == 0 else 1)

    B_half = [Buf("half0"), Buf("half1")]
    P.alias(B_half, [B_slot[1]])
    for hd in range(KHEADS if STAGE >= -2 else 0):
        hoff = (hd % 2) * 3072
        load_w(wd0_d[hd].rearrange("p k c -> p (k c)"), 3072, force=(1, hoff, B_half[hd % 2]))
        wv = slots[1][:, hoff:hoff + 3072].rearrange("p (k c) -> p k c", k=8)
        WQ, WK, WV = 0, 128, 256
        rs = [B_half[hd % 2]]
        mh = mod_load(0, 3 + hd, force=(0, 0, B_slot[0])) if hd < 5 else None
        if "c" not in KSKIP:
            dma_in("sp", cst[:, 0:512].rearrange("p (j c) -> p j c", j=4), ckd_d[hd], B_cst)
            bi = nbank()
            for j in range(4):
                P.op("pe", lambda e, bi=bi, j=j: e.transpose(banks[bi][:, j * 128:(j + 1) * 128],
                                                            cst[:, j * 128:(j + 1) * 128], ident[:, :]),
                     reads=[B_cst, B_const], writes=[Bbank[bi]])
            act(kTh[:, 2560:3072], banks[bi][:, :], AF.Copy, [Bbank[bi]], [B_kTh[2]])
            dma_in("sp", cst[:, 512:1024], cvd_d[hd].rearrange("p j c -> p (j c)"), B_cstv)
            dve_copy(Vh[:, 2560:3072], cst[:, 512:1024], [B_cstv], [B_Vh[2]])
        def pq_fill():
            bi = nbank()
            linear_fm(bi, lambda k: wv[:, k, WQ:WQ + 128], lambda k: hT[:, k, 0:512], 512, rs + [B_hT[0]])
            act(qTh[0:64, 0, 0:512], banks[bi][0:64, :], AF.Copy, [Bbank[bi]], [B_qTh[0]])
            act(qTh[64:128, 1, 0:512], banks[bi][64:128, :], AF.Copy, [Bbank[bi]], [B_qTh[0]])

        def pk_fill():
            bi = nbank()
            linear_fm(bi, lambda k: wv[:, k, WK:WK + 128], lambda k: hT[:, k, 0:512], 512, rs + [B_hT[0]])
            act(kTh[:, 0:512], banks[bi][:, :], AF.Copy, [Bbank[bi]], [B_kTh[0]])

        fillers = {1: pq_fill, 2: pk_fill}
        if "r" in KSKIP:
            pq_fill()
            pk_fill()
        for ci in (() if "r" in KSKIP else (1, 2)):
            c0, n = TCH[ci]
            rope_tile(lambda k: wv[:, k, WQ:WQ + 128], lambda k: hT[:, k, c0:c0 + n], n, rs + [B_hT[ci]],
                      ropeW[:, 0, c0 - 512:c0 - 512 + n], ropeW[:, 1, c0 - 512:c0 - 512 + n],
                      qTh[0:64, 0, c0:c0 + n], [B_qTh[1]], out_hi=qTh[64:128, 1, c0:c0 + n], filler=fillers[ci])
            maybe_flush()
        for t in range(0 if "t" in KSKIP else 4):
            bi = nbank()
            linear_fm(bi, lambda k, t=t: hT[:, k, t * 128:(t + 1) * 128], lambda k: wv[:, k, WK:WK + 256], 256,
                      rs + [B_hT[0]])
            oi = nxt("ost", 2)
            act(ostage[oi][:, 0:256], banks[bi][:, 0:256], AF.Copy, [Bbank[bi]], [B_ost[oi]])
            dve_copy(Vh[:, t * 128:(t + 1) * 128], banks[bi][:, 128:256], [Bbank[bi]], [B_Vh[0]])
            if "o" not in KSKIP:
                dma_out(ndk_d[t * 128:(t + 1) * 128, hd * 128:(hd + 1) * 128], ostage[oi][:, 0:128], B_ost[oi])
                dma_out(ndv_d[t * 128:(t + 1) * 128, hd * 128:(hd + 1) * 128], ostage[oi][:, 128:256], B_ost[oi])
        for fc in range(0 if "f" in KSKIP else 4):
            f0 = fc * 512

            def vfill(f0=f0, fc=fc):
                bi = nbank()
                for t in range(4):
                    for k in range(KC):
                        mm(bi, banks[bi][:, t * 128:(t + 1) * 128], hfT[:, k, f0 + t * 128:f0 + (t + 1) * 128],
                           wv[:, k, WV:WV + 128], k == 0, k == KC - 1, rs + [B_hfT[fc]], wt=0.25)
                dve_copy(Vh[:, 512 + f0:512 + f0 + 512], banks[bi][:, :], [Bbank[bi]], [B_Vh[1]])

            rope_tile(lambda k: wv[:, k, WK:WK + 128], lambda k: hfT[:, k, f0:f0 + 512], 512, rs + [B_hfT[fc]],
                      ropeF[:, 0, f0:f0 + 512], ropeF[:, 1, f0:f0 + 512],
                      kTh[:, 512 + f0:512 + f0 + 512], [B_kTh[1]], filler=vfill)
            maybe_flush()

        mh1 = None
        if mh is not None:
            mod_compute(mh)
        if STAGE >= 1:
            mh1 = mod_load(1, hd, force=(0, 0, B_slot[0]))
        def diff_pair(q0_ap, q1_ap, n, blocks, qreads, out_ap, obufs):
            run_pair([(q0_ap, n, blocks, qreads), (q1_ap, n, blocks, qreads)])
            par = pstate["par"]
            pstate["par"] ^= 1
            maybe_flush(force_par=par)
            T1, B_T1 = T1s[par], B_T1s[par]
            oA, oB = OBK
            ta = nxt("tmp", 4)
            act(tmps[ta][:, 0:n], banks[LBK[0]][:, 0:n], AF.Copy, [Bbank[LBK[0]]], [B_tmp[ta]])
            tb = nxt("tmp", 4)
            act(tmps[tb][:, 0:n], banks[LBK[1]][:, 0:n], AF.Copy, [Bbank[LBK[1]]], [B_tmp[tb]])
            tc = nxt("tmp", 4)
            dve_copy(tmps[tc][:, 0:n], banks[oA][:, 0:n], [Bbank[oA]], [B_tmp[tc]])
            td = nxt("tmp", 4)
            dve_copy(tmps[td][:, 0:n], banks[oB][:, 0:n], [Bbank[oB]], [B_tmp[td]])
            dve_recip(tmps[ta][:, 0:n], tmps[ta][:, 0:n], [B_tmp[ta]], [B_tmp[ta]])
            dve_tt(T1[:, 0:n], tmps[tc][:, 0:n], tmps[ta][:, 0:n], ALU.mult, [B_tmp[tc], B_tmp[ta]], [B_T1])
            dve_recip(tmps[tb][:, 0:n], tmps[tb][:, 0:n], [B_tmp[tb]], [B_tmp[tb]])
            dve_tt(tmps[td][:, 0:n], tmps[td][:, 0:n], tmps[tb][:, 0:n], ALU.mult, [B_tmp[td], B_tmp[tb]],
                   [B_tmp[td]])
            dve_stt(T1[:, 0:n], tmps[td][:, 0:n], lamt[:, 4:5], T1[:, 0:n], ALU.mult, ALU.add,
                    [B_tmp[td], B_lam, B_T1], [B_T1])

            def tail():
                q = nxt("sq", 2)
                act(sqs[q][:, 0:n], T1[:, 0:n], AF.Square, [B_T1], [B_sq[q]])
                bs = 2
                mm(bs, banks[bs][:, 0:n], ones[:, :], sqs[q][:, 0:n], True, True, [B_sq[q], B_const])
                br = 3
                rstd_from_bank(bs, n, 128, br)
                dve_stt(out_ap, T1[:, 0:n], lamt[:, 5:6], banks[br][:, 0:n], ALU.mult, ALU.mult,
                        [B_T1, B_lam, Bbank[br]], obufs)
            pstate["pending"][par] = (pe_work["v"], tail)

        for pb in range(0 if "p" in KSKIP else 2):
            blocks = []
            for j in range(2):
                kc = pb * 256 + j * 128
                blocks.append((kTh[:, kc:kc + 128], [B_kTh[0]], Vh[:, kc:kc + 128], [B_Vh[0]], None))
            diff_pair(qTh[:, 0, pb * 256:(pb + 1) * 256], qTh[:, 1, pb * 256:(pb + 1) * 256], 256, blocks,
                      [B_qTh[0]], oT[:, hd, pb * 256:(pb + 1) * 256], [B_oT[0]])
        for ci in (() if "w" in KSKIP else (1, 2)):
            c0, n = TCH[ci]
            blocks = []
            for j in range(16):
                kc = 512 + j * 128
                blocks.append((kTh[:, kc:kc + 128], [B_kTh[1]], Vh[:, kc:kc + 128], [B_Vh[1]], None))
            for j in range(4):
                kc = 2560 + j * 128
                blocks.append((kTh[:, kc:kc + 128], [B_kTh[2]], Vh[:, kc:kc + 128], [B_Vh[2]], None))
            diff_pair(qTh[:, 0, c0:c0 + n], qTh[:, 1, c0:c0 + n], n, blocks, [B_qTh[1]],
                      oT[:, hd, c0:c0 + n], [B_oT[ci]])
        if mh1 is not None:
            mod_compute(mh1)
    flush_pending()
    P.alias([B_slot[1]], B_half)
    if KHEADS < 5:
        for piece in range(3 + KHEADS, 8):
            mod_piece(0, piece)
    mod_finish(0, parts=(1, 2, 3))

    def out_proj_l0(l):
        P.alias([B_yA, B_yA2], B_hfT)
        ybs = [(yA, [B_yA]), (yA2, [B_yA2]), (yA, [B_yA])]
        for ci, (c0, n) in enumerate(TCH):
            pn = PostNorm(c0, n, [B_xT[ci]], l, 0, 0 if ci == 0 else 1, ybuf=ybs[ci])
            for half in range(2):
                si = load_w(wo0_d[half].rearrange("p k c -> p (k c)"), 4096)
                wv = slots[si][:, 0:4096].rearrange("p (k c) -> p k c", k=8)
                for dl in range(4):
                    bi = nbank()
                    linear_fm(bi, lambda k: wv[:, k, dl * 128:(dl + 1) * 128], lambda k: oT[:, k, c0:c0 + n], n,
                              [B_slot[si], B_oT[ci]])
                    pn.add(bi, half * 4 + dl)
            pn.finish()

    def ffn(l, chunks, down_groups=None):
        for (c0, n, xb, hb, ab, c) in chunks:
            prenorm(xT, xb, c0, n, hT, [hb], c0, l, 1, c)
        for piece in range(8):
            nf = min(3, NF - 3 * piece)
            si = load_w(wgu_d[l, piece, :, 0:nf].rearrange("p f k c -> p (f k c)"), nf * 2048)
            wv = slots[si][:, 0:nf * 2048].rearrange("p (f k c) -> p f k c", f=nf, k=8)
            for fl in range(nf):
                f = 3 * piece + fl
                for (c0, n, xb, hb, ab, c) in chunks:
                    bg = nbank()
                    linear_fm(bg, lambda k: wv[:, fl, k, 0:128], lambda k: hT[:, k, c0:c0 + n], n, [B_slot[si], hb])
                    bu = nbank()
                    linear_fm(bu, lambda k: wv[:, fl, k, 128:256], lambda k: hT[:, k, c0:c0 + n], n, [B_slot[si], hb])
                    t = nxt("tmp", 4)
                    act(tmps[t][:, 0:n], banks[bg][:, 0:n], AF.Silu, [Bbank[bg]], [B_tmp[t]])
                    dve_tt(aT[:, f, c0:c0 + n], tmps[t][:, 0:n], banks[bu][:, 0:n], ALU.mult,
                           [B_tmp[t], Bbank[bu]], [ab])
        if down_groups is None:
            down_groups = [[(c0, n, xb, [ab], c, 0)] for (c0, n, xb, hb, ab, c) in chunks]
        for grp in down_groups:
            pns = [PostNorm(c0, n, xb, l, 1, c, ybuf=(ytmp[:, :, y0:y0 + n], B_hT))
                   for (c0, n, xb, abl, c, y0) in grp]
            for j in range(4):
                si = load_w(wdn_d[l, j].rearrange("p d f c -> p (d f c)"), 5632)
                wv = slots[si][:, 0:5632].rearrange("p (d f c) -> p d f c", d=2, f=22)
                for dl in range(2):
                    for gi, (c0, n, xb, abl, c, y0) in enumerate(grp):
                        bi = nbank()
                        for f in range(NF):
                            mm(bi, banks[bi][:, 0:n], wv[:, dl, f, :], aT[:, f, c0:c0 + n], f == 0, f == NF - 1,
                               [B_slot[si]] + abl, wt=n / 512.0)
                        pns[gi].add(bi, 2 * j + dl)
            for pn in pns:
                pn.finish()

    if STAGE >= -1:
        out_proj_l0(0)
    if STAGE >= 1:
        mod_finish(1)
    P.alias(B_aT, B_hfT + B_oT + [B_yA, B_yA2])
    if STAGE >= 0:
        ffn(0, [(TCH[i][0], TCH[i][1], [B_xT[i]], B_hT[i], B_aT[i], 0 if i == 0 else 1) for i in range(3)],
            down_groups=[[(0, 512, [B_xT[0]], [B_aT[0]], 0, 0), (512, 128, [B_xT[1]], [B_aT[1]], 1, 512)],
                         [(640, 512, [B_xT[1], B_xT[2]], [B_aT[1], B_aT[2]], 1, 0),
                          (1152, 128, [B_xT[2]], [B_aT[2]], 1, 512)]])

    if STAGE >= 2:
        P.alias(B_qTs + [B_kTs] + B_oS, B_aT)
        for ci, (c0, n) in enumerate(TCH):
            prenorm(xT, [B_xT[ci]], c0, n, hT, [B_hT[ci]], c0, 1, 0, 0 if ci == 0 else 1)
        OWN0 = 640
        B_hown = [B_hT[1], B_hT[2]]
        P.op("dve", lambda e: e.memset(kTs[:, :, :], 0.0), reads=[], writes=[B_kTs])
        VS = Vh[:, 0:3584].rearrange("p (t c) -> p t c", t=14)
        B_VS = Buf("VS")
        P.alias([B_VS], list(B_Vh))
        for half in range(2):
            sa = load_w(ws1q_d[half].rearrange("p k c -> p (k c)"), 4096)
            wa = slots[sa][:, 0:4096].rearrange("p (k c) -> p k c", k=8)
            for cl in range(4):
                cq = half * 4 + cl

                def pfill(cl=cl, cq=cq):
                    bi = nbank()
                    linear_fm(bi, lambda k: wa[:, k, cl * 128:(cl + 1) * 128], lambda k: hT[:, k, 0:512], 512,
                              [B_slot[sa], B_hT[0]])
                    act(qTs[:, cq, 0:512], banks[bi][:, :], AF.Copy, [Bbank[bi]], [B_qTs[0]])

                rope_tile(lambda k: wa[:, k, cl * 128:(cl + 1) * 128], lambda k: hT[:, k, OWN0:OWN0 + 512], 512,
                          [B_slot[sa]] + B_hown, ropeW[:, 0, 128:640], ropeW[:, 1, 128:640],
                          qTs[:, cq, 512:1024], [B_qTs[1]], filler=pfill)
        sk = load_w(ws1k_d.rearrange("p k c -> p (k c)"), 4096)
        wk = slots[sk][:, 0:4096].rearrange("p (k c) -> p k c", k=8)
        rsk = [B_slot[sk]]
        for p in range(2):
            bi = nbank()
            linear_fm(bi, lambda k: wk[:, k, p * 128:(p + 1) * 128], lambda k: hT[:, k, 0:512], 512, rsk + [B_hT[0]])
            act(kTs[0:64, 2 * p, 0:512], banks[bi][0:64, :], AF.Copy, [Bbank[bi]], [B_kTs])
            act(kTs[64:128, 2 * p + 1, 0:512], banks[bi][64:128, :], AF.Copy, [Bbank[bi]], [B_kTs])
            for ci in (1, 2):
                c0, n = TCH[ci]
                rope_tile(lambda k: wk[:, k, p * 128:(p + 1) * 128], lambda k: hT[:, k, c0:c0 + n], n,
                          rsk + [B_hT[ci]], ropeW[:, 0, c0 - 512:c0 - 512 + n], ropeW[:, 1, c0 - 512:c0 - 512 + n],
                          kTs[0:64, 2 * p, c0:c0 + n], [B_kTs], out_hi=kTs[64:128, 2 * p + 1, c0:c0 + n])
        for t in range(4):
            bi = nbank()
            for k in range(KC):
                mm(bi, banks[bi][:, 0:256], hT[:, k, t * 128:(t + 1) * 128], wk[:, k, 0:256], k == 0, k == KC - 1,
                   rsk + [B_hT[0]])
            for k in range(KC):
                mm(bi, banks[bi][:, 256:512], hT[:, k, t * 128:(t + 1) * 128], wk[:, k, 256:512], k == 0, k == KC - 1,
                   rsk + [B_hT[0]])
            oi = nxt("ost", 2)
            act(ostage[oi][:, 0:512], banks[bi][:, :], AF.Copy, [Bbank[bi]], [B_ost[oi]])
            dve_copy(VS[:, t, :], banks[bi][:, 256:512], [Bbank[bi]], [B_VS])
            dma_out(nsk_d[t * 128:(t + 1) * 128, :], ostage[oi][:, 0:256], B_ost[oi])
            dma_out(nsv_d[t * 128:(t + 1) * 128, :], ostage[oi][:, 256:512], B_ost[oi])
        for t in range(6):
            bi = nbank()
            c0 = 512 + t * 128
            for k in range(KC):
                mm(bi, banks[bi][:, 0:256], hT[:, k, c0:c0 + 128], wk[:, k, 256:512], k == 0, k == KC - 1,
                   rsk + [B_hT[1] if t < 4 else B_hT[2]])
            dve_copy(VS[:, 4 + t, :], banks[bi][:, 0:256], [Bbank[bi]], [B_VS])
        dma_in("sp", cst[:, :].rearrange("p (j c) -> p j c", j=4), cks_d, B_cst)
        for p in range(2):
            bi = nbank()
            for j in range(4):
                P.op("pe", lambda e, bi=bi, j=j, p=p: e.transpose(
                    banks[bi][:, j * 128:(j + 1) * 128], cst[:, j * 256 + p * 128:j * 256 + (p + 1) * 128], ident[:, :]),
                    reads=[B_cst, B_const], writes=[Bbank[bi]])
            act(kTs[0:64, 2 * p, 1280:1792], banks[bi][0:64, :], AF.Copy, [Bbank[bi]], [B_kTs])
            act(kTs[64:128, 2 * p + 1, 1280:1792], banks[bi][64:128, :], AF.Copy, [Bbank[bi]], [B_kTs])
        xi = nxt("xin", 2)
        dma_in("sp", xin[xi][:, :], cvs_d.rearrange("p j c -> p (j c)"), B_xin[xi])
        dve_copy(Vh[:, 2560:3584], xin[xi][:, :], [B_xin[xi]], [B_VS])

        def swa_finalize(bo, bl, ng, qn, kv, g0, oS_col0, obuf):
            r0 = (kv % 2) * 64
            pr = kv // 2
            n = ng * qn
            hh0 = kv * 4 + g0
            t = nxt("tmp", 4)
            dve_copy(tmps[t][r0:r0 + 64, 0:n], banks[bl][r0:r0 + 64, 0:n], [Bbank[bl]], [B_tmp[t]])
            to = nxt("tmp", 4)
            dve_copy(tmps[to][r0:r0 + 64, 0:n], banks[bo][r0:r0 + 64, 0:n], [Bbank[bo]], [B_tmp[to]])
            lv = tmps[t][r0:r0 + 64, 0:n].rearrange("p (g q) -> p g q", g=ng)
            sk = sinkE[r0:r0 + 64, hh0:hh0 + ng]
            sk_b = bass.AP(sk.tensor, sk.offset, [list(sk.ap[0]), list(sk.ap[1]), [0, qn]])
            dve_tt(lv, lv, sk_b, ALU.add, [B_tmp[t], B_lam], [B_tmp[t]])
            act(tmps[t][r0:r0 + 64, 0:n], tmps[t][r0:r0 + 64, 0:n], AF.Ln, [B_tmp[t]], [B_tmp[t]])
            act(tmps[t][r0:r0 + 64, 0:n], tmps[t][r0:r0 + 64, 0:n], AF.Exp, [B_tmp[t]], [B_tmp[t]], scale=-1.0)
            ov = tmps[to][r0:r0 + 64, 0:n].rearrange("p (g q) -> p g q", g=ng)
            dve_tt(oS[r0:r0 + 64, pr * 4 + g0:pr * 4 + g0 + ng, oS_col0:oS_col0 + qn], ov, lv, ALU.mult,
                   [B_tmp[to], B_tmp[t]], [obuf], eng="pool")

        def run_swa(joblist):
            for i in range(0, len(joblist), 2):
                grp = joblist[i:i + 2]
                par = run_pair([g[0] for g in grp])
                for slot, g in enumerate(grp):
                    swa_finalize(obank(slot, par), LBK[slot], *g[1])

        jl = []
        for pb in range(2):
            for kv in range(4):
                p = kv // 2
                for gh in range(2):
                    q_ap = qTs[:, p * 4 + gh * 2:p * 4 + gh * 2 + 2, pb * 256:(pb + 1) * 256]
                    blocks = []
                    for j in range(2):
                        kc = pb * 256 + j * 128
                        blocks.append((kTs[:, kv, kc:kc + 128], [B_kTs],
                                       VS[:, pb * 2 + j, p * 128:(p + 1) * 128], [B_VS], None))
                    jl.append(((q_ap, 512, blocks, [B_qTs[0]]), (2, 256, kv, gh * 2, pb * 256, B_oS[0])))
        for qb in range(4):
            for kv in range(4):
                p = kv // 2
                q_ap = qTs[:, p * 4:p * 4 + 4, 512 + qb * 128:512 + (qb + 1) * 128]
                blocks = []
                for j in range(4):
                    kc = 1280 + j * 128
                    blocks.append((kTs[:, kv, kc:kc + 128], [B_kTs],
                                   VS[:, 10 + j, p * 128:(p + 1) * 128], [B_VS], None))
                for dj, mi in ((0, 2 * qb), (1, None), (2, 2 * qb + 1)):
                    w = qb + dj
                    kc = 512 + w * 128
                    blocks.append((kTs[:, kv, kc:kc + 128], [B_kTs],
                                   VS[:, 4 + w, p * 128:(p + 1) * 128], [B_VS], mi))
                jl.append(((q_ap, 512, blocks, [B_qTs[1]]), (4, 128, kv, 0, 512 + qb * 128, B_oS[1])))
        run_swa(jl)
        L1CH = [(0, 512, [B_xT[0]], 0, 0, B_oS[0]), (OWN0, 512, [B_xT[1], B_xT[2]], 1, 512, B_oS[1])]
        P.alias([B_yA], B_qTs)
        for (c0, n, xb, c, oc0, ob) in L1CH:
            pn = PostNorm(c0, n, xb, 1, 0, c, ybuf=(yA, [B_yA]) if c0 == 0 else None)
            for half in range(2):
                si = load_w(wo1_d[half].rearrange("p k c -> p (k c)"), 4096)
                wv = slots[si][:, 0:4096].rearrange("p (k c) -> p k c", k=8)
                for dl in range(4):
                    bi = nbank()
                    linear_fm(bi, lambda k: wv[:, k, dl * 128:(dl + 1) * 128], lambda k: oS[:, k, oc0:oc0 + n], n,
                              [B_slot[si], ob])
                    pn.add(bi, half * 4 + dl)
            pn.finish()
        P.alias(B_aT, B_qTs + [B_kTs] + B_oS + [B_yA])
        if STAGE >= 3:
            ffn(1, [(0, 512, [B_xT[0]], B_hT[0], B_aT[0], 0),
                    (OWN0, 512, [B_xT[1], B_xT[2]], B_hT[1], B_aT[1], 1)])

    OWN0 = 640
    for t in range(8):
        c0 = t * 128 if t < 4 else OWN0 + (t - 4) * 128
        xb = [B_xT[0]] if t < 4 else [B_xT[1], B_xT[2]]
        xi = nxt("xin", 2)
        for half in range(2):
            bi = nbank()
            for j in range(4):
                cidx = half * 4 + j
                P.op("pe", lambda e, bi=bi, j=j, cidx=cidx, c0=c0: e.transpose(
                    banks[bi][:, j * 128:(j + 1) * 128], xT[:, cidx, c0:c0 + 128], ident[:, :]),
                    reads=xb + [B_const], writes=[Bbank[bi]])
            if half == 0:
                act(xin[xi][:, 0:512], banks[bi][:, :], AF.Copy, [Bbank[bi]], [B_xin[xi]])
            else:
                dve_copy(xin[xi][:, 512:1024], banks[bi][:, :], [Bbank[bi]], [B_xin[xi]])
        dst = yp_d[t * 128:(t + 1) * 128, :] if t < 4 else ys_d[(t - 4) * 128:(t - 3) * 128, :]
        dma_out(dst, xin[xi][:, :], B_xin[xi])

    P.finish()
    stats = P.emit()
    return nc, es, stats


def _rope_tables(pos):
    pos = np.asarray(pos)
    row = (pos // 64).astype(np.float32)
    col = (pos % 64).astype(np.float32)
    nf = 16
    inv = (np.float32(10000.0) ** (-np.arange(nf, dtype=np.float32) / np.float32(nf))).astype(np.float32)
    ar = row[:, None] * inv[None, :]
    ac = col[:, None] * inv[None, :]
    ang = np.concatenate([ar, ar, ac, ac], axis=-1).astype(np.float32)
    cos = np.cos(ang).astype(np.float32)
    sin = np.sin(ang).astype(np.float32)
    sign = np.concatenate([-np.ones(16), np.ones(16), -np.ones(16), np.ones(16)]).astype(np.float32)
    sin = sin * sign[None, :]
    cos2 = np.concatenate([cos, cos], axis=1).T
    sin2 = np.concatenate([sin, sin], axis=1).T
    return np.ascontiguousarray(cos2), np.ascontiguousarray(sin2)


_ROTSRC = np.concatenate([np.arange(16, 32), np.arange(0, 16), np.arange(48, 64), np.arange(32, 48)])


def _rot_cols(w):
    n = w.shape[1] // 64
    idx = (np.arange(n)[:, None] * 64 + _ROTSRC[None, :]).reshape(-1)
    return w[:, idx]


def _kmaj(w):
    return np.ascontiguousarray(w.reshape(8, 128, -1).transpose(1, 0, 2))


_PROG_CACHE = {}


def prep_inputs(x_prompt, x_sample, cache_diff_k, cache_diff_v, cache_swa_k, cache_swa_v, c, c_ctx,
                w_mod, b_mod, norm_g, w_qkv_diff, diff_lambda, diff_subln_g, w_o_diff,
                w_qkv_swa, swa_sink, w_o_swa, w_gate, w_up, w_down):
    f = np.float32
    A = lambda a: np.ascontiguousarray(np.asarray(a, dtype=f))
    x_prompt, x_sample = A(x_prompt), A(x_sample)
    cache_diff_k, cache_diff_v = A(cache_diff_k), A(cache_diff_v)
    cache_swa_k, cache_swa_v = A(cache_swa_k), A(cache_swa_v)
    c, c_ctx = A(c), A(c_ctx)
    w_mod, b_mod, norm_g = A(w_mod), A(b_mod), A(norm_g)
    w_qkv_diff, diff_lambda, diff_subln_g, w_o_diff = A(w_qkv_diff), A(diff_lambda), A(diff_subln_g), A(w_o_diff)
    w_qkv_swa, swa_sink, w_o_swa = A(w_qkv_swa), A(swa_sink), A(w_o_swa)
    w_gate, w_up, w_down = A(w_gate), A(w_up), A(w_down)

    wmod = np.ascontiguousarray(
        w_mod.reshape(2, 8, 128, 8, 768).transpose(0, 3, 2, 1, 4))
    wq, wk, wv = w_qkv_diff[0][:, 0:1024], w_qkv_diff[0][:, 1024:2048], w_qkv_diff[0][:, 2048:3072]
    wd0 = np.empty((8, 128, 8, 384), f)
    for h in range(8):
        s = slice(h * 128, (h + 1) * 128)
        wd0[h] = _kmaj(np.concatenate([wq[:, s], wk[:, s], wv[:, s]], axis=1))
    permT = np.zeros((128, 128), f)
    for blk in range(2):
        for o in range(64):
            permT[blk * 64 + _ROTSRC[o], blk * 64 + o] = 1.0
    wo0 = np.stack([_kmaj(w_o_diff[0][:, 0:512]), _kmaj(w_o_diff[0][:, 512:1024])])
    ws = w_qkv_swa[0]
    sq_cols = []
    for p in range(2):
        for g in range(4):
            for kvl in range(2):
                hh = (2 * p + kvl) * 4 + g
                sq_cols.append(np.arange(hh * 64, (hh + 1) * 64))
    sq_cols = np.concatenate(sq_cols)
    wsq = ws[:, 0:1024]
    wsq_p = wsq[:, sq_cols]
    ws1q = np.stack([_kmaj(wsq_p[:, 0:512]), _kmaj(wsq_p[:, 512:1024])])
    wsk, wsv = ws[:, 1024:1280], ws[:, 1280:1536]
    ws1k = _kmaj(np.concatenate([wsk, wsv], axis=1))
    wos_p = w_o_swa[0][sq_cols, :]
    wo1 = np.stack([_kmaj(wos_p[:, 0:512]), _kmaj(wos_p[:, 512:1024])])
    wgu_f = np.zeros((2, 24, 128, 8, 256), f)
    for l in range(2):
        g_ = w_gate[l].reshape(8, 128, 22, 128).transpose(2, 1, 0, 3)
        u_ = w_up[l].reshape(8, 128, 22, 128).transpose(2, 1, 0, 3)
        wgu_f[l, 0:22, :, :, 0:128] = g_
        wgu_f[l, 0:22, :, :, 128:256] = u_
    wgu = np.ascontiguousarray(wgu_f.reshape(2, 8, 3, 128, 8, 256).transpose(0, 1, 3, 2, 4, 5))
    wdn = np.ascontiguousarray(w_down.reshape(2, 22, 128, 4, 2, 128).transpose(0, 3, 2, 4, 1, 5))
    ident = np.eye(128, dtype=f)
    cosf, sinf = _rope_tables(np.arange(TFULL))
    ropef = np.ascontiguousarray(np.stack([cosf, sinf], axis=1))

    normg_l = norm_g.reshape(8, 8, 128).transpose(2, 0, 1).reshape(128, 64)
    bmod_l = b_mod.reshape(2, 48, 128).transpose(2, 0, 1).reshape(128, 96)
    subg_l = diff_subln_g.reshape(128, 1)
    lam_l = np.broadcast_to(diff_lambda.reshape(1, 256), (128, 256))
    sink_l = np.broadcast_to(swa_sink.reshape(1, 16), (128, 16))

    in_maps = []
    for core in range(NCORES):
        b, ch = core // 4, core % 4
        xp = x_prompt[2 * core:2 * core + 2].reshape(512, D)
        pos = np.arange(ch * 512 - 128, ch * 512 + 640)
        valid = (pos >= 0) & (pos < 2048)
        xw = np.zeros((TW, D), f)
        xw[valid] = x_sample[b, pos[valid]]
        cosw, sinw = _rope_tables(np.clip(pos, 0, 2047))
        ropew = np.ascontiguousarray(np.stack([cosw, sinw], axis=1))
        maskb = np.zeros((128, 8, 128), f)
        for qb in range(4):
            for side in range(2):
                w = qb + (0 if side == 0 else 2)
                kpos = pos[w * 128:(w + 1) * 128]
                qpos = pos[(qb + 1) * 128:(qb + 2) * 128]
                ok = ((kpos[:, None] >= 0) & (kpos[:, None] < 2048)
                      & (np.abs(qpos[None, :] - kpos[:, None]) <= 128))
                maskb[:, 2 * qb + side, :] = np.where(ok, 0.0, -30000.0)
        cond_l = np.stack([c_ctx, c[b]], axis=-1).reshape(8, 128, 2).transpose(1, 0, 2).reshape(128, 16)
        sm = np.ascontiguousarray(np.concatenate([cond_l, normg_l, bmod_l, subg_l, lam_l, sink_l], axis=1), dtype=f)
        ckd = np.ascontiguousarray(cache_diff_k[b, 0].reshape(4, 128, 8, 128).transpose(2, 1, 0, 3))
        cvd = np.ascontiguousarray(cache_diff_v[b, 0].reshape(4, 128, 8, 128).transpose(2, 1, 0, 3))
        cks = np.ascontiguousarray(cache_swa_k[b, 0].reshape(4, 128, 256).transpose(1, 0, 2))
        cvs = np.ascontiguousarray(cache_swa_v[b, 0].reshape(4, 128, 256).transpose(1, 0, 2))
        in_maps.append(dict(xp=np.ascontiguousarray(xp), xw=xw, xf=x_sample[b], ckd=ckd, cvd=cvd, cks=cks, cvs=cvs,
                            sm=sm, ident=ident, permT=permT, ropew=ropew, ropef=ropef, maskb=maskb, wmod=wmod, wd0=wd0,
                            wo0=wo0,
                            ws1q=ws1q, ws1k=ws1k, wo1=wo1, wgu=wgu, wdn=wdn))
    return in_maps


def kernel(**inputs):
    f = np.float32
    in_maps = prep_inputs(**inputs)
    if "nc" not in _PROG_CACHE:
        nc, es, stats = build_program()
        _PROG_CACHE["nc"] = (nc, es)
        if os.environ.get("KVERBOSE"):
            print("ops per engine:", stats)
    nc, _ = _PROG_CACHE["nc"]
    res = run_bass_kernel_spmd(nc, in_maps, core_ids=list(range(NCORES)))
    R = res.results
    y_prompt = np.concatenate([R[i]["yp"].reshape(2, 256, D) for i in range(NCORES)], axis=0)
    y_sample = np.stack([np.concatenate([R[b * 4 + ch]["ys"] for ch in range(4)], axis=0) for b in range(2)], axis=0)
    ndk = np.concatenate([R[i]["ndk"].reshape(2, 1, 256, 8, 128) for i in range(NCORES)], axis=0)
    ndv = np.concatenate([R[i]["ndv"].reshape(2, 1, 256, 8, 128) for i in range(NCORES)], axis=0)
    nsk = np.concatenate([R[i]["nsk"].reshape(2, 1, 256, 4, 64) for i in range(NCORES)], axis=0)
    nsv = np.concatenate([R[i]["nsv"].reshape(2, 1, 256, 4, 64) for i in range(NCORES)], axis=0)
    return (y_prompt.astype(f), y_sample.astype(f), ndk.astype(f), ndv.astype(f), nsk.astype(f), nsv.astype(f))
```
